# Optimizing a Trainium2 kernel written in Bass

```python
import jax, jax.numpy as jnp
from jax import lax
import numpy as np

D_MODEL = 2048
BATCH = 8
SEQ = 2048
DEPTH = 2

N_A_LAYERS = DEPTH // 2
N_B_LAYERS = DEPTH - N_A_LAYERS

RW_HEAD = 64
RW_HEADS = D_MODEL // RW_HEAD
RW_DECAY_LORA = max(32, int(round(1.8 * D_MODEL ** 0.5 / 32)) * 32)
RW_A_LORA = max(32, int(round(1.8 * D_MODEL ** 0.5 / 32)) * 32)
RW_GATE_LORA = max(32, int(round(0.6 * D_MODEL ** 0.8 / 32)) * 32)
RW_GN_EPS = RW_HEAD * 1e-5

MLA_HEAD_V = 128
MLA_HEADS = D_MODEL // MLA_HEAD_V
MLA_NOPE = 128
MLA_ROPE = 64
MLA_Q_LORA = D_MODEL // 4
MLA_KV_LORA = 512
ROPE_THETA = 10000.0
BLOCK_Q = 128

FFN_HIDDEN = 4 * D_MODEL
NORM_EPS = 1e-6
N_MOD = 6

kernel_name = "yoco_rwkv7_mla_sandwich_adaln"


def rms_norm(x, g, eps=NORM_EPS):
    xf = x.astype(jnp.float32)
    y = xf * lax.rsqrt(jnp.mean(xf * xf, axis=-1, keepdims=True) + eps)
    return (y * g.astype(jnp.float32)).astype(x.dtype)


def token_shift(h):
    return jnp.pad(h, ((0, 0), (1, 0), (0, 0)))[:, :-1]


def apply_rope(t, cos, sin):
    half = t.shape[-1] // 2
    t1, t2 = t[..., :half], t[..., half:]
    return jnp.concatenate([t1 * cos - t2 * sin, t2 * cos + t1 * sin], axis=-1)


def squared_relu_mlp(h, w_up, w_down):
    u = jax.nn.relu(h @ w_up)
    return (u * u) @ w_down


def rwkv7_time_mix(h, mu, w_rkv, w0, w1, w2, a0, a1, a2, g1, g2, k_k, k_a, r_k, ln_x, w_o):
    B, T, D = h.shape
    H, N = RW_HEADS, RW_HEAD
    f32 = jnp.float32
    xx = token_shift(h) - h
    xs = h[None] + xx[None] * mu[:, None, None, :]
    rkv = jnp.einsum('jbtd,jde->jbte', xs[:3], w_rkv)
    r, k, v = rkv[0], rkv[1], rkv[2]
    xw, xa, xg = xs[3], xs[4], xs[5]
    w_log = -jax.nn.softplus(-(w0 + jnp.tanh(xw @ w1) @ w2)) - 0.5
    decay = jnp.exp(-jnp.exp(w_log.astype(f32)))
    a = jax.nn.sigmoid(a0 + (xa @ a1) @ a2)
    g = jax.nn.sigmoid(xg @ g1) @ g2

    kk = (k * k_k).reshape(B, T, H, N).astype(f32)
    kk = kk / jnp.maximum(jnp.sqrt(jnp.sum(kk * kk, axis=-1, keepdims=True)), 1e-12)
    k = k * (1.0 + (a - 1.0) * k_a)

    def heads(t):
        return t.reshape(B, T, H, N).astype(f32)

    rh, kh, vh, ah, wh = heads(r), heads(k), heads(v), heads(a), heads(decay)
    seq_first = lambda t: jnp.moveaxis(t, 1, 0)

    def step(S, inp):
        r_t, w_t, k_t, v_t, kk_t, a_t = inp
        sa = jnp.einsum('bhvk,bhk->bhv', S, -kk_t)
        S = (S * w_t[:, :, None, :]
             + sa[..., None] * (kk_t * a_t)[:, :, None, :]
             + v_t[..., None] * k_t[:, :, None, :])
        y_t = jnp.einsum('bhvk,bhk->bhv', S, r_t)
        return S, y_t

    S0 = jnp.zeros((B, H, N, N), f32)
    _, y = lax.scan(step, S0, (seq_first(rh), seq_first(wh), seq_first(kh),
                               seq_first(vh), seq_first(kk), seq_first(ah)))
    y = jnp.moveaxis(y, 0, 1)

    mean = jnp.mean(y, axis=-1, keepdims=True)
    var = jnp.mean(jnp.square(y - mean), axis=-1, keepdims=True)
    y = ((y - mean) * lax.rsqrt(var + RW_GN_EPS)).reshape(B, T, D)
    y = y * ln_x[0].astype(f32) + ln_x[1].astype(f32)
    bonus = jnp.sum(rh * kh * r_k.astype(f32), axis=-1, keepdims=True) * vh
    out = ((y + bonus.reshape(B, T, D)).astype(h.dtype)) * g
    return out @ w_o


def mla_shared_kv(x, kv_in_g, kv_down, kv_norm, kv_uk, kv_uv, cos, sin):
    hs = rms_norm(x, kv_in_g)
    ckr = hs @ kv_down
    c_kv = rms_norm(ckr[..., :MLA_KV_LORA], kv_norm)
    k_rope = apply_rope(ckr[..., MLA_KV_LORA:], cos, sin)
    k_nope = jnp.einsum('bsc,chd->bshd', c_kv, kv_uk)
    v = jnp.einsum('bsc,chd->bshd', c_kv, kv_uv)
    return k_nope, k_rope, v


def mla_attention(h, w_dq, q_norm, w_uq, w_o, k_nope, k_rope, v, cos, sin):
    B, S, _ = h.shape
    cq = rms_norm(h @ w_dq, q_norm)
    q = jnp.einsum('bsc,chd->bshd', cq, w_uq)
    q_nope = q[..., :MLA_NOPE]
    q_rope = apply_rope(q[..., MLA_NOPE:], cos[:, :, None, :], sin[:, :, None, :])
    nb = S // BLOCK_Q
    scale = (MLA_NOPE + MLA_ROPE) ** -0.5
    key_idx = jnp.arange(S)
    neg = jnp.finfo(jnp.float32).min

    def to_blocks(t):
        return t.reshape(B, nb, BLOCK_Q, *t.shape[2:]).swapaxes(0, 1)

    def attend(args):
        qn, qr, start = args
        s = (jnp.einsum('bqhd,bkhd->bhqk', qn, k_nope)
             + jnp.einsum('bqhd,bkd->bhqk', qr, k_rope)).astype(jnp.float32) * scale
        mask = (start + jnp.arange(BLOCK_Q))[:, None] >= key_idx[None, :]
        s = jnp.where(mask[None, None], s, neg)
        p = jax.nn.softmax(s, axis=-1).astype(v.dtype)
        return jnp.einsum('bhqk,bkhd->bqhd', p, v)

    o = lax.map(attend, (to_blocks(q_nope), to_blocks(q_rope), jnp.arange(nb) * BLOCK_Q))
    o = o.swapaxes(0, 1).reshape(B, S, MLA_HEADS * MLA_HEAD_V)
    return o @ w_o


def setup_inputs(seed: int = 0) -> dict:
    key = jax.random.key(seed)
    ks = iter(jax.random.split(key, 48))
    f32 = jnp.float32
    D, F = D_MODEL, FFN_HIDDEN
    NA, NB = N_A_LAYERS, N_B_LAYERS
    H, N = RW_HEADS, RW_HEAD
    DL, AL, GL = RW_DECAY_LORA, RW_A_LORA, RW_GATE_LORA
    QH, QK = MLA_HEADS, MLA_NOPE + MLA_ROPE

    def nrm(shape, scale):
        return jax.random.normal(next(ks), shape, f32) * scale

    x = nrm((BATCH, SEQ, D), 1.0)
    c = nrm((BATCH, D), 1.0)
    offset = jax.random.randint(next(ks), (BATCH, 1), 0, SEQ, dtype=jnp.int32)
    positions = offset + jnp.arange(SEQ, dtype=jnp.int32)[None, :]

    ada_w = nrm((DEPTH, D, N_MOD * D), D ** -0.5)
    ada_b = nrm((DEPTH, N_MOD * D), 0.02)
    norm_g = 1.0 + nrm((DEPTH, 4, D), 0.05)
    mlp_up = nrm((DEPTH, D, F), D ** -0.5)
    mlp_down = nrm((DEPTH, F, D), F ** -0.5)

    rw_mu = jax.random.uniform(next(ks), (NA, 6, D), f32)
    rw_rkv = nrm((NA, 3, D, D), D ** -0.5)
    rw_w0 = jax.random.uniform(next(ks), (NA, D), f32, -6.5, -1.5)
    rw_w1 = nrm((NA, D, DL), D ** -0.5)
    rw_w2 = nrm((NA, DL, D), 0.1 * DL ** -0.5)
    rw_a0 = nrm((NA, D), 0.1)
    rw_a1 = nrm((NA, D, AL), D ** -0.5)
    rw_a2 = nrm((NA, AL, D), 0.5 * AL ** -0.5)
    rw_g1 = nrm((NA, D, GL), D ** -0.5)
    rw_g2 = nrm((NA, GL, D), GL ** -0.5)
    rw_kk = 0.85 + nrm((NA, D), 0.05)
    rw_ka = 1.0 + nrm((NA, D), 0.05)
    rw_rk = nrm((NA, H, N), 0.1)
    rw_lnx = jnp.stack([1.0 + nrm((NA, D), 0.05), nrm((NA, D), 0.02)], axis=1)
    rw_o = nrm((NA, D, D), D ** -0.5)

    mla_dq = nrm((NB, D, MLA_Q_LORA), D ** -0.5)
    mla_qnorm = 1.0 + nrm((NB, MLA_Q_LORA), 0.05)
    mla_uq = nrm((NB, MLA_Q_LORA, QH, QK), MLA_Q_LORA ** -0.5)
    mla_o = nrm((NB, QH * MLA_HEAD_V, D), (QH * MLA_HEAD_V) ** -0.5)

    kv_in_g = 1.0 + nrm((D,), 0.05)
    kv_down = nrm((D, MLA_KV_LORA + MLA_ROPE), D ** -0.5)
    kv_norm = 1.0 + nrm((MLA_KV_LORA,), 0.05)
    kv_uk = nrm((MLA_KV_LORA, QH, MLA_NOPE), MLA_KV_LORA ** -0.5)
    kv_uv = nrm((MLA_KV_LORA, QH, MLA_HEAD_V), MLA_KV_LORA ** -0.5)

    return {
        "x": x, "c": c, "positions": positions,
        "ada_w": ada_w, "ada_b": ada_b, "norm_g": norm_g,
        "mlp_up": mlp_up, "mlp_down": mlp_down,
        "rw_mu": rw_mu, "rw_rkv": rw_rkv, "rw_w0": rw_w0, "rw_w1": rw_w1, "rw_w2": rw_w2,
        "rw_a0": rw_a0, "rw_a1": rw_a1, "rw_a2": rw_a2, "rw_g1": rw_g1, "rw_g2": rw_g2,
        "rw_kk": rw_kk, "rw_ka": rw_ka, "rw_rk": rw_rk, "rw_lnx": rw_lnx, "rw_o": rw_o,
        "mla_dq": mla_dq, "mla_qnorm": mla_qnorm, "mla_uq": mla_uq, "mla_o": mla_o,
        "kv_in_g": kv_in_g, "kv_down": kv_down, "kv_norm": kv_norm, "kv_uk": kv_uk, "kv_uv": kv_uv,
    }


def reference(x, c, positions, ada_w, ada_b, norm_g, mlp_up, mlp_down,
              rw_mu, rw_rkv, rw_w0, rw_w1, rw_w2, rw_a0, rw_a1, rw_a2, rw_g1, rw_g2,
              rw_kk, rw_ka, rw_rk, rw_lnx, rw_o,
              mla_dq, mla_qnorm, mla_uq, mla_o,
              kv_in_g, kv_down, kv_norm, kv_uk, kv_uv):
    B, S, D = x.shape
    inv_freq = 1.0 / (ROPE_THETA ** (jnp.arange(0, MLA_ROPE, 2, dtype=jnp.float32) / MLA_ROPE))
    ang = positions.astype(jnp.float32)[..., None] * inv_freq
    cos, sin = jnp.cos(ang).astype(x.dtype), jnp.sin(ang).astype(x.dtype)
    c_act = jax.nn.silu(c)
    shared = None

    for l in range(DEPTH):
        mod = (c_act @ ada_w[l] + ada_b[l]).reshape(B, N_MOD, 1, D)
        shift_m, scale_m, gate_m = mod[:, 0], mod[:, 1], mod[:, 2]
        shift_f, scale_f, gate_f = mod[:, 3], mod[:, 4], mod[:, 5]

        h = rms_norm(x, norm_g[l, 0]) * (1.0 + scale_m) + shift_m
        if l < N_A_LAYERS:
            i = l
            y = rwkv7_time_mix(h, rw_mu[i], rw_rkv[i], rw_w0[i], rw_w1[i], rw_w2[i],
                               rw_a0[i], rw_a1[i], rw_a2[i], rw_g1[i], rw_g2[i],
                               rw_kk[i], rw_ka[i], rw_rk[i], rw_lnx[i], rw_o[i])
        else:
            if shared is None:
                shared = mla_shared_kv(x, kv_in_g, kv_down, kv_norm, kv_uk, kv_uv, cos, sin)
            k_nope, k_rope, v = shared
            i = l - N_A_LAYERS
            y = mla_attention(h, mla_dq[i], mla_qnorm[i], mla_uq[i], mla_o[i],
                              k_nope, k_rope, v, cos, sin)
        x = x + gate_m * rms_norm(y, norm_g[l, 1])

        h = rms_norm(x, norm_g[l, 2]) * (1.0 + scale_f) + shift_f
        y = squared_relu_mlp(h, mlp_up[l], mlp_down[l])
        x = x + gate_f * rms_norm(y, norm_g[l, 3])

    return x
```

```python
import numpy as np
import concourse.bass as bass
import concourse.mybir as mybir
from concourse.bass_utils import run_bass_kernel_spmd
from contextlib import ExitStack

F32 = mybir.dt.float32
BF16 = mybir.dt.bfloat16
F32R = mybir.dt.float32r
CHAIN_DT = F32
I32 = mybir.dt.int32
ALU = mybir.AluOpType
AF = mybir.ActivationFunctionType

D = 2048
T = 2048
ND = 16
FF = 8192
NCORES = 8
EPS = 1e-6

EPOCH = 30000
N_DMA_SEMS = 6
SAME_ENGINE_SYNC = False


class Buf:
    __slots__ = ("w", "r")

    def __init__(self):
        self.w = None
        self.r = {}


class Prog:
    ENGS = ("pe", "dve", "act", "pool", "sp")
    NSEM = 0

    def __init__(self, nc, semstack=None):
        self.nc = nc
        self.semstack = semstack
        self.q = {e: [] for e in self.ENGS}
        self.cnt = {e: 0 for e in self.ENGS}
        self.seen = {e: {} for e in self.ENGS}
        self.dma_cnt = {}
        self.dma_rr = {e: 0 for e in self.ENGS}
        self.keys = set()

    def _wait(self, eng, key, val):
        if self.seen[eng].get(key, 0) >= val:
            return
        self.seen[eng][key] = val
        self.q[eng].append(("wait", key, val))

    def _deps(self, eng, reads, writes):
        deps = {}
        for b in reads:
            if b.w is not None:
                k, v = b.w
                if deps.get(k, 0) < v:
                    deps[k] = v
        for b in writes:
            if b.w is not None:
                k, v = b.w
                if deps.get(k, 0) < v:
                    deps[k] = v
            for k, v in b.r.items():
                if deps.get(k, 0) < v:
                    deps[k] = v
        for k, v in deps.items():
            if k[0] == "E" and k[1] == eng and (eng == "pe" or not SAME_ENGINE_SYNC):
                continue
            self._wait(eng, k, v)

    def _mark(self, tok, reads, writes):
        k, v = tok
        for b in reads:
            if b.r.get(k, 0) < v:
                b.r[k] = v
        for b in writes:
            b.w = tok
            b.r = {}

    def op(self, eng, fn, reads=(), writes=()):
        self._deps(eng, reads, writes)
        n = self.cnt[eng]
        self.cnt[eng] = n + 1
        key = ("E", eng, n // EPOCH)
        self.keys.add(key)
        self.q[eng].append(("op", fn, key))
        self._mark((key, n % EPOCH + 1), reads, writes)

    def dma(self, qeng, out, in_, reads=(), writes=()):
        self._deps(qeng, reads, writes)
        s = self.dma_rr[qeng]
        self.dma_rr[qeng] = (s + 1) % N_DMA_SEMS
        gen = 0
        while self.dma_cnt.get(("D", qeng, s, gen), 0) + 16 > EPOCH:
            gen += 1
        key = ("D", qeng, s, gen)
        prev = self.dma_cnt.get(key, 0)
        if prev > 0:
            self._wait(qeng, key, prev)
        elif gen > 0:
            pk = ("D", qeng, s, gen - 1)
            self._wait(qeng, pk, self.dma_cnt[pk])
        self.dma_cnt[key] = prev + 16
        self.keys.add(key)
        self.q[qeng].append(("dma", (out, in_), key))
        self._mark((key, prev + 16), reads, writes)

    def barrier(self):
        toks = []
        for e in self.ENGS:
            n = self.cnt[e]
            if n > 0:
                toks.append((("E", e, (n - 1) // EPOCH), (n - 1) % EPOCH + 1))
        toks += list(self.dma_cnt.items())
        for e in self.ENGS:
            for key, v in toks:
                if key[0] == "E" and key[1] == e:
                    continue
                self._wait(e, key, v)

    def finish(self, eng="sp"):
        for key, v in list(self.dma_cnt.items()):
            self._wait(eng, key, v)

    def emit(self):
        nc = self.nc
        engmap = {"pe": "tensor", "dve": "vector", "act": "scalar", "pool": "gpsimd", "sp": "sync"}
        with ExitStack() as st:
            sems = {}
            semst = self.semstack if self.semstack is not None else st
            for i, key in enumerate(sorted(self.keys, key=str)):
                Prog.NSEM += 1
                sems[key] = semst.enter_context(nc.semaphore("s%d" % Prog.NSEM))
            block = st.enter_context(nc.Block())
            for e in self.ENGS:
                items = self.q[e]
                if not items:
                    continue

                def body(eng, items=items):
                    for it in items:
                        if it[0] == "wait":
                            eng.wait_ge(sems[it[1]], it[2])
                        elif it[0] == "op":
                            it[1](eng).then_inc(sems[it[2]], 1)
                        else:
                            eng.dma_start(out=it[1][0], in_=it[1][1]).then_inc(sems[it[2]], 16)

                getattr(block, engmap[e])(body)


class Ctx:
    NT = 0

    def __init__(self, name, env=None):
        if env is None:
            self.nc = bass.Bass("TRN2", target_bir_lowering=False)
            self.semstack = None
            self.dmap = {}
            self.prefix = ""
            self.pre = []
        else:
            root, self.dmap, self.prefix = env[:3]
            self.pre = env[3] if len(env) > 3 else []
            self.nc = root.nc
            self.semstack = root.semstack
        self.P = Prog(self.nc, self.semstack)
        self.st = ExitStack()
        self.n = 0
        self.psl = []
        self.psi = 0
        self.rot = {}
        self.rotw = 512

    def dram(self, name, shape, dt, kind):
        if name in self.dmap:
            return self.dmap[name]
        return self.nc.dram_tensor(self.prefix + name, list(shape), dt, kind=kind).ap()

    def sb(self, shape, dt):
        Ctx.NT += 1
        return self.st.enter_context(self.nc.sbuf_tensor("t%d" % Ctx.NT, list(shape), dt))

    def init_psum(self, nf32=8):
        for i in range(nf32):
            Ctx.NT += 1
            t = self.st.enter_context(self.nc.psum_tensor("ps%d" % Ctx.NT, [128, 512], F32))
            self.psl.append((t, Buf()))

    def ps(self):
        r = self.psl[self.psi]
        self.psi = (self.psi + 1) % len(self.psl)
        return r

    def rotbuf(self, key, shape, dt, n=2):
        if key not in self.rot:
            self.rot[key] = [[(self.sb(shape, dt), Buf()) for _ in range(n)], 0]
        lst, i = self.rot[key]
        self.rot[key][1] = (i + 1) % len(lst)
        return lst[i]

    def do_pre(self):
        for (dst, src) in self.pre:
            cast_dma(self, dst, src)

    def close(self):
        self.P.finish("sp")
        self.P.emit()
        self.st.close()


class WT:
    def __init__(self, ap, buf):
        self.ap = ap
        self.buf = buf


def cast_dma(k, dst, src, buf=None, max_bytes=8 << 20):
    rows, cols = src.shape[0], src.shape[1]
    step = max(1, min(rows, max_bytes // (cols * 4)))
    for r0 in range(0, rows, step):
        r1 = min(rows, r0 + step)
        k.P.dma("pool", dst[r0:r1, :], src[r0:r1, :], writes=[buf] if buf is not None else [])


def wsrc(k, name, shape):
    if name + "_bf" in k.dmap:
        return WT(k.dmap[name + "_bf"], Buf())
    w = k.dram(name, shape, F32, "ExternalInput")
    wb = k.nc.dram_tensor(k.prefix + name + "_bf", list(shape), BF16, kind="Internal").ap()
    b = Buf()
    cast_dma(k, wb, w, b)
    return WT(wb, b)


class WLoader:
    def __init__(self, k, nk=16, ncols=256, nbf=3):
        self.k = k
        self.wbf = [(k.sb([128, nk, ncols], BF16), Buf()) for _ in range(nbf)]
        self.j = 0

    def load(self, W, r0, nk, c0, ncols, pp=128):
        P = self.k.P
        wb, wbb = self.wbf[self.j]
        self.j = (self.j + 1) % len(self.wbf)
        src = W.ap[r0:r0 + nk * pp, c0:c0 + ncols].rearrange("(k p) c -> p k c", p=pp)
        P.dma("sp", wb[:pp, :nk, :ncols], src, reads=[W.buf], writes=[wbb])
        return wb, wbb


def make_consts(k):
    P = k.P
    c = {}
    ones = k.sb([128, 128], BF16)
    c["ones"] = ones
    c["onesb"] = Buf()
    P.op("pool", lambda e: e.memset(ones[:], 1.0), writes=[c["onesb"]])
    eps = k.sb([128, 2], F32)
    c["eps"] = eps
    P.op("pool", lambda e: e.memset(eps[:, 0:1], EPS), writes=[c["onesb"]])
    P.op("pool", lambda e: e.memset(eps[:, 1:2], 64e-5), writes=[c["onesb"]])
    return c


def rms_rstd(k, c, X, Xb, TT, scale_div=D, epsap=None, ntile=ND):
    P = k.P
    if epsap is None:
        epsap = c["eps"][:, 0:1]
    ps, psb = k.ps()
    for dt in range(ntile):
        sq, sqb = k.rotbuf("sq", [128, k.rotw], BF16, 3)
        P.op("act", lambda e, o=sq[:, :TT], i=X[:, dt, :]: e.activation(out=o, in_=i, func=AF.Square),
             reads=[Xb], writes=[sqb])
        P.op("pe", lambda e, o=ps[:, :TT], r=sq[:, :TT], s=(dt == 0), t=(dt == ntile - 1):
             e.matmul(o, lhsT=c["ones"][:], rhs=r, start=s, stop=t), reads=[sqb, c["onesb"]], writes=[psb])
    rstd, rb = k.rotbuf("rstd", [128, k.rotw], F32, 2)
    P.op("act", lambda e, o=rstd[:, :TT], i=ps[:, :TT]: e.activation(
        out=o, in_=i, func=AF.Sqrt, bias=epsap, scale=1.0 / scale_div), reads=[psb, c["onesb"]], writes=[rb])
    P.op("dve", lambda e, o=rstd[:, :TT]: e.reciprocal(out=o, in_=o), reads=[rb], writes=[rb])
    return rstd, rb


def norm_mod(k, X, Xb, rstd, rb, A, Sh, mb, H, Hb, TT, col0=0):
    P = k.P
    for dt in range(ND):
        tmp, tb = k.rotbuf("nm_tmp", [128, k.rotw], F32, 2)
        P.op("dve", lambda e, o=tmp[:, :TT], i=X[:, dt, :], s=A[:, dt:dt + 1], r=rstd[:, :TT]:
             e.scalar_tensor_tensor(out=o, in0=i, scalar=s, in1=r, op0=ALU.mult, op1=ALU.mult),
             reads=[Xb, rb, mb], writes=[tb])
        P.op("act", lambda e, o=H[:, dt, col0:col0 + TT], i=tmp[:, :TT], s=Sh[:, dt:dt + 1]:
             e.activation(out=o, in_=i, func=AF.Identity, bias=s, scale=1.0),
             reads=[tb, mb], writes=[Hb])


def post_residual(k, c, X, Xb, Y, Yb, G, mb, TT):
    P = k.P
    rstd, rb = rms_rstd(k, c, Y, Yb, TT)
    for dt in range(ND):
        tmp, tb = k.rotbuf("nm_tmp", [128, k.rotw], F32, 2)
        P.op("dve", lambda e, o=tmp[:, :TT], i=Y[:, dt, :], s=G[:, dt:dt + 1], r=rstd[:, :TT]:
             e.scalar_tensor_tensor(out=o, in0=i, scalar=s, in1=r, op0=ALU.mult, op1=ALU.mult),
             reads=[Yb, rb, mb], writes=[tb])
        P.op("pool" if dt % 2 else "dve", lambda e, o=X[:, dt, :], i=tmp[:, :TT]: e.tensor_tensor(out=o, in0=o, in1=i, op=ALU.add),
             reads=[tb], writes=[Xb])


def build_mods(env=None):
    k = Ctx("mods", env)
    nc, P = k.nc, k.P
    k.do_pre()
    c_pd = k.dram("c_pd", [128, 16], F32, "ExternalInput")
    ada_w = k.dram("ada_w", [2, D, 6 * D], F32, "ExternalInput")
    ada_b = k.dram("ada_b_pd", [128, 2, 96], F32, "ExternalInput")
    ng = k.dram("norm_g_pd", [128, 2, 4, 16], F32, "ExternalInput")
    mods = k.dram("mods", [128, 2 * 96], F32, "ExternalOutput")
    k.init_psum(2)
    cin = k.sb([128, 16], F32)
    cact = k.sb([128, 16], F32)
    abt = k.sb([128, 2, 96], F32)
    ngt = k.sb([128, 2, 4, 16], F32)
    raw = k.sb([128, 2, 96], F32)
    outt = k.sb([128, 2, 6, 16], F32)
    cb_, sm_ = Buf(), Buf()
    P.dma("sp", cin[:], c_pd, writes=[cb_])
    P.dma("sp", abt[:], ada_b, writes=[sm_])
    P.dma("sp", ngt[:], ng, writes=[sm_])
    P.op("act", lambda e: e.activation(out=cact[:], in_=cin[:], func=AF.Silu), reads=[cb_], writes=[cb_])
    stg = [(k.sb([128, 16, 512], F32), Buf()) for _ in range(3)]
    rawb, ob = Buf(), Buf()
    for l in range(2):
        ps, psb = k.ps()
        for cb in range(24):
            st, stb = stg[(l * 24 + cb) % 3]
            src = ada_w[l, :, cb * 512:(cb + 1) * 512].rearrange("(k p) c -> p k c", p=128)
            P.dma("sp", st[:], src, writes=[stb])
            for j in range(4):
                e_ = cb * 4 + j
                for dt in range(16):
                    P.op("pe", lambda e, o=ps[:, e_:e_ + 1], w=st[:, dt, j * 128:(j + 1) * 128], r=cact[:, dt:dt + 1],
                         s=(dt == 0), t=(dt == 15): e.matmul(o, lhsT=w, rhs=r, start=s, stop=t),
                         reads=[stb, cb_], writes=[psb])
        P.op("dve", lambda e, o=raw[:, l, :], i=ps[:, 0:96], b=abt[:, l, :]: e.tensor_tensor(out=o, in0=i, in1=b, op=ALU.add),
             reads=[psb, sm_], writes=[rawb])
        for half, (gpre, gpost) in enumerate(((0, 1), (2, 3))):
            b0 = half * 3
            P.op("dve", lambda e, o=outt[:, l, b0 + 0, :], i=raw[:, l, (b0 + 1) * 16:(b0 + 2) * 16], g=ngt[:, l, gpre, :]:
                 e.scalar_tensor_tensor(out=o, in0=i, scalar=1.0, in1=g, op0=ALU.add, op1=ALU.mult),
                 reads=[rawb, sm_], writes=[ob])
            P.op("dve", lambda e, o=outt[:, l, b0 + 1, :], i=raw[:, l, (b0 + 0) * 16:(b0 + 1) * 16]:
                 e.tensor_copy(out=o, in_=i), reads=[rawb], writes=[ob])
            P.op("dve", lambda e, o=outt[:, l, b0 + 2, :], i=raw[:, l, (b0 + 2) * 16:(b0 + 3) * 16], g=ngt[:, l, gpost, :]:
                 e.tensor_tensor(out=o, in0=i, in1=g, op=ALU.mult), reads=[rawb, sm_], writes=[ob])
    P.dma("sp", mods, outt[:].rearrange("p l j d -> p (l j d)"), reads=[ob])
    k.close()
    return nc


def build_mlp(l, env=None):
    k = Ctx("mlp", env)
    nc, P = k.nc, k.P
    TT = 512
    xT = k.dram("xT", [D, T], F32, "ExternalInput")
    modsd = k.dram("mods", [128, 192], F32, "ExternalInput")
    k.do_pre()
    wup = wsrc(k, "w_up", [D, FF])
    wdn = wsrc(k, "w_dn", [FF, D])
    oT = k.dram("oT", [D, T], F32, "ExternalOutput")
    k.init_psum(8)
    c = make_consts(k)
    mt = k.sb([128, 2, 6, 16], F32)
    mb = Buf()
    P.dma("sp", mt[:].rearrange("p l j d -> p (l j d)"), modsd, writes=[mb])
    A, Sh, G = mt[:, l, 3, :], mt[:, l, 4, :], mt[:, l, 5, :]
    Xs = [(k.sb([128, ND, TT], F32), Buf()) for _ in range(2)]
    Hs_ = [(k.sb([128, ND, TT], BF16), Buf()) for _ in range(2)]
    U = k.sb([128, 32, TT], BF16)
    Y = k.sb([128, ND, TT], F32)
    Ub, Yb = Buf(), Buf()
    wl = WLoader(k, 16, 256, 3)
    xT3 = xT.rearrange("(k p) t -> p k t", p=128)
    oT3 = oT.rearrange("(k p) t -> p k t", p=128)
    NT_ = T // TT

    def load_norm(tt):
        X, Xb = Xs[tt % 2]
        H, Hb = Hs_[tt % 2]
        P.dma("sp", X[:], xT3[:, :, tt * TT:(tt + 1) * TT], writes=[Xb])
        rstd, rb = rms_rstd(k, c, X, Xb, TT)
        norm_mod(k, X, Xb, rstd, rb, A, Sh, mb, H, Hb, TT)

    load_norm(0)
    for tt in range(NT_):
        X, Xb = Xs[tt % 2]
        H, Hb = Hs_[tt % 2]
        for fh in range(2):
            for fb in range(16):
                f0 = fh * 4096 + fb * 256
                wb, wbb = wl.load(wup, 0, 16, f0, 256)
                for j in range(2):
                    ps, psb = k.ps()
                    for dt in range(16):
                        P.op("pe", lambda e, o=ps[:, :TT], w=wb[:, dt, j * 128:(j + 1) * 128], r=H[:, dt, :],
                             s=(dt == 0), t=(dt == 15): e.matmul(o, lhsT=w, rhs=r, start=s, stop=t),
                             reads=[wbb, Hb], writes=[psb])
                    rl, rlb = k.rotbuf("relu", [128, 512], F32, 3)
                    P.op("act", lambda e, o=rl[:, :TT], i=ps[:, :TT]: e.activation(out=o, in_=i, func=AF.Relu),
                         reads=[psb], writes=[rlb])
                    P.op("dve", lambda e, o=U[:, fb * 2 + j, :], i=rl[:, :TT]: e.tensor_tensor(out=o, in0=i, in1=i, op=ALU.mult),
                         reads=[rlb], writes=[Ub])
            if fh == 0 and tt + 1 < NT_:
                load_norm(tt + 1)
            for db in range(8):
                pss = [k.ps(), k.ps()]
                for kb in range(2):
                    wb, wbb = wl.load(wdn, fh * 4096 + kb * 2048, 16, db * 256, 256)
                    for j in range(2):
                        ps, psb = pss[j]
                        for ft in range(16):
                            P.op("pe", lambda e, o=ps[:, :TT], w=wb[:, ft, j * 128:(j + 1) * 128], r=U[:, kb * 16 + ft, :],
                                 s=(kb == 0 and ft == 0), t=(kb == 1 and ft == 15): e.matmul(o, lhsT=w, rhs=r, start=s, stop=t),
                                 reads=[wbb, Ub], writes=[psb])
                for j in range(2):
                    ps, psb = pss[j]
                    if fh == 0:
                        P.op("act", lambda e, o=Y[:, db * 2 + j, :], i=ps[:, :TT]: e.activation(out=o, in_=i, func=AF.Copy),
                             reads=[psb], writes=[Yb])
                    else:
                        P.op("dve", lambda e, o=Y[:, db * 2 + j, :], i=ps[:, :TT]: e.tensor_tensor(out=o, in0=o, in1=i, op=ALU.add),
                             reads=[psb], writes=[Yb])
        post_residual(k, c, X, Xb, Y, Yb, G, mb, TT)
        P.dma("sp", oT3[:, :, tt * TT:(tt + 1) * TT], X[:], reads=[Xb])
    k.close()
    return nc


def load_swap(wl, W, nk, c0):
    P = wl.k.P
    wb, wbb = wl.wbf[wl.j]
    wl.j = (wl.j + 1) % len(wl.wbf)
    for (a, b_) in ((0, 32), (32, 0)):
        src = W.ap[0:nk * 128, c0 + b_:c0 + b_ + 32].rearrange("(k p) c -> p k c", p=128)
        P.dma("sp", wb[:, :nk, a:a + 32], src, reads=[W.buf], writes=[wbb])
    return wb, wbb


def angle_reduce(k, ang, kf, ki, ab):
    import math
    P = k.P
    P.op("dve", lambda e: e.tensor_scalar(out=kf, in0=ang, scalar1=1.0 / (2 * math.pi), scalar2=None, op0=ALU.mult), reads=[ab], writes=[ab])
    P.op("dve", lambda e: e.tensor_copy(out=ki, in_=kf), reads=[ab], writes=[ab])
    P.op("dve", lambda e: e.tensor_copy(out=kf, in_=ki), reads=[ab], writes=[ab])
    P.op("dve", lambda e: e.scalar_tensor_tensor(out=ang, in0=kf, scalar=-2 * math.pi, in1=ang, op0=ALU.mult, op1=ALU.add), reads=[ab], writes=[ab])
    P.op("dve", lambda e: e.tensor_scalar(out=kf, in0=ang, scalar1=math.pi, scalar2=-2 * math.pi, op0=ALU.is_gt, op1=ALU.mult), reads=[ab], writes=[ab])
    P.op("dve", lambda e: e.tensor_tensor(out=ang, in0=ang, in1=kf, op=ALU.add), reads=[ab], writes=[ab])
    P.op("dve", lambda e: e.tensor_scalar(out=kf, in0=ang, scalar1=-math.pi, scalar2=2 * math.pi, op0=ALU.is_lt, op1=ALU.mult), reads=[ab], writes=[ab])
    P.op("dve", lambda e: e.tensor_tensor(out=ang, in0=ang, in1=kf, op=ALU.add), reads=[ab], writes=[ab])


def build_mla(l=1, env=None):
    import math
    k = Ctx("mla", env)
    nc, P = k.nc, k.P
    TT = 512
    NTT = T // TT
    xT = k.dram("xT", [D, T], F32, "ExternalInput")
    modsd = k.dram("mods", [128, 192], F32, "ExternalInput")
    posr = k.dram("posr", [64, T], I32, "ExternalInput")
    ropec = k.dram("ropec", [64, 2], F32, "ExternalInput")
    vec = k.dram("mla_vec", [128, 24], F32, "ExternalInput")
    k.do_pre()
    kvd = wsrc(k, "kv_down", [D, 576])
    wuk = wsrc(k, "kv_uk", [512, D])
    wuv = wsrc(k, "kv_uv", [512, D])
    wdq = wsrc(k, "w_dq", [D, 512])
    wuq = wsrc(k, "w_uq", [512, 16 * 192])
    wo = wsrc(k, "w_o", [D, D])
    oT = k.dram("oT", [D, T], F32, "ExternalOutput")
    otd = k.dram("ot_scratch", [16, 128, T], BF16, "Internal")
    k.init_psum(8)
    oacc = k.psl[4:]
    k.psl = k.psl[:4]
    c = make_consts(k)
    mt = k.sb([128, 2, 6, 16], F32)
    vt = k.sb([128, 24], F32)
    zer = k.sb([128, 16], F32)
    mb = Buf()
    P.dma("sp", mt[:].rearrange("p l j d -> p (l j d)"), modsd, writes=[mb])
    P.dma("sp", vt[:], vec, writes=[mb])
    P.op("pool", lambda e: e.memset(zer[:], 0.0), writes=[mb])
    A, Sh, G = mt[:, l, 0, :], mt[:, l, 1, :], mt[:, l, 2, :]

    X = k.sb([128, ND, TT], F32)
    Y = k.sb([128, ND, TT], F32)
    Xb, Yb = Buf(), Buf()
    Yf = Y[:].rearrange("p a b -> p (a b)")
    Ybf = Yf.bitcast(BF16)
    Xbf = X[:].rearrange("p a b -> p (a b)").bitcast(BF16)
    HS = Ybf[:, 0:8192].rearrange("p (a b) -> p a b", a=ND)
    HH = Ybf[:, 8192:16384].rearrange("p (a b) -> p a b", a=ND)
    rc = k.sb([64, 2], F32)
    cos2 = k.sb([64, T], F32)
    sinS = k.sb([64, T], F32)
    csb = Buf()
    P.dma("sp", rc[:], ropec, writes=[mb])
    pi_t = Yf[:64, 0:512].bitcast(I32)
    ang = Yf[:64, 512:1024]
    tmp = Yf[:64, 1024:1536]
    kf = Yf[:64, 1536:2048]
    ki = Yf[:64, 2048:2560].bitcast(I32)
    for ch in range(4):
        t0 = ch * 512
        P.dma("sp", pi_t, posr[:, t0:t0 + 512], writes=[Yb])
        P.op("dve", lambda e: e.tensor_copy(out=ang, in_=pi_t), reads=[Yb], writes=[Yb])
        P.op("dve", lambda e: e.tensor_scalar(out=ang, in0=ang, scalar1=rc[:, 0:1], scalar2=None, op0=ALU.mult), reads=[Yb, mb], writes=[Yb])
        P.op("dve", lambda e: e.tensor_scalar(out=tmp, in0=ang, scalar1=math.pi / 2, scalar2=None, op0=ALU.add), reads=[Yb], writes=[Yb])
        angle_reduce(k, tmp, kf, ki, Yb)
        P.op("act", lambda e, o=cos2[:, t0:t0 + 512]: e.activation(out=o, in_=tmp, func=AF.Sin), reads=[Yb], writes=[csb])
        angle_reduce(k, ang, kf, ki, Yb)
        P.op("act", lambda e, o=sinS[:, t0:t0 + 512]: e.activation(out=o, in_=ang, func=AF.Sin), reads=[Yb], writes=[csb])
        P.op("dve", lambda e, o=sinS[:, t0:t0 + 512]: e.tensor_scalar(out=o, in0=o, scalar1=rc[:, 1:2], scalar2=None, op0=ALU.mult), reads=[csb, mb], writes=[csb])

    CKQ = k.sb([128, 8, TT], F32)
    CK = CKQ[:, 0:4, :]
    CQ = CKQ[:, 4:8, :]
    CKb, CQb = Buf(), Buf()
    CKN = k.sb([128, 4, T], BF16)
    CQN = k.sb([128, 4, T], BF16)
    KR = k.sb([128, T], BF16)
    CKNb, CQNb, KRb = Buf(), Buf(), Buf()
    P.op("pool", lambda e: e.memset(KR[:], 0.0), writes=[KRb])
    wl = WLoader(k, 16, 128, 4)
    xT3 = xT.rearrange("(k p) t -> p k t", p=128)
    oT3 = oT.rearrange("(k p) t -> p k t", p=128)

    def rope_out(ps1, ps1b, ps2, ps2b, dst, dstb, t0):
        t1, t1b = k.rotbuf("rp1", [64, 512], F32, 1)
        t2, t2b = k.rotbuf("rp2", [64, 512], F32, 1)
        P.op("dve", lambda e: e.tensor_tensor(out=t1[:], in0=ps1[:64, :TT], in1=cos2[:, t0:t0 + TT], op=ALU.mult), reads=[ps1b, csb], writes=[t1b])
        P.op("dve", lambda e: e.tensor_tensor(out=t2[:], in0=ps2[:64, :TT], in1=sinS[:, t0:t0 + TT], op=ALU.mult), reads=[ps2b, csb], writes=[t2b])
        P.op("pool", lambda e: e.tensor_tensor(out=dst[:64, t0:t0 + TT], in0=t1[:], in1=t2[:], op=ALU.add), reads=[t1b, t2b], writes=[dstb])

    for tt in range(NTT):
        t0 = tt * TT
        P.dma("sp", X[:], xT3[:, :, t0:t0 + TT], writes=[Xb])
        rstd, rb = rms_rstd(k, c, X, Xb, TT)
        norm_mod(k, X, Xb, rstd, rb, vt[:, 0:16], zer, mb, HS, Yb, TT)
        norm_mod(k, X, Xb, rstd, rb, A, Sh, mb, HH, Yb, TT)
        for (W, src, dst, dstb) in ((kvd, HS, CK, CKb), (wdq, HH, CQ, CQb)):
            for cb in range(4):
                wb, wbb = wl.load(W, 0, 16, cb * 128, 128)
                ps, psb = k.ps()
                for dt in range(16):
                    P.op("pe", lambda e, o=ps[:, :TT], w=wb[:, dt, :], r=src[:, dt, :], s=(dt == 0), t=(dt == 15):
                         e.matmul(o, lhsT=w, rhs=r, start=s, stop=t), reads=[wbb, Yb], writes=[psb])
                P.op("act", lambda e, o=dst[:, cb, :], i=ps[:, :TT]: e.activation(out=o, in_=i, func=AF.Copy), reads=[psb], writes=[dstb])
        pss = []
        for sw in range(2):
            if sw == 0:
                wb, wbb = wl.load(kvd, 0, 16, 512, 64)
            else:
                wb, wbb = load_swap(wl, kvd, 16, 512)
            ps, psb = k.ps()
            for dt in range(16):
                P.op("pe", lambda e, o=ps[:64, :TT], w=wb[:, dt, 0:64], r=HS[:, dt, :], s=(dt == 0), t=(dt == 15):
                     e.matmul(o, lhsT=w, rhs=r, start=s, stop=t), reads=[wbb, Yb], writes=[psb])
            pss.append((ps, psb))
        rope_out(pss[0][0], pss[0][1], pss[1][0], pss[1][1], KR, KRb, t0)
        for (src, srcb, dst, dstb, v0) in ((CK, CKb, CKN, CKNb, 16), (CQ, CQb, CQN, CQNb, 20)):
            rs, rsb = rms_rstd(k, c, src, srcb, TT, scale_div=512, ntile=4)
            for ct in range(4):
                P.op("dve", lambda e, o=dst[:, ct, t0:t0 + TT], i=src[:, ct, :], s=vt[:, v0 + ct:v0 + ct + 1], r=rs[:, :TT]:
                     e.scalar_tensor_tensor(out=o, in0=i, scalar=s, in1=r, op0=ALU.mult, op1=ALU.mult),
                     reads=[srcb, rsb, mb], writes=[dstb])

    P.barrier()
    tri = k.sb([128, 128], BF16)
    trib = Buf()
    P.op("pool", lambda e: e.memset(tri[:], 1.0), writes=[trib])
    P.op("pool", lambda e: e.affine_select(out=tri[:], in_=tri[:], pattern=[[1, 128]], compare_op=ALU.is_ge, fill=0.0,
                                           base=0, channel_multiplier=-1), reads=[trib], writes=[trib])
    wl2 = WLoader(k, 4, 128, 8)
    scale = 192.0 ** -0.5
    hb = []
    for reg in (Ybf, Xbf):
        hb.append(dict(KN=reg[:, 0:2048], QN=reg[:, 2048:4096], QR=reg[:, 4096:6144], OH=reg[:, 6144:8192],
                       VH=reg[:, 8192:10240].rearrange("p (a b) -> p a b", a=16),
                       KNb=Buf(), QNb=Buf(), QRb=Buf(), OHb=Buf(), VHb=Buf()))
    for s_ in hb:
        P.op("pool", lambda e, o=s_["QR"]: e.memset(o, 0.0), writes=[s_["QRb"]])
    for h in range(16):
        s_ = hb[h % 2]
        KN, QN, QR, OH, VH = s_["KN"], s_["QN"], s_["QR"], s_["OH"], s_["VH"]
        KNb, QNb, QRb, OHb, VHb = s_["KNb"], s_["QNb"], s_["QRb"], s_["OHb"], s_["VHb"]
        wk, wkb = wl2.load(wuk, 0, 4, h * 128, 128)
        wq, wqb = wl2.load(wuq, 0, 4, h * 192, 128)
        wv, wvb = wl2.load(wuv, 0, 4, h * 128, 128)
        wr, wrb = wl2.load(wuq, 0, 4, h * 192 + 128, 64)
        ws, wsb = load_swap(wl2, wuq, 4, h * 192 + 128)
        for tq in range(NTT):
            t0 = tq * TT
            for (w_, wb_, src, srcb, dst, dstb) in ((wk, wkb, CKN, CKNb, KN, KNb), (wq, wqb, CQN, CQNb, QN, QNb)):
                ps, psb = k.ps()
                for ct in range(4):
                    P.op("pe", lambda e, o=ps[:, :TT], w=w_[:, ct, :], r=src[:, ct, t0:t0 + TT], s=(ct == 0), t=(ct == 3):
                         e.matmul(o, lhsT=w, rhs=r, start=s, stop=t), reads=[wb_, srcb], writes=[psb])
                P.op("act", lambda e, o=dst[:, t0:t0 + TT], i=ps[:, :TT]: e.activation(out=o, in_=i, func=AF.Copy), reads=[psb], writes=[dstb])
            pss = []
            for (w_, wb_) in ((wr, wrb), (ws, wsb)):
                ps, psb = k.ps()
                for ct in range(4):
                    P.op("pe", lambda e, o=ps[:64, :TT], w=w_[:, ct, 0:64], r=CQN[:, ct, t0:t0 + TT], s=(ct == 0), t=(ct == 3):
                         e.matmul(o, lhsT=w, rhs=r, start=s, stop=t), reads=[wb_, CQNb], writes=[psb])
                pss.append((ps, psb))
            rope_out(pss[0][0], pss[0][1], pss[1][0], pss[1][1], QR, QRb, t0)
        for tk4 in range(4):
            ps, psb = k.ps()
            for i in range(4):
                tk = tk4 * 4 + i
                for ct in range(4):
                    P.op("pe", lambda e, o=ps[:, i * 128:(i + 1) * 128], w=CKN[:, ct, tk * 128:(tk + 1) * 128], r=wv[:, ct, :], s=(ct == 0), t=(ct == 3):
                         e.matmul(o, lhsT=w, rhs=r, start=s, stop=t), reads=[wvb, CKNb], writes=[psb])
            P.op("act", lambda e, o=VH[:, tk4 * 4:tk4 * 4 + 4, :], i=ps[:, :].rearrange("p (a b) -> p a b", a=4):
                 e.activation(out=o, in_=i, func=AF.Copy), reads=[psb], writes=[VHb])
        for qt in range(NTT):
            oa, oab = oacc[(qt % 2) * 2]
            da, dab = oacc[(qt % 2) * 2 + 1]
            nk_ = 4 * (qt + 1)
            for kt in range(nk_):
                off = max(0, (kt - 4 * qt) * 128)
                q0 = qt * TT + off
                q1 = (qt + 1) * TT
                sp_, spb = k.ps()
                P.op("pe", lambda e, o=sp_[:, off:TT], w=KN[:, kt * 128:(kt + 1) * 128], r=QN[:, q0:q1]:
                     e.matmul(o, lhsT=w, rhs=r, start=True, stop=False), reads=[KNb, QNb], writes=[spb])
                P.op("pe", lambda e, o=sp_[:, off:TT], w=KR[:, kt * 128:(kt + 1) * 128], r=QR[:, q0:q1]:
                     e.matmul(o, lhsT=w, rhs=r, start=False, stop=True), reads=[KRb, QRb], writes=[spb])
                PT, PTb = k.rotbuf("PT", [128, TT], BF16, 4)
                P.op("act", lambda e, o=PT[:, off:TT], i=sp_[:, off:TT]: e.activation(out=o, in_=i, func=AF.Exp, scale=scale),
                     reads=[spb], writes=[PTb])
                if kt >= 4 * qt:
                    P.op("pool", lambda e, o=PT[:, off:off + 128]: e.tensor_tensor(out=o, in0=o, in1=tri[:], op=ALU.mult),
                         reads=[PTb, trib], writes=[PTb])
                P.op("pe", lambda e, o=oa[:, off:TT], w=VH[:, kt, :], r=PT[:, off:TT], s=(kt == 0), t=(kt == nk_ - 1):
                     e.matmul(o, lhsT=w, rhs=r, start=s, stop=t), reads=[VHb, PTb], writes=[oab])
                P.op("pe", lambda e, o=da[:, off:TT], r=PT[:, off:TT], s=(kt == 0), t=(kt == nk_ - 1):
                     e.matmul(o, lhsT=c["ones"][:], rhs=r, start=s, stop=t), reads=[c["onesb"], PTb], writes=[dab])
            rd, rdb = k.rotbuf("rden", [128, TT], F32, 2)
            P.op("dve", lambda e, o=rd[:], i=da[:, :TT]: e.reciprocal(out=o, in_=i), reads=[dab], writes=[rdb])
            P.op("dve", lambda e, o=OH[:, qt * TT:(qt + 1) * TT], i=oa[:, :TT], r=rd[:]: e.tensor_tensor(out=o, in0=i, in1=r, op=ALU.mult),
                 reads=[oab, rdb], writes=[OHb])
        P.dma("sp", otd[h], OH, reads=[OHb])

    P.barrier()
    OTt = CKQ[:].rearrange("p a b -> p (a b)").bitcast(BF16).rearrange("p (a b) -> p a b", a=16)
    OTb = Buf()
    otd3 = otd.rearrange("h p t -> p h t")
    Xb, Yb = Buf(), Buf()
    for tt in range(NTT):
        t0 = tt * TT
        P.dma("sp", OTt, otd3[:, :, t0:t0 + TT], writes=[OTb])
        P.dma("sp", X[:], xT3[:, :, t0:t0 + TT], writes=[Xb])
        for eb in range(16):
            wb, wbb = wl.load(wo, 0, 16, eb * 128, 128)
            ps, psb = k.ps()
            for hh in range(16):
                P.op("pe", lambda e, o=ps[:, :TT], w=wb[:, hh, :], r=OTt[:, hh, :], s=(hh == 0), t=(hh == 15):
                     e.matmul(o, lhsT=w, rhs=r, start=s, stop=t), reads=[wbb, OTb], writes=[psb])
            P.op("act", lambda e, o=Y[:, eb, :], i=ps[:, :TT]: e.activation(out=o, in_=i, func=AF.Copy), reads=[psb], writes=[Yb])
        post_residual(k, c, X, Xb, Y, Yb, G, mb, TT)
        P.dma("sp", oT3[:, :, t0:t0 + TT], X[:], reads=[Xb])
    k.close()
    return nc


def build_rwkv(l=0, dbg=False, env=None):
    k = Ctx("rwkv", env)
    k.rotw = 256
    nc, P = k.nc, k.P
    TT = 256
    NTT = T // TT
    C = 64
    NCH = TT // C
    xT = k.dram("xT", [D, T], F32, "ExternalInput")
    modsd = k.dram("mods", [128, 192], F32, "ExternalInput")
    vec = k.dram("rw_vec", [128, 13, 16], F32, "ExternalInput")
    k.do_pre()
    wrkv = wsrc(k, "w_rkv", [3 * D, D])
    w1 = wsrc(k, "w1", [D, 96])
    w2 = wsrc(k, "w2", [96, D])
    a1 = wsrc(k, "a1", [D, 96])
    a2 = wsrc(k, "a2", [96, D])
    g1 = wsrc(k, "g1", [D, 256])
    g2 = wsrc(k, "g2", [256, D])
    wo = wsrc(k, "w_o", [D, D])
    oT = k.dram("oT", [D, T], F32, "ExternalOutput")
    k.init_psum(8)
    c = make_consts(k)
    mt = k.sb([128, 2, 6, 16], F32)
    vt = k.sb([128, 13, 16], F32)
    mb = Buf()
    P.dma("sp", mt[:].rearrange("p l j d -> p (l j d)"), modsd, writes=[mb])
    P.dma("sp", vt[:], vec, writes=[mb])
    A, Sh, G = mt[:, l, 0, :], mt[:, l, 1, :], mt[:, l, 2, :]
    MU, W0, A0, KKv, KA, RK, LNW, LNB = (lambda j: vt[:, j, :]), vt[:, 6, :], vt[:, 7, :], vt[:, 8, :], vt[:, 9, :], vt[:, 10, :], vt[:, 11, :], vt[:, 12, :]

    NEG = k.sb([128, 2, 16], F32)
    P.op("dve", lambda e: e.tensor_scalar(out=NEG[:], in0=vt[:, 6:8, :], scalar1=-1.0, scalar2=None, op0=ALU.mult), reads=[mb], writes=[mb])
    cb_ = Buf()
    bo16 = k.sb([128, 128], BF16)
    bo32 = k.sb([128, 128], F32)
    idn = k.sb([128, 4, 128], BF16)
    mS = k.sb([128, 4, 64], BF16)
    mI = k.sb([128, 4, 64], BF16)
    mL = k.sb([128, 4, 64], BF16)
    ones64 = k.sb([128, 64], F32)
    for t_ in (bo16, bo32):
        P.op("pool", lambda e, t_=t_: e.memset(t_[:], 0.0), writes=[cb_])
        P.op("pool", lambda e, t_=t_: e.memset(t_[0:64, 0:64], 1.0), writes=[cb_])
        P.op("pool", lambda e, t_=t_: e.memset(t_[64:128, 64:128], 1.0), writes=[cb_])
    P.op("pool", lambda e: e.memset(ones64[:], 1.0), writes=[cb_])
    P.op("pool", lambda e: e.memset(idn[:], 1.0), writes=[cb_])
    P.op("pool", lambda e: e.memset(mS[:], 1.0), writes=[cb_])
    P.op("pool", lambda e: e.memset(mI[:], 1.0), writes=[cb_])
    P.op("pool", lambda e: e.memset(mL[:], 1.0), writes=[cb_])
    for g_ in range(4):
        P.op("pool", lambda e, o=idn[:, g_, :]: e.affine_select(out=o, in_=o, pattern=[[1, 128]], compare_op=ALU.is_equal, fill=0.0,
                                                               base=0, channel_multiplier=-1), reads=[cb_], writes=[cb_])
        for hf in range(2):
            sl = slice(64 * hf, 64 * hf + 64)
            P.op("pool", lambda e, o=mS[sl, g_, :]: e.affine_select(out=o, in_=o, pattern=[[1, 64]], compare_op=ALU.is_ge, fill=0.0,
                                                                    base=-1, channel_multiplier=-1), reads=[cb_], writes=[cb_])
            P.op("pool", lambda e, o=mI[sl, g_, :]: e.affine_select(out=o, in_=o, pattern=[[1, 64]], compare_op=ALU.is_ge, fill=0.0,
                                                                    base=0, channel_multiplier=-1), reads=[cb_], writes=[cb_])
            P.op("pool", lambda e, o=mL[sl, g_, :]: e.affine_select(out=o, in_=o, pattern=[[-1, 64]], compare_op=ALU.is_ge, fill=0.0,
                                                                    base=-1, channel_multiplier=1), reads=[cb_], writes=[cb_])

    RT = k.sb([128, 16, TT], BF16)
    KT = k.sb([128, 16, TT], BF16)
    BT = k.sb([128, 16, TT], BF16)
    AT = k.sb([128, 16, TT], BF16)
    VT = k.sb([128, 16, TT], BF16)
    GT = k.sb([128, 16, TT], BF16)
    BON = k.sb([128, 16, TT], BF16)
    GC = k.sb([128, 16, NCH], F32)
    RTb, KTb, BTb, ATb, VTb, GTb, BONb, GCb, YTb = (Buf() for _ in range(9))
    Hf = k.sb([128, 16, 64], F32)
    Hstk = k.sb([128, 16, 64], BF16)
    Hbd = k.sb([128, 16, 128], BF16)
    Hb_ = [Buf() for _ in range(4)]
    HL = k.sb([128, 16, 1], F32)
    HLb = Buf()
    P.op("pool", lambda e: e.memset(Hf[:], 0.0), writes=Hb_)
    P.op("pool", lambda e: e.memset(Hstk[:], 0.0), writes=Hb_)
    P.op("pool", lambda e: e.memset(Hbd[:], 0.0), writes=Hb_)
    P.op("pool", lambda e: e.memset(HL[:], 0.0), writes=[HLb])
    TW = k.sb([128, TT], BF16)
    TA = k.sb([128, TT], BF16)
    TG = k.sb([128, 2, TT], BF16)
    TWb, TAb, TGb = Buf(), Buf(), Buf()
    wl = WLoader(k, 16, 128, 3)
    wls = WLoader(k, 2, 128, 3)
    REG = k.sb([128, 18688], F32)
    REGbf = REG[:].bitcast(BF16)

    def f32v(o, n, a):
        return REG[:, o:o + n].rearrange("p (a b) -> p a b", a=a)

    def bfv(o, n, a):
        return REGbf[:, 2 * o:2 * o + 2 * n].rearrange("p (a b) -> p a b", a=a)

    X = f32v(0, 4096, 16)
    Hs = f32v(4096, 4352, 16)
    XX = bfv(8448, 2048, 16)
    XS = bfv(10496, 2048, 16)
    XR = bfv(12544, 2048, 16)
    XK = bfv(14592, 2048, 16)
    XV = bfv(16640, 2048, 16)
    o_ = [0]

    def nxt(n, a):
        v = bfv(o_[0], n, a)
        o_[0] += n
        return v
    ATbd, BTbd, KTbd, VTbd = nxt(1024, 16), nxt(1024, 16), nxt(1024, 16), nxt(1024, 16)
    def chbuf():
        t = k.sb([128, 4, 128], F32)
        return {"r": t[:], "w": t[:].bitcast(CHAIN_DT), "m": t[:].bitcast(CHAIN_DT)}
    CH = [dict(N=[chbuf(), chbuf()], L=[chbuf(), chbuf()], P=chbuf()) for _ in range(2)]
    PF = nxt(1024, 16)
    MakT = nxt(1024, 16)
    MrbT, MrkT = nxt(512, 16), nxt(512, 16)
    Vbd, Vstk = nxt(1024, 16), nxt(512, 16)
    Bbd, Kbd = nxt(1024, 16), nxt(1024, 16)
    Zs, Us, Ubd = nxt(512, 16), nxt(512, 16), nxt(1024, 16)
    ZERO_LIST = [ATbd, BTbd, KTbd, VTbd, CH[0]['N'][0]['w'], CH[0]['L'][0]['w'], CH[1]['N'][0]['w'], CH[1]['L'][0]['w'], MakT, Ubd]
    YT = f32v(12800, 4096, 16)
    OIN = bfv(4096, 2048, 16)
    Y2 = f32v(8448, 4096, 16)

    xT3 = xT.rearrange("(k p) t -> p k t", p=128)
    oT3 = oT.rearrange("(k p) t -> p k t", p=128)
    if dbg:
        dbf = k.dram("dbg_bf", [7, 128, 16 * TT], BF16, "ExternalOutput")
        dyt = k.dram("dbg_yt", [128, 16 * TT], F32, "ExternalOutput")
        dgc = k.dram("dbg_gc", [128, 16 * NCH], F32, "ExternalOutput")
        doin = k.dram("dbg_oin", [128, 16 * TT], BF16, "ExternalOutput")

    def tmp(name, n=1, dt=F32, w=TT):
        return k.rotbuf(name, [128, w], dt, n)

    for tt in range(1 if dbg else NTT):
        t0 = tt * TT
        Xb, Hsb, XXb, XSb, XRb, XKb, XVb = (Buf() for _ in range(7))
        P.dma("sp", X, xT3[:, :, t0:t0 + TT], writes=[Xb])
        rstd, rb = rms_rstd(k, c, X, Xb, TT)
        norm_mod(k, X, Xb, rstd, rb, A, Sh, mb, Hs, Hsb, TT, col0=1)
        P.op("pool", lambda e: e.tensor_copy(out=Hs[:, :, 0:1], in_=HL[:]), reads=[HLb], writes=[Hsb])
        P.op("dve", lambda e: e.tensor_tensor(out=XX, in0=Hs[:, :, 0:TT], in1=Hs[:, :, 1:TT + 1], op=ALU.subtract), reads=[Hsb], writes=[XXb])
        P.op("pool", lambda e: e.tensor_copy(out=HL[:], in_=Hs[:, :, TT:TT + 1]), reads=[Hsb], writes=[HLb])

        def make_xs(j, dst, dstb):
            for dt in range(16):
                P.op("dve", lambda e, o=dst[:, dt, :], i=XX[:, dt, :], s=vt[:, j, dt:dt + 1], h=Hs[:, dt, 1:TT + 1]:
                     e.scalar_tensor_tensor(out=o, in0=i, scalar=s, in1=h, op0=ALU.mult, op1=ALU.add),
                     reads=[XXb, Hsb, mb], writes=[dstb])
        for (j, W, ncol) in ((3, w1, 96), (4, a1, 96), (5, g1, 256)):
            make_xs(j, XS, XSb)
            for cbk in range((ncol + 127) // 128):
                nc_ = min(128, ncol - cbk * 128)
                wb, wbb = wl.load(W, 0, 16, cbk * 128, nc_)
                ps, psb = k.ps()
                for dt in range(16):
                    P.op("pe", lambda e, o=ps[:nc_, :TT], w=wb[:, dt, :nc_], r=XS[:, dt, :], s=(dt == 0), t=(dt == 15):
                         e.matmul(o, lhsT=w, rhs=r, start=s, stop=t), reads=[wbb, XSb], writes=[psb])
                if j == 3:
                    P.op("act", lambda e, i=ps[:96, :TT]: e.activation(out=TW[:96, :], in_=i, func=AF.Tanh), reads=[psb], writes=[TWb])
                elif j == 4:
                    P.op("act", lambda e, i=ps[:96, :TT]: e.activation(out=TA[:96, :], in_=i, func=AF.Copy), reads=[psb], writes=[TAb])
                else:
                    P.op("act", lambda e, i=ps[:, :TT], o=TG[:, cbk, :]: e.activation(out=o, in_=i, func=AF.Sigmoid), reads=[psb], writes=[TGb])
        make_xs(0, XR, XRb)
        make_xs(1, XK, XKb)
        make_xs(2, XV, XVb)
        for p in range(16):
            e0 = p * 128
            pA, pAb = k.ps()
            pB, pBb = k.ps()
            pC, pCb = k.ps()
            pD, pDb = k.ps()
            for (jj, src, srcb, ps, psb, co) in ((0, XR, XRb, pA, pAb, 0), (1, XK, XKb, pA, pAb, TT), (2, XV, XVb, pB, pBb, 0)):
                wb, wbb = wl.load(wrkv, jj * D, 16, e0, 128)
                for dt in range(16):
                    P.op("pe", lambda e, o=ps[:, co:co + TT], w=wb[:, dt, :], r=src[:, dt, :], s=(dt == 0), t=(dt == 15):
                         e.matmul(o, lhsT=w, rhs=r, start=s, stop=t), reads=[wbb, srcb], writes=[psb])
            wb, wbb = wls.load(w2, 0, 1, e0, 128, pp=96)
            P.op("pe", lambda e, o=pB[:, TT:2 * TT], w=wb[:96, 0, :]: e.matmul(o, lhsT=w, rhs=TW[:96, :], start=True, stop=True),
                 reads=[wbb, TWb], writes=[pBb])
            wb, wbb = wls.load(a2, 0, 1, e0, 128, pp=96)
            P.op("pe", lambda e, o=pC[:, 0:TT], w=wb[:96, 0, :]: e.matmul(o, lhsT=w, rhs=TA[:96, :], start=True, stop=True),
                 reads=[wbb, TAb], writes=[pCb])
            wb, wbb = wls.load(g2, 0, 2, e0, 128)
            for kt in range(2):
                P.op("pe", lambda e, o=pC[:, TT:2 * TT], w=wb[:, kt, :], r=TG[:, kt, :], s=(kt == 0), t=(kt == 1):
                     e.matmul(o, lhsT=w, rhs=r, start=s, stop=t), reads=[wbb, TGb], writes=[pCb])
            r_ps, k_ps, v_ps, w_ps, a_ps, g_ps = pA[:, 0:TT], pA[:, TT:2 * TT], pB[:, 0:TT], pB[:, TT:2 * TT], pC[:, 0:TT], pC[:, TT:2 * TT]
            sg, sgb = tmp("sg")
            P.op("act", lambda e, o=sg[:], i=w_ps, b=NEG[:, 0, p:p + 1]: e.activation(out=o, in_=i, func=AF.Exp, bias=b, scale=-1.0),
                 reads=[pBb, mb], writes=[sgb])
            P.op("dve", lambda e, o=sg[:]: e.tensor_scalar(out=o, in0=o, scalar1=1.0, scalar2=None, op0=ALU.add), reads=[sgb], writes=[sgb])
            P.op("dve", lambda e, o=sg[:]: e.reciprocal(out=o, in_=o), reads=[sgb], writes=[sgb])
            lw, lwb = sg, sgb
            LWS = -0.6065306597126334
            cl, clb = tmp("cl")
            for ch in range(NCH):
                P.op("dve", lambda e, o=cl[:, ch * C:(ch + 1) * C], i=lw[:, ch * C:(ch + 1) * C]:
                     e.tensor_tensor_scan(out=o, data0=ones64[:], data1=i, initial=0.0, op0=ALU.mult, op1=ALU.add),
                     reads=[lwb, cb_], writes=[clb])
            eg, egb = tmp("eg")
            eig, eigb = tmp("eig")
            eex, eexb = tmp("eex")
            P.op("act", lambda e, o=eg[:], i=cl[:]: e.activation(out=o, in_=i, func=AF.Exp, scale=LWS), reads=[clb], writes=[egb])
            P.op("act", lambda e, o=eig[:], i=cl[:]: e.activation(out=o, in_=i, func=AF.Exp, scale=-LWS), reads=[clb], writes=[eigb])
            P.op("dve", lambda e, o=eex[:], i=cl[:], j_=lw[:]: e.tensor_tensor(out=o, in0=i, in1=j_, op=ALU.subtract), reads=[clb, lwb], writes=[eexb])
            P.op("act", lambda e, o=eex[:]: e.activation(out=o, in_=o, func=AF.Exp, scale=LWS), reads=[eexb], writes=[eexb])
            P.op("pool", lambda e, o=GC[:, p, :], i=eg[:].rearrange("p (a b) -> p a b", a=NCH)[:, :, C - 1]: e.tensor_copy(out=o, in_=i),
                 reads=[egb], writes=[GCb])
            av, avb = tmp("av")
            P.op("act", lambda e, o=av[:], i=a_ps, b=NEG[:, 1, p:p + 1]: e.activation(out=o, in_=i, func=AF.Exp, bias=b, scale=-1.0),
                 reads=[pCb, mb], writes=[avb])
            P.op("dve", lambda e, o=av[:]: e.tensor_scalar(out=o, in0=o, scalar1=1.0, scalar2=None, op0=ALU.add), reads=[avb], writes=[avb])
            P.op("dve", lambda e, o=av[:]: e.reciprocal(out=o, in_=o), reads=[avb], writes=[avb])
            P.op("act", lambda e, o=GT[:, p, :], i=g_ps: e.activation(out=o, in_=i, func=AF.Copy), reads=[pCb], writes=[GTb])
            vf, vfb = tmp("vf")
            P.op("act", lambda e, o=vf[:], i=v_ps: e.activation(out=o, in_=i, func=AF.Copy), reads=[pBb], writes=[vfb])
            P.op("pool", lambda e, o=VT[:, p, :], i=vf[:]: e.tensor_copy(out=o, in_=i), reads=[vfb], writes=[VTb])
            kk, kkb = tmp("kk")
            P.op("dve", lambda e, o=kk[:], i=k_ps, s=KKv[:, p:p + 1]: e.tensor_scalar(out=o, in0=i, scalar1=s, scalar2=None, op0=ALU.mult),
                 reads=[pAb, mb], writes=[kkb])
            k2, k2b = tmp("ksq", 1, BF16)
            P.op("pool", lambda e, o=k2[:], i=kk[:]: e.tensor_tensor(out=o, in0=i, in1=i, op=ALU.mult), reads=[kkb], writes=[k2b])
            P.op("pe", lambda e, o=pD[:, 0:TT], r=k2[:]: e.matmul(o, lhsT=bo16[:], rhs=r, start=True, stop=True), reads=[k2b, cb_], writes=[pDb])
            rn, rnb = tmp("rn")
            P.op("dve", lambda e, o=rn[:], i=pD[:, 0:TT]: e.tensor_scalar(out=o, in0=i, scalar1=5.5e-20, scalar2=None, op0=ALU.max), reads=[pDb], writes=[rnb])
            P.op("act", lambda e, o=rn[:]: e.activation(out=o, in_=o, func=AF.Ln), reads=[rnb], writes=[rnb])
            P.op("act", lambda e, o=rn[:]: e.activation(out=o, in_=o, func=AF.Exp, scale=-0.5), reads=[rnb], writes=[rnb])
            P.op("dve", lambda e, o=kk[:], r=rn[:]: e.tensor_tensor(out=o, in0=o, in1=r, op=ALU.mult), reads=[kkb, rnb], writes=[kkb])
            kf_, kfb = tmp("kf")
            P.op("dve", lambda e, o=kf_[:], i=av[:], s=KA[:, p:p + 1]: e.tensor_scalar(out=o, in0=i, scalar1=-1.0, scalar2=s, op0=ALU.add, op1=ALU.mult),
                 reads=[avb, mb], writes=[kfb])
            P.op("dve", lambda e, o=kf_[:], i=k_ps: e.scalar_tensor_tensor(out=o, in0=o, scalar=1.0, in1=i, op0=ALU.add, op1=ALU.mult),
                 reads=[kfb, pAb], writes=[kfb])
            P.op("dve", lambda e, o=RT[:, p, :], i=r_ps, g_=eg[:]: e.tensor_tensor(out=o, in0=i, in1=g_, op=ALU.mult), reads=[pAb, egb], writes=[RTb])
            P.op("pool", lambda e, o=KT[:, p, :], i=kf_[:], g_=eig[:]: e.tensor_tensor(out=o, in0=i, in1=g_, op=ALU.mult), reads=[kfb, eigb], writes=[KTb])
            bb, bbb = tmp("bb")
            P.op("pool", lambda e, o=bb[:], i=kk[:], a_=av[:]: e.tensor_tensor(out=o, in0=i, in1=a_, op=ALU.mult), reads=[kkb, avb], writes=[bbb])
            P.op("pool", lambda e, o=BT[:, p, :], i=bb[:], g_=eig[:]: e.tensor_tensor(out=o, in0=i, in1=g_, op=ALU.mult), reads=[bbb, eigb], writes=[BTb])
            P.op("dve", lambda e, o=AT[:, p, :], i=kk[:], g_=eex[:]: e.scalar_tensor_tensor(out=o, in0=i, scalar=-1.0, in1=g_, op0=ALU.mult, op1=ALU.mult),
                 reads=[kkb, eexb], writes=[ATb])
            rk_, rkb = tmp("rkp", 1, BF16)
            P.op("dve", lambda e, o=rk_[:], i=r_ps, s=RK[:, p:p + 1], k_=kf_[:]: e.scalar_tensor_tensor(out=o, in0=i, scalar=s, in1=k_, op0=ALU.mult, op1=ALU.mult),
                 reads=[pAb, kfb, mb], writes=[rkb])
            P.op("pe", lambda e, o=pD[:, TT:2 * TT], r=rk_[:]: e.matmul(o, lhsT=bo16[:], rhs=r, start=True, stop=True), reads=[rkb, cb_], writes=[pDb])
            P.op("dve", lambda e, o=BON[:, p, :], i=pD[:, TT:2 * TT], v_=vf[:]: e.tensor_tensor(out=o, in0=i, in1=v_, op=ALU.mult),
                 reads=[pDb, vfb], writes=[BONb])

        P.barrier()
        if dbg:
            for i_, (t_, b_) in enumerate(((RT, RTb), (KT, KTb), (BT, BTb), (AT, ATb), (VT, VTb), (GT, GTb), (BON, BONb))):
                P.dma("sp", dbf[i_], t_[:].rearrange("p a b -> p (a b)"), reads=[b_])
            P.dma("sp", dgc, GC[:].rearrange("p a b -> p (a b)"), reads=[GCb])
            P.barrier()
        zb = Buf()
        SB = [dict(N=Buf(), L=Buf(), P=Buf()) for _ in range(2)]
        for z_ in ZERO_LIST:
            if z_.dtype == BF16:
                P.op("pool", lambda e, z_=z_: e.memset(z_, 0.0), writes=[zb])
            else:
                P.op("dve", lambda e, z_=z_: e.tensor_scalar(out=z_, in0=idn[:], scalar1=0.0, scalar2=None, op0=ALU.mult),
                     reads=[cb_], writes=[zb])
        for ch in range(NCH):
            cc = slice(ch * C, (ch + 1) * C)
            inb = [Buf() for _ in range(4)]
            for (src, srcb, dst) in ((AT, ATb, ATbd), (BT, BTb, BTbd), (KT, KTb, KTbd), (VT, VTb, VTbd)):
                for hf in range(2):
                    sl = slice(64 * hf, 64 * hf + 64)
                    P.op("dve" if hf == 0 else "act",
                         (lambda e, o=dst[sl, :, 64 * hf:64 * hf + 64], i=src[sl, :, cc]: e.tensor_copy(out=o, in_=i)) if hf == 0 else
                         (lambda e, o=dst[sl, :, 64 * hf:64 * hf + 64], i=src[sl, :, cc]: e.activation(out=o, in_=i, func=AF.Copy)),
                         reads=[srcb, zb], writes=inb)
            gb = [dict((n, Buf()) for n in ("N", "L", "Mak", "Mrb", "Mrk", "V", "B", "K", "P", "Z", "U")) for _ in range(4)]
            for g_ in range(4):
                pg = slice(g_ * 4, g_ * 4 + 4)
                b_ = gb[g_]
                for (src, dst, nm) in ((VTbd, Vbd, "V"), (BTbd, Bbd, "B"), (KTbd, Kbd, "K")):
                    ps, psb = k.ps()
                    psv = ps[:].bitcast(BF16)[:, 0:512].rearrange("p (a b) -> p a b", a=4)
                    for i in range(4):
                        P.op("pe", lambda e, o=psv[:, i, :], w=src[:, g_ * 4 + i, :]: e.transpose(out=o, in_=w, identity=idn[:, 0, :]),
                             reads=[inb[g_], cb_], writes=[psb])
                    P.op("act", lambda e, o=dst[:, pg, :], i=psv: e.activation(out=o, in_=i, func=AF.Copy), reads=[psb], writes=[b_[nm]])
                    if nm == "V":
                        for hf in range(2):
                            sl = slice(64 * hf, 64 * hf + 64)
                            P.op("dve", lambda e, o=Vstk[sl, pg, :], i=psv[sl, :, 64 * hf:64 * hf + 64]: e.tensor_copy(out=o, in_=i),
                                 reads=[psb], writes=[b_[nm]])
            def step1(g_):
                ps1, ps1b = k.ps()
                ps2, ps2b = k.ps()
                ps3, ps3b = k.ps()
                b_ = gb[g_]
                cs_ = CH[g_ % 2]
                sb_ = SB[g_ % 2]
                for i in range(4):
                    p = g_ * 4 + i
                    cs = slice(i * 64, i * 64 + 64)
                    cs2 = slice(256 + i * 64, 256 + i * 64 + 64)
                    for (ps, psb, csl, lh, rh, rhb) in ((ps1, ps1b, cs, BTbd, AT, ATb), (ps1, ps1b, cs2, ATbd, BT, BTb),
                                                        (ps2, ps2b, cs, KTbd, AT, ATb), (ps2, ps2b, cs2, BTbd, RT, RTb),
                                                        (ps3, ps3b, cs, KTbd, RT, RTb)):
                        P.op("pe", lambda e, o=ps[:, csl], w=lh[:, p, :], r=rh[:, p, cc]: e.matmul(o, lhsT=w, rhs=r, start=True, stop=True),
                             reads=[inb[g_], rhb], writes=[psb])
                pg = slice(g_ * 4, g_ * 4 + 4)
                v1 = ps1[:, 0:256].rearrange("p (a b) -> p a b", a=4)
                v1b = ps1[:, 256:512].rearrange("p (a b) -> p a b", a=4)
                v2 = ps2[:, 0:256].rearrange("p (a b) -> p a b", a=4)
                v2b = ps2[:, 256:512].rearrange("p (a b) -> p a b", a=4)
                v3 = ps3[:, 0:256].rearrange("p (a b) -> p a b", a=4)
                for hf in range(2):
                    sl = slice(64 * hf, 64 * hf + 64)
                    fs = slice(64 * hf, 64 * hf + 64)
                    P.op("dve", lambda e, o=cs_["N"][0]["w"][sl, :, fs], i=v1[sl], m=mS[sl]: e.tensor_tensor(out=o, in0=i, in1=m, op=ALU.mult),
                         reads=[ps1b, cb_, zb], writes=[sb_["N"]])
                    P.op("dve", lambda e, o=cs_["L"][0]["w"][sl, :, fs], i=v1b[sl], m=mL[sl]: e.tensor_tensor(out=o, in0=i, in1=m, op=ALU.mult),
                         reads=[ps1b, cb_, zb], writes=[sb_["L"]])
                    P.op("dve", lambda e, o=MakT[sl, pg, fs], i=v2[sl], m=mS[sl]: e.tensor_tensor(out=o, in0=i, in1=m, op=ALU.mult),
                         reads=[ps2b, cb_, zb], writes=[b_["Mak"]])
                P.op("dve", lambda e, o=MrbT[:, pg, :], i=v2b, m=mI[:]: e.tensor_tensor(out=o, in0=i, in1=m, op=ALU.mult),
                     reads=[ps2b, cb_], writes=[b_["Mrb"]])
                P.op("dve", lambda e, o=MrkT[:, pg, :], i=v3, m=mI[:]: e.tensor_tensor(out=o, in0=i, in1=m, op=ALU.mult),
                     reads=[ps3b, cb_], writes=[b_["Mrk"]])
                P.op("dve", lambda e, o=cs_["P"]["w"], i=cs_["N"][0]["r"]: e.tensor_tensor(out=o, in0=i, in1=idn[:], op=ALU.add),
                     reads=[sb_["N"], cb_], writes=[sb_["P"]])

            def chain_level(g_, lev):
                a_, n_ = (lev - 1) % 2, lev % 2
                pg = slice(g_ * 4, g_ * 4 + 4)
                cs_ = CH[g_ % 2]
                sb_ = SB[g_ % 2]
                Ns, Ls, Pc = cs_["N"], cs_["L"], cs_["P"]
                if lev < 5:
                    psn, psnb = k.ps()
                    for i in range(4):
                        P.op("pe", lambda e, o=psn[:, i * 128:(i + 1) * 128], w=Ls[a_]["m"][:, i, :], r=Ns[a_]["m"][:, i, :]:
                             e.matmul(o, lhsT=w, rhs=r, start=True, stop=True), reads=[sb_["L"], sb_["N"]], writes=[psnb])
                psl_, pslb = k.ps()
                for i in range(4):
                    P.op("pe", lambda e, o=psl_[:, i * 128:(i + 1) * 128], w=Ns[a_]["m"][:, i, :], r=Ls[a_]["m"][:, i, :]:
                         e.matmul(o, lhsT=w, rhs=r, start=True, stop=True), reads=[sb_["L"], sb_["N"]], writes=[pslb])
                if lev < 5:
                    P.op("act", lambda e, o=Ns[n_]["w"], i=psn[:].rearrange("p (a b) -> p a b", a=4): e.activation(out=o, in_=i, func=AF.Copy),
                         reads=[psnb], writes=[sb_["N"]])
                P.op("act", lambda e, o=Ls[n_]["w"], i=psl_[:].rearrange("p (a b) -> p a b", a=4): e.activation(out=o, in_=i, func=AF.Copy),
                     reads=[pslb], writes=[sb_["L"]])
                psp, pspb = k.ps()
                for i in range(4):
                    P.op("pe", lambda e, o=psp[:, i * 128:(i + 1) * 128], w=Ls[n_]["m"][:, i, :], r=Pc["m"][:, i, :]:
                         e.matmul(o, lhsT=w, rhs=r, start=True, stop=True), reads=[sb_["L"], sb_["P"]], writes=[pspb])
                if lev < 5:
                    P.op("dve", lambda e, o=Pc["w"], q=Pc["r"], i=psp[:].rearrange("p (a b) -> p a b", a=4): e.tensor_tensor(out=o, in0=i, in1=q, op=ALU.add),
                         reads=[pspb, sb_["P"]], writes=[sb_["P"]])
                else:
                    P.op("dve", lambda e, o=PF[:, pg, :], i=psp[:].rearrange("p (a b) -> p a b", a=4), q=Pc["r"]: e.tensor_tensor(out=o, in0=i, in1=q, op=ALU.add),
                         reads=[pspb, sb_["P"]], writes=[gb[g_]["P"]])

            for gp in ((0, 1), (2, 3)):
                for g_ in gp:
                    step1(g_)
                for lev in range(1, 6):
                    for g_ in gp:
                        chain_level(g_, lev)
            for g_ in range(4):
                pg = slice(g_ * 4, g_ * 4 + 4)
                b_ = gb[g_]
                hb_ = Hb_[g_]
                psz, pszb = k.ps()
                for i in range(4):
                    p = g_ * 4 + i
                    P.op("pe", lambda e, o=psz[:, i * 64:(i + 1) * 64], w=ATbd[:, p, :], r=Hstk[:, p, :]: e.matmul(o, lhsT=w, rhs=r, start=True, stop=False),
                         reads=[inb[g_], hb_], writes=[pszb])
                    P.op("pe", lambda e, o=psz[:, i * 64:(i + 1) * 64], w=MakT[:, p, :], r=Vstk[:, p, :]: e.matmul(o, lhsT=w, rhs=r, start=False, stop=True),
                         reads=[b_["Mak"], b_["V"]], writes=[pszb])
                P.op("act", lambda e, o=Zs[:, pg, :], i=psz[:, 0:256].rearrange("p (a b) -> p a b", a=4): e.activation(out=o, in_=i, func=AF.Copy),
                     reads=[pszb], writes=[b_["Z"]])
                psu, psub = k.ps()
                for i in range(4):
                    p = g_ * 4 + i
                    P.op("pe", lambda e, o=psu[:, i * 64:(i + 1) * 64], w=PF[:, p, :], r=Zs[:, p, :]: e.matmul(o, lhsT=w, rhs=r, start=True, stop=True),
                         reads=[b_["P"], b_["Z"]], writes=[psub])
                psuv = psu[:, 0:256].rearrange("p (a b) -> p a b", a=4)
                P.op("act", lambda e, o=Us[:, pg, :], i=psuv: e.activation(out=o, in_=i, func=AF.Copy), reads=[psub], writes=[b_["U"]])
                for hf in range(2):
                    sl = slice(64 * hf, 64 * hf + 64)
                    P.op("dve", lambda e, o=Ubd[sl, pg, 64 * hf:64 * hf + 64], i=psuv[sl]: e.tensor_copy(out=o, in_=i), reads=[psub, zb], writes=[b_["U"]])
                psy, psyb = k.ps()
                psh, pshb = k.ps()
                for i in range(4):
                    p = g_ * 4 + i
                    oy = psy[:, i * 64:(i + 1) * 64]
                    P.op("pe", lambda e, o=oy, w=Hbd[:, p, :], r=RT[:, p, cc]: e.matmul(o, lhsT=w, rhs=r, start=True, stop=False),
                         reads=[hb_, RTb], writes=[psyb])
                    P.op("pe", lambda e, o=oy, w=Ubd[:, p, :], r=MrbT[:, p, :]: e.matmul(o, lhsT=w, rhs=r, start=False, stop=False),
                         reads=[b_["U"], b_["Mrb"]], writes=[psyb])
                    P.op("pe", lambda e, o=oy, w=Vbd[:, p, :], r=MrkT[:, p, :]: e.matmul(o, lhsT=w, rhs=r, start=False, stop=True),
                         reads=[b_["V"], b_["Mrk"]], writes=[psyb])
                    oh = psh[:, i * 64:(i + 1) * 64]
                    P.op("pe", lambda e, o=oh, w=Bbd[:, p, :], r=Us[:, p, :]: e.matmul(o, lhsT=w, rhs=r, start=True, stop=False),
                         reads=[b_["B"], b_["U"]], writes=[pshb])
                    P.op("pe", lambda e, o=oh, w=Kbd[:, p, :], r=Vstk[:, p, :]: e.matmul(o, lhsT=w, rhs=r, start=False, stop=True),
                         reads=[b_["K"], b_["V"]], writes=[pshb])
                P.op("act", lambda e, o=YT[:, pg, cc], i=psy[:, 0:256].rearrange("p (a b) -> p a b", a=4): e.activation(out=o, in_=i, func=AF.Copy),
                     reads=[psyb], writes=[YTb])
                P.op("dve", lambda e, o=Hf[:, pg, :], i=psh[:, 0:256].rearrange("p (a b) -> p a b", a=4): e.tensor_tensor(out=o, in0=o, in1=i, op=ALU.add),
                     reads=[pshb, hb_], writes=[hb_])
                for i in range(4):
                    p = g_ * 4 + i
                    P.op("dve", lambda e, o=Hf[:, p, :], s=GC[:, p, ch:ch + 1]: e.tensor_scalar(out=o, in0=o, scalar1=s, scalar2=None, op0=ALU.mult),
                         reads=[hb_, GCb], writes=[hb_])
                P.op("pool", lambda e, o=Hstk[:, pg, :], i=Hf[:, pg, :]: e.tensor_copy(out=o, in_=i), reads=[hb_], writes=[hb_])
                for hf in range(2):
                    sl = slice(64 * hf, 64 * hf + 64)
                    P.op("pool", lambda e, o=Hbd[sl, pg, 64 * hf:64 * hf + 64], i=Hf[sl, pg, :]: e.tensor_copy(out=o, in_=i), reads=[hb_], writes=[hb_])

        P.barrier()
        OINb, Y2b, Xb = Buf(), Buf(), Buf()
        P.dma("sp", X, xT3[:, :, t0:t0 + TT], writes=[Xb])
        for p in range(16):
            psm, psmb = k.ps()
            P.op("pe", lambda e, o=psm[:, 0:TT], r=YT[:, p, :]: e.matmul(o, lhsT=bo32[:], rhs=r, start=True, stop=True), reads=[YTb, cb_], writes=[psmb])
            yc, ycb = tmp("cl")
            P.op("dve", lambda e, o=yc[:], m=psm[:, 0:TT], y=YT[:, p, :]: e.scalar_tensor_tensor(out=o, in0=m, scalar=-1.0 / 64, in1=y, op0=ALU.mult, op1=ALU.add),
                 reads=[psmb, YTb], writes=[ycb])
            ysq, ysqb = tmp("rn")
            P.op("pool", lambda e, o=ysq[:], i=yc[:]: e.tensor_tensor(out=o, in0=i, in1=i, op=ALU.mult), reads=[ycb], writes=[ysqb])
            P.op("pe", lambda e, o=psm[:, TT:2 * TT], r=ysq[:]: e.matmul(o, lhsT=bo32[:], rhs=r, start=True, stop=True), reads=[ysqb, cb_], writes=[psmb])
            rs, rsb = tmp("bb")
            P.op("act", lambda e, o=rs[:], i=psm[:, TT:2 * TT]: e.activation(out=o, in_=i, func=AF.Sqrt, bias=c["eps"][:, 1:2], scale=1.0 / 64),
                 reads=[psmb, c["onesb"]], writes=[rsb])
            P.op("dve", lambda e, o=rs[:]: e.reciprocal(out=o, in_=o), reads=[rsb], writes=[rsb])
            P.op("dve", lambda e, o=yc[:], r=rs[:]: e.tensor_tensor(out=o, in0=o, in1=r, op=ALU.mult), reads=[ycb, rsb], writes=[ycb])
            P.op("dve", lambda e, o=yc[:], s=LNW[:, p:p + 1], b=BON[:, p, :]: e.scalar_tensor_tensor(out=o, in0=o, scalar=s, in1=b, op0=ALU.mult, op1=ALU.add),
                 reads=[ycb, BONb, mb], writes=[ycb])
            P.op("dve", lambda e, o=OIN[:, p, :], i=yc[:], s=LNB[:, p:p + 1], g_=GT[:, p, :]: e.scalar_tensor_tensor(out=o, in0=i, scalar=s, in1=g_, op0=ALU.add, op1=ALU.mult),
                 reads=[ycb, GTb, mb], writes=[OINb])
        if dbg:
            P.dma("sp", dyt, YT.rearrange("p a b -> p (a b)"), reads=[YTb])
            P.dma("sp", doin, OIN.rearrange("p a b -> p (a b)"), reads=[OINb])
        for eb in range(16):
            wb, wbb = wl.load(wo, 0, 16, eb * 128, 128)
            ps, psb = k.ps()
            for p in range(16):
                P.op("pe", lambda e, o=ps[:, :TT], w=wb[:, p, :], r=OIN[:, p, :], s=(p == 0), t=(p == 15):
                     e.matmul(o, lhsT=w, rhs=r, start=s, stop=t), reads=[wbb, OINb], writes=[psb])
            P.op("act", lambda e, o=Y2[:, eb, :], i=ps[:, :TT]: e.activation(out=o, in_=i, func=AF.Copy), reads=[psb], writes=[Y2b])
        post_residual(k, c, X, Xb, Y2, Y2b, G, mb, TT)
        P.dma("sp", oT3[:, :, t0:t0 + TT], X, reads=[Xb])
        P.barrier()
    k.close()
    return nc


def _pd(v, n=16):
    return np.ascontiguousarray(np.asarray(v, dtype=np.float32).reshape(n, 128).T)


def _run(nc, maps):
    res = run_bass_kernel_spmd(nc, maps, core_ids=list(range(len(maps))))
    return res.results


class _Root:
    pass


def build_fused():
    root = _Root()
    root.nc = bass.Bass("TRN2", target_bir_lowering=False)
    root.semstack = ExitStack()
    nc = root.nc
    xT = nc.dram_tensor("xT", [D, T], F32, kind="ExternalInput").ap()
    oT = nc.dram_tensor("oT", [D, T], F32, kind="ExternalOutput").ap()
    modsI = nc.dram_tensor("mods_i", [128, 192], F32, kind="Internal").ap()
    xa = nc.dram_tensor("xa_i", [D, T], F32, kind="Internal").ap()
    xb = nc.dram_tensor("xb_i", [D, T], F32, kind="Internal").ap()
    xc = nc.dram_tensor("xc_i", [D, T], F32, kind="Internal").ap()
    def decl(prefix, items):
        dm, pre = {}, []
        for name, shape in items:
            w = nc.dram_tensor(prefix + name, list(shape), F32, kind="ExternalInput").ap()
            wb = nc.dram_tensor(prefix + name + "_bf", list(shape), BF16, kind="Internal").ap()
            dm[name + "_bf"] = wb
            pre.append((wb, w))
        return dm, pre
    rw_dm, rw_pre = decl("r_", [("w_rkv", [3 * D, D]), ("w1", [D, 96]), ("w2", [96, D]), ("a1", [D, 96]), ("a2", [96, D]),
                                ("g1", [D, 256]), ("g2", [256, D]), ("w_o", [D, D])])
    f0_dm, f0_pre = decl("f0_", [("w_up", [D, FF]), ("w_dn", [FF, D])])
    f1_dm, f1_pre = decl("f1_", [("w_up", [D, FF]), ("w_dn", [FF, D])])
    a_dm, a_pre = decl("a_", [("kv_down", [D, 576]), ("kv_uk", [512, D]), ("kv_uv", [512, D]), ("w_dq", [D, 512]),
                              ("w_uq", [512, 16 * 192]), ("w_o", [D, D])])
    build_mods(env=(root, {"mods": modsI}, "m_", rw_pre))
    build_rwkv(0, env=(root, dict(rw_dm, xT=xT, mods=modsI, oT=xa), "r_", f0_pre + a_pre))
    build_mlp(0, env=(root, dict(f0_dm, xT=xa, mods=modsI, oT=xb), "f0_", f1_pre))
    build_mla(1, env=(root, dict(a_dm, xT=xb, mods=modsI, oT=xc), "a_"))
    build_mlp(1, env=(root, dict(f1_dm, xT=xc, mods=modsI, oT=oT), "f1_"))
    root.semstack.close()
    return nc


def kernel(x, c, positions, ada_w, ada_b, norm_g, mlp_up, mlp_down,
           rw_mu, rw_rkv, rw_w0, rw_w1, rw_w2, rw_a0, rw_a1, rw_a2, rw_g1, rw_g2,
           rw_kk, rw_ka, rw_rk, rw_lnx, rw_o,
           mla_dq, mla_qnorm, mla_uq, mla_o,
           kv_in_g, kv_down, kv_norm, kv_uk, kv_uv):
    f32 = np.float32
    A = lambda a: np.ascontiguousarray(np.asarray(a))
    x = A(x).astype(f32, copy=False)
    B = x.shape[0]
    cc = A(c)
    pos = A(positions).astype(np.int32, copy=False)
    vl = [A(rw_mu)[0][j] for j in range(6)] + [A(rw_w0)[0], A(rw_a0)[0], A(rw_kk)[0], A(rw_ka)[0], A(rw_rk)[0].reshape(-1),
                                              A(rw_lnx)[0][0], A(rw_lnx)[0][1]]
    inv = (1.0 / (10000.0 ** (np.arange(0, 64, 2, dtype=np.float32) / 64))).astype(f32)
    shared = {
        "m_ada_w": A(ada_w),
        "m_ada_b_pd": np.ascontiguousarray(A(ada_b).reshape(2, 96, 128).transpose(2, 0, 1)),
        "m_norm_g_pd": np.ascontiguousarray(A(norm_g).reshape(2, 4, 16, 128).transpose(3, 0, 1, 2)),
        "r_rw_vec": np.ascontiguousarray(np.stack([_pd(v) for v in vl], 1)).astype(f32),
        "r_w_rkv": A(rw_rkv)[0].reshape(3 * D, D), "r_w1": A(rw_w1)[0], "r_w2": A(rw_w2)[0], "r_a1": A(rw_a1)[0],
        "r_a2": A(rw_a2)[0], "r_g1": A(rw_g1)[0], "r_g2": A(rw_g2)[0], "r_w_o": A(rw_o)[0],
        "f0_w_up": A(mlp_up)[0], "f0_w_dn": A(mlp_down)[0], "f1_w_up": A(mlp_up)[1], "f1_w_dn": A(mlp_down)[1],
        "a_ropec": np.ascontiguousarray(np.stack([np.concatenate([inv, inv]), np.concatenate([-np.ones(32), np.ones(32)])], 1)).astype(f32),
        "a_mla_vec": np.ascontiguousarray(np.concatenate([_pd(kv_in_g), _pd(kv_norm, 4), _pd(A(mla_qnorm)[0], 4)], 1)).astype(f32),
        "a_kv_down": A(kv_down), "a_kv_uk": A(kv_uk).reshape(512, D), "a_kv_uv": A(kv_uv).reshape(512, D),
        "a_w_dq": A(mla_dq)[0], "a_w_uq": A(mla_uq)[0].reshape(512, 16 * 192), "a_w_o": A(mla_o)[0],
    }
    maps = [dict(shared, xT=np.ascontiguousarray(x[b].T), m_c_pd=_pd(cc[b]),
                 a_posr=np.ascontiguousarray(np.broadcast_to(pos[b][None, :], (64, T)))) for b in range(B)]
    r = _run(build_fused(), maps)
    return np.stack([np.ascontiguousarray(r[b]["oT"].T) for b in range(B)]).astype(f32, copy=False)
```

```python
import numpy as np
import concourse.bass as bass
import concourse.mybir as mybir
from concourse.bass_utils import run_bass_kernel_spmd
from contextlib import ExitStack

F32 = mybir.dt.float32
BF16 = mybir.dt.bfloat16
F32R = mybir.dt.float32r
CHAIN_DT = F32
I32 = mybir.dt.int32
ALU = mybir.AluOpType
AF = mybir.ActivationFunctionType

D = 2048
T = 2048
ND = 16
FF = 8192
NCORES = 8
EPS = 1e-6

EPOCH = 30000
N_DMA_SEMS = 6
SAME_ENGINE_SYNC = False


class Buf:
    __slots__ = ("w", "r")

    def __init__(self):
        self.w = None
        self.r = {}


class Prog:
    ENGS = ("pe", "dve", "act", "pool", "sp")
    NSEM = 0

    def __init__(self, nc, semstack=None):
        self.nc = nc
        self.semstack = semstack
        self.q = {e: [] for e in self.ENGS}
        self.cnt = {e: 0 for e in self.ENGS}
        self.seen = {e: {} for e in self.ENGS}
        self.dma_cnt = {}
        self.dma_rr = {e: 0 for e in self.ENGS}
        self.keys = set()

    def _wait(self, eng, key, val):
        if self.seen[eng].get(key, 0) >= val:
            return
        self.seen[eng][key] = val
        self.q[eng].append(("wait", key, val))

    def _deps(self, eng, reads, writes):
        deps = {}
        for b in reads:
            if b.w is not None:
                k, v = b.w
                if deps.get(k, 0) < v:
                    deps[k] = v
        for b in writes:
            if b.w is not None:
                k, v = b.w
                if deps.get(k, 0) < v:
                    deps[k] = v
            for k, v in b.r.items():
                if deps.get(k, 0) < v:
                    deps[k] = v
        for k, v in deps.items():
            if k[0] == "E" and k[1] == eng and (eng == "pe" or not SAME_ENGINE_SYNC):
                continue
            self._wait(eng, k, v)

    def _mark(self, tok, reads, writes):
        k, v = tok
        for b in reads:
            if b.r.get(k, 0) < v:
                b.r[k] = v
        for b in writes:
            b.w = tok
            b.r = {}

    def op(self, eng, fn, reads=(), writes=()):
        self._deps(eng, reads, writes)
        n = self.cnt[eng]
        self.cnt[eng] = n + 1
        key = ("E", eng, n // EPOCH)
        self.keys.add(key)
        self.q[eng].append(("op", fn, key))
        self._mark((key, n % EPOCH + 1), reads, writes)

    def dma(self, qeng, out, in_, reads=(), writes=()):
        self._deps(qeng, reads, writes)
        s = self.dma_rr[qeng]
        self.dma_rr[qeng] = (s + 1) % N_DMA_SEMS
        gen = 0
        while self.dma_cnt.get(("D", qeng, s, gen), 0) + 16 > EPOCH:
            gen += 1
        key = ("D", qeng, s, gen)
        prev = self.dma_cnt.get(key, 0)
        if prev > 0:
            self._wait(qeng, key, prev)
        elif gen > 0:
            pk = ("D", qeng, s, gen - 1)
            self._wait(qeng, pk, self.dma_cnt[pk])
        self.dma_cnt[key] = prev + 16
        self.keys.add(key)
        self.q[qeng].append(("dma", (out, in_), key))
        self._mark((key, prev + 16), reads, writes)

    def barrier(self):
        toks = []
        for e in self.ENGS:
            n = self.cnt[e]
            if n > 0:
                toks.append((("E", e, (n - 1) // EPOCH), (n - 1) % EPOCH + 1))
        toks += list(self.dma_cnt.items())
        for e in self.ENGS:
            for key, v in toks:
                if key[0] == "E" and key[1] == e:
                    continue
                self._wait(e, key, v)

    def finish(self, eng="sp"):
        for key, v in list(self.dma_cnt.items()):
            self._wait(eng, key, v)

    def emit(self):
        nc = self.nc
        engmap = {"pe": "tensor", "dve": "vector", "act": "scalar", "pool": "gpsimd", "sp": "sync"}
        with ExitStack() as st:
            sems = {}
            semst = self.semstack if self.semstack is not None else st
            for i, key in enumerate(sorted(self.keys, key=str)):
                Prog.NSEM += 1
                sems[key] = semst.enter_context(nc.semaphore("s%d" % Prog.NSEM))
            block = st.enter_context(nc.Block())
            for e in self.ENGS:
                items = self.q[e]
                if not items:
                    continue

                def body(eng, items=items):
                    for it in items:
                        if it[0] == "wait":
                            eng.wait_ge(sems[it[1]], it[2])
                        elif it[0] == "op":
                            it[1](eng).then_inc(sems[it[2]], 1)
                        else:
                            eng.dma_start(out=it[1][0], in_=it[1][1]).then_inc(sems[it[2]], 16)

                getattr(block, engmap[e])(body)


class Ctx:
    NT = 0

    def __init__(self, name, env=None):
        if env is None:
            self.nc = bass.Bass("TRN2", target_bir_lowering=False)
            self.semstack = None
            self.dmap = {}
            self.prefix = ""
            self.pre = []
        else:
            root, self.dmap, self.prefix = env[:3]
            self.pre = env[3] if len(env) > 3 else []
            self.nc = root.nc
            self.semstack = root.semstack
        self.P = Prog(self.nc, self.semstack)
        self.st = ExitStack()
        self.n = 0
        self.psl = []
        self.psi = 0
        self.rot = {}
        self.rotw = 512

    def dram(self, name, shape, dt, kind):
        if name in self.dmap:
            return self.dmap[name]
        return self.nc.dram_tensor(self.prefix + name, list(shape), dt, kind=kind).ap()

    def sb(self, shape, dt):
        Ctx.NT += 1
        return self.st.enter_context(self.nc.sbuf_tensor("t%d" % Ctx.NT, list(shape), dt))

    def init_psum(self, nf32=8):
        for i in range(nf32):
            Ctx.NT += 1
            t = self.st.enter_context(self.nc.psum_tensor("ps%d" % Ctx.NT, [128, 512], F32))
            self.psl.append((t, Buf()))

    def ps(self):
        r = self.psl[self.psi]
        self.psi = (self.psi + 1) % len(self.psl)
        return r

    def rotbuf(self, key, shape, dt, n=2):
        if key not in self.rot:
            self.rot[key] = [[(self.sb(shape, dt), Buf()) for _ in range(n)], 0]
        lst, i = self.rot[key]
        self.rot[key][1] = (i + 1) % len(lst)
        return lst[i]

    def do_pre(self):
        for (dst, src) in self.pre:
            cast_dma(self, dst, src)

    def close(self):
        self.P.finish("sp")
        self.P.emit()
        self.st.close()


class WT:
    def __init__(self, ap, buf):
        self.ap = ap
        self.buf = buf


def cast_dma(k, dst, src, buf=None, max_bytes=8 << 20):
    rows, cols = src.shape[0], src.shape[1]
    step = max(1, min(rows, max_bytes // (cols * 4)))
    for r0 in range(0, rows, step):
        r1 = min(rows, r0 + step)
        k.P.dma("pool", dst[r0:r1, :], src[r0:r1, :], writes=[buf] if buf is not None else [])


def wsrc(k, name, shape):
    if name + "_bf" in k.dmap:
        return WT(k.dmap[name + "_bf"], Buf())
    w = k.dram(name, shape, F32, "ExternalInput")
    wb = k.nc.dram_tensor(k.prefix + name + "_bf", list(shape), BF16, kind="Internal").ap()
    b = Buf()
    cast_dma(k, wb, w, b)
    return WT(wb, b)


class WLoader:
    def __init__(self, k, nk=16, ncols=256, nbf=3):
        self.k = k
        self.wbf = [(k.sb([128, nk, ncols], BF16), Buf()) for _ in range(nbf)]
        self.j = 0

    def load(self, W, r0, nk, c0, ncols, pp=128):
        P = self.k.P
        wb, wbb = self.wbf[self.j]
        self.j = (self.j + 1) % len(self.wbf)
        src = W.ap[r0:r0 + nk * pp, c0:c0 + ncols].rearrange("(k p) c -> p k c", p=pp)
        P.dma("sp", wb[:pp, :nk, :ncols], src, reads=[W.buf], writes=[wbb])
        return wb, wbb


def make_consts(k):
    P = k.P
    c = {}
    ones = k.sb([128, 128], BF16)
    c["ones"] = ones
    c["onesb"] = Buf()
    P.op("pool", lambda e: e.memset(ones[:], 1.0), writes=[c["onesb"]])
    eps = k.sb([128, 2], F32)
    c["eps"] = eps
    P.op("pool", lambda e: e.memset(eps[:, 0:1], EPS), writes=[c["onesb"]])
    P.op("pool", lambda e: e.memset(eps[:, 1:2], 64e-5), writes=[c["onesb"]])
    return c


def rms_rstd(k, c, X, Xb, TT, scale_div=D, epsap=None, ntile=ND):
    P = k.P
    if epsap is None:
        epsap = c["eps"][:, 0:1]
    ps, psb = k.ps()
    for dt in range(ntile):
        sq, sqb = k.rotbuf("sq", [128, k.rotw], BF16, 3)
        P.op("act", lambda e, o=sq[:, :TT], i=X[:, dt, :]: e.activation(out=o, in_=i, func=AF.Square),
             reads=[Xb], writes=[sqb])
        P.op("pe", lambda e, o=ps[:, :TT], r=sq[:, :TT], s=(dt == 0), t=(dt == ntile - 1):
             e.matmul(o, lhsT=c["ones"][:], rhs=r, start=s, stop=t), reads=[sqb, c["onesb"]], writes=[psb])
    rstd, rb = k.rotbuf("rstd", [128, k.rotw], F32, 2)
    P.op("act", lambda e, o=rstd[:, :TT], i=ps[:, :TT]: e.activation(
        out=o, in_=i, func=AF.Sqrt, bias=epsap, scale=1.0 / scale_div), reads=[psb, c["onesb"]], writes=[rb])
    P.op("dve", lambda e, o=rstd[:, :TT]: e.reciprocal(out=o, in_=o), reads=[rb], writes=[rb])
    return rstd, rb


def norm_mod(k, X, Xb, rstd, rb, A, Sh, mb, H, Hb, TT, col0=0):
    P = k.P
    for dt in range(ND):
        tmp, tb = k.rotbuf("nm_tmp", [128, k.rotw], F32, 2)
        P.op("dve", lambda e, o=tmp[:, :TT], i=X[:, dt, :], s=A[:, dt:dt + 1], r=rstd[:, :TT]:
             e.scalar_tensor_tensor(out=o, in0=i, scalar=s, in1=r, op0=ALU.mult, op1=ALU.mult),
             reads=[Xb, rb, mb], writes=[tb])
        P.op("act", lambda e, o=H[:, dt, col0:col0 + TT], i=tmp[:, :TT], s=Sh[:, dt:dt + 1]:
             e.activation(out=o, in_=i, func=AF.Identity, bias=s, scale=1.0),
             reads=[tb, mb], writes=[Hb])


def post_residual(k, c, X, Xb, Y, Yb, G, mb, TT):
    P = k.P
    rstd, rb = rms_rstd(k, c, Y, Yb, TT)
    for dt in range(ND):
        tmp, tb = k.rotbuf("nm_tmp", [128, k.rotw], F32, 2)
        P.op("dve", lambda e, o=tmp[:, :TT], i=Y[:, dt, :], s=G[:, dt:dt + 1], r=rstd[:, :TT]:
             e.scalar_tensor_tensor(out=o, in0=i, scalar=s, in1=r, op0=ALU.mult, op1=ALU.mult),
             reads=[Yb, rb, mb], writes=[tb])
        P.op("pool" if dt % 2 else "dve", lambda e, o=X[:, dt, :], i=tmp[:, :TT]: e.tensor_tensor(out=o, in0=o, in1=i, op=ALU.add),
             reads=[tb], writes=[Xb])


def build_mods(env=None):
    k = Ctx("mods", env)
    nc, P = k.nc, k.P
    k.do_pre()
    c_pd = k.dram("c_pd", [128, 16], F32, "ExternalInput")
    ada_w = k.dram("ada_w", [2, D, 6 * D], F32, "ExternalInput")
    ada_b = k.dram("ada_b_pd", [128, 2, 96], F32, "ExternalInput")
    ng = k.dram("norm_g_pd", [128, 2, 4, 16], F32, "ExternalInput")
    mods = k.dram("mods", [128, 2 * 96], F32, "ExternalOutput")
    k.init_psum(2)
    cin = k.sb([128, 16], F32)
    cact = k.sb([128, 16], F32)
    abt = k.sb([128, 2, 96], F32)
    ngt = k.sb([128, 2, 4, 16], F32)
    raw = k.sb([128, 2, 96], F32)
    outt = k.sb([128, 2, 6, 16], F32)
    cb_, sm_ = Buf(), Buf()
    P.dma("sp", cin[:], c_pd, writes=[cb_])
    P.dma("sp", abt[:], ada_b, writes=[sm_])
    P.dma("sp", ngt[:], ng, writes=[sm_])
    P.op("act", lambda e: e.activation(out=cact[:], in_=cin[:], func=AF.Silu), reads=[cb_], writes=[cb_])
    stg = [(k.sb([128, 16, 512], F32), Buf()) for _ in range(3)]
    rawb, ob = Buf(), Buf()
    one1 = k.sb([1, 1], F32)
    row = k.sb([1, 6 * D], F32)
    rowb = Buf()
    P.op("dve", lambda e: e.memset(one1[:], 1.0), writes=[cb_])
    k.psl = k.psl + [(k.st.enter_context(nc.psum_tensor("psx%d" % i, [128, 512], F32)), Buf()) for i in range(4)]
    for l in range(2):
        for cb in range(24):
            st, stb = stg[(l * 24 + cb) % 3]
            src = ada_w[l, :, cb * 512:(cb + 1) * 512].rearrange("(k p) c -> p k c", p=128)
            P.dma("sp", st[:], src, writes=[stb])
            psr, psrb = k.ps()
            for dt in range(16):
                P.op("pe", lambda e, o=psr[0:1, :], w=cact[:, dt:dt + 1], r=st[:, dt, :], s=(dt == 0), t=(dt == 15):
                     e.matmul(o, lhsT=w, rhs=r, start=s, stop=t), reads=[stb, cb_], writes=[psrb])
            P.op("act" if cb % 2 else "dve",
                 (lambda e, o=row[0:1, cb * 512:(cb + 1) * 512], i=psr[0:1, :]: e.activation(out=o, in_=i, func=AF.Copy)) if cb % 2 else
                 (lambda e, o=row[0:1, cb * 512:(cb + 1) * 512], i=psr[0:1, :]: e.tensor_copy(out=o, in_=i)),
                 reads=[psrb], writes=[rowb])
        ps, psb = k.ps()
        for e_ in range(96):
            P.op("pe", lambda e, o=ps[:, e_:e_ + 1], w=row[0:1, e_ * 128:(e_ + 1) * 128]: e.matmul(o, lhsT=w, rhs=one1[0:1, 0:1], start=True, stop=True),
                 reads=[rowb, cb_], writes=[psb])
        P.op("dve", lambda e, o=raw[:, l, :], i=ps[:, 0:96], b=abt[:, l, :]: e.tensor_tensor(out=o, in0=i, in1=b, op=ALU.add),
             reads=[psb, sm_], writes=[rawb])
        for half, (gpre, gpost) in enumerate(((0, 1), (2, 3))):
            b0 = half * 3
            P.op("dve", lambda e, o=outt[:, l, b0 + 0, :], i=raw[:, l, (b0 + 1) * 16:(b0 + 2) * 16], g=ngt[:, l, gpre, :]:
                 e.scalar_tensor_tensor(out=o, in0=i, scalar=1.0, in1=g, op0=ALU.add, op1=ALU.mult),
                 reads=[rawb, sm_], writes=[ob])
            P.op("dve", lambda e, o=outt[:, l, b0 + 1, :], i=raw[:, l, (b0 + 0) * 16:(b0 + 1) * 16]:
                 e.tensor_copy(out=o, in_=i), reads=[rawb], writes=[ob])
            P.op("dve", lambda e, o=outt[:, l, b0 + 2, :], i=raw[:, l, (b0 + 2) * 16:(b0 + 3) * 16], g=ngt[:, l, gpost, :]:
                 e.tensor_tensor(out=o, in0=i, in1=g, op=ALU.mult), reads=[rawb, sm_], writes=[ob])
    P.dma("sp", mods, outt[:].rearrange("p l j d -> p (l j d)"), reads=[ob])
    k.close()
    return nc


def build_mlp(l, env=None):
    k = Ctx("mlp", env)
    nc, P = k.nc, k.P
    TT = 512
    xT = k.dram("xT", [D, T], F32, "ExternalInput")
    modsd = k.dram("mods", [128, 192], F32, "ExternalInput")
    k.do_pre()
    wup = wsrc(k, "w_up", [D, FF])
    wdn = wsrc(k, "w_dn", [FF, D])
    oT = k.dram("oT", [D, T], F32, "ExternalOutput")
    k.init_psum(8)
    c = make_consts(k)
    mt = k.sb([128, 2, 6, 16], F32)
    mb = Buf()
    P.dma("sp", mt[:].rearrange("p l j d -> p (l j d)"), modsd, writes=[mb])
    A, Sh, G = mt[:, l, 3, :], mt[:, l, 4, :], mt[:, l, 5, :]
    Xs = [(k.sb([128, ND, TT], F32), Buf()) for _ in range(2)]
    Hs_ = [(k.sb([128, ND, TT], BF16), Buf()) for _ in range(2)]
    U = k.sb([128, 32, TT], BF16)
    Y = k.sb([128, ND, TT], F32)
    Ub, Yb = Buf(), Buf()
    wl = WLoader(k, 16, 256, 3)
    xT3 = xT.rearrange("(k p) t -> p k t", p=128)
    oT3 = oT.rearrange("(k p) t -> p k t", p=128)
    NT_ = T // TT

    def load_norm(tt):
        X, Xb = Xs[tt % 2]
        H, Hb = Hs_[tt % 2]
        P.dma("sp", X[:], xT3[:, :, tt * TT:(tt + 1) * TT], writes=[Xb])
        rstd, rb = rms_rstd(k, c, X, Xb, TT)
        norm_mod(k, X, Xb, rstd, rb, A, Sh, mb, H, Hb, TT)

    load_norm(0)
    for tt in range(NT_):
        X, Xb = Xs[tt % 2]
        H, Hb = Hs_[tt % 2]
        for fh in range(2):
            for fb in range(16):
                f0 = fh * 4096 + fb * 256
                wb, wbb = wl.load(wup, 0, 16, f0, 256)
                for j in range(2):
                    ps, psb = k.ps()
                    for dt in range(16):
                        P.op("pe", lambda e, o=ps[:, :TT], w=wb[:, dt, j * 128:(j + 1) * 128], r=H[:, dt, :],
                             s=(dt == 0), t=(dt == 15): e.matmul(o, lhsT=w, rhs=r, start=s, stop=t),
                             reads=[wbb, Hb], writes=[psb])
                    rl, rlb = k.rotbuf("relu", [128, 512], F32, 3)
                    P.op("act", lambda e, o=rl[:, :TT], i=ps[:, :TT]: e.activation(out=o, in_=i, func=AF.Relu),
                         reads=[psb], writes=[rlb])
                    P.op("dve", lambda e, o=U[:, fb * 2 + j, :], i=rl[:, :TT]: e.tensor_tensor(out=o, in0=i, in1=i, op=ALU.mult),
                         reads=[rlb], writes=[Ub])
            if fh == 0 and tt + 1 < NT_:
                load_norm(tt + 1)
            for db in range(8):
                pss = [k.ps(), k.ps()]
                for kb in range(2):
                    wb, wbb = wl.load(wdn, fh * 4096 + kb * 2048, 16, db * 256, 256)
                    for j in range(2):
                        ps, psb = pss[j]
                        for ft in range(16):
                            P.op("pe", lambda e, o=ps[:, :TT], w=wb[:, ft, j * 128:(j + 1) * 128], r=U[:, kb * 16 + ft, :],
                                 s=(kb == 0 and ft == 0), t=(kb == 1 and ft == 15): e.matmul(o, lhsT=w, rhs=r, start=s, stop=t),
                                 reads=[wbb, Ub], writes=[psb])
                for j in range(2):
                    ps, psb = pss[j]
                    if fh == 0:
                        P.op("act", lambda e, o=Y[:, db * 2 + j, :], i=ps[:, :TT]: e.activation(out=o, in_=i, func=AF.Copy),
                             reads=[psb], writes=[Yb])
                    else:
                        P.op("dve", lambda e, o=Y[:, db * 2 + j, :], i=ps[:, :TT]: e.tensor_tensor(out=o, in0=o, in1=i, op=ALU.add),
                             reads=[psb], writes=[Yb])
        post_residual(k, c, X, Xb, Y, Yb, G, mb, TT)
        P.dma("sp", oT3[:, :, tt * TT:(tt + 1) * TT], X[:], reads=[Xb])
    k.close()
    return nc


def load_swap(wl, W, nk, c0):
    P = wl.k.P
    wb, wbb = wl.wbf[wl.j]
    wl.j = (wl.j + 1) % len(wl.wbf)
    for (a, b_) in ((0, 32), (32, 0)):
        src = W.ap[0:nk * 128, c0 + b_:c0 + b_ + 32].rearrange("(k p) c -> p k c", p=128)
        P.dma("sp", wb[:, :nk, a:a + 32], src, reads=[W.buf], writes=[wbb])
    return wb, wbb


def angle_reduce(k, ang, kf, ki, ab):
    import math
    P = k.P
    P.op("dve", lambda e: e.tensor_scalar(out=kf, in0=ang, scalar1=1.0 / (2 * math.pi), scalar2=None, op0=ALU.mult), reads=[ab], writes=[ab])
    P.op("dve", lambda e: e.tensor_copy(out=ki, in_=kf), reads=[ab], writes=[ab])
    P.op("dve", lambda e: e.tensor_copy(out=kf, in_=ki), reads=[ab], writes=[ab])
    P.op("dve", lambda e: e.scalar_tensor_tensor(out=ang, in0=kf, scalar=-2 * math.pi, in1=ang, op0=ALU.mult, op1=ALU.add), reads=[ab], writes=[ab])
    P.op("dve", lambda e: e.tensor_scalar(out=kf, in0=ang, scalar1=math.pi, scalar2=-2 * math.pi, op0=ALU.is_gt, op1=ALU.mult), reads=[ab], writes=[ab])
    P.op("dve", lambda e: e.tensor_tensor(out=ang, in0=ang, in1=kf, op=ALU.add), reads=[ab], writes=[ab])
    P.op("dve", lambda e: e.tensor_scalar(out=kf, in0=ang, scalar1=-math.pi, scalar2=2 * math.pi, op0=ALU.is_lt, op1=ALU.mult), reads=[ab], writes=[ab])
    P.op("dve", lambda e: e.tensor_tensor(out=ang, in0=ang, in1=kf, op=ALU.add), reads=[ab], writes=[ab])


def build_mla(l=1, env=None):
    import math
    k = Ctx("mla", env)
    nc, P = k.nc, k.P
    TT = 512
    NTT = T // TT
    xT = k.dram("xT", [D, T], F32, "ExternalInput")
    modsd = k.dram("mods", [128, 192], F32, "ExternalInput")
    posr = k.dram("posr", [64, T], I32, "ExternalInput")
    ropec = k.dram("ropec", [64, 2], F32, "ExternalInput")
    vec = k.dram("mla_vec", [128, 24], F32, "ExternalInput")
    k.do_pre()
    kvd = wsrc(k, "kv_down", [D, 576])
    wuk = wsrc(k, "kv_uk", [512, D])
    wuv = wsrc(k, "kv_uv", [512, D])
    wdq = wsrc(k, "w_dq", [D, 512])
    wuq = wsrc(k, "w_uq", [512, 16 * 192])
    wo = wsrc(k, "w_o", [D, D])
    oT = k.dram("oT", [D, T], F32, "ExternalOutput")
    otd = k.dram("ot_scratch", [16, 128, T], BF16, "Internal")
    k.init_psum(8)
    oacc = k.psl[4:]
    k.psl = k.psl[:4]
    c = make_consts(k)
    mt = k.sb([128, 2, 6, 16], F32)
    vt = k.sb([128, 24], F32)
    zer = k.sb([128, 16], F32)
    mb = Buf()
    P.dma("sp", mt[:].rearrange("p l j d -> p (l j d)"), modsd, writes=[mb])
    P.dma("sp", vt[:], vec, writes=[mb])
    P.op("pool", lambda e: e.memset(zer[:], 0.0), writes=[mb])
    A, Sh, G = mt[:, l, 0, :], mt[:, l, 1, :], mt[:, l, 2, :]

    X = k.sb([128, ND, TT], F32)
    Y = k.sb([128, ND, TT], F32)
    Xb, Yb = Buf(), Buf()
    Yf = Y[:].rearrange("p a b -> p (a b)")
    Ybf = Yf.bitcast(BF16)
    Xbf = X[:].rearrange("p a b -> p (a b)").bitcast(BF16)
    HS = Ybf[:, 0:8192].rearrange("p (a b) -> p a b", a=ND)
    HH = Ybf[:, 8192:16384].rearrange("p (a b) -> p a b", a=ND)
    rc = k.sb([64, 2], F32)
    cos2 = k.sb([64, T], F32)
    sinS = k.sb([64, T], F32)
    csb = Buf()
    P.dma("sp", rc[:], ropec, writes=[mb])
    pi_t = Yf[:64, 0:512].bitcast(I32)
    ang = Yf[:64, 512:1024]
    tmp = Yf[:64, 1024:1536]
    kf = Yf[:64, 1536:2048]
    ki = Yf[:64, 2048:2560].bitcast(I32)
    for ch in range(4):
        t0 = ch * 512
        P.dma("sp", pi_t, posr[:, t0:t0 + 512], writes=[Yb])
        P.op("dve", lambda e: e.tensor_copy(out=ang, in_=pi_t), reads=[Yb], writes=[Yb])
        P.op("dve", lambda e: e.tensor_scalar(out=ang, in0=ang, scalar1=rc[:, 0:1], scalar2=None, op0=ALU.mult), reads=[Yb, mb], writes=[Yb])
        P.op("dve", lambda e: e.tensor_scalar(out=tmp, in0=ang, scalar1=math.pi / 2, scalar2=None, op0=ALU.add), reads=[Yb], writes=[Yb])
        angle_reduce(k, tmp, kf, ki, Yb)
        P.op("act", lambda e, o=cos2[:, t0:t0 + 512]: e.activation(out=o, in_=tmp, func=AF.Sin), reads=[Yb], writes=[csb])
        angle_reduce(k, ang, kf, ki, Yb)
        P.op("act", lambda e, o=sinS[:, t0:t0 + 512]: e.activation(out=o, in_=ang, func=AF.Sin), reads=[Yb], writes=[csb])
        P.op("dve", lambda e, o=sinS[:, t0:t0 + 512]: e.tensor_scalar(out=o, in0=o, scalar1=rc[:, 1:2], scalar2=None, op0=ALU.mult), reads=[csb, mb], writes=[csb])

    CKQ = k.sb([128, 8, TT], F32)
    CK = CKQ[:, 0:4, :]
    CQ = CKQ[:, 4:8, :]
    CKb, CQb = Buf(), Buf()
    CKN = k.sb([128, 4, T], BF16)
    CQN = k.sb([128, 4, T], BF16)
    KR = k.sb([128, T], BF16)
    CKNb, CQNb, KRb = Buf(), Buf(), Buf()
    P.op("pool", lambda e: e.memset(KR[:], 0.0), writes=[KRb])
    wl = WLoader(k, 16, 128, 4)
    xT3 = xT.rearrange("(k p) t -> p k t", p=128)
    oT3 = oT.rearrange("(k p) t -> p k t", p=128)

    def rope_out(ps1, ps1b, ps2, ps2b, dst, dstb, t0):
        t1, t1b = k.rotbuf("rp1", [64, 512], F32, 1)
        t2, t2b = k.rotbuf("rp2", [64, 512], F32, 1)
        P.op("dve", lambda e: e.tensor_tensor(out=t1[:], in0=ps1[:64, :TT], in1=cos2[:, t0:t0 + TT], op=ALU.mult), reads=[ps1b, csb], writes=[t1b])
        P.op("dve", lambda e: e.tensor_tensor(out=t2[:], in0=ps2[:64, :TT], in1=sinS[:, t0:t0 + TT], op=ALU.mult), reads=[ps2b, csb], writes=[t2b])
        P.op("pool", lambda e: e.tensor_tensor(out=dst[:64, t0:t0 + TT], in0=t1[:], in1=t2[:], op=ALU.add), reads=[t1b, t2b], writes=[dstb])

    for tt in range(NTT):
        t0 = tt * TT
        P.dma("sp", X[:], xT3[:, :, t0:t0 + TT], writes=[Xb])
        rstd, rb = rms_rstd(k, c, X, Xb, TT)
        norm_mod(k, X, Xb, rstd, rb, vt[:, 0:16], zer, mb, HS, Yb, TT)
        norm_mod(k, X, Xb, rstd, rb, A, Sh, mb, HH, Yb, TT)
        for (W, src, dst, dstb) in ((kvd, HS, CK, CKb), (wdq, HH, CQ, CQb)):
            for cb in range(4):
                wb, wbb = wl.load(W, 0, 16, cb * 128, 128)
                ps, psb = k.ps()
                for dt in range(16):
                    P.op("pe", lambda e, o=ps[:, :TT], w=wb[:, dt, :], r=src[:, dt, :], s=(dt == 0), t=(dt == 15):
                         e.matmul(o, lhsT=w, rhs=r, start=s, stop=t), reads=[wbb, Yb], writes=[psb])
                P.op("act", lambda e, o=dst[:, cb, :], i=ps[:, :TT]: e.activation(out=o, in_=i, func=AF.Copy), reads=[psb], writes=[dstb])
        pss = []
        for sw in range(2):
            if sw == 0:
                wb, wbb = wl.load(kvd, 0, 16, 512, 64)
            else:
                wb, wbb = load_swap(wl, kvd, 16, 512)
            ps, psb = k.ps()
            for dt in range(16):
                P.op("pe", lambda e, o=ps[:64, :TT], w=wb[:, dt, 0:64], r=HS[:, dt, :], s=(dt == 0), t=(dt == 15):
                     e.matmul(o, lhsT=w, rhs=r, start=s, stop=t), reads=[wbb, Yb], writes=[psb])
            pss.append((ps, psb))
        rope_out(pss[0][0], pss[0][1], pss[1][0], pss[1][1], KR, KRb, t0)
        for (src, srcb, dst, dstb, v0) in ((CK, CKb, CKN, CKNb, 16), (CQ, CQb, CQN, CQNb, 20)):
            rs, rsb = rms_rstd(k, c, src, srcb, TT, scale_div=512, ntile=4)
            for ct in range(4):
                P.op("dve", lambda e, o=dst[:, ct, t0:t0 + TT], i=src[:, ct, :], s=vt[:, v0 + ct:v0 + ct + 1], r=rs[:, :TT]:
                     e.scalar_tensor_tensor(out=o, in0=i, scalar=s, in1=r, op0=ALU.mult, op1=ALU.mult),
                     reads=[srcb, rsb, mb], writes=[dstb])

    P.barrier()
    tri = k.sb([128, 128], BF16)
    trib = Buf()
    P.op("pool", lambda e: e.memset(tri[:], 1.0), writes=[trib])
    P.op("pool", lambda e: e.affine_select(out=tri[:], in_=tri[:], pattern=[[1, 128]], compare_op=ALU.is_ge, fill=0.0,
                                           base=0, channel_multiplier=-1), reads=[trib], writes=[trib])
    wl2 = WLoader(k, 4, 128, 8)
    scale = 192.0 ** -0.5
    hb = []
    for reg in (Ybf, Xbf):
        hb.append(dict(KN=reg[:, 0:2048], QN=reg[:, 2048:4096], QR=reg[:, 4096:6144], OH=reg[:, 6144:8192],
                       VH=reg[:, 8192:10240].rearrange("p (a b) -> p a b", a=16),
                       KNb=Buf(), QNb=Buf(), QRb=Buf(), OHb=Buf(), VHb=Buf()))
    for s_ in hb:
        P.op("pool", lambda e, o=s_["QR"]: e.memset(o, 0.0), writes=[s_["QRb"]])
    for h in range(16):
        s_ = hb[h % 2]
        KN, QN, QR, OH, VH = s_["KN"], s_["QN"], s_["QR"], s_["OH"], s_["VH"]
        KNb, QNb, QRb, OHb, VHb = s_["KNb"], s_["QNb"], s_["QRb"], s_["OHb"], s_["VHb"]
        wk, wkb = wl2.load(wuk, 0, 4, h * 128, 128)
        wq, wqb = wl2.load(wuq, 0, 4, h * 192, 128)
        wv, wvb = wl2.load(wuv, 0, 4, h * 128, 128)
        wr, wrb = wl2.load(wuq, 0, 4, h * 192 + 128, 64)
        ws, wsb = load_swap(wl2, wuq, 4, h * 192 + 128)
        for tq in range(NTT):
            t0 = tq * TT
            for (w_, wb_, src, srcb, dst, dstb) in ((wk, wkb, CKN, CKNb, KN, KNb), (wq, wqb, CQN, CQNb, QN, QNb)):
                ps, psb = k.ps()
                for ct in range(4):
                    P.op("pe", lambda e, o=ps[:, :TT], w=w_[:, ct, :], r=src[:, ct, t0:t0 + TT], s=(ct == 0), t=(ct == 3):
                         e.matmul(o, lhsT=w, rhs=r, start=s, stop=t), reads=[wb_, srcb], writes=[psb])
                P.op("act", lambda e, o=dst[:, t0:t0 + TT], i=ps[:, :TT]: e.activation(out=o, in_=i, func=AF.Copy), reads=[psb], writes=[dstb])
            pss = []
            for (w_, wb_) in ((wr, wrb), (ws, wsb)):
                ps, psb = k.ps()
                for ct in range(4):
                    P.op("pe", lambda e, o=ps[:64, :TT], w=w_[:, ct, 0:64], r=CQN[:, ct, t0:t0 + TT], s=(ct == 0), t=(ct == 3):
                         e.matmul(o, lhsT=w, rhs=r, start=s, stop=t), reads=[wb_, CQNb], writes=[psb])
                pss.append((ps, psb))
            rope_out(pss[0][0], pss[0][1], pss[1][0], pss[1][1], QR, QRb, t0)
        for tk4 in range(4):
            ps, psb = k.ps()
            for i in range(4):
                tk = tk4 * 4 + i
                for ct in range(4):
                    P.op("pe", lambda e, o=ps[:, i * 128:(i + 1) * 128], w=CKN[:, ct, tk * 128:(tk + 1) * 128], r=wv[:, ct, :], s=(ct == 0), t=(ct == 3):
                         e.matmul(o, lhsT=w, rhs=r, start=s, stop=t), reads=[wvb, CKNb], writes=[psb])
            P.op("act", lambda e, o=VH[:, tk4 * 4:tk4 * 4 + 4, :], i=ps[:, :].rearrange("p (a b) -> p a b", a=4):
                 e.activation(out=o, in_=i, func=AF.Copy), reads=[psb], writes=[VHb])
        for qt in range(NTT):
            oa, oab = oacc[(qt % 2) * 2]
            da, dab = oacc[(qt % 2) * 2 + 1]
            nk_ = 4 * (qt + 1)
            for kt in range(nk_):
                off = max(0, (kt - 4 * qt) * 128)
                q0 = qt * TT + off
                q1 = (qt + 1) * TT
                sp_, spb = k.ps()
                P.op("pe", lambda e, o=sp_[:, off:TT], w=KN[:, kt * 128:(kt + 1) * 128], r=QN[:, q0:q1]:
                     e.matmul(o, lhsT=w, rhs=r, start=True, stop=False), reads=[KNb, QNb], writes=[spb])
                P.op("pe", lambda e, o=sp_[:, off:TT], w=KR[:, kt * 128:(kt + 1) * 128], r=QR[:, q0:q1]:
                     e.matmul(o, lhsT=w, rhs=r, start=False, stop=True), reads=[KRb, QRb], writes=[spb])
                PT, PTb = k.rotbuf("PT", [128, TT], BF16, 4)
                P.op("act", lambda e, o=PT[:, off:TT], i=sp_[:, off:TT]: e.activation(out=o, in_=i, func=AF.Exp, scale=scale),
                     reads=[spb], writes=[PTb])
                if kt >= 4 * qt:
                    P.op("pool", lambda e, o=PT[:, off:off + 128]: e.tensor_tensor(out=o, in0=o, in1=tri[:], op=ALU.mult),
                         reads=[PTb, trib], writes=[PTb])
                P.op("pe", lambda e, o=oa[:, off:TT], w=VH[:, kt, :], r=PT[:, off:TT], s=(kt == 0), t=(kt == nk_ - 1):
                     e.matmul(o, lhsT=w, rhs=r, start=s, stop=t), reads=[VHb, PTb], writes=[oab])
                P.op("pe", lambda e, o=da[:, off:TT], r=PT[:, off:TT], s=(kt == 0), t=(kt == nk_ - 1):
                     e.matmul(o, lhsT=c["ones"][:], rhs=r, start=s, stop=t), reads=[c["onesb"], PTb], writes=[dab])
            rd, rdb = k.rotbuf("rden", [128, TT], F32, 2)
            P.op("dve", lambda e, o=rd[:], i=da[:, :TT]: e.reciprocal(out=o, in_=i), reads=[dab], writes=[rdb])
            P.op("dve", lambda e, o=OH[:, qt * TT:(qt + 1) * TT], i=oa[:, :TT], r=rd[:]: e.tensor_tensor(out=o, in0=i, in1=r, op=ALU.mult),
                 reads=[oab, rdb], writes=[OHb])
        P.dma("sp", otd[h], OH, reads=[OHb])

    P.barrier()
    OTt = CKQ[:].rearrange("p a b -> p (a b)").bitcast(BF16).rearrange("p (a b) -> p a b", a=16)
    OTb = Buf()
    otd3 = otd.rearrange("h p t -> p h t")
    Xb, Yb = Buf(), Buf()
    for tt in range(NTT):
        t0 = tt * TT
        P.dma("sp", OTt, otd3[:, :, t0:t0 + TT], writes=[OTb])
        P.dma("sp", X[:], xT3[:, :, t0:t0 + TT], writes=[Xb])
        for eb in range(16):
            wb, wbb = wl.load(wo, 0, 16, eb * 128, 128)
            ps, psb = k.ps()
            for hh in range(16):
                P.op("pe", lambda e, o=ps[:, :TT], w=wb[:, hh, :], r=OTt[:, hh, :], s=(hh == 0), t=(hh == 15):
                     e.matmul(o, lhsT=w, rhs=r, start=s, stop=t), reads=[wbb, OTb], writes=[psb])
            P.op("act", lambda e, o=Y[:, eb, :], i=ps[:, :TT]: e.activation(out=o, in_=i, func=AF.Copy), reads=[psb], writes=[Yb])
        post_residual(k, c, X, Xb, Y, Yb, G, mb, TT)
        P.dma("sp", oT3[:, :, t0:t0 + TT], X[:], reads=[Xb])
    k.close()
    return nc


def build_rwkv(l=0, dbg=False, env=None):
    k = Ctx("rwkv", env)
    k.rotw = 256
    nc, P = k.nc, k.P
    TT = 256
    NTT = T // TT
    C = 64
    NCH = TT // C
    xT = k.dram("xT", [D, T], F32, "ExternalInput")
    modsd = k.dram("mods", [128, 192], F32, "ExternalInput")
    vec = k.dram("rw_vec", [128, 13, 16], F32, "ExternalInput")
    k.do_pre()
    wrkv = wsrc(k, "w_rkv", [3 * D, D])
    w1 = wsrc(k, "w1", [D, 96])
    w2 = wsrc(k, "w2", [96, D])
    a1 = wsrc(k, "a1", [D, 96])
    a2 = wsrc(k, "a2", [96, D])
    g1 = wsrc(k, "g1", [D, 256])
    g2 = wsrc(k, "g2", [256, D])
    wo = wsrc(k, "w_o", [D, D])
    oT = k.dram("oT", [D, T], F32, "ExternalOutput")
    k.init_psum(8)
    c = make_consts(k)
    mt = k.sb([128, 2, 6, 16], F32)
    vt = k.sb([128, 13, 16], F32)
    mb = Buf()
    P.dma("sp", mt[:].rearrange("p l j d -> p (l j d)"), modsd, writes=[mb])
    P.dma("sp", vt[:], vec, writes=[mb])
    A, Sh, G = mt[:, l, 0, :], mt[:, l, 1, :], mt[:, l, 2, :]
    MU, W0, A0, KKv, KA, RK, LNW, LNB = (lambda j: vt[:, j, :]), vt[:, 6, :], vt[:, 7, :], vt[:, 8, :], vt[:, 9, :], vt[:, 10, :], vt[:, 11, :], vt[:, 12, :]

    NEG = k.sb([128, 2, 16], F32)
    P.op("dve", lambda e: e.tensor_scalar(out=NEG[:], in0=vt[:, 6:8, :], scalar1=-1.0, scalar2=None, op0=ALU.mult), reads=[mb], writes=[mb])
    cb_ = Buf()
    bo16 = k.sb([128, 128], BF16)
    bo32 = k.sb([128, 128], F32)
    idn = k.sb([128, 4, 128], BF16)
    mS = k.sb([128, 4, 64], BF16)
    mI = k.sb([128, 4, 64], BF16)
    mL = k.sb([128, 4, 64], BF16)
    ones64 = k.sb([128, 64], F32)
    for t_ in (bo16, bo32):
        P.op("pool", lambda e, t_=t_: e.memset(t_[:], 0.0), writes=[cb_])
        P.op("pool", lambda e, t_=t_: e.memset(t_[0:64, 0:64], 1.0), writes=[cb_])
        P.op("pool", lambda e, t_=t_: e.memset(t_[64:128, 64:128], 1.0), writes=[cb_])
    P.op("pool", lambda e: e.memset(ones64[:], 1.0), writes=[cb_])
    P.op("pool", lambda e: e.memset(idn[:], 1.0), writes=[cb_])
    P.op("pool", lambda e: e.memset(mS[:], 1.0), writes=[cb_])
    P.op("pool", lambda e: e.memset(mI[:], 1.0), writes=[cb_])
    P.op("pool", lambda e: e.memset(mL[:], 1.0), writes=[cb_])
    for g_ in range(4):
        P.op("pool", lambda e, o=idn[:, g_, :]: e.affine_select(out=o, in_=o, pattern=[[1, 128]], compare_op=ALU.is_equal, fill=0.0,
                                                               base=0, channel_multiplier=-1), reads=[cb_], writes=[cb_])
        for hf in range(2):
            sl = slice(64 * hf, 64 * hf + 64)
            P.op("pool", lambda e, o=mS[sl, g_, :]: e.affine_select(out=o, in_=o, pattern=[[1, 64]], compare_op=ALU.is_ge, fill=0.0,
                                                                    base=-1, channel_multiplier=-1), reads=[cb_], writes=[cb_])
            P.op("pool", lambda e, o=mI[sl, g_, :]: e.affine_select(out=o, in_=o, pattern=[[1, 64]], compare_op=ALU.is_ge, fill=0.0,
                                                                    base=0, channel_multiplier=-1), reads=[cb_], writes=[cb_])
            P.op("pool", lambda e, o=mL[sl, g_, :]: e.affine_select(out=o, in_=o, pattern=[[-1, 64]], compare_op=ALU.is_ge, fill=0.0,
                                                                    base=-1, channel_multiplier=1), reads=[cb_], writes=[cb_])

    RT = k.sb([128, 16, TT], BF16)
    KT = k.sb([128, 16, TT], BF16)
    BT = k.sb([128, 16, TT], BF16)
    AT = k.sb([128, 16, TT], BF16)
    VT = k.sb([128, 16, TT], BF16)
    GT = k.sb([128, 16, TT], BF16)
    BON = k.sb([128, 16, TT], BF16)
    GC = k.sb([128, 16, NCH], F32)
    RTb, KTb, BTb, ATb, VTb, GTb, BONb, GCb, YTb = (Buf() for _ in range(9))
    Hf = k.sb([128, 16, 64], F32)
    Hstk = k.sb([128, 16, 64], BF16)
    Hbd = k.sb([128, 16, 128], BF16)
    Hb_ = [Buf() for _ in range(4)]
    HL = k.sb([128, 16, 1], F32)
    HLb = Buf()
    P.op("pool", lambda e: e.memset(Hf[:], 0.0), writes=Hb_)
    P.op("pool", lambda e: e.memset(Hstk[:], 0.0), writes=Hb_)
    P.op("pool", lambda e: e.memset(Hbd[:], 0.0), writes=Hb_)
    P.op("pool", lambda e: e.memset(HL[:], 0.0), writes=[HLb])
    TW = k.sb([128, TT], BF16)
    TA = k.sb([128, TT], BF16)
    TG = k.sb([128, 2, TT], BF16)
    TWb, TAb, TGb = Buf(), Buf(), Buf()
    wl = WLoader(k, 16, 128, 3)
    wls = WLoader(k, 2, 128, 3)
    REG = k.sb([128, 18688], F32)
    REGbf = REG[:].bitcast(BF16)

    def f32v(o, n, a):
        return REG[:, o:o + n].rearrange("p (a b) -> p a b", a=a)

    def bfv(o, n, a):
        return REGbf[:, 2 * o:2 * o + 2 * n].rearrange("p (a b) -> p a b", a=a)

    X = f32v(0, 4096, 16)
    Hs = f32v(4096, 4352, 16)
    XX = bfv(8448, 2048, 16)
    XS = bfv(10496, 2048, 16)
    XR = bfv(12544, 2048, 16)
    XK = bfv(14592, 2048, 16)
    XV = bfv(16640, 2048, 16)
    o_ = [0]

    def nxt(n, a):
        v = bfv(o_[0], n, a)
        o_[0] += n
        return v
    ATbd, BTbd, KTbd, VTbd = nxt(1024, 16), nxt(1024, 16), nxt(1024, 16), nxt(1024, 16)
    def chbuf():
        t = k.sb([128, 4, 128], F32)
        return {"r": t[:], "w": t[:].bitcast(CHAIN_DT), "m": t[:].bitcast(CHAIN_DT)}
    CH = [dict(N=[chbuf(), chbuf()], L=[chbuf(), chbuf()], P=chbuf()) for _ in range(2)]
    PF = nxt(1024, 16)
    MakT = nxt(1024, 16)
    MrbT, MrkT = nxt(512, 16), nxt(512, 16)
    Vbd, Vstk = nxt(1024, 16), nxt(512, 16)
    Bbd, Kbd = nxt(1024, 16), nxt(1024, 16)
    Zs, Us, Ubd = nxt(512, 16), nxt(512, 16), nxt(1024, 16)
    ZERO_LIST = [ATbd, BTbd, KTbd, VTbd, CH[0]['N'][0]['w'], CH[0]['L'][0]['w'], CH[1]['N'][0]['w'], CH[1]['L'][0]['w'], MakT, Ubd]
    YT = f32v(12800, 4096, 16)
    OIN = bfv(4096, 2048, 16)
    Y2 = f32v(8448, 4096, 16)

    xT3 = xT.rearrange("(k p) t -> p k t", p=128)
    oT3 = oT.rearrange("(k p) t -> p k t", p=128)
    if dbg:
        dbf = k.dram("dbg_bf", [7, 128, 16 * TT], BF16, "ExternalOutput")
        dyt = k.dram("dbg_yt", [128, 16 * TT], F32, "ExternalOutput")
        dgc = k.dram("dbg_gc", [128, 16 * NCH], F32, "ExternalOutput")
        doin = k.dram("dbg_oin", [128, 16 * TT], BF16, "ExternalOutput")

    def tmp(name, n=1, dt=F32, w=TT):
        return k.rotbuf(name, [128, w], dt, n)

    for tt in range(1 if dbg else NTT):
        t0 = tt * TT
        Xb, Hsb, XXb, XSb, XRb, XKb, XVb = (Buf() for _ in range(7))
        P.dma("sp", X, xT3[:, :, t0:t0 + TT], writes=[Xb])
        rstd, rb = rms_rstd(k, c, X, Xb, TT)
        norm_mod(k, X, Xb, rstd, rb, A, Sh, mb, Hs, Hsb, TT, col0=1)
        P.op("pool", lambda e: e.tensor_copy(out=Hs[:, :, 0:1], in_=HL[:]), reads=[HLb], writes=[Hsb])
        P.op("dve", lambda e: e.tensor_tensor(out=XX, in0=Hs[:, :, 0:TT], in1=Hs[:, :, 1:TT + 1], op=ALU.subtract), reads=[Hsb], writes=[XXb])
        P.op("pool", lambda e: e.tensor_copy(out=HL[:], in_=Hs[:, :, TT:TT + 1]), reads=[Hsb], writes=[HLb])

        def make_xs(j, dst, dstb):
            for dt in range(16):
                P.op("dve", lambda e, o=dst[:, dt, :], i=XX[:, dt, :], s=vt[:, j, dt:dt + 1], h=Hs[:, dt, 1:TT + 1]:
                     e.scalar_tensor_tensor(out=o, in0=i, scalar=s, in1=h, op0=ALU.mult, op1=ALU.add),
                     reads=[XXb, Hsb, mb], writes=[dstb])
        for (j, W, ncol) in ((3, w1, 96), (4, a1, 96), (5, g1, 256)):
            make_xs(j, XS, XSb)
            for cbk in range((ncol + 127) // 128):
                nc_ = min(128, ncol - cbk * 128)
                wb, wbb = wl.load(W, 0, 16, cbk * 128, nc_)
                ps, psb = k.ps()
                for dt in range(16):
                    P.op("pe", lambda e, o=ps[:nc_, :TT], w=wb[:, dt, :nc_], r=XS[:, dt, :], s=(dt == 0), t=(dt == 15):
                         e.matmul(o, lhsT=w, rhs=r, start=s, stop=t), reads=[wbb, XSb], writes=[psb])
                if j == 3:
                    P.op("act", lambda e, i=ps[:96, :TT]: e.activation(out=TW[:96, :], in_=i, func=AF.Tanh), reads=[psb], writes=[TWb])
                elif j == 4:
                    P.op("act", lambda e, i=ps[:96, :TT]: e.activation(out=TA[:96, :], in_=i, func=AF.Copy), reads=[psb], writes=[TAb])
                else:
                    P.op("act", lambda e, i=ps[:, :TT], o=TG[:, cbk, :]: e.activation(out=o, in_=i, func=AF.Sigmoid), reads=[psb], writes=[TGb])
        make_xs(0, XR, XRb)
        make_xs(1, XK, XKb)
        make_xs(2, XV, XVb)
        for p in range(16):
            e0 = p * 128
            pA, pAb = k.ps()
            pB, pBb = k.ps()
            pC, pCb = k.ps()
            pD, pDb = k.ps()
            for (jj, src, srcb, ps, psb, co) in ((0, XR, XRb, pA, pAb, 0), (1, XK, XKb, pA, pAb, TT), (2, XV, XVb, pB, pBb, 0)):
                wb, wbb = wl.load(wrkv, jj * D, 16, e0, 128)
                for dt in range(16):
                    P.op("pe", lambda e, o=ps[:, co:co + TT], w=wb[:, dt, :], r=src[:, dt, :], s=(dt == 0), t=(dt == 15):
                         e.matmul(o, lhsT=w, rhs=r, start=s, stop=t), reads=[wbb, srcb], writes=[psb])
            wb, wbb = wls.load(w2, 0, 1, e0, 128, pp=96)
            P.op("pe", lambda e, o=pB[:, TT:2 * TT], w=wb[:96, 0, :]: e.matmul(o, lhsT=w, rhs=TW[:96, :], start=True, stop=True),
                 reads=[wbb, TWb], writes=[pBb])
            wb, wbb = wls.load(a2, 0, 1, e0, 128, pp=96)
            P.op("pe", lambda e, o=pC[:, 0:TT], w=wb[:96, 0, :]: e.matmul(o, lhsT=w, rhs=TA[:96, :], start=True, stop=True),
                 reads=[wbb, TAb], writes=[pCb])
            wb, wbb = wls.load(g2, 0, 2, e0, 128)
            for kt in range(2):
                P.op("pe", lambda e, o=pC[:, TT:2 * TT], w=wb[:, kt, :], r=TG[:, kt, :], s=(kt == 0), t=(kt == 1):
                     e.matmul(o, lhsT=w, rhs=r, start=s, stop=t), reads=[wbb, TGb], writes=[pCb])
            r_ps, k_ps, v_ps, w_ps, a_ps, g_ps = pA[:, 0:TT], pA[:, TT:2 * TT], pB[:, 0:TT], pB[:, TT:2 * TT], pC[:, 0:TT], pC[:, TT:2 * TT]
            LWS = -0.6065306597126334
            sg, sgb = tmp("sg", 2)
            cl, clb = tmp("cl", 2)
            av, avb = tmp("av", 2)
            vf, vfb = tmp("vf", 2)
            kk, kkb = tmp("kk", 2)
            rn, rnb = tmp("rn", 2)
            kf_, kfb = tmp("kf", 2)
            eg, egb = tmp("eg")
            eig, eigb = tmp("eig")
            eex, eexb = tmp("eex")
            k2, k2b = tmp("ksq", 1, BF16)
            bb, bbb = tmp("bb")
            rk_, rkb = tmp("rkp", 1, BF16)
            lw, lwb = sg, sgb
            P.op("act", lambda e, o=sg[:], i=w_ps, b=NEG[:, 0, p:p + 1]: e.activation(out=o, in_=i, func=AF.Exp, bias=b, scale=-1.0),
                 reads=[pBb, mb], writes=[sgb])
            P.op("dve", lambda e, o=kk[:], i=k_ps, s=KKv[:, p:p + 1]: e.tensor_scalar(out=o, in0=i, scalar1=s, scalar2=None, op0=ALU.mult),
                 reads=[pAb, mb], writes=[kkb])
            P.op("pool", lambda e, o=k2[:], i=kk[:]: e.tensor_tensor(out=o, in0=i, in1=i, op=ALU.mult), reads=[kkb], writes=[k2b])
            P.op("pe", lambda e, o=pD[:, 0:TT], r=k2[:]: e.matmul(o, lhsT=bo16[:], rhs=r, start=True, stop=True), reads=[k2b, cb_], writes=[pDb])
            P.op("act", lambda e, o=av[:], i=a_ps, b=NEG[:, 1, p:p + 1]: e.activation(out=o, in_=i, func=AF.Exp, bias=b, scale=-1.0),
                 reads=[pCb, mb], writes=[avb])
            P.op("act", lambda e, o=GT[:, p, :], i=g_ps: e.activation(out=o, in_=i, func=AF.Copy), reads=[pCb], writes=[GTb])
            P.op("act", lambda e, o=vf[:], i=v_ps: e.activation(out=o, in_=i, func=AF.Copy), reads=[pBb], writes=[vfb])
            P.op("pool", lambda e, o=VT[:, p, :], i=vf[:]: e.tensor_copy(out=o, in_=i), reads=[vfb], writes=[VTb])
            P.op("dve", lambda e, o=sg[:]: e.tensor_scalar(out=o, in0=o, scalar1=1.0, scalar2=None, op0=ALU.add), reads=[sgb], writes=[sgb])
            P.op("dve", lambda e, o=sg[:]: e.reciprocal(out=o, in_=o), reads=[sgb], writes=[sgb])
            for ch in range(NCH):
                P.op("dve", lambda e, o=cl[:, ch * C:(ch + 1) * C], i=lw[:, ch * C:(ch + 1) * C]:
                     e.tensor_tensor_scan(out=o, data0=ones64[:], data1=i, initial=0.0, op0=ALU.mult, op1=ALU.add),
                     reads=[lwb, cb_], writes=[clb])
            P.op("dve", lambda e, o=rn[:], i=pD[:, 0:TT]: e.tensor_scalar(out=o, in0=i, scalar1=5.5e-20, scalar2=None, op0=ALU.max), reads=[pDb], writes=[rnb])
            P.op("dve", lambda e, o=eex[:], i=cl[:], j_=lw[:]: e.tensor_tensor(out=o, in0=i, in1=j_, op=ALU.subtract), reads=[clb, lwb], writes=[eexb])
            P.op("act", lambda e, o=rn[:]: e.activation(out=o, in_=o, func=AF.Ln), reads=[rnb], writes=[rnb])
            P.op("act", lambda e, o=rn[:]: e.activation(out=o, in_=o, func=AF.Exp, scale=-0.5), reads=[rnb], writes=[rnb])
            P.op("act", lambda e, o=eg[:], i=cl[:]: e.activation(out=o, in_=i, func=AF.Exp, scale=LWS), reads=[clb], writes=[egb])
            P.op("act", lambda e, o=eig[:], i=cl[:]: e.activation(out=o, in_=i, func=AF.Exp, scale=-LWS), reads=[clb], writes=[eigb])
            P.op("act", lambda e, o=eex[:]: e.activation(out=o, in_=o, func=AF.Exp, scale=LWS), reads=[eexb], writes=[eexb])
            P.op("pool", lambda e, o=GC[:, p, :], i=eg[:].rearrange("p (a b) -> p a b", a=NCH)[:, :, C - 1]: e.tensor_copy(out=o, in_=i),
                 reads=[egb], writes=[GCb])
            P.op("dve", lambda e, o=av[:]: e.tensor_scalar(out=o, in0=o, scalar1=1.0, scalar2=None, op0=ALU.add), reads=[avb], writes=[avb])
            P.op("dve", lambda e, o=av[:]: e.reciprocal(out=o, in_=o), reads=[avb], writes=[avb])
            P.op("dve", lambda e, o=kf_[:], i=av[:], s=KA[:, p:p + 1]: e.tensor_scalar(out=o, in0=i, scalar1=-1.0, scalar2=s, op0=ALU.add, op1=ALU.mult),
                 reads=[avb, mb], writes=[kfb])
            P.op("dve", lambda e, o=kf_[:], i=k_ps: e.scalar_tensor_tensor(out=o, in0=o, scalar=1.0, in1=i, op0=ALU.add, op1=ALU.mult),
                 reads=[kfb, pAb], writes=[kfb])
            P.op("dve", lambda e, o=kk[:], r=rn[:]: e.tensor_tensor(out=o, in0=o, in1=r, op=ALU.mult), reads=[kkb, rnb], writes=[kkb])
            P.op("dve", lambda e, o=RT[:, p, :], i=r_ps, g_=eg[:]: e.tensor_tensor(out=o, in0=i, in1=g_, op=ALU.mult), reads=[pAb, egb], writes=[RTb])
            P.op("pool", lambda e, o=KT[:, p, :], i=kf_[:], g_=eig[:]: e.tensor_tensor(out=o, in0=i, in1=g_, op=ALU.mult), reads=[kfb, eigb], writes=[KTb])
            P.op("pool", lambda e, o=bb[:], i=kk[:], a_=av[:]: e.tensor_tensor(out=o, in0=i, in1=a_, op=ALU.mult), reads=[kkb, avb], writes=[bbb])
            P.op("pool", lambda e, o=BT[:, p, :], i=bb[:], g_=eig[:]: e.tensor_tensor(out=o, in0=i, in1=g_, op=ALU.mult), reads=[bbb, eigb], writes=[BTb])
            P.op("dve", lambda e, o=AT[:, p, :], i=kk[:], g_=eex[:]: e.scalar_tensor_tensor(out=o, in0=i, scalar=-1.0, in1=g_, op0=ALU.mult, op1=ALU.mult),
                 reads=[kkb, eexb], writes=[ATb])
            P.op("dve", lambda e, o=rk_[:], i=r_ps, s=RK[:, p:p + 1], k_=kf_[:]: e.scalar_tensor_tensor(out=o, in0=i, scalar=s, in1=k_, op0=ALU.mult, op1=ALU.mult),
                 reads=[pAb, kfb, mb], writes=[rkb])
            P.op("pe", lambda e, o=pD[:, TT:2 * TT], r=rk_[:]: e.matmul(o, lhsT=bo16[:], rhs=r, start=True, stop=True), reads=[rkb, cb_], writes=[pDb])
            P.op("dve", lambda e, o=BON[:, p, :], i=pD[:, TT:2 * TT], v_=vf[:]: e.tensor_tensor(out=o, in0=i, in1=v_, op=ALU.mult),
                 reads=[pDb, vfb], writes=[BONb])

        P.barrier()
        if dbg:
            for i_, (t_, b_) in enumerate(((RT, RTb), (KT, KTb), (BT, BTb), (AT, ATb), (VT, VTb), (GT, GTb), (BON, BONb))):
                P.dma("sp", dbf[i_], t_[:].rearrange("p a b -> p (a b)"), reads=[b_])
            P.dma("sp", dgc, GC[:].rearrange("p a b -> p (a b)"), reads=[GCb])
            P.barrier()
        zb = Buf()
        SB = [dict(N=Buf(), L=Buf(), P=Buf()) for _ in range(2)]
        for z_ in ZERO_LIST:
            if z_.dtype == BF16:
                P.op("pool", lambda e, z_=z_: e.memset(z_, 0.0), writes=[zb])
            else:
                P.op("dve", lambda e, z_=z_: e.tensor_scalar(out=z_, in0=idn[:], scalar1=0.0, scalar2=None, op0=ALU.mult),
                     reads=[cb_], writes=[zb])
        for ch in range(NCH):
            cc = slice(ch * C, (ch + 1) * C)
            inb = [Buf() for _ in range(4)]
            for (src, srcb, dst) in ((AT, ATb, ATbd), (BT, BTb, BTbd), (KT, KTb, KTbd), (VT, VTb, VTbd)):
                for hf in range(2):
                    sl = slice(64 * hf, 64 * hf + 64)
                    P.op("dve" if hf == 0 else "act",
                         (lambda e, o=dst[sl, :, 64 * hf:64 * hf + 64], i=src[sl, :, cc]: e.tensor_copy(out=o, in_=i)) if hf == 0 else
                         (lambda e, o=dst[sl, :, 64 * hf:64 * hf + 64], i=src[sl, :, cc]: e.activation(out=o, in_=i, func=AF.Copy)),
                         reads=[srcb, zb], writes=inb)
            gb = [dict((n, Buf()) for n in ("N", "L", "Mak", "Mrb", "Mrk", "V", "B", "K", "P", "Z", "U")) for _ in range(4)]
            for g_ in range(4):
                pg = slice(g_ * 4, g_ * 4 + 4)
                b_ = gb[g_]
                for (src, dst, nm) in ((VTbd, Vbd, "V"), (BTbd, Bbd, "B"), (KTbd, Kbd, "K")):
                    ps, psb = k.ps()
                    psv = ps[:].bitcast(BF16)[:, 0:512].rearrange("p (a b) -> p a b", a=4)
                    for i in range(4):
                        P.op("pe", lambda e, o=psv[:, i, :], w=src[:, g_ * 4 + i, :]: e.transpose(out=o, in_=w, identity=idn[:, 0, :]),
                             reads=[inb[g_], cb_], writes=[psb])
                    P.op("act", lambda e, o=dst[:, pg, :], i=psv: e.activation(out=o, in_=i, func=AF.Copy), reads=[psb], writes=[b_[nm]])
                    if nm == "V":
                        for hf in range(2):
                            sl = slice(64 * hf, 64 * hf + 64)
                            P.op("dve", lambda e, o=Vstk[sl, pg, :], i=psv[sl, :, 64 * hf:64 * hf + 64]: e.tensor_copy(out=o, in_=i),
                                 reads=[psb], writes=[b_[nm]])
            def step1(g_):
                ps1, ps1b = k.ps()
                ps2, ps2b = k.ps()
                ps3, ps3b = k.ps()
                b_ = gb[g_]
                cs_ = CH[g_ % 2]
                sb_ = SB[g_ % 2]
                for i in range(4):
                    p = g_ * 4 + i
                    cs = slice(i * 64, i * 64 + 64)
                    cs2 = slice(256 + i * 64, 256 + i * 64 + 64)
                    for (ps, psb, csl, lh, rh, rhb) in ((ps1, ps1b, cs, BTbd, AT, ATb), (ps1, ps1b, cs2, ATbd, BT, BTb),
                                                        (ps2, ps2b, cs, KTbd, AT, ATb), (ps2, ps2b, cs2, BTbd, RT, RTb),
                                                        (ps3, ps3b, cs, KTbd, RT, RTb)):
                        P.op("pe", lambda e, o=ps[:, csl], w=lh[:, p, :], r=rh[:, p, cc]: e.matmul(o, lhsT=w, rhs=r, start=True, stop=True),
                             reads=[inb[g_], rhb], writes=[psb])
                pg = slice(g_ * 4, g_ * 4 + 4)
                v1 = ps1[:, 0:256].rearrange("p (a b) -> p a b", a=4)
                v1b = ps1[:, 256:512].rearrange("p (a b) -> p a b", a=4)
                v2 = ps2[:, 0:256].rearrange("p (a b) -> p a b", a=4)
                v2b = ps2[:, 256:512].rearrange("p (a b) -> p a b", a=4)
                v3 = ps3[:, 0:256].rearrange("p (a b) -> p a b", a=4)
                for hf in range(2):
                    sl = slice(64 * hf, 64 * hf + 64)
                    fs = slice(64 * hf, 64 * hf + 64)
                    P.op("dve", lambda e, o=cs_["N"][0]["w"][sl, :, fs], i=v1[sl], m=mS[sl]: e.tensor_tensor(out=o, in0=i, in1=m, op=ALU.mult),
                         reads=[ps1b, cb_, zb], writes=[sb_["N"]])
                    P.op("dve", lambda e, o=cs_["L"][0]["w"][sl, :, fs], i=v1b[sl], m=mL[sl]: e.tensor_tensor(out=o, in0=i, in1=m, op=ALU.mult),
                         reads=[ps1b, cb_, zb], writes=[sb_["L"]])
                    P.op("dve", lambda e, o=MakT[sl, pg, fs], i=v2[sl], m=mS[sl]: e.tensor_tensor(out=o, in0=i, in1=m, op=ALU.mult),
                         reads=[ps2b, cb_, zb], writes=[b_["Mak"]])
                P.op("dve", lambda e, o=MrbT[:, pg, :], i=v2b, m=mI[:]: e.tensor_tensor(out=o, in0=i, in1=m, op=ALU.mult),
                     reads=[ps2b, cb_], writes=[b_["Mrb"]])
                P.op("dve", lambda e, o=MrkT[:, pg, :], i=v3, m=mI[:]: e.tensor_tensor(out=o, in0=i, in1=m, op=ALU.mult),
                     reads=[ps3b, cb_], writes=[b_["Mrk"]])
                P.op("dve", lambda e, o=cs_["P"]["w"], i=cs_["N"][0]["r"]: e.tensor_tensor(out=o, in0=i, in1=idn[:], op=ALU.add),
                     reads=[sb_["N"], cb_], writes=[sb_["P"]])

            def chain_level(g_, lev):
                a_, n_ = (lev - 1) % 2, lev % 2
                pg = slice(g_ * 4, g_ * 4 + 4)
                cs_ = CH[g_ % 2]
                sb_ = SB[g_ % 2]
                Ns, Ls, Pc = cs_["N"], cs_["L"], cs_["P"]
                if lev < 5:
                    psn, psnb = k.ps()
                    for i in range(4):
                        P.op("pe", lambda e, o=psn[:, i * 128:(i + 1) * 128], w=Ls[a_]["m"][:, i, :], r=Ns[a_]["m"][:, i, :]:
                             e.matmul(o, lhsT=w, rhs=r, start=True, stop=True), reads=[sb_["L"], sb_["N"]], writes=[psnb])
                psl_, pslb = k.ps()
                for i in range(4):
                    P.op("pe", lambda e, o=psl_[:, i * 128:(i + 1) * 128], w=Ns[a_]["m"][:, i, :], r=Ls[a_]["m"][:, i, :]:
                         e.matmul(o, lhsT=w, rhs=r, start=True, stop=True), reads=[sb_["L"], sb_["N"]], writes=[pslb])
                if lev < 5:
                    P.op("act", lambda e, o=Ns[n_]["w"], i=psn[:].rearrange("p (a b) -> p a b", a=4): e.activation(out=o, in_=i, func=AF.Copy),
                         reads=[psnb], writes=[sb_["N"]])
                P.op("act", lambda e, o=Ls[n_]["w"], i=psl_[:].rearrange("p (a b) -> p a b", a=4): e.activation(out=o, in_=i, func=AF.Copy),
                     reads=[pslb], writes=[sb_["L"]])
                psp, pspb = k.ps()
                for i in range(4):
                    P.op("pe", lambda e, o=psp[:, i * 128:(i + 1) * 128], w=Ls[n_]["m"][:, i, :], r=Pc["m"][:, i, :]:
                         e.matmul(o, lhsT=w, rhs=r, start=True, stop=True), reads=[sb_["L"], sb_["P"]], writes=[pspb])
                if lev < 5:
                    P.op("dve", lambda e, o=Pc["w"], q=Pc["r"], i=psp[:].rearrange("p (a b) -> p a b", a=4): e.tensor_tensor(out=o, in0=i, in1=q, op=ALU.add),
                         reads=[pspb, sb_["P"]], writes=[sb_["P"]])
                else:
                    P.op("dve", lambda e, o=PF[:, pg, :], i=psp[:].rearrange("p (a b) -> p a b", a=4), q=Pc["r"]: e.tensor_tensor(out=o, in0=i, in1=q, op=ALU.add),
                         reads=[pspb, sb_["P"]], writes=[gb[g_]["P"]])

            for gp in ((0, 1), (2, 3)):
                for g_ in gp:
                    step1(g_)
                for lev in range(1, 6):
                    for g_ in gp:
                        chain_level(g_, lev)
            for g_ in range(4):
                pg = slice(g_ * 4, g_ * 4 + 4)
                b_ = gb[g_]
                hb_ = Hb_[g_]
                psz, pszb = k.ps()
                for i in range(4):
                    p = g_ * 4 + i
                    P.op("pe", lambda e, o=psz[:, i * 64:(i + 1) * 64], w=ATbd[:, p, :], r=Hstk[:, p, :]: e.matmul(o, lhsT=w, rhs=r, start=True, stop=False),
                         reads=[inb[g_], hb_], writes=[pszb])
                    P.op("pe", lambda e, o=psz[:, i * 64:(i + 1) * 64], w=MakT[:, p, :], r=Vstk[:, p, :]: e.matmul(o, lhsT=w, rhs=r, start=False, stop=True),
                         reads=[b_["Mak"], b_["V"]], writes=[pszb])
                P.op("act", lambda e, o=Zs[:, pg, :], i=psz[:, 0:256].rearrange("p (a b) -> p a b", a=4): e.activation(out=o, in_=i, func=AF.Copy),
                     reads=[pszb], writes=[b_["Z"]])
                psu, psub = k.ps()
                for i in range(4):
                    p = g_ * 4 + i
                    P.op("pe", lambda e, o=psu[:, i * 64:(i + 1) * 64], w=PF[:, p, :], r=Zs[:, p, :]: e.matmul(o, lhsT=w, rhs=r, start=True, stop=True),
                         reads=[b_["P"], b_["Z"]], writes=[psub])
                psuv = psu[:, 0:256].rearrange("p (a b) -> p a b", a=4)
                P.op("act", lambda e, o=Us[:, pg, :], i=psuv: e.activation(out=o, in_=i, func=AF.Copy), reads=[psub], writes=[b_["U"]])
                for hf in range(2):
                    sl = slice(64 * hf, 64 * hf + 64)
                    P.op("dve", lambda e, o=Ubd[sl, pg, 64 * hf:64 * hf + 64], i=psuv[sl]: e.tensor_copy(out=o, in_=i), reads=[psub, zb], writes=[b_["U"]])
                psy, psyb = k.ps()
                psh, pshb = k.ps()
                for i in range(4):
                    p = g_ * 4 + i
                    oy = psy[:, i * 64:(i + 1) * 64]
                    P.op("pe", lambda e, o=oy, w=Hbd[:, p, :], r=RT[:, p, cc]: e.matmul(o, lhsT=w, rhs=r, start=True, stop=False),
                         reads=[hb_, RTb], writes=[psyb])
                    P.op("pe", lambda e, o=oy, w=Ubd[:, p, :], r=MrbT[:, p, :]: e.matmul(o, lhsT=w, rhs=r, start=False, stop=False),
                         reads=[b_["U"], b_["Mrb"]], writes=[psyb])
                    P.op("pe", lambda e, o=oy, w=Vbd[:, p, :], r=MrkT[:, p, :]: e.matmul(o, lhsT=w, rhs=r, start=False, stop=True),
                         reads=[b_["V"], b_["Mrk"]], writes=[psyb])
                    oh = psh[:, i * 64:(i + 1) * 64]
                    P.op("pe", lambda e, o=oh, w=Bbd[:, p, :], r=Us[:, p, :]: e.matmul(o, lhsT=w, rhs=r, start=True, stop=False),
                         reads=[b_["B"], b_["U"]], writes=[pshb])
                    P.op("pe", lambda e, o=oh, w=Kbd[:, p, :], r=Vstk[:, p, :]: e.matmul(o, lhsT=w, rhs=r, start=False, stop=True),
                         reads=[b_["K"], b_["V"]], writes=[pshb])
                P.op("act", lambda e, o=YT[:, pg, cc], i=psy[:, 0:256].rearrange("p (a b) -> p a b", a=4): e.activation(out=o, in_=i, func=AF.Copy),
                     reads=[psyb], writes=[YTb])
                P.op("dve", lambda e, o=Hf[:, pg, :], i=psh[:, 0:256].rearrange("p (a b) -> p a b", a=4): e.tensor_tensor(out=o, in0=o, in1=i, op=ALU.add),
                     reads=[pshb, hb_], writes=[hb_])
                for i in range(4):
                    p = g_ * 4 + i
                    P.op("dve", lambda e, o=Hf[:, p, :], s=GC[:, p, ch:ch + 1]: e.tensor_scalar(out=o, in0=o, scalar1=s, scalar2=None, op0=ALU.mult),
                         reads=[hb_, GCb], writes=[hb_])
                P.op("pool", lambda e, o=Hstk[:, pg, :], i=Hf[:, pg, :]: e.tensor_copy(out=o, in_=i), reads=[hb_], writes=[hb_])
                for hf in range(2):
                    sl = slice(64 * hf, 64 * hf + 64)
                    P.op("pool", lambda e, o=Hbd[sl, pg, 64 * hf:64 * hf + 64], i=Hf[sl, pg, :]: e.tensor_copy(out=o, in_=i), reads=[hb_], writes=[hb_])

        P.barrier()
        OINb, Y2b, Xb = Buf(), Buf(), Buf()
        P.dma("sp", X, xT3[:, :, t0:t0 + TT], writes=[Xb])
        for p in range(16):
            psm, psmb = k.ps()
            P.op("pe", lambda e, o=psm[:, 0:TT], r=YT[:, p, :]: e.matmul(o, lhsT=bo32[:], rhs=r, start=True, stop=True), reads=[YTb, cb_], writes=[psmb])
            yc, ycb = tmp("cl")
            P.op("dve", lambda e, o=yc[:], m=psm[:, 0:TT], y=YT[:, p, :]: e.scalar_tensor_tensor(out=o, in0=m, scalar=-1.0 / 64, in1=y, op0=ALU.mult, op1=ALU.add),
                 reads=[psmb, YTb], writes=[ycb])
            ysq, ysqb = tmp("rn")
            P.op("pool", lambda e, o=ysq[:], i=yc[:]: e.tensor_tensor(out=o, in0=i, in1=i, op=ALU.mult), reads=[ycb], writes=[ysqb])
            P.op("pe", lambda e, o=psm[:, TT:2 * TT], r=ysq[:]: e.matmul(o, lhsT=bo32[:], rhs=r, start=True, stop=True), reads=[ysqb, cb_], writes=[psmb])
            rs, rsb = tmp("bb")
            P.op("act", lambda e, o=rs[:], i=psm[:, TT:2 * TT]: e.activation(out=o, in_=i, func=AF.Sqrt, bias=c["eps"][:, 1:2], scale=1.0 / 64),
                 reads=[psmb, c["onesb"]], writes=[rsb])
            P.op("dve", lambda e, o=rs[:]: e.reciprocal(out=o, in_=o), reads=[rsb], writes=[rsb])
            P.op("dve", lambda e, o=yc[:], r=rs[:]: e.tensor_tensor(out=o, in0=o, in1=r, op=ALU.mult), reads=[ycb, rsb], writes=[ycb])
            P.op("dve", lambda e, o=yc[:], s=LNW[:, p:p + 1], b=BON[:, p, :]: e.scalar_tensor_tensor(out=o, in0=o, scalar=s, in1=b, op0=ALU.mult, op1=ALU.add),
                 reads=[ycb, BONb, mb], writes=[ycb])
            P.op("dve", lambda e, o=OIN[:, p, :], i=yc[:], s=LNB[:, p:p + 1], g_=GT[:, p, :]: e.scalar_tensor_tensor(out=o, in0=i, scalar=s, in1=g_, op0=ALU.add, op1=ALU.mult),
                 reads=[ycb, GTb, mb], writes=[OINb])
        if dbg:
            P.dma("sp", dyt, YT.rearrange("p a b -> p (a b)"), reads=[YTb])
            P.dma("sp", doin, OIN.rearrange("p a b -> p (a b)"), reads=[OINb])
        for eb in range(16):
            wb, wbb = wl.load(wo, 0, 16, eb * 128, 128)
            ps, psb = k.ps()
            for p in range(16):
                P.op("pe", lambda e, o=ps[:, :TT], w=wb[:, p, :], r=OIN[:, p, :], s=(p == 0), t=(p == 15):
                     e.matmul(o, lhsT=w, rhs=r, start=s, stop=t), reads=[wbb, OINb], writes=[psb])
            P.op("act", lambda e, o=Y2[:, eb, :], i=ps[:, :TT]: e.activation(out=o, in_=i, func=AF.Copy), reads=[psb], writes=[Y2b])
        post_residual(k, c, X, Xb, Y2, Y2b, G, mb, TT)
        P.dma("sp", oT3[:, :, t0:t0 + TT], X, reads=[Xb])
        P.barrier()
    k.close()
    return nc


def _pd(v, n=16):
    return np.ascontiguousarray(np.asarray(v, dtype=np.float32).reshape(n, 128).T)


def _run(nc, maps):
    res = run_bass_kernel_spmd(nc, maps, core_ids=list(range(len(maps))))
    return res.results


class _Root:
    pass


def build_fused():
    root = _Root()
    root.nc = bass.Bass("TRN2", target_bir_lowering=False)
    root.semstack = ExitStack()
    nc = root.nc
    xT = nc.dram_tensor("xT", [D, T], F32, kind="ExternalInput").ap()
    oT = nc.dram_tensor("oT", [D, T], F32, kind="ExternalOutput").ap()
    modsI = nc.dram_tensor("mods_i", [128, 192], F32, kind="Internal").ap()
    xa = nc.dram_tensor("xa_i", [D, T], F32, kind="Internal").ap()
    xb = nc.dram_tensor("xb_i", [D, T], F32, kind="Internal").ap()
    xc = nc.dram_tensor("xc_i", [D, T], F32, kind="Internal").ap()
    def decl(prefix, items):
        dm, pre = {}, []
        for name, shape in items:
            w = nc.dram_tensor(prefix + name, list(shape), F32, kind="ExternalInput").ap()
            wb = nc.dram_tensor(prefix + name + "_bf", list(shape), BF16, kind="Internal").ap()
            dm[name + "_bf"] = wb
            pre.append((wb, w))
        return dm, pre
    rw_dm, rw_pre = decl("r_", [("w_rkv", [3 * D, D]), ("w1", [D, 96]), ("w2", [96, D]), ("a1", [D, 96]), ("a2", [96, D]),
                                ("g1", [D, 256]), ("g2", [256, D]), ("w_o", [D, D])])
    f0_dm, f0_pre = decl("f0_", [("w_up", [D, FF]), ("w_dn", [FF, D])])
    f1_dm, f1_pre = decl("f1_", [("w_up", [D, FF]), ("w_dn", [FF, D])])
    a_dm, a_pre = decl("a_", [("kv_down", [D, 576]), ("kv_uk", [512, D]), ("kv_uv", [512, D]), ("w_dq", [D, 512]),
                              ("w_uq", [512, 16 * 192]), ("w_o", [D, D])])
    build_mods(env=(root, {"mods": modsI}, "m_", rw_pre))
    build_rwkv(0, env=(root, dict(rw_dm, xT=xT, mods=modsI, oT=xa), "r_", f0_pre + a_pre))
    build_mlp(0, env=(root, dict(f0_dm, xT=xa, mods=modsI, oT=xb), "f0_", f1_pre))
    build_mla(1, env=(root, dict(a_dm, xT=xb, mods=modsI, oT=xc), "a_"))
    build_mlp(1, env=(root, dict(f1_dm, xT=xc, mods=modsI, oT=oT), "f1_"))
    root.semstack.close()
    return nc


def kernel(x, c, positions, ada_w, ada_b, norm_g, mlp_up, mlp_down,
           rw_mu, rw_rkv, rw_w0, rw_w1, rw_w2, rw_a0, rw_a1, rw_a2, rw_g1, rw_g2,
           rw_kk, rw_ka, rw_rk, rw_lnx, rw_o,
           mla_dq, mla_qnorm, mla_uq, mla_o,
           kv_in_g, kv_down, kv_norm, kv_uk, kv_uv):
    f32 = np.float32
    A = lambda a: np.ascontiguousarray(np.asarray(a))
    x = A(x).astype(f32, copy=False)
    B = x.shape[0]
    cc = A(c)
    pos = A(positions).astype(np.int32, copy=False)
    vl = [A(rw_mu)[0][j] for j in range(6)] + [A(rw_w0)[0], A(rw_a0)[0], A(rw_kk)[0], A(rw_ka)[0], A(rw_rk)[0].reshape(-1),
                                              A(rw_lnx)[0][0], A(rw_lnx)[0][1]]
    inv = (1.0 / (10000.0 ** (np.arange(0, 64, 2, dtype=np.float32) / 64))).astype(f32)
    shared = {
        "m_ada_w": A(ada_w),
        "m_ada_b_pd": np.ascontiguousarray(A(ada_b).reshape(2, 96, 128).transpose(2, 0, 1)),
        "m_norm_g_pd": np.ascontiguousarray(A(norm_g).reshape(2, 4, 16, 128).transpose(3, 0, 1, 2)),
        "r_rw_vec": np.ascontiguousarray(np.stack([_pd(v) for v in vl], 1)).astype(f32),
        "r_w_rkv": A(rw_rkv)[0].reshape(3 * D, D), "r_w1": A(rw_w1)[0], "r_w2": A(rw_w2)[0], "r_a1": A(rw_a1)[0],
        "r_a2": A(rw_a2)[0], "r_g1": A(rw_g1)[0], "r_g2": A(rw_g2)[0], "r_w_o": A(rw_o)[0],
        "f0_w_up": A(mlp_up)[0], "f0_w_dn": A(mlp_down)[0], "f1_w_up": A(mlp_up)[1], "f1_w_dn": A(mlp_down)[1],
        "a_ropec": np.ascontiguousarray(np.stack([np.concatenate([inv, inv]), np.concatenate([-np.ones(32), np.ones(32)])], 1)).astype(f32),
        "a_mla_vec": np.ascontiguousarray(np.concatenate([_pd(kv_in_g), _pd(kv_norm, 4), _pd(A(mla_qnorm)[0], 4)], 1)).astype(f32),
        "a_kv_down": A(kv_down), "a_kv_uk": A(kv_uk).reshape(512, D), "a_kv_uv": A(kv_uv).reshape(512, D),
        "a_w_dq": A(mla_dq)[0], "a_w_uq": A(mla_uq)[0].reshape(512, 16 * 192), "a_w_o": A(mla_o)[0],
    }
    maps = [dict(shared, xT=np.ascontiguousarray(x[b].T), m_c_pd=_pd(cc[b]),
                 a_posr=np.ascontiguousarray(np.broadcast_to(pos[b][None, :], (64, T)))) for b in range(B)]
    r = _run(build_fused(), maps)
    return np.stack([np.ascontiguousarray(r[b]["oT"].T) for b in range(B)]).astype(f32, copy=False)
```

```python
import numpy as np
import concourse.bass as bass
import concourse.mybir as mybir
from concourse.bass_utils import run_bass_kernel_spmd
from contextlib import ExitStack

F32 = mybir.dt.float32
BF16 = mybir.dt.bfloat16
F32R = mybir.dt.float32r
CHAIN_DT = F32
I32 = mybir.dt.int32
ALU = mybir.AluOpType
AF = mybir.ActivationFunctionType

D = 2048
T = 2048
ND = 16
FF = 8192
NCORES = 8
EPS = 1e-6

EPOCH = 30000
N_DMA_SEMS = 6
SAME_ENGINE_SYNC = False


class Buf:
    __slots__ = ("w", "r")

    def __init__(self):
        self.w = None
        self.r = {}


class Prog:
    ENGS = ("pe", "dve", "act", "pool", "sp")
    NSEM = 0

    def __init__(self, nc, semstack=None):
        self.nc = nc
        self.semstack = semstack
        self.q = {e: [] for e in self.ENGS}
        self.cnt = {e: 0 for e in self.ENGS}
        self.seen = {e: {} for e in self.ENGS}
        self.dma_cnt = {}
        self.dma_rr = {e: 0 for e in self.ENGS}
        self.keys = set()

    def _wait(self, eng, key, val):
        if self.seen[eng].get(key, 0) >= val:
            return
        self.seen[eng][key] = val
        self.q[eng].append(("wait", key, val))

    def _deps(self, eng, reads, writes):
        deps = {}
        for b in reads:
            if b.w is not None:
                k, v = b.w
                if deps.get(k, 0) < v:
                    deps[k] = v
        for b in writes:
            if b.w is not None:
                k, v = b.w
                if deps.get(k, 0) < v:
                    deps[k] = v
            for k, v in b.r.items():
                if deps.get(k, 0) < v:
                    deps[k] = v
        for k, v in deps.items():
            if k[0] == "E" and k[1] == eng and (eng == "pe" or not SAME_ENGINE_SYNC):
                continue
            self._wait(eng, k, v)

    def _mark(self, tok, reads, writes):
        k, v = tok
        for b in reads:
            if b.r.get(k, 0) < v:
                b.r[k] = v
        for b in writes:
            b.w = tok
            b.r = {}

    def op(self, eng, fn, reads=(), writes=()):
        self._deps(eng, reads, writes)
        n = self.cnt[eng]
        self.cnt[eng] = n + 1
        key = ("E", eng, n // EPOCH)
        self.keys.add(key)
        self.q[eng].append(("op", fn, key))
        self._mark((key, n % EPOCH + 1), reads, writes)

    def dma(self, qeng, out, in_, reads=(), writes=()):
        self._deps(qeng, reads, writes)
        s = self.dma_rr[qeng]
        self.dma_rr[qeng] = (s + 1) % N_DMA_SEMS
        gen = 0
        while self.dma_cnt.get(("D", qeng, s, gen), 0) + 16 > EPOCH:
            gen += 1
        key = ("D", qeng, s, gen)
        prev = self.dma_cnt.get(key, 0)
        if prev > 0:
            self._wait(qeng, key, prev)
        elif gen > 0:
            pk = ("D", qeng, s, gen - 1)
            self._wait(qeng, pk, self.dma_cnt[pk])
        self.dma_cnt[key] = prev + 16
        self.keys.add(key)
        self.q[qeng].append(("dma", (out, in_), key))
        self._mark((key, prev + 16), reads, writes)

    def barrier(self):
        toks = []
        for e in self.ENGS:
            n = self.cnt[e]
            if n > 0:
                toks.append((("E", e, (n - 1) // EPOCH), (n - 1) % EPOCH + 1))
        toks += list(self.dma_cnt.items())
        for e in self.ENGS:
            for key, v in toks:
                if key[0] == "E" and key[1] == e:
                    continue
                self._wait(e, key, v)

    def finish(self, eng="sp"):
        for key, v in list(self.dma_cnt.items()):
            self._wait(eng, key, v)

    def emit(self):
        nc = self.nc
        engmap = {"pe": "tensor", "dve": "vector", "act": "scalar", "pool": "gpsimd", "sp": "sync"}
        with ExitStack() as st:
            sems = {}
            semst = self.semstack if self.semstack is not None else st
            for i, key in enumerate(sorted(self.keys, key=str)):
                Prog.NSEM += 1
                sems[key] = semst.enter_context(nc.semaphore("s%d" % Prog.NSEM))
            block = st.enter_context(nc.Block())
            for e in self.ENGS:
                items = self.q[e]
                if not items:
                    continue

                def body(eng, items=items):
                    for it in items:
                        if it[0] == "wait":
                            eng.wait_ge(sems[it[1]], it[2])
                        elif it[0] == "op":
                            it[1](eng).then_inc(sems[it[2]], 1)
                        else:
                            eng.dma_start(out=it[1][0], in_=it[1][1]).then_inc(sems[it[2]], 16)

                getattr(block, engmap[e])(body)


class Ctx:
    NT = 0

    def __init__(self, name, env=None):
        if env is None:
            self.nc = bass.Bass("TRN2", target_bir_lowering=False)
            self.semstack = None
            self.dmap = {}
            self.prefix = ""
            self.pre = []
        else:
            root, self.dmap, self.prefix = env[:3]
            self.pre = env[3] if len(env) > 3 else []
            self.nc = root.nc
            self.semstack = root.semstack
        self.P = Prog(self.nc, self.semstack)
        self.st = ExitStack()
        self.n = 0
        self.psl = []
        self.psi = 0
        self.rot = {}
        self.rotw = 512

    def dram(self, name, shape, dt, kind):
        if name in self.dmap:
            return self.dmap[name]
        return self.nc.dram_tensor(self.prefix + name, list(shape), dt, kind=kind).ap()

    def sb(self, shape, dt):
        Ctx.NT += 1
        return self.st.enter_context(self.nc.sbuf_tensor("t%d" % Ctx.NT, list(shape), dt))

    def init_psum(self, nf32=8):
        for i in range(nf32):
            Ctx.NT += 1
            t = self.st.enter_context(self.nc.psum_tensor("ps%d" % Ctx.NT, [128, 512], F32))
            self.psl.append((t, Buf()))

    def ps(self):
        r = self.psl[self.psi]
        self.psi = (self.psi + 1) % len(self.psl)
        return r

    def rotbuf(self, key, shape, dt, n=2):
        if key not in self.rot:
            self.rot[key] = [[(self.sb(shape, dt), Buf()) for _ in range(n)], 0]
        lst, i = self.rot[key]
        self.rot[key][1] = (i + 1) % len(lst)
        return lst[i]

    def do_pre(self):
        for (dst, src) in self.pre:
            cast_dma(self, dst, src)

    def close(self):
        self.P.finish("sp")
        self.P.emit()
        self.st.close()


class WT:
    def __init__(self, ap, buf):
        self.ap = ap
        self.buf = buf


def cast_dma(k, dst, src, buf=None, max_bytes=8 << 20):
    rows, cols = src.shape[0], src.shape[1]
    step = max(1, min(rows, max_bytes // (cols * 4)))
    for r0 in range(0, rows, step):
        r1 = min(rows, r0 + step)
        k.P.dma("pool", dst[r0:r1, :], src[r0:r1, :], writes=[buf] if buf is not None else [])


def wsrc(k, name, shape):
    if name + "_bf" in k.dmap:
        return WT(k.dmap[name + "_bf"], Buf())
    w = k.dram(name, shape, F32, "ExternalInput")
    wb = k.nc.dram_tensor(k.prefix + name + "_bf", list(shape), BF16, kind="Internal").ap()
    b = Buf()
    cast_dma(k, wb, w, b)
    return WT(wb, b)


class WLoader:
    def __init__(self, k, nk=16, ncols=256, nbf=3):
        self.k = k
        self.wbf = [(k.sb([128, nk, ncols], BF16), Buf()) for _ in range(nbf)]
        self.j = 0

    def load(self, W, r0, nk, c0, ncols, pp=128):
        P = self.k.P
        wb, wbb = self.wbf[self.j]
        self.j = (self.j + 1) % len(self.wbf)
        src = W.ap[r0:r0 + nk * pp, c0:c0 + ncols].rearrange("(k p) c -> p k c", p=pp)
        P.dma("sp", wb[:pp, :nk, :ncols], src, reads=[W.buf], writes=[wbb])
        return wb, wbb


def make_consts(k):
    P = k.P
    c = {}
    ones = k.sb([128, 128], BF16)
    c["ones"] = ones
    c["onesb"] = Buf()
    P.op("pool", lambda e: e.memset(ones[:], 1.0), writes=[c["onesb"]])
    eps = k.sb([128, 2], F32)
    c["eps"] = eps
    P.op("pool", lambda e: e.memset(eps[:, 0:1], EPS), writes=[c["onesb"]])
    P.op("pool", lambda e: e.memset(eps[:, 1:2], 64e-5), writes=[c["onesb"]])
    return c


def rms_rstd(k, c, X, Xb, TT, scale_div=D, epsap=None, ntile=ND):
    P = k.P
    if epsap is None:
        epsap = c["eps"][:, 0:1]
    ps, psb = k.ps()
    for dt in range(ntile):
        sq, sqb = k.rotbuf("sq", [128, k.rotw], BF16, 3)
        P.op("act", lambda e, o=sq[:, :TT], i=X[:, dt, :]: e.activation(out=o, in_=i, func=AF.Square),
             reads=[Xb], writes=[sqb])
        P.op("pe", lambda e, o=ps[:, :TT], r=sq[:, :TT], s=(dt == 0), t=(dt == ntile - 1):
             e.matmul(o, lhsT=c["ones"][:], rhs=r, start=s, stop=t), reads=[sqb, c["onesb"]], writes=[psb])
    rstd, rb = k.rotbuf("rstd", [128, k.rotw], F32, 2)
    P.op("act", lambda e, o=rstd[:, :TT], i=ps[:, :TT]: e.activation(
        out=o, in_=i, func=AF.Sqrt, bias=epsap, scale=1.0 / scale_div), reads=[psb, c["onesb"]], writes=[rb])
    P.op("dve", lambda e, o=rstd[:, :TT]: e.reciprocal(out=o, in_=o), reads=[rb], writes=[rb])
    return rstd, rb


def norm_mod(k, X, Xb, rstd, rb, A, Sh, mb, H, Hb, TT, col0=0):
    P = k.P
    for dt in range(ND):
        tmp, tb = k.rotbuf("nm_tmp", [128, k.rotw], F32, 2)
        P.op("dve", lambda e, o=tmp[:, :TT], i=X[:, dt, :], s=A[:, dt:dt + 1], r=rstd[:, :TT]:
             e.scalar_tensor_tensor(out=o, in0=i, scalar=s, in1=r, op0=ALU.mult, op1=ALU.mult),
             reads=[Xb, rb, mb], writes=[tb])
        P.op("act", lambda e, o=H[:, dt, col0:col0 + TT], i=tmp[:, :TT], s=Sh[:, dt:dt + 1]:
             e.activation(out=o, in_=i, func=AF.Identity, bias=s, scale=1.0),
             reads=[tb, mb], writes=[Hb])


def post_residual(k, c, X, Xb, Y, Yb, G, mb, TT):
    P = k.P
    rstd, rb = rms_rstd(k, c, Y, Yb, TT)
    for dt in range(ND):
        tmp, tb = k.rotbuf("nm_tmp", [128, k.rotw], F32, 2)
        P.op("dve", lambda e, o=tmp[:, :TT], i=Y[:, dt, :], s=G[:, dt:dt + 1], r=rstd[:, :TT]:
             e.scalar_tensor_tensor(out=o, in0=i, scalar=s, in1=r, op0=ALU.mult, op1=ALU.mult),
             reads=[Yb, rb, mb], writes=[tb])
        P.op("pool" if dt % 2 else "dve", lambda e, o=X[:, dt, :], i=tmp[:, :TT]: e.tensor_tensor(out=o, in0=o, in1=i, op=ALU.add),
             reads=[tb], writes=[Xb])


def build_mods(env=None):
    k = Ctx("mods", env)
    nc, P = k.nc, k.P
    k.do_pre()
    c_pd = k.dram("c_pd", [128, 16], F32, "ExternalInput")
    ada_w = k.dram("ada_w", [2, D, 6 * D], F32, "ExternalInput")
    ada_b = k.dram("ada_b_pd", [128, 2, 96], F32, "ExternalInput")
    ng = k.dram("norm_g_pd", [128, 2, 4, 16], F32, "ExternalInput")
    mods = k.dram("mods", [128, 2 * 96], F32, "ExternalOutput")
    k.init_psum(2)
    cin = k.sb([128, 16], F32)
    cact = k.sb([128, 16], F32)
    abt = k.sb([128, 2, 96], F32)
    ngt = k.sb([128, 2, 4, 16], F32)
    raw = k.sb([128, 2, 96], F32)
    outt = k.sb([128, 2, 6, 16], F32)
    cb_, sm_ = Buf(), Buf()
    P.dma("sp", cin[:], c_pd, writes=[cb_])
    P.dma("sp", abt[:], ada_b, writes=[sm_])
    P.dma("sp", ngt[:], ng, writes=[sm_])
    P.op("act", lambda e: e.activation(out=cact[:], in_=cin[:], func=AF.Silu), reads=[cb_], writes=[cb_])
    stg = [(k.sb([128, 16, 512], F32), Buf()) for _ in range(3)]
    rawb, ob = Buf(), Buf()
    one1 = k.sb([1, 1], F32)
    row = k.sb([1, 6 * D], F32)
    rowb = Buf()
    P.op("dve", lambda e: e.memset(one1[:], 1.0), writes=[cb_])
    k.psl = k.psl + [(k.st.enter_context(nc.psum_tensor("psx%d" % i, [128, 512], F32)), Buf()) for i in range(4)]
    for l in range(2):
        for cb in range(24):
            st, stb = stg[(l * 24 + cb) % 3]
            src = ada_w[l, :, cb * 512:(cb + 1) * 512].rearrange("(k p) c -> p k c", p=128)
            P.dma("sp", st[:], src, writes=[stb])
            psr, psrb = k.ps()
            for dt in range(16):
                P.op("pe", lambda e, o=psr[0:1, :], w=cact[:, dt:dt + 1], r=st[:, dt, :], s=(dt == 0), t=(dt == 15):
                     e.matmul(o, lhsT=w, rhs=r, start=s, stop=t), reads=[stb, cb_], writes=[psrb])
            P.op("act" if cb % 2 else "dve",
                 (lambda e, o=row[0:1, cb * 512:(cb + 1) * 512], i=psr[0:1, :]: e.activation(out=o, in_=i, func=AF.Copy)) if cb % 2 else
                 (lambda e, o=row[0:1, cb * 512:(cb + 1) * 512], i=psr[0:1, :]: e.tensor_copy(out=o, in_=i)),
                 reads=[psrb], writes=[rowb])
        ps, psb = k.ps()
        for e_ in range(96):
            P.op("pe", lambda e, o=ps[:, e_:e_ + 1], w=row[0:1, e_ * 128:(e_ + 1) * 128]: e.matmul(o, lhsT=w, rhs=one1[0:1, 0:1], start=True, stop=True),
                 reads=[rowb, cb_], writes=[psb])
        P.op("dve", lambda e, o=raw[:, l, :], i=ps[:, 0:96], b=abt[:, l, :]: e.tensor_tensor(out=o, in0=i, in1=b, op=ALU.add),
             reads=[psb, sm_], writes=[rawb])
        for half, (gpre, gpost) in enumerate(((0, 1), (2, 3))):
            b0 = half * 3
            P.op("dve", lambda e, o=outt[:, l, b0 + 0, :], i=raw[:, l, (b0 + 1) * 16:(b0 + 2) * 16], g=ngt[:, l, gpre, :]:
                 e.scalar_tensor_tensor(out=o, in0=i, scalar=1.0, in1=g, op0=ALU.add, op1=ALU.mult),
                 reads=[rawb, sm_], writes=[ob])
            P.op("dve", lambda e, o=outt[:, l, b0 + 1, :], i=raw[:, l, (b0 + 0) * 16:(b0 + 1) * 16]:
                 e.tensor_copy(out=o, in_=i), reads=[rawb], writes=[ob])
            P.op("dve", lambda e, o=outt[:, l, b0 + 2, :], i=raw[:, l, (b0 + 2) * 16:(b0 + 3) * 16], g=ngt[:, l, gpost, :]:
                 e.tensor_tensor(out=o, in0=i, in1=g, op=ALU.mult), reads=[rawb, sm_], writes=[ob])
    P.dma("sp", mods, outt[:].rearrange("p l j d -> p (l j d)"), reads=[ob])
    k.close()
    return nc


def build_mlp(l, env=None):
    k = Ctx("mlp", env)
    nc, P = k.nc, k.P
    TT = 512
    xT = k.dram("xT", [D, T], F32, "ExternalInput")
    modsd = k.dram("mods", [128, 192], F32, "ExternalInput")
    k.do_pre()
    wup = wsrc(k, "w_up", [D, FF])
    wdn = wsrc(k, "w_dn", [FF, D])
    oT = k.dram("oT", [D, T], F32, "ExternalOutput")
    k.init_psum(8)
    c = make_consts(k)
    mt = k.sb([128, 2, 6, 16], F32)
    mb = Buf()
    P.dma("sp", mt[:].rearrange("p l j d -> p (l j d)"), modsd, writes=[mb])
    A, Sh, G = mt[:, l, 3, :], mt[:, l, 4, :], mt[:, l, 5, :]
    Xs = [(k.sb([128, ND, TT], F32), Buf()) for _ in range(2)]
    Hs_ = [(k.sb([128, ND, TT], BF16), Buf()) for _ in range(2)]
    U = k.sb([128, 32, TT], BF16)
    Y = k.sb([128, ND, TT], F32)
    Ub, Yb = Buf(), Buf()
    wl = WLoader(k, 16, 256, 3)
    xT3 = xT.rearrange("(k p) t -> p k t", p=128)
    oT3 = oT.rearrange("(k p) t -> p k t", p=128)
    NT_ = T // TT

    def load_norm(tt):
        X, Xb = Xs[tt % 2]
        H, Hb = Hs_[tt % 2]
        P.dma("sp", X[:], xT3[:, :, tt * TT:(tt + 1) * TT], writes=[Xb])
        rstd, rb = rms_rstd(k, c, X, Xb, TT)
        norm_mod(k, X, Xb, rstd, rb, A, Sh, mb, H, Hb, TT)

    load_norm(0)
    for tt in range(NT_):
        X, Xb = Xs[tt % 2]
        H, Hb = Hs_[tt % 2]
        for fh in range(2):
            for fb in range(16):
                f0 = fh * 4096 + fb * 256
                wb, wbb = wl.load(wup, 0, 16, f0, 256)
                for j in range(2):
                    ps, psb = k.ps()
                    for dt in range(16):
                        P.op("pe", lambda e, o=ps[:, :TT], w=wb[:, dt, j * 128:(j + 1) * 128], r=H[:, dt, :],
                             s=(dt == 0), t=(dt == 15): e.matmul(o, lhsT=w, rhs=r, start=s, stop=t),
                             reads=[wbb, Hb], writes=[psb])
                    rl, rlb = k.rotbuf("relu", [128, 512], F32, 3)
                    P.op("act", lambda e, o=rl[:, :TT], i=ps[:, :TT]: e.activation(out=o, in_=i, func=AF.Relu),
                         reads=[psb], writes=[rlb])
                    P.op("dve", lambda e, o=U[:, fb * 2 + j, :], i=rl[:, :TT]: e.tensor_tensor(out=o, in0=i, in1=i, op=ALU.mult),
                         reads=[rlb], writes=[Ub])
            if fh == 0 and tt + 1 < NT_:
                load_norm(tt + 1)
            for db in range(8):
                pss = [k.ps(), k.ps()]
                for kb in range(2):
                    wb, wbb = wl.load(wdn, fh * 4096 + kb * 2048, 16, db * 256, 256)
                    for j in range(2):
                        ps, psb = pss[j]
                        for ft in range(16):
                            P.op("pe", lambda e, o=ps[:, :TT], w=wb[:, ft, j * 128:(j + 1) * 128], r=U[:, kb * 16 + ft, :],
                                 s=(kb == 0 and ft == 0), t=(kb == 1 and ft == 15): e.matmul(o, lhsT=w, rhs=r, start=s, stop=t),
                                 reads=[wbb, Ub], writes=[psb])
                for j in range(2):
                    ps, psb = pss[j]
                    if fh == 0:
                        P.op("act", lambda e, o=Y[:, db * 2 + j, :], i=ps[:, :TT]: e.activation(out=o, in_=i, func=AF.Copy),
                             reads=[psb], writes=[Yb])
                    else:
                        P.op("dve", lambda e, o=Y[:, db * 2 + j, :], i=ps[:, :TT]: e.tensor_tensor(out=o, in0=o, in1=i, op=ALU.add),
                             reads=[psb], writes=[Yb])
        post_residual(k, c, X, Xb, Y, Yb, G, mb, TT)
        P.dma("sp", oT3[:, :, tt * TT:(tt + 1) * TT], X[:], reads=[Xb])
    k.close()
    return nc


def load_swap(wl, W, nk, c0):
    P = wl.k.P
    wb, wbb = wl.wbf[wl.j]
    wl.j = (wl.j + 1) % len(wl.wbf)
    for (a, b_) in ((0, 32), (32, 0)):
        src = W.ap[0:nk * 128, c0 + b_:c0 + b_ + 32].rearrange("(k p) c -> p k c", p=128)
        P.dma("sp", wb[:, :nk, a:a + 32], src, reads=[W.buf], writes=[wbb])
    return wb, wbb


def angle_reduce(k, ang, kf, ki, ab):
    import math
    P = k.P
    P.op("dve", lambda e: e.tensor_scalar(out=kf, in0=ang, scalar1=1.0 / (2 * math.pi), scalar2=None, op0=ALU.mult), reads=[ab], writes=[ab])
    P.op("dve", lambda e: e.tensor_copy(out=ki, in_=kf), reads=[ab], writes=[ab])
    P.op("dve", lambda e: e.tensor_copy(out=kf, in_=ki), reads=[ab], writes=[ab])
    P.op("dve", lambda e: e.scalar_tensor_tensor(out=ang, in0=kf, scalar=-2 * math.pi, in1=ang, op0=ALU.mult, op1=ALU.add), reads=[ab], writes=[ab])
    P.op("dve", lambda e: e.tensor_scalar(out=kf, in0=ang, scalar1=math.pi, scalar2=-2 * math.pi, op0=ALU.is_gt, op1=ALU.mult), reads=[ab], writes=[ab])
    P.op("dve", lambda e: e.tensor_tensor(out=ang, in0=ang, in1=kf, op=ALU.add), reads=[ab], writes=[ab])
    P.op("dve", lambda e: e.tensor_scalar(out=kf, in0=ang, scalar1=-math.pi, scalar2=2 * math.pi, op0=ALU.is_lt, op1=ALU.mult), reads=[ab], writes=[ab])
    P.op("dve", lambda e: e.tensor_tensor(out=ang, in0=ang, in1=kf, op=ALU.add), reads=[ab], writes=[ab])


def build_mla(l=1, env=None):
    import math
    k = Ctx("mla", env)
    nc, P = k.nc, k.P
    TT = 512
    NTT = T // TT
    xT = k.dram("xT", [D, T], F32, "ExternalInput")
    modsd = k.dram("mods", [128, 192], F32, "ExternalInput")
    posr = k.dram("posr", [64, T], I32, "ExternalInput")
    ropec = k.dram("ropec", [64, 2], F32, "ExternalInput")
    vec = k.dram("mla_vec", [128, 24], F32, "ExternalInput")
    k.do_pre()
    kvd = wsrc(k, "kv_down", [D, 576])
    wuk = wsrc(k, "kv_uk", [512, D])
    wuv = wsrc(k, "kv_uv", [512, D])
    wdq = wsrc(k, "w_dq", [D, 512])
    wuq = wsrc(k, "w_uq", [512, 16 * 192])
    wo = wsrc(k, "w_o", [D, D])
    oT = k.dram("oT", [D, T], F32, "ExternalOutput")
    otd = k.dram("ot_scratch", [16, 128, T], BF16, "Internal")
    k.init_psum(8)
    oacc = k.psl[4:]
    k.psl = k.psl[:4]
    c = make_consts(k)
    mt = k.sb([128, 2, 6, 16], F32)
    vt = k.sb([128, 24], F32)
    zer = k.sb([128, 16], F32)
    mb = Buf()
    P.dma("sp", mt[:].rearrange("p l j d -> p (l j d)"), modsd, writes=[mb])
    P.dma("sp", vt[:], vec, writes=[mb])
    P.op("pool", lambda e: e.memset(zer[:], 0.0), writes=[mb])
    A, Sh, G = mt[:, l, 0, :], mt[:, l, 1, :], mt[:, l, 2, :]

    X = k.sb([128, ND, TT], F32)
    Y = k.sb([128, ND, TT], F32)
    Xb, Yb = Buf(), Buf()
    Yf = Y[:].rearrange("p a b -> p (a b)")
    Ybf = Yf.bitcast(BF16)
    Xbf = X[:].rearrange("p a b -> p (a b)").bitcast(BF16)
    HS = Ybf[:, 0:8192].rearrange("p (a b) -> p a b", a=ND)
    HH = Ybf[:, 8192:16384].rearrange("p (a b) -> p a b", a=ND)
    rc = k.sb([64, 2], F32)
    cos2 = k.sb([64, T], F32)
    sinS = k.sb([64, T], F32)
    csb = Buf()
    P.dma("sp", rc[:], ropec, writes=[mb])
    pi_t = Yf[:64, 0:512].bitcast(I32)
    ang = Yf[:64, 512:1024]
    tmp = Yf[:64, 1024:1536]
    kf = Yf[:64, 1536:2048]
    ki = Yf[:64, 2048:2560].bitcast(I32)
    for ch in range(4):
        t0 = ch * 512
        P.dma("sp", pi_t, posr[:, t0:t0 + 512], writes=[Yb])
        P.op("dve", lambda e: e.tensor_copy(out=ang, in_=pi_t), reads=[Yb], writes=[Yb])
        P.op("dve", lambda e: e.tensor_scalar(out=ang, in0=ang, scalar1=rc[:, 0:1], scalar2=None, op0=ALU.mult), reads=[Yb, mb], writes=[Yb])
        P.op("dve", lambda e: e.tensor_scalar(out=tmp, in0=ang, scalar1=math.pi / 2, scalar2=None, op0=ALU.add), reads=[Yb], writes=[Yb])
        angle_reduce(k, tmp, kf, ki, Yb)
        P.op("act", lambda e, o=cos2[:, t0:t0 + 512]: e.activation(out=o, in_=tmp, func=AF.Sin), reads=[Yb], writes=[csb])
        angle_reduce(k, ang, kf, ki, Yb)
        P.op("act", lambda e, o=sinS[:, t0:t0 + 512]: e.activation(out=o, in_=ang, func=AF.Sin), reads=[Yb], writes=[csb])
        P.op("dve", lambda e, o=sinS[:, t0:t0 + 512]: e.tensor_scalar(out=o, in0=o, scalar1=rc[:, 1:2], scalar2=None, op0=ALU.mult), reads=[csb, mb], writes=[csb])

    CKQ = k.sb([128, 8, TT], F32)
    CK = CKQ[:, 0:4, :]
    CQ = CKQ[:, 4:8, :]
    CKb, CQb = Buf(), Buf()
    CKN = k.sb([128, 4, T], BF16)
    CQN = k.sb([128, 4, T], BF16)
    KR = k.sb([128, T], BF16)
    CKNb, CQNb, KRb = Buf(), Buf(), Buf()
    P.op("pool", lambda e: e.memset(KR[:], 0.0), writes=[KRb])
    wl = WLoader(k, 16, 128, 4)
    xT3 = xT.rearrange("(k p) t -> p k t", p=128)
    oT3 = oT.rearrange("(k p) t -> p k t", p=128)

    def rope_out(ps1, ps1b, ps2, ps2b, dst, dstb, t0):
        t1, t1b = k.rotbuf("rp1", [64, 512], F32, 1)
        t2, t2b = k.rotbuf("rp2", [64, 512], F32, 1)
        P.op("dve", lambda e: e.tensor_tensor(out=t1[:], in0=ps1[:64, :TT], in1=cos2[:, t0:t0 + TT], op=ALU.mult), reads=[ps1b, csb], writes=[t1b])
        P.op("dve", lambda e: e.tensor_tensor(out=t2[:], in0=ps2[:64, :TT], in1=sinS[:, t0:t0 + TT], op=ALU.mult), reads=[ps2b, csb], writes=[t2b])
        P.op("pool", lambda e: e.tensor_tensor(out=dst[:64, t0:t0 + TT], in0=t1[:], in1=t2[:], op=ALU.add), reads=[t1b, t2b], writes=[dstb])

    for tt in range(NTT):
        t0 = tt * TT
        P.dma("sp", X[:], xT3[:, :, t0:t0 + TT], writes=[Xb])
        rstd, rb = rms_rstd(k, c, X, Xb, TT)
        norm_mod(k, X, Xb, rstd, rb, vt[:, 0:16], zer, mb, HS, Yb, TT)
        norm_mod(k, X, Xb, rstd, rb, A, Sh, mb, HH, Yb, TT)
        for (W, src, dst, dstb) in ((kvd, HS, CK, CKb), (wdq, HH, CQ, CQb)):
            for cb in range(4):
                wb, wbb = wl.load(W, 0, 16, cb * 128, 128)
                ps, psb = k.ps()
                for dt in range(16):
                    P.op("pe", lambda e, o=ps[:, :TT], w=wb[:, dt, :], r=src[:, dt, :], s=(dt == 0), t=(dt == 15):
                         e.matmul(o, lhsT=w, rhs=r, start=s, stop=t), reads=[wbb, Yb], writes=[psb])
                P.op("act", lambda e, o=dst[:, cb, :], i=ps[:, :TT]: e.activation(out=o, in_=i, func=AF.Copy), reads=[psb], writes=[dstb])
        pss = []
        for sw in range(2):
            if sw == 0:
                wb, wbb = wl.load(kvd, 0, 16, 512, 64)
            else:
                wb, wbb = load_swap(wl, kvd, 16, 512)
            ps, psb = k.ps()
            for dt in range(16):
                P.op("pe", lambda e, o=ps[:64, :TT], w=wb[:, dt, 0:64], r=HS[:, dt, :], s=(dt == 0), t=(dt == 15):
                     e.matmul(o, lhsT=w, rhs=r, start=s, stop=t), reads=[wbb, Yb], writes=[psb])
            pss.append((ps, psb))
        rope_out(pss[0][0], pss[0][1], pss[1][0], pss[1][1], KR, KRb, t0)
        for (src, srcb, dst, dstb, v0) in ((CK, CKb, CKN, CKNb, 16), (CQ, CQb, CQN, CQNb, 20)):
            rs, rsb = rms_rstd(k, c, src, srcb, TT, scale_div=512, ntile=4)
            for ct in range(4):
                P.op("dve", lambda e, o=dst[:, ct, t0:t0 + TT], i=src[:, ct, :], s=vt[:, v0 + ct:v0 + ct + 1], r=rs[:, :TT]:
                     e.scalar_tensor_tensor(out=o, in0=i, scalar=s, in1=r, op0=ALU.mult, op1=ALU.mult),
                     reads=[srcb, rsb, mb], writes=[dstb])

    P.barrier()
    tri = k.sb([128, 128], BF16)
    trib = Buf()
    P.op("pool", lambda e: e.memset(tri[:], 1.0), writes=[trib])
    P.op("pool", lambda e: e.affine_select(out=tri[:], in_=tri[:], pattern=[[1, 128]], compare_op=ALU.is_ge, fill=0.0,
                                           base=0, channel_multiplier=-1), reads=[trib], writes=[trib])
    wl2 = WLoader(k, 4, 128, 8)
    scale = 192.0 ** -0.5
    hb = []
    for reg in (Ybf, Xbf):
        hb.append(dict(KN=reg[:, 0:2048], QN=reg[:, 2048:4096], QR=reg[:, 4096:6144], OH=reg[:, 6144:8192],
                       VH=reg[:, 8192:10240].rearrange("p (a b) -> p a b", a=16),
                       KNb=Buf(), QNb=Buf(), QRb=Buf(), OHb=Buf(), VHb=Buf()))
    for s_ in hb:
        P.op("pool", lambda e, o=s_["QR"]: e.memset(o, 0.0), writes=[s_["QRb"]])
    for h in range(16):
        s_ = hb[h % 2]
        KN, QN, QR, OH, VH = s_["KN"], s_["QN"], s_["QR"], s_["OH"], s_["VH"]
        KNb, QNb, QRb, OHb, VHb = s_["KNb"], s_["QNb"], s_["QRb"], s_["OHb"], s_["VHb"]
        wk, wkb = wl2.load(wuk, 0, 4, h * 128, 128)
        wq, wqb = wl2.load(wuq, 0, 4, h * 192, 128)
        wv, wvb = wl2.load(wuv, 0, 4, h * 128, 128)
        wr, wrb = wl2.load(wuq, 0, 4, h * 192 + 128, 64)
        ws, wsb = load_swap(wl2, wuq, 4, h * 192 + 128)
        for tq in range(NTT):
            t0 = tq * TT
            for (w_, wb_, src, srcb, dst, dstb) in ((wk, wkb, CKN, CKNb, KN, KNb), (wq, wqb, CQN, CQNb, QN, QNb)):
                ps, psb = k.ps()
                for ct in range(4):
                    P.op("pe", lambda e, o=ps[:, :TT], w=w_[:, ct, :], r=src[:, ct, t0:t0 + TT], s=(ct == 0), t=(ct == 3):
                         e.matmul(o, lhsT=w, rhs=r, start=s, stop=t), reads=[wb_, srcb], writes=[psb])
                P.op("act", lambda e, o=dst[:, t0:t0 + TT], i=ps[:, :TT]: e.activation(out=o, in_=i, func=AF.Copy), reads=[psb], writes=[dstb])
            pss = []
            for (w_, wb_) in ((wr, wrb), (ws, wsb)):
                ps, psb = k.ps()
                for ct in range(4):
                    P.op("pe", lambda e, o=ps[:64, :TT], w=w_[:, ct, 0:64], r=CQN[:, ct, t0:t0 + TT], s=(ct == 0), t=(ct == 3):
                         e.matmul(o, lhsT=w, rhs=r, start=s, stop=t), reads=[wb_, CQNb], writes=[psb])
                pss.append((ps, psb))
            rope_out(pss[0][0], pss[0][1], pss[1][0], pss[1][1], QR, QRb, t0)
        for tk4 in range(4):
            ps, psb = k.ps()
            for i in range(4):
                tk = tk4 * 4 + i
                for ct in range(4):
                    P.op("pe", lambda e, o=ps[:, i * 128:(i + 1) * 128], w=CKN[:, ct, tk * 128:(tk + 1) * 128], r=wv[:, ct, :], s=(ct == 0), t=(ct == 3):
                         e.matmul(o, lhsT=w, rhs=r, start=s, stop=t), reads=[wvb, CKNb], writes=[psb])
            P.op("act", lambda e, o=VH[:, tk4 * 4:tk4 * 4 + 4, :], i=ps[:, :].rearrange("p (a b) -> p a b", a=4):
                 e.activation(out=o, in_=i, func=AF.Copy), reads=[psb], writes=[VHb])
        for qt in range(NTT):
            oa, oab = oacc[(qt % 2) * 2]
            da, dab = oacc[(qt % 2) * 2 + 1]
            nk_ = 4 * (qt + 1)
            def score(kt):
                off = max(0, (kt - 4 * qt) * 128)
                q0 = qt * TT + off
                q1 = (qt + 1) * TT
                sp_, spb = k.ps()
                P.op("pe", lambda e, o=sp_[:, off:TT], w=KN[:, kt * 128:(kt + 1) * 128], r=QN[:, q0:q1]:
                     e.matmul(o, lhsT=w, rhs=r, start=True, stop=False), reads=[KNb, QNb], writes=[spb])
                P.op("pe", lambda e, o=sp_[:, off:TT], w=KR[:, kt * 128:(kt + 1) * 128], r=QR[:, q0:q1]:
                     e.matmul(o, lhsT=w, rhs=r, start=False, stop=True), reads=[KRb, QRb], writes=[spb])
                PT, PTb = k.rotbuf("PT", [128, TT], BF16, 4)
                P.op("act", lambda e, o=PT[:, off:TT], i=sp_[:, off:TT]: e.activation(out=o, in_=i, func=AF.Exp, scale=scale),
                     reads=[spb], writes=[PTb])
                if kt >= 4 * qt:
                    P.op("pool", lambda e, o=PT[:, off:off + 128]: e.tensor_tensor(out=o, in0=o, in1=tri[:], op=ALU.mult),
                         reads=[PTb, trib], writes=[PTb])
                return (kt, off, PT, PTb)

            def pv(st_):
                kt, off, PT, PTb = st_
                P.op("pe", lambda e, o=oa[:, off:TT], w=VH[:, kt, :], r=PT[:, off:TT], s=(kt == 0), t=(kt == nk_ - 1):
                     e.matmul(o, lhsT=w, rhs=r, start=s, stop=t), reads=[VHb, PTb], writes=[oab])
                P.op("pe", lambda e, o=da[:, off:TT], r=PT[:, off:TT], s=(kt == 0), t=(kt == nk_ - 1):
                     e.matmul(o, lhsT=c["ones"][:], rhs=r, start=s, stop=t), reads=[c["onesb"], PTb], writes=[dab])

            pend = []
            for kt in range(nk_):
                pend.append(score(kt))
                if len(pend) > 2:
                    pv(pend.pop(0))
            while pend:
                pv(pend.pop(0))
            rd, rdb = k.rotbuf("rden", [128, TT], F32, 2)
            P.op("dve", lambda e, o=rd[:], i=da[:, :TT]: e.reciprocal(out=o, in_=i), reads=[dab], writes=[rdb])
            P.op("dve", lambda e, o=OH[:, qt * TT:(qt + 1) * TT], i=oa[:, :TT], r=rd[:]: e.tensor_tensor(out=o, in0=i, in1=r, op=ALU.mult),
                 reads=[oab, rdb], writes=[OHb])
        P.dma("sp", otd[h], OH, reads=[OHb])

    P.barrier()
    OTt = CKQ[:].rearrange("p a b -> p (a b)").bitcast(BF16).rearrange("p (a b) -> p a b", a=16)
    OTb = Buf()
    otd3 = otd.rearrange("h p t -> p h t")
    Xb, Yb = Buf(), Buf()
    for tt in range(NTT):
        t0 = tt * TT
        P.dma("sp", OTt, otd3[:, :, t0:t0 + TT], writes=[OTb])
        P.dma("sp", X[:], xT3[:, :, t0:t0 + TT], writes=[Xb])
        for eb in range(16):
            wb, wbb = wl.load(wo, 0, 16, eb * 128, 128)
            ps, psb = k.ps()
            for hh in range(16):
                P.op("pe", lambda e, o=ps[:, :TT], w=wb[:, hh, :], r=OTt[:, hh, :], s=(hh == 0), t=(hh == 15):
                     e.matmul(o, lhsT=w, rhs=r, start=s, stop=t), reads=[wbb, OTb], writes=[psb])
            P.op("act", lambda e, o=Y[:, eb, :], i=ps[:, :TT]: e.activation(out=o, in_=i, func=AF.Copy), reads=[psb], writes=[Yb])
        post_residual(k, c, X, Xb, Y, Yb, G, mb, TT)
        P.dma("sp", oT3[:, :, t0:t0 + TT], X[:], reads=[Xb])
    k.close()
    return nc


def build_rwkv(l=0, dbg=False, env=None):
    k = Ctx("rwkv", env)
    k.rotw = 256
    nc, P = k.nc, k.P
    TT = 256
    NTT = T // TT
    C = 64
    NCH = TT // C
    xT = k.dram("xT", [D, T], F32, "ExternalInput")
    modsd = k.dram("mods", [128, 192], F32, "ExternalInput")
    vec = k.dram("rw_vec", [128, 13, 16], F32, "ExternalInput")
    k.do_pre()
    wrkv = wsrc(k, "w_rkv", [3 * D, D])
    w1 = wsrc(k, "w1", [D, 96])
    w2 = wsrc(k, "w2", [96, D])
    a1 = wsrc(k, "a1", [D, 96])
    a2 = wsrc(k, "a2", [96, D])
    g1 = wsrc(k, "g1", [D, 256])
    g2 = wsrc(k, "g2", [256, D])
    wo = wsrc(k, "w_o", [D, D])
    oT = k.dram("oT", [D, T], F32, "ExternalOutput")
    k.init_psum(8)
    c = make_consts(k)
    mt = k.sb([128, 2, 6, 16], F32)
    vt = k.sb([128, 13, 16], F32)
    mb = Buf()
    P.dma("sp", mt[:].rearrange("p l j d -> p (l j d)"), modsd, writes=[mb])
    P.dma("sp", vt[:], vec, writes=[mb])
    A, Sh, G = mt[:, l, 0, :], mt[:, l, 1, :], mt[:, l, 2, :]
    MU, W0, A0, KKv, KA, RK, LNW, LNB = (lambda j: vt[:, j, :]), vt[:, 6, :], vt[:, 7, :], vt[:, 8, :], vt[:, 9, :], vt[:, 10, :], vt[:, 11, :], vt[:, 12, :]

    NEG = k.sb([128, 2, 16], F32)
    P.op("dve", lambda e: e.tensor_scalar(out=NEG[:], in0=vt[:, 6:8, :], scalar1=-1.0, scalar2=None, op0=ALU.mult), reads=[mb], writes=[mb])
    cb_ = Buf()
    bo16 = k.sb([128, 128], BF16)
    bo32 = k.sb([128, 128], F32)
    idn = k.sb([128, 4, 128], BF16)
    mS = k.sb([128, 4, 64], BF16)
    mI = k.sb([128, 4, 64], BF16)
    mL = k.sb([128, 4, 64], BF16)
    ones64 = k.sb([128, 64], F32)
    for t_ in (bo16, bo32):
        P.op("pool", lambda e, t_=t_: e.memset(t_[:], 0.0), writes=[cb_])
        P.op("pool", lambda e, t_=t_: e.memset(t_[0:64, 0:64], 1.0), writes=[cb_])
        P.op("pool", lambda e, t_=t_: e.memset(t_[64:128, 64:128], 1.0), writes=[cb_])
    P.op("pool", lambda e: e.memset(ones64[:], 1.0), writes=[cb_])
    P.op("pool", lambda e: e.memset(idn[:], 1.0), writes=[cb_])
    P.op("pool", lambda e: e.memset(mS[:], 1.0), writes=[cb_])
    P.op("pool", lambda e: e.memset(mI[:], 1.0), writes=[cb_])
    P.op("pool", lambda e: e.memset(mL[:], 1.0), writes=[cb_])
    for g_ in range(4):
        P.op("pool", lambda e, o=idn[:, g_, :]: e.affine_select(out=o, in_=o, pattern=[[1, 128]], compare_op=ALU.is_equal, fill=0.0,
                                                               base=0, channel_multiplier=-1), reads=[cb_], writes=[cb_])
        for hf in range(2):
            sl = slice(64 * hf, 64 * hf + 64)
            P.op("pool", lambda e, o=mS[sl, g_, :]: e.affine_select(out=o, in_=o, pattern=[[1, 64]], compare_op=ALU.is_ge, fill=0.0,
                                                                    base=-1, channel_multiplier=-1), reads=[cb_], writes=[cb_])
            P.op("pool", lambda e, o=mI[sl, g_, :]: e.affine_select(out=o, in_=o, pattern=[[1, 64]], compare_op=ALU.is_ge, fill=0.0,
                                                                    base=0, channel_multiplier=-1), reads=[cb_], writes=[cb_])
            P.op("pool", lambda e, o=mL[sl, g_, :]: e.affine_select(out=o, in_=o, pattern=[[-1, 64]], compare_op=ALU.is_ge, fill=0.0,
                                                                    base=-1, channel_multiplier=1), reads=[cb_], writes=[cb_])

    RT = k.sb([128, 16, TT], BF16)
    KT = k.sb([128, 16, TT], BF16)
    BT = k.sb([128, 16, TT], BF16)
    AT = k.sb([128, 16, TT], BF16)
    VT = k.sb([128, 16, TT], BF16)
    GT = k.sb([128, 16, TT], BF16)
    BON = k.sb([128, 16, TT], BF16)
    GC = k.sb([128, 16, NCH], F32)
    RTb, KTb, BTb, ATb, VTb, GTb, BONb, GCb, YTb = (Buf() for _ in range(9))
    Hf = k.sb([128, 16, 64], F32)
    Hstk = k.sb([128, 16, 64], BF16)
    Hbd = k.sb([128, 16, 128], BF16)
    Hb_ = [Buf() for _ in range(4)]
    HL = k.sb([128, 16, 1], F32)
    HLb = Buf()
    P.op("pool", lambda e: e.memset(Hf[:], 0.0), writes=Hb_)
    P.op("pool", lambda e: e.memset(Hstk[:], 0.0), writes=Hb_)
    P.op("pool", lambda e: e.memset(Hbd[:], 0.0), writes=Hb_)
    P.op("pool", lambda e: e.memset(HL[:], 0.0), writes=[HLb])
    TW = k.sb([128, TT], BF16)
    TA = k.sb([128, TT], BF16)
    TG = k.sb([128, 2, TT], BF16)
    TWb, TAb, TGb = Buf(), Buf(), Buf()
    wl = WLoader(k, 16, 128, 3)
    wls = WLoader(k, 2, 128, 3)
    REG = k.sb([128, 18688], F32)
    REGbf = REG[:].bitcast(BF16)

    def f32v(o, n, a):
        return REG[:, o:o + n].rearrange("p (a b) -> p a b", a=a)

    def bfv(o, n, a):
        return REGbf[:, 2 * o:2 * o + 2 * n].rearrange("p (a b) -> p a b", a=a)

    X = f32v(0, 4096, 16)
    Hs = f32v(4096, 4352, 16)
    XX = bfv(8448, 2048, 16)
    XS = bfv(10496, 2048, 16)
    XR = bfv(12544, 2048, 16)
    XK = bfv(14592, 2048, 16)
    XV = bfv(16640, 2048, 16)
    o_ = [0]

    def nxt(n, a):
        v = bfv(o_[0], n, a)
        o_[0] += n
        return v
    ATbd, BTbd, KTbd, VTbd = nxt(1024, 16), nxt(1024, 16), nxt(1024, 16), nxt(1024, 16)
    def chbuf():
        t = k.sb([128, 4, 128], F32)
        return {"r": t[:], "w": t[:].bitcast(CHAIN_DT), "m": t[:].bitcast(CHAIN_DT)}
    CH = [dict(N=[chbuf(), chbuf()], L=[chbuf(), chbuf()], P=chbuf()) for _ in range(2)]
    PF = nxt(1024, 16)
    MakT = nxt(1024, 16)
    MrbT, MrkT = nxt(512, 16), nxt(512, 16)
    Vbd, Vstk = nxt(1024, 16), nxt(512, 16)
    Bbd, Kbd = nxt(1024, 16), nxt(1024, 16)
    Zs, Us, Ubd = nxt(512, 16), nxt(512, 16), nxt(1024, 16)
    ZERO_LIST = [ATbd, BTbd, KTbd, VTbd, CH[0]['N'][0]['w'], CH[0]['L'][0]['w'], CH[1]['N'][0]['w'], CH[1]['L'][0]['w'], MakT, Ubd]
    YT = f32v(12800, 4096, 16)
    OIN = bfv(4096, 2048, 16)
    Y2 = f32v(8448, 4096, 16)

    xT3 = xT.rearrange("(k p) t -> p k t", p=128)
    oT3 = oT.rearrange("(k p) t -> p k t", p=128)
    if dbg:
        dbf = k.dram("dbg_bf", [7, 128, 16 * TT], BF16, "ExternalOutput")
        dyt = k.dram("dbg_yt", [128, 16 * TT], F32, "ExternalOutput")
        dgc = k.dram("dbg_gc", [128, 16 * NCH], F32, "ExternalOutput")
        doin = k.dram("dbg_oin", [128, 16 * TT], BF16, "ExternalOutput")

    def tmp(name, n=1, dt=F32, w=TT):
        return k.rotbuf(name, [128, w], dt, n)

    for tt in range(1 if dbg else NTT):
        t0 = tt * TT
        Xb, Hsb, XXb, XSb, XRb, XKb, XVb = (Buf() for _ in range(7))
        P.dma("sp", X, xT3[:, :, t0:t0 + TT], writes=[Xb])
        rstd, rb = rms_rstd(k, c, X, Xb, TT)
        norm_mod(k, X, Xb, rstd, rb, A, Sh, mb, Hs, Hsb, TT, col0=1)
        P.op("pool", lambda e: e.tensor_copy(out=Hs[:, :, 0:1], in_=HL[:]), reads=[HLb], writes=[Hsb])
        P.op("dve", lambda e: e.tensor_tensor(out=XX, in0=Hs[:, :, 0:TT], in1=Hs[:, :, 1:TT + 1], op=ALU.subtract), reads=[Hsb], writes=[XXb])
        P.op("pool", lambda e: e.tensor_copy(out=HL[:], in_=Hs[:, :, TT:TT + 1]), reads=[Hsb], writes=[HLb])

        def make_xs(j, dst, dstb):
            for dt in range(16):
                P.op("dve", lambda e, o=dst[:, dt, :], i=XX[:, dt, :], s=vt[:, j, dt:dt + 1], h=Hs[:, dt, 1:TT + 1]:
                     e.scalar_tensor_tensor(out=o, in0=i, scalar=s, in1=h, op0=ALU.mult, op1=ALU.add),
                     reads=[XXb, Hsb, mb], writes=[dstb])
        for (j, W, ncol) in ((3, w1, 96), (4, a1, 96), (5, g1, 256)):
            make_xs(j, XS, XSb)
            for cbk in range((ncol + 127) // 128):
                nc_ = min(128, ncol - cbk * 128)
                wb, wbb = wl.load(W, 0, 16, cbk * 128, nc_)
                ps, psb = k.ps()
                for dt in range(16):
                    P.op("pe", lambda e, o=ps[:nc_, :TT], w=wb[:, dt, :nc_], r=XS[:, dt, :], s=(dt == 0), t=(dt == 15):
                         e.matmul(o, lhsT=w, rhs=r, start=s, stop=t), reads=[wbb, XSb], writes=[psb])
                if j == 3:
                    P.op("act", lambda e, i=ps[:96, :TT]: e.activation(out=TW[:96, :], in_=i, func=AF.Tanh), reads=[psb], writes=[TWb])
                elif j == 4:
                    P.op("act", lambda e, i=ps[:96, :TT]: e.activation(out=TA[:96, :], in_=i, func=AF.Copy), reads=[psb], writes=[TAb])
                else:
                    P.op("act", lambda e, i=ps[:, :TT], o=TG[:, cbk, :]: e.activation(out=o, in_=i, func=AF.Sigmoid), reads=[psb], writes=[TGb])
        make_xs(0, XR, XRb)
        make_xs(1, XK, XKb)
        make_xs(2, XV, XVb)
        for p in range(16):
            e0 = p * 128
            pA, pAb = k.ps()
            pB, pBb = k.ps()
            pC, pCb = k.ps()
            pD, pDb = k.ps()
            for (jj, src, srcb, ps, psb, co) in ((0, XR, XRb, pA, pAb, 0), (1, XK, XKb, pA, pAb, TT), (2, XV, XVb, pB, pBb, 0)):
                wb, wbb = wl.load(wrkv, jj * D, 16, e0, 128)
                for dt in range(16):
                    P.op("pe", lambda e, o=ps[:, co:co + TT], w=wb[:, dt, :], r=src[:, dt, :], s=(dt == 0), t=(dt == 15):
                         e.matmul(o, lhsT=w, rhs=r, start=s, stop=t), reads=[wbb, srcb], writes=[psb])
            wb, wbb = wls.load(w2, 0, 1, e0, 128, pp=96)
            P.op("pe", lambda e, o=pB[:, TT:2 * TT], w=wb[:96, 0, :]: e.matmul(o, lhsT=w, rhs=TW[:96, :], start=True, stop=True),
                 reads=[wbb, TWb], writes=[pBb])
            wb, wbb = wls.load(a2, 0, 1, e0, 128, pp=96)
            P.op("pe", lambda e, o=pC[:, 0:TT], w=wb[:96, 0, :]: e.matmul(o, lhsT=w, rhs=TA[:96, :], start=True, stop=True),
                 reads=[wbb, TAb], writes=[pCb])
            wb, wbb = wls.load(g2, 0, 2, e0, 128)
            for kt in range(2):
                P.op("pe", lambda e, o=pC[:, TT:2 * TT], w=wb[:, kt, :], r=TG[:, kt, :], s=(kt == 0), t=(kt == 1):
                     e.matmul(o, lhsT=w, rhs=r, start=s, stop=t), reads=[wbb, TGb], writes=[pCb])
            r_ps, k_ps, v_ps, w_ps, a_ps, g_ps = pA[:, 0:TT], pA[:, TT:2 * TT], pB[:, 0:TT], pB[:, TT:2 * TT], pC[:, 0:TT], pC[:, TT:2 * TT]
            LWS = -0.6065306597126334
            sg, sgb = tmp("sg", 2)
            cl, clb = tmp("cl", 2)
            av, avb = tmp("av", 2)
            vf, vfb = tmp("vf", 2)
            kk, kkb = tmp("kk", 2)
            rn, rnb = tmp("rn", 2)
            kf_, kfb = tmp("kf", 2)
            eg, egb = tmp("eg")
            eig, eigb = tmp("eig")
            eex, eexb = tmp("eex")
            k2, k2b = tmp("ksq", 1, BF16)
            bb, bbb = tmp("bb")
            rk_, rkb = tmp("rkp", 1, BF16)
            lw, lwb = sg, sgb
            P.op("act", lambda e, o=sg[:], i=w_ps, b=NEG[:, 0, p:p + 1]: e.activation(out=o, in_=i, func=AF.Exp, bias=b, scale=-1.0),
                 reads=[pBb, mb], writes=[sgb])
            P.op("dve", lambda e, o=kk[:], i=k_ps, s=KKv[:, p:p + 1]: e.tensor_scalar(out=o, in0=i, scalar1=s, scalar2=None, op0=ALU.mult),
                 reads=[pAb, mb], writes=[kkb])
            P.op("pool", lambda e, o=k2[:], i=kk[:]: e.tensor_tensor(out=o, in0=i, in1=i, op=ALU.mult), reads=[kkb], writes=[k2b])
            P.op("pe", lambda e, o=pD[:, 0:TT], r=k2[:]: e.matmul(o, lhsT=bo16[:], rhs=r, start=True, stop=True), reads=[k2b, cb_], writes=[pDb])
            P.op("act", lambda e, o=av[:], i=a_ps, b=NEG[:, 1, p:p + 1]: e.activation(out=o, in_=i, func=AF.Exp, bias=b, scale=-1.0),
                 reads=[pCb, mb], writes=[avb])
            P.op("act", lambda e, o=GT[:, p, :], i=g_ps: e.activation(out=o, in_=i, func=AF.Copy), reads=[pCb], writes=[GTb])
            P.op("act", lambda e, o=vf[:], i=v_ps: e.activation(out=o, in_=i, func=AF.Copy), reads=[pBb], writes=[vfb])
            P.op("pool", lambda e, o=VT[:, p, :], i=vf[:]: e.tensor_copy(out=o, in_=i), reads=[vfb], writes=[VTb])
            P.op("dve", lambda e, o=sg[:]: e.tensor_scalar(out=o, in0=o, scalar1=1.0, scalar2=None, op0=ALU.add), reads=[sgb], writes=[sgb])
            P.op("dve", lambda e, o=sg[:]: e.reciprocal(out=o, in_=o), reads=[sgb], writes=[sgb])
            for ch in range(NCH):
                P.op("dve", lambda e, o=cl[:, ch * C:(ch + 1) * C], i=lw[:, ch * C:(ch + 1) * C]:
                     e.tensor_tensor_scan(out=o, data0=ones64[:], data1=i, initial=0.0, op0=ALU.mult, op1=ALU.add),
                     reads=[lwb, cb_], writes=[clb])
            P.op("dve", lambda e, o=rn[:], i=pD[:, 0:TT]: e.tensor_scalar(out=o, in0=i, scalar1=5.5e-20, scalar2=None, op0=ALU.max), reads=[pDb], writes=[rnb])
            P.op("dve", lambda e, o=eex[:], i=cl[:], j_=lw[:]: e.tensor_tensor(out=o, in0=i, in1=j_, op=ALU.subtract), reads=[clb, lwb], writes=[eexb])
            P.op("act", lambda e, o=rn[:]: e.activation(out=o, in_=o, func=AF.Ln), reads=[rnb], writes=[rnb])
            P.op("act", lambda e, o=rn[:]: e.activation(out=o, in_=o, func=AF.Exp, scale=-0.5), reads=[rnb], writes=[rnb])
            P.op("act", lambda e, o=eg[:], i=cl[:]: e.activation(out=o, in_=i, func=AF.Exp, scale=LWS), reads=[clb], writes=[egb])
            P.op("act", lambda e, o=eig[:], i=cl[:]: e.activation(out=o, in_=i, func=AF.Exp, scale=-LWS), reads=[clb], writes=[eigb])
            P.op("act", lambda e, o=eex[:]: e.activation(out=o, in_=o, func=AF.Exp, scale=LWS), reads=[eexb], writes=[eexb])
            P.op("pool", lambda e, o=GC[:, p, :], i=eg[:].rearrange("p (a b) -> p a b", a=NCH)[:, :, C - 1]: e.tensor_copy(out=o, in_=i),
                 reads=[egb], writes=[GCb])
            P.op("dve", lambda e, o=av[:]: e.tensor_scalar(out=o, in0=o, scalar1=1.0, scalar2=None, op0=ALU.add), reads=[avb], writes=[avb])
            P.op("dve", lambda e, o=av[:]: e.reciprocal(out=o, in_=o), reads=[avb], writes=[avb])
            P.op("dve", lambda e, o=kf_[:], i=av[:], s=KA[:, p:p + 1]: e.tensor_scalar(out=o, in0=i, scalar1=-1.0, scalar2=s, op0=ALU.add, op1=ALU.mult),
                 reads=[avb, mb], writes=[kfb])
            P.op("dve", lambda e, o=kf_[:], i=k_ps: e.scalar_tensor_tensor(out=o, in0=o, scalar=1.0, in1=i, op0=ALU.add, op1=ALU.mult),
                 reads=[kfb, pAb], writes=[kfb])
            P.op("dve", lambda e, o=kk[:], r=rn[:]: e.tensor_tensor(out=o, in0=o, in1=r, op=ALU.mult), reads=[kkb, rnb], writes=[kkb])
            P.op("dve", lambda e, o=RT[:, p, :], i=r_ps, g_=eg[:]: e.tensor_tensor(out=o, in0=i, in1=g_, op=ALU.mult), reads=[pAb, egb], writes=[RTb])
            P.op("pool", lambda e, o=KT[:, p, :], i=kf_[:], g_=eig[:]: e.tensor_tensor(out=o, in0=i, in1=g_, op=ALU.mult), reads=[kfb, eigb], writes=[KTb])
            P.op("pool", lambda e, o=bb[:], i=kk[:], a_=av[:]: e.tensor_tensor(out=o, in0=i, in1=a_, op=ALU.mult), reads=[kkb, avb], writes=[bbb])
            P.op("pool", lambda e, o=BT[:, p, :], i=bb[:], g_=eig[:]: e.tensor_tensor(out=o, in0=i, in1=g_, op=ALU.mult), reads=[bbb, eigb], writes=[BTb])
            P.op("dve", lambda e, o=AT[:, p, :], i=kk[:], g_=eex[:]: e.scalar_tensor_tensor(out=o, in0=i, scalar=-1.0, in1=g_, op0=ALU.mult, op1=ALU.mult),
                 reads=[kkb, eexb], writes=[ATb])
            P.op("dve", lambda e, o=rk_[:], i=r_ps, s=RK[:, p:p + 1], k_=kf_[:]: e.scalar_tensor_tensor(out=o, in0=i, scalar=s, in1=k_, op0=ALU.mult, op1=ALU.mult),
                 reads=[pAb, kfb, mb], writes=[rkb])
            P.op("pe", lambda e, o=pD[:, TT:2 * TT], r=rk_[:]: e.matmul(o, lhsT=bo16[:], rhs=r, start=True, stop=True), reads=[rkb, cb_], writes=[pDb])
            P.op("dve", lambda e, o=BON[:, p, :], i=pD[:, TT:2 * TT], v_=vf[:]: e.tensor_tensor(out=o, in0=i, in1=v_, op=ALU.mult),
                 reads=[pDb, vfb], writes=[BONb])

        P.barrier()
        if dbg:
            for i_, (t_, b_) in enumerate(((RT, RTb), (KT, KTb), (BT, BTb), (AT, ATb), (VT, VTb), (GT, GTb), (BON, BONb))):
                P.dma("sp", dbf[i_], t_[:].rearrange("p a b -> p (a b)"), reads=[b_])
            P.dma("sp", dgc, GC[:].rearrange("p a b -> p (a b)"), reads=[GCb])
            P.barrier()
        zb = Buf()
        SB = [dict(N=Buf(), L=Buf(), P=Buf()) for _ in range(2)]
        for z_ in ZERO_LIST:
            if z_.dtype == BF16:
                P.op("pool", lambda e, z_=z_: e.memset(z_, 0.0), writes=[zb])
            else:
                P.op("dve", lambda e, z_=z_: e.tensor_scalar(out=z_, in0=idn[:], scalar1=0.0, scalar2=None, op0=ALU.mult),
                     reads=[cb_], writes=[zb])
        for ch in range(NCH):
            cc = slice(ch * C, (ch + 1) * C)
            inb = [Buf() for _ in range(4)]
            for (src, srcb, dst) in ((AT, ATb, ATbd), (BT, BTb, BTbd), (KT, KTb, KTbd), (VT, VTb, VTbd)):
                for hf in range(2):
                    sl = slice(64 * hf, 64 * hf + 64)
                    P.op("dve" if hf == 0 else "act",
                         (lambda e, o=dst[sl, :, 64 * hf:64 * hf + 64], i=src[sl, :, cc]: e.tensor_copy(out=o, in_=i)) if hf == 0 else
                         (lambda e, o=dst[sl, :, 64 * hf:64 * hf + 64], i=src[sl, :, cc]: e.activation(out=o, in_=i, func=AF.Copy)),
                         reads=[srcb, zb], writes=inb)
            gb = [dict((n, Buf()) for n in ("N", "L", "Mak", "Mrb", "Mrk", "V", "B", "K", "P", "Z", "U")) for _ in range(4)]
            for g_ in range(4):
                pg = slice(g_ * 4, g_ * 4 + 4)
                b_ = gb[g_]
                for (src, dst, nm) in ((VTbd, Vbd, "V"), (BTbd, Bbd, "B"), (KTbd, Kbd, "K")):
                    ps, psb = k.ps()
                    psv = ps[:].bitcast(BF16)[:, 0:512].rearrange("p (a b) -> p a b", a=4)
                    for i in range(4):
                        P.op("pe", lambda e, o=psv[:, i, :], w=src[:, g_ * 4 + i, :]: e.transpose(out=o, in_=w, identity=idn[:, 0, :]),
                             reads=[inb[g_], cb_], writes=[psb])
                    P.op("act", lambda e, o=dst[:, pg, :], i=psv: e.activation(out=o, in_=i, func=AF.Copy), reads=[psb], writes=[b_[nm]])
                    if nm == "V":
                        for hf in range(2):
                            sl = slice(64 * hf, 64 * hf + 64)
                            P.op("dve", lambda e, o=Vstk[sl, pg, :], i=psv[sl, :, 64 * hf:64 * hf + 64]: e.tensor_copy(out=o, in_=i),
                                 reads=[psb], writes=[b_[nm]])
            def step1(g_):
                ps1, ps1b = k.ps()
                ps2, ps2b = k.ps()
                ps3, ps3b = k.ps()
                b_ = gb[g_]
                cs_ = CH[g_ % 2]
                sb_ = SB[g_ % 2]
                for i in range(4):
                    p = g_ * 4 + i
                    cs = slice(i * 64, i * 64 + 64)
                    cs2 = slice(256 + i * 64, 256 + i * 64 + 64)
                    for (ps, psb, csl, lh, rh, rhb) in ((ps1, ps1b, cs, BTbd, AT, ATb), (ps1, ps1b, cs2, ATbd, BT, BTb),
                                                        (ps2, ps2b, cs, KTbd, AT, ATb), (ps2, ps2b, cs2, BTbd, RT, RTb),
                                                        (ps3, ps3b, cs, KTbd, RT, RTb)):
                        P.op("pe", lambda e, o=ps[:, csl], w=lh[:, p, :], r=rh[:, p, cc]: e.matmul(o, lhsT=w, rhs=r, start=True, stop=True),
                             reads=[inb[g_], rhb], writes=[psb])
                pg = slice(g_ * 4, g_ * 4 + 4)
                v1 = ps1[:, 0:256].rearrange("p (a b) -> p a b", a=4)
                v1b = ps1[:, 256:512].rearrange("p (a b) -> p a b", a=4)
                v2 = ps2[:, 0:256].rearrange("p (a b) -> p a b", a=4)
                v2b = ps2[:, 256:512].rearrange("p (a b) -> p a b", a=4)
                v3 = ps3[:, 0:256].rearrange("p (a b) -> p a b", a=4)
                for hf in range(2):
                    sl = slice(64 * hf, 64 * hf + 64)
                    fs = slice(64 * hf, 64 * hf + 64)
                    P.op("dve", lambda e, o=cs_["N"][0]["w"][sl, :, fs], i=v1[sl], m=mS[sl]: e.tensor_tensor(out=o, in0=i, in1=m, op=ALU.mult),
                         reads=[ps1b, cb_, zb], writes=[sb_["N"]])
                    P.op("dve", lambda e, o=cs_["L"][0]["w"][sl, :, fs], i=v1b[sl], m=mL[sl]: e.tensor_tensor(out=o, in0=i, in1=m, op=ALU.mult),
                         reads=[ps1b, cb_, zb], writes=[sb_["L"]])
                    P.op("dve", lambda e, o=MakT[sl, pg, fs], i=v2[sl], m=mS[sl]: e.tensor_tensor(out=o, in0=i, in1=m, op=ALU.mult),
                         reads=[ps2b, cb_, zb], writes=[b_["Mak"]])
                P.op("dve", lambda e, o=MrbT[:, pg, :], i=v2b, m=mI[:]: e.tensor_tensor(out=o, in0=i, in1=m, op=ALU.mult),
                     reads=[ps2b, cb_], writes=[b_["Mrb"]])
                P.op("dve", lambda e, o=MrkT[:, pg, :], i=v3, m=mI[:]: e.tensor_tensor(out=o, in0=i, in1=m, op=ALU.mult),
                     reads=[ps3b, cb_], writes=[b_["Mrk"]])
                P.op("dve", lambda e, o=cs_["P"]["w"], i=cs_["N"][0]["r"]: e.tensor_tensor(out=o, in0=i, in1=idn[:], op=ALU.add),
                     reads=[sb_["N"], cb_], writes=[sb_["P"]])

            def chain_sq(g_, lev):
                a_, n_ = (lev - 1) % 2, lev % 2
                cs_ = CH[g_ % 2]
                sb_ = SB[g_ % 2]
                Ns, Ls = cs_["N"], cs_["L"]
                if lev < 5:
                    psn, psnb = k.ps()
                    for i in range(4):
                        P.op("pe", lambda e, o=psn[:, i * 128:(i + 1) * 128], w=Ls[a_]["m"][:, i, :], r=Ns[a_]["m"][:, i, :]:
                             e.matmul(o, lhsT=w, rhs=r, start=True, stop=True), reads=[sb_["L"], sb_["N"]], writes=[psnb])
                psl_, pslb = k.ps()
                for i in range(4):
                    P.op("pe", lambda e, o=psl_[:, i * 128:(i + 1) * 128], w=Ns[a_]["m"][:, i, :], r=Ls[a_]["m"][:, i, :]:
                         e.matmul(o, lhsT=w, rhs=r, start=True, stop=True), reads=[sb_["L"], sb_["N"]], writes=[pslb])
                P.op("act", lambda e, o=Ls[n_]["w"], i=psl_[:].rearrange("p (a b) -> p a b", a=4): e.activation(out=o, in_=i, func=AF.Copy),
                     reads=[pslb], writes=[sb_["L"]])
                if lev < 5:
                    P.op("act", lambda e, o=Ns[n_]["w"], i=psn[:].rearrange("p (a b) -> p a b", a=4): e.activation(out=o, in_=i, func=AF.Copy),
                         reads=[psnb], writes=[sb_["N"]])

            def chain_p(g_, lev):
                n_ = lev % 2
                pg = slice(g_ * 4, g_ * 4 + 4)
                cs_ = CH[g_ % 2]
                sb_ = SB[g_ % 2]
                Ls, Pc = cs_["L"], cs_["P"]
                psp, pspb = k.ps()
                for i in range(4):
                    P.op("pe", lambda e, o=psp[:, i * 128:(i + 1) * 128], w=Ls[n_]["m"][:, i, :], r=Pc["m"][:, i, :]:
                         e.matmul(o, lhsT=w, rhs=r, start=True, stop=True), reads=[sb_["L"], sb_["P"]], writes=[pspb])
                if lev < 5:
                    P.op("dve", lambda e, o=Pc["w"], q=Pc["r"], i=psp[:].rearrange("p (a b) -> p a b", a=4): e.tensor_tensor(out=o, in0=i, in1=q, op=ALU.add),
                         reads=[pspb, sb_["P"]], writes=[sb_["P"]])
                else:
                    P.op("dve", lambda e, o=PF[:, pg, :], i=psp[:].rearrange("p (a b) -> p a b", a=4), q=Pc["r"]: e.tensor_tensor(out=o, in0=i, in1=q, op=ALU.add),
                         reads=[pspb, sb_["P"]], writes=[gb[g_]["P"]])

            for gp in ((0, 1), (2, 3)):
                for g_ in gp:
                    step1(g_)
                for lev in range(1, 6):
                    for g_ in gp:
                        chain_sq(g_, lev)
                    for g_ in gp:
                        chain_p(g_, lev)
            zps, ups = {}, {}
            for g_ in range(4):
                pg = slice(g_ * 4, g_ * 4 + 4)
                b_ = gb[g_]
                hb_ = Hb_[g_]
                psz, pszb = k.ps()
                for i in range(4):
                    p = g_ * 4 + i
                    P.op("pe", lambda e, o=psz[:, i * 64:(i + 1) * 64], w=ATbd[:, p, :], r=Hstk[:, p, :]: e.matmul(o, lhsT=w, rhs=r, start=True, stop=False),
                         reads=[inb[g_], hb_], writes=[pszb])
                    P.op("pe", lambda e, o=psz[:, i * 64:(i + 1) * 64], w=MakT[:, p, :], r=Vstk[:, p, :]: e.matmul(o, lhsT=w, rhs=r, start=False, stop=True),
                         reads=[b_["Mak"], b_["V"]], writes=[pszb])
                P.op("act", lambda e, o=Zs[:, pg, :], i=psz[:, 0:256].rearrange("p (a b) -> p a b", a=4): e.activation(out=o, in_=i, func=AF.Copy),
                     reads=[pszb], writes=[b_["Z"]])
            for g_ in range(4):
                pg = slice(g_ * 4, g_ * 4 + 4)
                b_ = gb[g_]
                psu, psub = k.ps()
                for i in range(4):
                    p = g_ * 4 + i
                    P.op("pe", lambda e, o=psu[:, i * 64:(i + 1) * 64], w=PF[:, p, :], r=Zs[:, p, :]: e.matmul(o, lhsT=w, rhs=r, start=True, stop=True),
                         reads=[b_["P"], b_["Z"]], writes=[psub])
                psuv = psu[:, 0:256].rearrange("p (a b) -> p a b", a=4)
                P.op("act", lambda e, o=Us[:, pg, :], i=psuv: e.activation(out=o, in_=i, func=AF.Copy), reads=[psub], writes=[b_["U"]])
                for hf in range(2):
                    sl = slice(64 * hf, 64 * hf + 64)
                    P.op("dve", lambda e, o=Ubd[sl, pg, 64 * hf:64 * hf + 64], i=psuv[sl]: e.tensor_copy(out=o, in_=i), reads=[psub, zb], writes=[b_["U"]])
            for g_ in range(4):
                pg = slice(g_ * 4, g_ * 4 + 4)
                b_ = gb[g_]
                hb_ = Hb_[g_]
                psy, psyb = k.ps()
                psh, pshb = k.ps()
                for i in range(4):
                    p = g_ * 4 + i
                    oy = psy[:, i * 64:(i + 1) * 64]
                    P.op("pe", lambda e, o=oy, w=Hbd[:, p, :], r=RT[:, p, cc]: e.matmul(o, lhsT=w, rhs=r, start=True, stop=False),
                         reads=[hb_, RTb], writes=[psyb])
                    P.op("pe", lambda e, o=oy, w=Ubd[:, p, :], r=MrbT[:, p, :]: e.matmul(o, lhsT=w, rhs=r, start=False, stop=False),
                         reads=[b_["U"], b_["Mrb"]], writes=[psyb])
                    P.op("pe", lambda e, o=oy, w=Vbd[:, p, :], r=MrkT[:, p, :]: e.matmul(o, lhsT=w, rhs=r, start=False, stop=True),
                         reads=[b_["V"], b_["Mrk"]], writes=[psyb])
                    oh = psh[:, i * 64:(i + 1) * 64]
                    P.op("pe", lambda e, o=oh, w=Bbd[:, p, :], r=Us[:, p, :]: e.matmul(o, lhsT=w, rhs=r, start=True, stop=False),
                         reads=[b_["B"], b_["U"]], writes=[pshb])
                    P.op("pe", lambda e, o=oh, w=Kbd[:, p, :], r=Vstk[:, p, :]: e.matmul(o, lhsT=w, rhs=r, start=False, stop=True),
                         reads=[b_["K"], b_["V"]], writes=[pshb])
                P.op("act", lambda e, o=YT[:, pg, cc], i=psy[:, 0:256].rearrange("p (a b) -> p a b", a=4): e.activation(out=o, in_=i, func=AF.Copy),
                     reads=[psyb], writes=[YTb])
                P.op("dve", lambda e, o=Hf[:, pg, :], i=psh[:, 0:256].rearrange("p (a b) -> p a b", a=4): e.tensor_tensor(out=o, in0=o, in1=i, op=ALU.add),
                     reads=[pshb, hb_], writes=[hb_])
                for i in range(4):
                    p = g_ * 4 + i
                    P.op("dve", lambda e, o=Hf[:, p, :], s=GC[:, p, ch:ch + 1]: e.tensor_scalar(out=o, in0=o, scalar1=s, scalar2=None, op0=ALU.mult),
                         reads=[hb_, GCb], writes=[hb_])
                P.op("pool", lambda e, o=Hstk[:, pg, :], i=Hf[:, pg, :]: e.tensor_copy(out=o, in_=i), reads=[hb_], writes=[hb_])
                for hf in range(2):
                    sl = slice(64 * hf, 64 * hf + 64)
                    P.op("pool", lambda e, o=Hbd[sl, pg, 64 * hf:64 * hf + 64], i=Hf[sl, pg, :]: e.tensor_copy(out=o, in_=i), reads=[hb_], writes=[hb_])

        P.barrier()
        OINb, Y2b, Xb = Buf(), Buf(), Buf()
        P.dma("sp", X, xT3[:, :, t0:t0 + TT], writes=[Xb])
        for p in range(16):
            psm, psmb = k.ps()
            P.op("pe", lambda e, o=psm[:, 0:TT], r=YT[:, p, :]: e.matmul(o, lhsT=bo32[:], rhs=r, start=True, stop=True), reads=[YTb, cb_], writes=[psmb])
            yc, ycb = tmp("cl")
            P.op("dve", lambda e, o=yc[:], m=psm[:, 0:TT], y=YT[:, p, :]: e.scalar_tensor_tensor(out=o, in0=m, scalar=-1.0 / 64, in1=y, op0=ALU.mult, op1=ALU.add),
                 reads=[psmb, YTb], writes=[ycb])
            ysq, ysqb = tmp("rn")
            P.op("pool", lambda e, o=ysq[:], i=yc[:]: e.tensor_tensor(out=o, in0=i, in1=i, op=ALU.mult), reads=[ycb], writes=[ysqb])
            P.op("pe", lambda e, o=psm[:, TT:2 * TT], r=ysq[:]: e.matmul(o, lhsT=bo32[:], rhs=r, start=True, stop=True), reads=[ysqb, cb_], writes=[psmb])
            rs, rsb = tmp("bb")
            P.op("act", lambda e, o=rs[:], i=psm[:, TT:2 * TT]: e.activation(out=o, in_=i, func=AF.Sqrt, bias=c["eps"][:, 1:2], scale=1.0 / 64),
                 reads=[psmb, c["onesb"]], writes=[rsb])
            P.op("dve", lambda e, o=rs[:]: e.reciprocal(out=o, in_=o), reads=[rsb], writes=[rsb])
            P.op("dve", lambda e, o=yc[:], r=rs[:]: e.tensor_tensor(out=o, in0=o, in1=r, op=ALU.mult), reads=[ycb, rsb], writes=[ycb])
            P.op("dve", lambda e, o=yc[:], s=LNW[:, p:p + 1], b=BON[:, p, :]: e.scalar_tensor_tensor(out=o, in0=o, scalar=s, in1=b, op0=ALU.mult, op1=ALU.add),
                 reads=[ycb, BONb, mb], writes=[ycb])
            P.op("dve", lambda e, o=OIN[:, p, :], i=yc[:], s=LNB[:, p:p + 1], g_=GT[:, p, :]: e.scalar_tensor_tensor(out=o, in0=i, scalar=s, in1=g_, op0=ALU.add, op1=ALU.mult),
                 reads=[ycb, GTb, mb], writes=[OINb])
        if dbg:
            P.dma("sp", dyt, YT.rearrange("p a b -> p (a b)"), reads=[YTb])
            P.dma("sp", doin, OIN.rearrange("p a b -> p (a b)"), reads=[OINb])
        for eb in range(16):
            wb, wbb = wl.load(wo, 0, 16, eb * 128, 128)
            ps, psb = k.ps()
            for p in range(16):
                P.op("pe", lambda e, o=ps[:, :TT], w=wb[:, p, :], r=OIN[:, p, :], s=(p == 0), t=(p == 15):
                     e.matmul(o, lhsT=w, rhs=r, start=s, stop=t), reads=[wbb, OINb], writes=[psb])
            P.op("act", lambda e, o=Y2[:, eb, :], i=ps[:, :TT]: e.activation(out=o, in_=i, func=AF.Copy), reads=[psb], writes=[Y2b])
        post_residual(k, c, X, Xb, Y2, Y2b, G, mb, TT)
        P.dma("sp", oT3[:, :, t0:t0 + TT], X, reads=[Xb])
        P.barrier()
    k.close()
    return nc


def _pd(v, n=16):
    return np.ascontiguousarray(np.asarray(v, dtype=np.float32).reshape(n, 128).T)


def _run(nc, maps):
    res = run_bass_kernel_spmd(nc, maps, core_ids=list(range(len(maps))))
    return res.results


class _Root:
    pass


def build_fused():
    root = _Root()
    root.nc = bass.Bass("TRN2", target_bir_lowering=False)
    root.semstack = ExitStack()
    nc = root.nc
    xT = nc.dram_tensor("xT", [D, T], F32, kind="ExternalInput").ap()
    oT = nc.dram_tensor("oT", [D, T], F32, kind="ExternalOutput").ap()
    modsI = nc.dram_tensor("mods_i", [128, 192], F32, kind="Internal").ap()
    xa = nc.dram_tensor("xa_i", [D, T], F32, kind="Internal").ap()
    xb = nc.dram_tensor("xb_i", [D, T], F32, kind="Internal").ap()
    xc = nc.dram_tensor("xc_i", [D, T], F32, kind="Internal").ap()
    def decl(prefix, items):
        dm, pre = {}, []
        for name, shape in items:
            w = nc.dram_tensor(prefix + name, list(shape), F32, kind="ExternalInput").ap()
            wb = nc.dram_tensor(prefix + name + "_bf", list(shape), BF16, kind="Internal").ap()
            dm[name + "_bf"] = wb
            pre.append((wb, w))
        return dm, pre
    rw_dm, rw_pre = decl("r_", [("w_rkv", [3 * D, D]), ("w1", [D, 96]), ("w2", [96, D]), ("a1", [D, 96]), ("a2", [96, D]),
                                ("g1", [D, 256]), ("g2", [256, D]), ("w_o", [D, D])])
    f0_dm, f0_pre = decl("f0_", [("w_up", [D, FF]), ("w_dn", [FF, D])])
    f1_dm, f1_pre = decl("f1_", [("w_up", [D, FF]), ("w_dn", [FF, D])])
    a_dm, a_pre = decl("a_", [("kv_down", [D, 576]), ("kv_uk", [512, D]), ("kv_uv", [512, D]), ("w_dq", [D, 512]),
                              ("w_uq", [512, 16 * 192]), ("w_o", [D, D])])
    build_mods(env=(root, {"mods": modsI}, "m_", rw_pre))
    build_rwkv(0, env=(root, dict(rw_dm, xT=xT, mods=modsI, oT=xa), "r_", f0_pre + a_pre))
    build_mlp(0, env=(root, dict(f0_dm, xT=xa, mods=modsI, oT=xb), "f0_", f1_pre))
    build_mla(1, env=(root, dict(a_dm, xT=xb, mods=modsI, oT=xc), "a_"))
    build_mlp(1, env=(root, dict(f1_dm, xT=xc, mods=modsI, oT=oT), "f1_"))
    root.semstack.close()
    return nc


def kernel(x, c, positions, ada_w, ada_b, norm_g, mlp_up, mlp_down,
           rw_mu, rw_rkv, rw_w0, rw_w1, rw_w2, rw_a0, rw_a1, rw_a2, rw_g1, rw_g2,
           rw_kk, rw_ka, rw_rk, rw_lnx, rw_o,
           mla_dq, mla_qnorm, mla_uq, mla_o,
           kv_in_g, kv_down, kv_norm, kv_uk, kv_uv):
    f32 = np.float32
    A = lambda a: np.ascontiguousarray(np.asarray(a))
    x = A(x).astype(f32, copy=False)
    B = x.shape[0]
    cc = A(c)
    pos = A(positions).astype(np.int32, copy=False)
    vl = [A(rw_mu)[0][j] for j in range(6)] + [A(rw_w0)[0], A(rw_a0)[0], A(rw_kk)[0], A(rw_ka)[0], A(rw_rk)[0].reshape(-1),
                                              A(rw_lnx)[0][0], A(rw_lnx)[0][1]]
    inv = (1.0 / (10000.0 ** (np.arange(0, 64, 2, dtype=np.float32) / 64))).astype(f32)
    shared = {
        "m_ada_w": A(ada_w),
        "m_ada_b_pd": np.ascontiguousarray(A(ada_b).reshape(2, 96, 128).transpose(2, 0, 1)),
        "m_norm_g_pd": np.ascontiguousarray(A(norm_g).reshape(2, 4, 16, 128).transpose(3, 0, 1, 2)),
        "r_rw_vec": np.ascontiguousarray(np.stack([_pd(v) for v in vl], 1)).astype(f32),
        "r_w_rkv": A(rw_rkv)[0].reshape(3 * D, D), "r_w1": A(rw_w1)[0], "r_w2": A(rw_w2)[0], "r_a1": A(rw_a1)[0],
        "r_a2": A(rw_a2)[0], "r_g1": A(rw_g1)[0], "r_g2": A(rw_g2)[0], "r_w_o": A(rw_o)[0],
        "f0_w_up": A(mlp_up)[0], "f0_w_dn": A(mlp_down)[0], "f1_w_up": A(mlp_up)[1], "f1_w_dn": A(mlp_down)[1],
        "a_ropec": np.ascontiguousarray(np.stack([np.concatenate([inv, inv]), np.concatenate([-np.ones(32), np.ones(32)])], 1)).astype(f32),
        "a_mla_vec": np.ascontiguousarray(np.concatenate([_pd(kv_in_g), _pd(kv_norm, 4), _pd(A(mla_qnorm)[0], 4)], 1)).astype(f32),
        "a_kv_down": A(kv_down), "a_kv_uk": A(kv_uk).reshape(512, D), "a_kv_uv": A(kv_uv).reshape(512, D),
        "a_w_dq": A(mla_dq)[0], "a_w_uq": A(mla_uq)[0].reshape(512, 16 * 192), "a_w_o": A(mla_o)[0],
    }
    maps = [dict(shared, xT=np.ascontiguousarray(x[b].T), m_c_pd=_pd(cc[b]),
                 a_posr=np.ascontiguousarray(np.broadcast_to(pos[b][None, :], (64, T)))) for b in range(B)]
    r = _run(build_fused(), maps)
    return np.stack([np.ascontiguousarray(r[b]["oT"].T) for b in range(B)]).astype(f32, copy=False)
```

```python
import numpy as np
import concourse.bass as bass
import concourse.mybir as mybir
from concourse.bass_utils import run_bass_kernel_spmd
from contextlib import ExitStack

F32 = mybir.dt.float32
BF16 = mybir.dt.bfloat16
F32R = mybir.dt.float32r
MODS_DT = F32R
CHAIN_DT = F32
I32 = mybir.dt.int32
ALU = mybir.AluOpType
AF = mybir.ActivationFunctionType

D = 2048
T = 2048
ND = 16
FF = 8192
NCORES = 8
EPS = 1e-6

EPOCH = 30000
N_DMA_SEMS = 6
SAME_ENGINE_SYNC = False


class Buf:
    __slots__ = ("w", "r")

    def __init__(self):
        self.w = None
        self.r = {}


class Prog:
    ENGS = ("pe", "dve", "act", "pool", "sp")
    NSEM = 0

    def __init__(self, nc, semstack=None):
        self.nc = nc
        self.semstack = semstack
        self.q = {e: [] for e in self.ENGS}
        self.cnt = {e: 0 for e in self.ENGS}
        self.seen = {e: {} for e in self.ENGS}
        self.dma_cnt = {}
        self.dma_rr = {e: 0 for e in self.ENGS}
        self.keys = set()

    def _wait(self, eng, key, val):
        if self.seen[eng].get(key, 0) >= val:
            return
        self.seen[eng][key] = val
        self.q[eng].append(("wait", key, val))

    def _deps(self, eng, reads, writes):
        deps = {}
        for b in reads:
            if b.w is not None:
                k, v = b.w
                if deps.get(k, 0) < v:
                    deps[k] = v
        for b in writes:
            if b.w is not None:
                k, v = b.w
                if deps.get(k, 0) < v:
                    deps[k] = v
            for k, v in b.r.items():
                if deps.get(k, 0) < v:
                    deps[k] = v
        for k, v in deps.items():
            if k[0] == "E" and k[1] == eng and (eng == "pe" or not SAME_ENGINE_SYNC):
                continue
            self._wait(eng, k, v)

    def _mark(self, tok, reads, writes):
        k, v = tok
        for b in reads:
            if b.r.get(k, 0) < v:
                b.r[k] = v
        for b in writes:
            b.w = tok
            b.r = {}

    def op(self, eng, fn, reads=(), writes=()):
        self._deps(eng, reads, writes)
        n = self.cnt[eng]
        self.cnt[eng] = n + 1
        key = ("E", eng, n // EPOCH)
        self.keys.add(key)
        self.q[eng].append(("op", fn, key))
        self._mark((key, n % EPOCH + 1), reads, writes)

    def dma(self, qeng, out, in_, reads=(), writes=()):
        self._deps(qeng, reads, writes)
        s = self.dma_rr[qeng]
        self.dma_rr[qeng] = (s + 1) % N_DMA_SEMS
        gen = 0
        while self.dma_cnt.get(("D", qeng, s, gen), 0) + 16 > EPOCH:
            gen += 1
        key = ("D", qeng, s, gen)
        prev = self.dma_cnt.get(key, 0)
        if prev > 0:
            self._wait(qeng, key, prev)
        elif gen > 0:
            pk = ("D", qeng, s, gen - 1)
            self._wait(qeng, pk, self.dma_cnt[pk])
        self.dma_cnt[key] = prev + 16
        self.keys.add(key)
        self.q[qeng].append(("dma", (out, in_), key))
        self._mark((key, prev + 16), reads, writes)

    def barrier(self):
        toks = []
        for e in self.ENGS:
            n = self.cnt[e]
            if n > 0:
                toks.append((("E", e, (n - 1) // EPOCH), (n - 1) % EPOCH + 1))
        toks += list(self.dma_cnt.items())
        for e in self.ENGS:
            for key, v in toks:
                if key[0] == "E" and key[1] == e:
                    continue
                self._wait(e, key, v)

    def finish(self, eng="sp"):
        for key, v in list(self.dma_cnt.items()):
            self._wait(eng, key, v)

    def emit(self):
        nc = self.nc
        engmap = {"pe": "tensor", "dve": "vector", "act": "scalar", "pool": "gpsimd", "sp": "sync"}
        with ExitStack() as st:
            sems = {}
            semst = self.semstack if self.semstack is not None else st
            for i, key in enumerate(sorted(self.keys, key=str)):
                Prog.NSEM += 1
                sems[key] = semst.enter_context(nc.semaphore("s%d" % Prog.NSEM))
            block = st.enter_context(nc.Block())
            for e in self.ENGS:
                items = self.q[e]
                if not items:
                    continue

                def body(eng, items=items):
                    for it in items:
                        if it[0] == "wait":
                            eng.wait_ge(sems[it[1]], it[2])
                        elif it[0] == "op":
                            it[1](eng).then_inc(sems[it[2]], 1)
                        else:
                            eng.dma_start(out=it[1][0], in_=it[1][1]).then_inc(sems[it[2]], 16)

                getattr(block, engmap[e])(body)


class Ctx:
    NT = 0

    def __init__(self, name, env=None):
        if env is None:
            self.nc = bass.Bass("TRN2", target_bir_lowering=False)
            self.semstack = None
            self.dmap = {}
            self.prefix = ""
            self.pre = []
        else:
            root, self.dmap, self.prefix = env[:3]
            self.pre = env[3] if len(env) > 3 else []
            self.nc = root.nc
            self.semstack = root.semstack
        self.P = Prog(self.nc, self.semstack)
        self.st = ExitStack()
        self.n = 0
        self.psl = []
        self.psi = 0
        self.rot = {}
        self.rotw = 512

    def dram(self, name, shape, dt, kind):
        if name in self.dmap:
            return self.dmap[name]
        return self.nc.dram_tensor(self.prefix + name, list(shape), dt, kind=kind).ap()

    def sb(self, shape, dt):
        Ctx.NT += 1
        return self.st.enter_context(self.nc.sbuf_tensor("t%d" % Ctx.NT, list(shape), dt))

    def init_psum(self, nf32=8):
        for i in range(nf32):
            Ctx.NT += 1
            t = self.st.enter_context(self.nc.psum_tensor("ps%d" % Ctx.NT, [128, 512], F32))
            self.psl.append((t, Buf()))

    def ps(self):
        r = self.psl[self.psi]
        self.psi = (self.psi + 1) % len(self.psl)
        return r

    def rotbuf(self, key, shape, dt, n=2):
        if key not in self.rot:
            self.rot[key] = [[(self.sb(shape, dt), Buf()) for _ in range(n)], 0]
        lst, i = self.rot[key]
        self.rot[key][1] = (i + 1) % len(lst)
        return lst[i]

    def do_pre(self):
        for (dst, src) in self.pre:
            cast_dma(self, dst, src)

    def close(self):
        self.P.finish("sp")
        self.P.emit()
        self.st.close()


class WT:
    def __init__(self, ap, buf):
        self.ap = ap
        self.buf = buf


def cast_dma(k, dst, src, buf=None, max_bytes=8 << 20):
    rows, cols = src.shape[0], src.shape[1]
    step = max(1, min(rows, max_bytes // (cols * 4)))
    for r0 in range(0, rows, step):
        r1 = min(rows, r0 + step)
        k.P.dma("pool", dst[r0:r1, :], src[r0:r1, :], writes=[buf] if buf is not None else [])


def wsrc(k, name, shape):
    if name + "_bf" in k.dmap:
        return WT(k.dmap[name + "_bf"], Buf())
    w = k.dram(name, shape, F32, "ExternalInput")
    wb = k.nc.dram_tensor(k.prefix + name + "_bf", list(shape), BF16, kind="Internal").ap()
    b = Buf()
    cast_dma(k, wb, w, b)
    return WT(wb, b)


class WLoader:
    def __init__(self, k, nk=16, ncols=256, nbf=3):
        self.k = k
        self.wbf = [(k.sb([128, nk, ncols], BF16), Buf()) for _ in range(nbf)]
        self.j = 0

    def load(self, W, r0, nk, c0, ncols, pp=128):
        P = self.k.P
        wb, wbb = self.wbf[self.j]
        self.j = (self.j + 1) % len(self.wbf)
        src = W.ap[r0:r0 + nk * pp, c0:c0 + ncols].rearrange("(k p) c -> p k c", p=pp)
        P.dma("sp", wb[:pp, :nk, :ncols], src, reads=[W.buf], writes=[wbb])
        return wb, wbb


def make_consts(k):
    P = k.P
    c = {}
    ones = k.sb([128, 128], BF16)
    c["ones"] = ones
    c["onesb"] = Buf()
    P.op("pool", lambda e: e.memset(ones[:], 1.0), writes=[c["onesb"]])
    eps = k.sb([128, 2], F32)
    c["eps"] = eps
    P.op("pool", lambda e: e.memset(eps[:, 0:1], EPS), writes=[c["onesb"]])
    P.op("pool", lambda e: e.memset(eps[:, 1:2], 64e-5), writes=[c["onesb"]])
    return c


def rms_rstd(k, c, X, Xb, TT, scale_div=D, epsap=None, ntile=ND):
    P = k.P
    if epsap is None:
        epsap = c["eps"][:, 0:1]
    ps, psb = k.ps()
    for dt in range(ntile):
        sq, sqb = k.rotbuf("sq", [128, k.rotw], BF16, 3)
        P.op("act", lambda e, o=sq[:, :TT], i=X[:, dt, :]: e.activation(out=o, in_=i, func=AF.Square),
             reads=[Xb], writes=[sqb])
        P.op("pe", lambda e, o=ps[:, :TT], r=sq[:, :TT], s=(dt == 0), t=(dt == ntile - 1):
             e.matmul(o, lhsT=c["ones"][:], rhs=r, start=s, stop=t), reads=[sqb, c["onesb"]], writes=[psb])
    rstd, rb = k.rotbuf("rstd", [128, k.rotw], F32, 2)
    P.op("act", lambda e, o=rstd[:, :TT], i=ps[:, :TT]: e.activation(
        out=o, in_=i, func=AF.Sqrt, bias=epsap, scale=1.0 / scale_div), reads=[psb, c["onesb"]], writes=[rb])
    P.op("dve", lambda e, o=rstd[:, :TT]: e.reciprocal(out=o, in_=o), reads=[rb], writes=[rb])
    return rstd, rb


def norm_mod(k, X, Xb, rstd, rb, A, Sh, mb, H, Hb, TT, col0=0):
    P = k.P
    for dt in range(ND):
        tmp, tb = k.rotbuf("nm_tmp", [128, k.rotw], F32, 2)
        P.op("dve", lambda e, o=tmp[:, :TT], i=X[:, dt, :], s=A[:, dt:dt + 1], r=rstd[:, :TT]:
             e.scalar_tensor_tensor(out=o, in0=i, scalar=s, in1=r, op0=ALU.mult, op1=ALU.mult),
             reads=[Xb, rb, mb], writes=[tb])
        P.op("act", lambda e, o=H[:, dt, col0:col0 + TT], i=tmp[:, :TT], s=Sh[:, dt:dt + 1]:
             e.activation(out=o, in_=i, func=AF.Identity, bias=s, scale=1.0),
             reads=[tb, mb], writes=[Hb])


def post_residual(k, c, X, Xb, Y, Yb, G, mb, TT):
    P = k.P
    rstd, rb = rms_rstd(k, c, Y, Yb, TT)
    for dt in range(ND):
        tmp, tb = k.rotbuf("nm_tmp", [128, k.rotw], F32, 2)
        P.op("dve", lambda e, o=tmp[:, :TT], i=Y[:, dt, :], s=G[:, dt:dt + 1], r=rstd[:, :TT]:
             e.scalar_tensor_tensor(out=o, in0=i, scalar=s, in1=r, op0=ALU.mult, op1=ALU.mult),
             reads=[Yb, rb, mb], writes=[tb])
        P.op("pool" if dt % 2 else "dve", lambda e, o=X[:, dt, :], i=tmp[:, :TT]: e.tensor_tensor(out=o, in0=o, in1=i, op=ALU.add),
             reads=[tb], writes=[Xb])


def build_mods(env=None):
    k = Ctx("mods", env)
    nc, P = k.nc, k.P
    k.do_pre()
    c_pd = k.dram("c_pd", [128, 16], F32, "ExternalInput")
    ada_w = k.dram("ada_w", [2, D, 6 * D], F32, "ExternalInput")
    ada_b = k.dram("ada_b_pd", [128, 2, 96], F32, "ExternalInput")
    ng = k.dram("norm_g_pd", [128, 2, 4, 16], F32, "ExternalInput")
    mods = k.dram("mods", [128, 2 * 96], F32, "ExternalOutput")
    k.init_psum(2)
    cin = k.sb([128, 16], F32)
    cact = k.sb([128, 16], F32)
    abt = k.sb([128, 2, 96], F32)
    ngt = k.sb([128, 2, 4, 16], F32)
    raw = k.sb([128, 2, 96], F32)
    outt = k.sb([128, 2, 6, 16], F32)
    cb_, sm_ = Buf(), Buf()
    P.dma("sp", cin[:], c_pd, writes=[cb_])
    P.dma("sp", abt[:], ada_b, writes=[sm_])
    P.dma("sp", ngt[:], ng, writes=[sm_])
    P.op("act", lambda e: e.activation(out=cact[:].bitcast(MODS_DT), in_=cin[:], func=AF.Silu), reads=[cb_], writes=[cb_])
    stg = [(k.sb([128, 16, 512], F32), Buf()) for _ in range(3)]
    rawb, ob = Buf(), Buf()
    one1 = k.sb([1, 1], F32)
    row = k.sb([1, 6 * D], F32)
    rowb = Buf()
    P.op("dve", lambda e: e.memset(one1[:], 1.0), writes=[cb_])
    k.psl = k.psl + [(k.st.enter_context(nc.psum_tensor("psx%d" % i, [128, 512], F32)), Buf()) for i in range(4)]
    for l in range(2):
        for cb in range(24):
            st, stb = stg[(l * 24 + cb) % 3]
            src = ada_w[l, :, cb * 512:(cb + 1) * 512].rearrange("(k p) c -> p k c", p=128)
            if MODS_DT == F32:
                P.dma("sp", st[:], src, writes=[stb])
            else:
                P.dma("pool", st[:].bitcast(MODS_DT), src, writes=[stb])
            psr, psrb = k.ps()
            for dt in range(16):
                P.op("pe", lambda e, o=psr[0:1, :], w=cact[:, dt:dt + 1].bitcast(MODS_DT), r=st[:, dt, :].bitcast(MODS_DT), s=(dt == 0), t=(dt == 15):
                     e.matmul(o, lhsT=w, rhs=r, start=s, stop=t), reads=[stb, cb_], writes=[psrb])
            P.op("act" if cb % 2 else "dve",
                 (lambda e, o=row[0:1, cb * 512:(cb + 1) * 512], i=psr[0:1, :]: e.activation(out=o, in_=i, func=AF.Copy)) if cb % 2 else
                 (lambda e, o=row[0:1, cb * 512:(cb + 1) * 512], i=psr[0:1, :]: e.tensor_copy(out=o, in_=i)),
                 reads=[psrb], writes=[rowb])
        ps, psb = k.ps()
        for e_ in range(96):
            P.op("pe", lambda e, o=ps[:, e_:e_ + 1], w=row[0:1, e_ * 128:(e_ + 1) * 128]: e.matmul(o, lhsT=w, rhs=one1[0:1, 0:1], start=True, stop=True),
                 reads=[rowb, cb_], writes=[psb])
        P.op("dve", lambda e, o=raw[:, l, :], i=ps[:, 0:96], b=abt[:, l, :]: e.tensor_tensor(out=o, in0=i, in1=b, op=ALU.add),
             reads=[psb, sm_], writes=[rawb])
        for half, (gpre, gpost) in enumerate(((0, 1), (2, 3))):
            b0 = half * 3
            P.op("dve", lambda e, o=outt[:, l, b0 + 0, :], i=raw[:, l, (b0 + 1) * 16:(b0 + 2) * 16], g=ngt[:, l, gpre, :]:
                 e.scalar_tensor_tensor(out=o, in0=i, scalar=1.0, in1=g, op0=ALU.add, op1=ALU.mult),
                 reads=[rawb, sm_], writes=[ob])
            P.op("dve", lambda e, o=outt[:, l, b0 + 1, :], i=raw[:, l, (b0 + 0) * 16:(b0 + 1) * 16]:
                 e.tensor_copy(out=o, in_=i), reads=[rawb], writes=[ob])
            P.op("dve", lambda e, o=outt[:, l, b0 + 2, :], i=raw[:, l, (b0 + 2) * 16:(b0 + 3) * 16], g=ngt[:, l, gpost, :]:
                 e.tensor_tensor(out=o, in0=i, in1=g, op=ALU.mult), reads=[rawb, sm_], writes=[ob])
    P.dma("sp", mods, outt[:].rearrange("p l j d -> p (l j d)"), reads=[ob])
    k.close()
    return nc


def build_mlp(l, env=None):
    k = Ctx("mlp", env)
    nc, P = k.nc, k.P
    TT = 512
    xT = k.dram("xT", [D, T], F32, "ExternalInput")
    modsd = k.dram("mods", [128, 192], F32, "ExternalInput")
    k.do_pre()
    wup = wsrc(k, "w_up", [D, FF])
    wdn = wsrc(k, "w_dn", [FF, D])
    oT = k.dram("oT", [D, T], F32, "ExternalOutput")
    k.init_psum(8)
    c = make_consts(k)
    mt = k.sb([128, 2, 6, 16], F32)
    mb = Buf()
    P.dma("sp", mt[:].rearrange("p l j d -> p (l j d)"), modsd, writes=[mb])
    A, Sh, G = mt[:, l, 3, :], mt[:, l, 4, :], mt[:, l, 5, :]
    Xs = [(k.sb([128, ND, TT], F32), Buf()) for _ in range(2)]
    Hs_ = [(k.sb([128, ND, TT], BF16), Buf()) for _ in range(2)]
    U = k.sb([128, 32, TT], BF16)
    Y = k.sb([128, ND, TT], F32)
    Ub, Yb = Buf(), Buf()
    wl = WLoader(k, 16, 256, 3)
    xT3 = xT.rearrange("(k p) t -> p k t", p=128)
    oT3 = oT.rearrange("(k p) t -> p k t", p=128)
    NT_ = T // TT

    def load_norm(tt):
        X, Xb = Xs[tt % 2]
        H, Hb = Hs_[tt % 2]
        P.dma("sp", X[:], xT3[:, :, tt * TT:(tt + 1) * TT], writes=[Xb])
        rstd, rb = rms_rstd(k, c, X, Xb, TT)
        norm_mod(k, X, Xb, rstd, rb, A, Sh, mb, H, Hb, TT)

    load_norm(0)
    for tt in range(NT_):
        X, Xb = Xs[tt % 2]
        H, Hb = Hs_[tt % 2]
        for fh in range(2):
            for fb in range(16):
                f0 = fh * 4096 + fb * 256
                wb, wbb = wl.load(wup, 0, 16, f0, 256)
                for j in range(2):
                    ps, psb = k.ps()
                    for dt in range(16):
                        P.op("pe", lambda e, o=ps[:, :TT], w=wb[:, dt, j * 128:(j + 1) * 128], r=H[:, dt, :],
                             s=(dt == 0), t=(dt == 15): e.matmul(o, lhsT=w, rhs=r, start=s, stop=t),
                             reads=[wbb, Hb], writes=[psb])
                    rl, rlb = k.rotbuf("relu", [128, 512], F32, 3)
                    P.op("act", lambda e, o=rl[:, :TT], i=ps[:, :TT]: e.activation(out=o, in_=i, func=AF.Relu),
                         reads=[psb], writes=[rlb])
                    P.op("dve", lambda e, o=U[:, fb * 2 + j, :], i=rl[:, :TT]: e.tensor_tensor(out=o, in0=i, in1=i, op=ALU.mult),
                         reads=[rlb], writes=[Ub])
            if fh == 0 and tt + 1 < NT_:
                load_norm(tt + 1)
            for db in range(8):
                pss = [k.ps(), k.ps()]
                for kb in range(2):
                    wb, wbb = wl.load(wdn, fh * 4096 + kb * 2048, 16, db * 256, 256)
                    for j in range(2):
                        ps, psb = pss[j]
                        for ft in range(16):
                            P.op("pe", lambda e, o=ps[:, :TT], w=wb[:, ft, j * 128:(j + 1) * 128], r=U[:, kb * 16 + ft, :],
                                 s=(kb == 0 and ft == 0), t=(kb == 1 and ft == 15): e.matmul(o, lhsT=w, rhs=r, start=s, stop=t),
                                 reads=[wbb, Ub], writes=[psb])
                for j in range(2):
                    ps, psb = pss[j]
                    if fh == 0:
                        P.op("act", lambda e, o=Y[:, db * 2 + j, :], i=ps[:, :TT]: e.activation(out=o, in_=i, func=AF.Copy),
                             reads=[psb], writes=[Yb])
                    else:
                        P.op("dve", lambda e, o=Y[:, db * 2 + j, :], i=ps[:, :TT]: e.tensor_tensor(out=o, in0=o, in1=i, op=ALU.add),
                             reads=[psb], writes=[Yb])
        post_residual(k, c, X, Xb, Y, Yb, G, mb, TT)
        P.dma("sp", oT3[:, :, tt * TT:(tt + 1) * TT], X[:], reads=[Xb])
    k.close()
    return nc


def load_swap(wl, W, nk, c0):
    P = wl.k.P
    wb, wbb = wl.wbf[wl.j]
    wl.j = (wl.j + 1) % len(wl.wbf)
    for (a, b_) in ((0, 32), (32, 0)):
        src = W.ap[0:nk * 128, c0 + b_:c0 + b_ + 32].rearrange("(k p) c -> p k c", p=128)
        P.dma("sp", wb[:, :nk, a:a + 32], src, reads=[W.buf], writes=[wbb])
    return wb, wbb


def angle_reduce(k, ang, kf, ki, ab):
    import math
    P = k.P
    P.op("dve", lambda e: e.tensor_scalar(out=kf, in0=ang, scalar1=1.0 / (2 * math.pi), scalar2=None, op0=ALU.mult), reads=[ab], writes=[ab])
    P.op("dve", lambda e: e.tensor_copy(out=ki, in_=kf), reads=[ab], writes=[ab])
    P.op("dve", lambda e: e.tensor_copy(out=kf, in_=ki), reads=[ab], writes=[ab])
    P.op("dve", lambda e: e.scalar_tensor_tensor(out=ang, in0=kf, scalar=-2 * math.pi, in1=ang, op0=ALU.mult, op1=ALU.add), reads=[ab], writes=[ab])
    P.op("dve", lambda e: e.tensor_scalar(out=kf, in0=ang, scalar1=math.pi, scalar2=-2 * math.pi, op0=ALU.is_gt, op1=ALU.mult), reads=[ab], writes=[ab])
    P.op("dve", lambda e: e.tensor_tensor(out=ang, in0=ang, in1=kf, op=ALU.add), reads=[ab], writes=[ab])
    P.op("dve", lambda e: e.tensor_scalar(out=kf, in0=ang, scalar1=-math.pi, scalar2=2 * math.pi, op0=ALU.is_lt, op1=ALU.mult), reads=[ab], writes=[ab])
    P.op("dve", lambda e: e.tensor_tensor(out=ang, in0=ang, in1=kf, op=ALU.add), reads=[ab], writes=[ab])


def build_mla(l=1, env=None):
    import math
    k = Ctx("mla", env)
    nc, P = k.nc, k.P
    TT = 512
    NTT = T // TT
    xT = k.dram("xT", [D, T], F32, "ExternalInput")
    modsd = k.dram("mods", [128, 192], F32, "ExternalInput")
    posr = k.dram("posr", [64, T], I32, "ExternalInput")
    ropec = k.dram("ropec", [64, 2], F32, "ExternalInput")
    vec = k.dram("mla_vec", [128, 24], F32, "ExternalInput")
    k.do_pre()
    kvd = wsrc(k, "kv_down", [D, 576])
    wuk = wsrc(k, "kv_uk", [512, D])
    wuv = wsrc(k, "kv_uv", [512, D])
    wdq = wsrc(k, "w_dq", [D, 512])
    wuq = wsrc(k, "w_uq", [512, 16 * 192])
    wo = wsrc(k, "w_o", [D, D])
    oT = k.dram("oT", [D, T], F32, "ExternalOutput")
    otd = k.dram("ot_scratch", [16, 128, T], BF16, "Internal")
    k.init_psum(8)
    oacc = k.psl[4:]
    k.psl = k.psl[:4]
    c = make_consts(k)
    mt = k.sb([128, 2, 6, 16], F32)
    vt = k.sb([128, 24], F32)
    zer = k.sb([128, 16], F32)
    mb = Buf()
    P.dma("sp", mt[:].rearrange("p l j d -> p (l j d)"), modsd, writes=[mb])
    P.dma("sp", vt[:], vec, writes=[mb])
    P.op("pool", lambda e: e.memset(zer[:], 0.0), writes=[mb])
    A, Sh, G = mt[:, l, 0, :], mt[:, l, 1, :], mt[:, l, 2, :]

    X = k.sb([128, ND, TT], F32)
    Y = k.sb([128, ND, TT], F32)
    Xb, Yb = Buf(), Buf()
    Yf = Y[:].rearrange("p a b -> p (a b)")
    Ybf = Yf.bitcast(BF16)
    Xbf = X[:].rearrange("p a b -> p (a b)").bitcast(BF16)
    HS = Ybf[:, 0:8192].rearrange("p (a b) -> p a b", a=ND)
    HH = Ybf[:, 8192:16384].rearrange("p (a b) -> p a b", a=ND)
    rc = k.sb([64, 2], F32)
    cos2 = k.sb([64, T], F32)
    sinS = k.sb([64, T], F32)
    csb = Buf()
    P.dma("sp", rc[:], ropec, writes=[mb])
    pi_t = Yf[:64, 0:512].bitcast(I32)
    ang = Yf[:64, 512:1024]
    tmp = Yf[:64, 1024:1536]
    kf = Yf[:64, 1536:2048]
    ki = Yf[:64, 2048:2560].bitcast(I32)
    for ch in range(4):
        t0 = ch * 512
        P.dma("sp", pi_t, posr[:, t0:t0 + 512], writes=[Yb])
        P.op("dve", lambda e: e.tensor_copy(out=ang, in_=pi_t), reads=[Yb], writes=[Yb])
        P.op("dve", lambda e: e.tensor_scalar(out=ang, in0=ang, scalar1=rc[:, 0:1], scalar2=None, op0=ALU.mult), reads=[Yb, mb], writes=[Yb])
        P.op("dve", lambda e: e.tensor_scalar(out=tmp, in0=ang, scalar1=math.pi / 2, scalar2=None, op0=ALU.add), reads=[Yb], writes=[Yb])
        angle_reduce(k, tmp, kf, ki, Yb)
        P.op("act", lambda e, o=cos2[:, t0:t0 + 512]: e.activation(out=o, in_=tmp, func=AF.Sin), reads=[Yb], writes=[csb])
        angle_reduce(k, ang, kf, ki, Yb)
        P.op("act", lambda e, o=sinS[:, t0:t0 + 512]: e.activation(out=o, in_=ang, func=AF.Sin), reads=[Yb], writes=[csb])
        P.op("dve", lambda e, o=sinS[:, t0:t0 + 512]: e.tensor_scalar(out=o, in0=o, scalar1=rc[:, 1:2], scalar2=None, op0=ALU.mult), reads=[csb, mb], writes=[csb])

    CKQ = k.sb([128, 8, TT], F32)
    CK = CKQ[:, 0:4, :]
    CQ = CKQ[:, 4:8, :]
    CKb, CQb = Buf(), Buf()
    CKN = k.sb([128, 4, T], BF16)
    CQN = k.sb([128, 4, T], BF16)
    KR = k.sb([128, T], BF16)
    CKNb, CQNb, KRb = Buf(), Buf(), Buf()
    P.op("pool", lambda e: e.memset(KR[:], 0.0), writes=[KRb])
    wl = WLoader(k, 16, 128, 4)
    xT3 = xT.rearrange("(k p) t -> p k t", p=128)
    oT3 = oT.rearrange("(k p) t -> p k t", p=128)

    def rope_out(ps1, ps1b, ps2, ps2b, dst, dstb, t0):
        t1, t1b = k.rotbuf("rp1", [64, 512], F32, 1)
        t2, t2b = k.rotbuf("rp2", [64, 512], F32, 1)
        P.op("dve", lambda e: e.tensor_tensor(out=t1[:], in0=ps1[:64, :TT], in1=cos2[:, t0:t0 + TT], op=ALU.mult), reads=[ps1b, csb], writes=[t1b])
        P.op("dve", lambda e: e.tensor_tensor(out=t2[:], in0=ps2[:64, :TT], in1=sinS[:, t0:t0 + TT], op=ALU.mult), reads=[ps2b, csb], writes=[t2b])
        P.op("pool", lambda e: e.tensor_tensor(out=dst[:64, t0:t0 + TT], in0=t1[:], in1=t2[:], op=ALU.add), reads=[t1b, t2b], writes=[dstb])

    for tt in range(NTT):
        t0 = tt * TT
        P.dma("sp", X[:], xT3[:, :, t0:t0 + TT], writes=[Xb])
        rstd, rb = rms_rstd(k, c, X, Xb, TT)
        norm_mod(k, X, Xb, rstd, rb, vt[:, 0:16], zer, mb, HS, Yb, TT)
        norm_mod(k, X, Xb, rstd, rb, A, Sh, mb, HH, Yb, TT)
        for (W, src, dst, dstb) in ((kvd, HS, CK, CKb), (wdq, HH, CQ, CQb)):
            for cb in range(4):
                wb, wbb = wl.load(W, 0, 16, cb * 128, 128)
                ps, psb = k.ps()
                for dt in range(16):
                    P.op("pe", lambda e, o=ps[:, :TT], w=wb[:, dt, :], r=src[:, dt, :], s=(dt == 0), t=(dt == 15):
                         e.matmul(o, lhsT=w, rhs=r, start=s, stop=t), reads=[wbb, Yb], writes=[psb])
                P.op("act", lambda e, o=dst[:, cb, :], i=ps[:, :TT]: e.activation(out=o, in_=i, func=AF.Copy), reads=[psb], writes=[dstb])
        pss = []
        for sw in range(2):
            if sw == 0:
                wb, wbb = wl.load(kvd, 0, 16, 512, 64)
            else:
                wb, wbb = load_swap(wl, kvd, 16, 512)
            ps, psb = k.ps()
            for dt in range(16):
                P.op("pe", lambda e, o=ps[:64, :TT], w=wb[:, dt, 0:64], r=HS[:, dt, :], s=(dt == 0), t=(dt == 15):
                     e.matmul(o, lhsT=w, rhs=r, start=s, stop=t), reads=[wbb, Yb], writes=[psb])
            pss.append((ps, psb))
        rope_out(pss[0][0], pss[0][1], pss[1][0], pss[1][1], KR, KRb, t0)
        for (src, srcb, dst, dstb, v0) in ((CK, CKb, CKN, CKNb, 16), (CQ, CQb, CQN, CQNb, 20)):
            rs, rsb = rms_rstd(k, c, src, srcb, TT, scale_div=512, ntile=4)
            for ct in range(4):
                P.op("dve", lambda e, o=dst[:, ct, t0:t0 + TT], i=src[:, ct, :], s=vt[:, v0 + ct:v0 + ct + 1], r=rs[:, :TT]:
                     e.scalar_tensor_tensor(out=o, in0=i, scalar=s, in1=r, op0=ALU.mult, op1=ALU.mult),
                     reads=[srcb, rsb, mb], writes=[dstb])

    P.barrier()
    tri = k.sb([128, 128], BF16)
    trib = Buf()
    P.op("pool", lambda e: e.memset(tri[:], 1.0), writes=[trib])
    P.op("pool", lambda e: e.affine_select(out=tri[:], in_=tri[:], pattern=[[1, 128]], compare_op=ALU.is_ge, fill=0.0,
                                           base=0, channel_multiplier=-1), reads=[trib], writes=[trib])
    wl2 = WLoader(k, 4, 128, 8)
    scale = 192.0 ** -0.5
    hb = []
    for reg in (Ybf, Xbf):
        hb.append(dict(KN=reg[:, 0:2048], QN=reg[:, 2048:4096], QR=reg[:, 4096:6144], OH=reg[:, 6144:8192],
                       VH=reg[:, 8192:10240].rearrange("p (a b) -> p a b", a=16),
                       KNb=Buf(), QNb=Buf(), QRb=Buf(), OHb=Buf(), VHb=Buf()))
    for s_ in hb:
        P.op("pool", lambda e, o=s_["QR"]: e.memset(o, 0.0), writes=[s_["QRb"]])
    for h in range(16):
        s_ = hb[h % 2]
        KN, QN, QR, OH, VH = s_["KN"], s_["QN"], s_["QR"], s_["OH"], s_["VH"]
        KNb, QNb, QRb, OHb, VHb = s_["KNb"], s_["QNb"], s_["QRb"], s_["OHb"], s_["VHb"]
        wk, wkb = wl2.load(wuk, 0, 4, h * 128, 128)
        wq, wqb = wl2.load(wuq, 0, 4, h * 192, 128)
        wv, wvb = wl2.load(wuv, 0, 4, h * 128, 128)
        wr, wrb = wl2.load(wuq, 0, 4, h * 192 + 128, 64)
        ws, wsb = load_swap(wl2, wuq, 4, h * 192 + 128)
        for tq in range(NTT):
            t0 = tq * TT
            for (w_, wb_, src, srcb, dst, dstb) in ((wk, wkb, CKN, CKNb, KN, KNb), (wq, wqb, CQN, CQNb, QN, QNb)):
                ps, psb = k.ps()
                for ct in range(4):
                    P.op("pe", lambda e, o=ps[:, :TT], w=w_[:, ct, :], r=src[:, ct, t0:t0 + TT], s=(ct == 0), t=(ct == 3):
                         e.matmul(o, lhsT=w, rhs=r, start=s, stop=t), reads=[wb_, srcb], writes=[psb])
                P.op("act", lambda e, o=dst[:, t0:t0 + TT], i=ps[:, :TT]: e.activation(out=o, in_=i, func=AF.Copy), reads=[psb], writes=[dstb])
            pss = []
            for (w_, wb_) in ((wr, wrb), (ws, wsb)):
                ps, psb = k.ps()
                for ct in range(4):
                    P.op("pe", lambda e, o=ps[:64, :TT], w=w_[:, ct, 0:64], r=CQN[:, ct, t0:t0 + TT], s=(ct == 0), t=(ct == 3):
                         e.matmul(o, lhsT=w, rhs=r, start=s, stop=t), reads=[wb_, CQNb], writes=[psb])
                pss.append((ps, psb))
            rope_out(pss[0][0], pss[0][1], pss[1][0], pss[1][1], QR, QRb, t0)
        for tk4 in range(4):
            ps, psb = k.ps()
            for i in range(4):
                tk = tk4 * 4 + i
                for ct in range(4):
                    P.op("pe", lambda e, o=ps[:, i * 128:(i + 1) * 128], w=CKN[:, ct, tk * 128:(tk + 1) * 128], r=wv[:, ct, :], s=(ct == 0), t=(ct == 3):
                         e.matmul(o, lhsT=w, rhs=r, start=s, stop=t), reads=[wvb, CKNb], writes=[psb])
            P.op("act", lambda e, o=VH[:, tk4 * 4:tk4 * 4 + 4, :], i=ps[:, :].rearrange("p (a b) -> p a b", a=4):
                 e.activation(out=o, in_=i, func=AF.Copy), reads=[psb], writes=[VHb])
        for qt in range(NTT):
            oa, oab = oacc[(qt % 2) * 2]
            da, dab = oacc[(qt % 2) * 2 + 1]
            nk_ = 4 * (qt + 1)
            def score(kt):
                off = max(0, (kt - 4 * qt) * 128)
                q0 = qt * TT + off
                q1 = (qt + 1) * TT
                sp_, spb = k.ps()
                P.op("pe", lambda e, o=sp_[:, off:TT], w=KN[:, kt * 128:(kt + 1) * 128], r=QN[:, q0:q1]:
                     e.matmul(o, lhsT=w, rhs=r, start=True, stop=False), reads=[KNb, QNb], writes=[spb])
                P.op("pe", lambda e, o=sp_[:, off:TT], w=KR[:, kt * 128:(kt + 1) * 128], r=QR[:, q0:q1]:
                     e.matmul(o, lhsT=w, rhs=r, start=False, stop=True), reads=[KRb, QRb], writes=[spb])
                PT, PTb = k.rotbuf("PT", [128, TT], BF16, 4)
                P.op("act", lambda e, o=PT[:, off:TT], i=sp_[:, off:TT]: e.activation(out=o, in_=i, func=AF.Exp, scale=scale),
                     reads=[spb], writes=[PTb])
                if kt >= 4 * qt:
                    P.op("pool", lambda e, o=PT[:, off:off + 128]: e.tensor_tensor(out=o, in0=o, in1=tri[:], op=ALU.mult),
                         reads=[PTb, trib], writes=[PTb])
                return (kt, off, PT, PTb)

            def pv(st_):
                kt, off, PT, PTb = st_
                P.op("pe", lambda e, o=oa[:, off:TT], w=VH[:, kt, :], r=PT[:, off:TT], s=(kt == 0), t=(kt == nk_ - 1):
                     e.matmul(o, lhsT=w, rhs=r, start=s, stop=t), reads=[VHb, PTb], writes=[oab])
                P.op("pe", lambda e, o=da[:, off:TT], r=PT[:, off:TT], s=(kt == 0), t=(kt == nk_ - 1):
                     e.matmul(o, lhsT=c["ones"][:], rhs=r, start=s, stop=t), reads=[c["onesb"], PTb], writes=[dab])

            pend = []
            for kt in range(nk_):
                pend.append(score(kt))
                if len(pend) > 2:
                    pv(pend.pop(0))
            while pend:
                pv(pend.pop(0))
            rd, rdb = k.rotbuf("rden", [128, TT], F32, 2)
            P.op("dve", lambda e, o=rd[:], i=da[:, :TT]: e.reciprocal(out=o, in_=i), reads=[dab], writes=[rdb])
            P.op("dve", lambda e, o=OH[:, qt * TT:(qt + 1) * TT], i=oa[:, :TT], r=rd[:]: e.tensor_tensor(out=o, in0=i, in1=r, op=ALU.mult),
                 reads=[oab, rdb], writes=[OHb])
        P.dma("sp", otd[h], OH, reads=[OHb])

    P.barrier()
    OTt = CKQ[:].rearrange("p a b -> p (a b)").bitcast(BF16).rearrange("p (a b) -> p a b", a=16)
    OTb = Buf()
    otd3 = otd.rearrange("h p t -> p h t")
    Xb, Yb = Buf(), Buf()
    for tt in range(NTT):
        t0 = tt * TT
        P.dma("sp", OTt, otd3[:, :, t0:t0 + TT], writes=[OTb])
        P.dma("sp", X[:], xT3[:, :, t0:t0 + TT], writes=[Xb])
        for eb in range(16):
            wb, wbb = wl.load(wo, 0, 16, eb * 128, 128)
            ps, psb = k.ps()
            for hh in range(16):
                P.op("pe", lambda e, o=ps[:, :TT], w=wb[:, hh, :], r=OTt[:, hh, :], s=(hh == 0), t=(hh == 15):
                     e.matmul(o, lhsT=w, rhs=r, start=s, stop=t), reads=[wbb, OTb], writes=[psb])
            P.op("act", lambda e, o=Y[:, eb, :], i=ps[:, :TT]: e.activation(out=o, in_=i, func=AF.Copy), reads=[psb], writes=[Yb])
        post_residual(k, c, X, Xb, Y, Yb, G, mb, TT)
        P.dma("sp", oT3[:, :, t0:t0 + TT], X[:], reads=[Xb])
    k.close()
    return nc


def build_rwkv(l=0, dbg=False, env=None):
    k = Ctx("rwkv", env)
    k.rotw = 256
    nc, P = k.nc, k.P
    TT = 256
    NTT = T // TT
    C = 64
    NCH = TT // C
    xT = k.dram("xT", [D, T], F32, "ExternalInput")
    modsd = k.dram("mods", [128, 192], F32, "ExternalInput")
    vec = k.dram("rw_vec", [128, 13, 16], F32, "ExternalInput")
    k.do_pre()
    wrkv = wsrc(k, "w_rkv", [3 * D, D])
    w1 = wsrc(k, "w1", [D, 96])
    w2 = wsrc(k, "w2", [96, D])
    a1 = wsrc(k, "a1", [D, 96])
    a2 = wsrc(k, "a2", [96, D])
    g1 = wsrc(k, "g1", [D, 256])
    g2 = wsrc(k, "g2", [256, D])
    wo = wsrc(k, "w_o", [D, D])
    oT = k.dram("oT", [D, T], F32, "ExternalOutput")
    k.init_psum(8)
    c = make_consts(k)
    mt = k.sb([128, 2, 6, 16], F32)
    vt = k.sb([128, 13, 16], F32)
    mb = Buf()
    P.dma("sp", mt[:].rearrange("p l j d -> p (l j d)"), modsd, writes=[mb])
    P.dma("sp", vt[:], vec, writes=[mb])
    A, Sh, G = mt[:, l, 0, :], mt[:, l, 1, :], mt[:, l, 2, :]
    MU, W0, A0, KKv, KA, RK, LNW, LNB = (lambda j: vt[:, j, :]), vt[:, 6, :], vt[:, 7, :], vt[:, 8, :], vt[:, 9, :], vt[:, 10, :], vt[:, 11, :], vt[:, 12, :]

    NEG = k.sb([128, 2, 16], F32)
    P.op("dve", lambda e: e.tensor_scalar(out=NEG[:], in0=vt[:, 6:8, :], scalar1=-1.0, scalar2=None, op0=ALU.mult), reads=[mb], writes=[mb])
    cb_ = Buf()
    bo16 = k.sb([128, 128], BF16)
    bo32 = k.sb([128, 128], F32)
    idn = k.sb([128, 4, 128], BF16)
    mS = k.sb([128, 4, 64], BF16)
    mI = k.sb([128, 4, 64], BF16)
    mL = k.sb([128, 4, 64], BF16)
    ones64 = k.sb([128, 64], F32)
    for t_ in (bo16, bo32):
        P.op("pool", lambda e, t_=t_: e.memset(t_[:], 0.0), writes=[cb_])
        P.op("pool", lambda e, t_=t_: e.memset(t_[0:64, 0:64], 1.0), writes=[cb_])
        P.op("pool", lambda e, t_=t_: e.memset(t_[64:128, 64:128], 1.0), writes=[cb_])
    P.op("pool", lambda e: e.memset(ones64[:], 1.0), writes=[cb_])
    rmask = k.sb([128, TT], F32)
    P.op("pool", lambda e: e.memset(rmask[:], 1.0), writes=[cb_])
    for ch_ in range(NCH):
        P.op("pool", lambda e, o=rmask[:, ch_ * C:ch_ * C + 1]: e.memset(o, 0.0), writes=[cb_])
    P.op("pool", lambda e: e.memset(idn[:], 1.0), writes=[cb_])
    P.op("pool", lambda e: e.memset(mS[:], 1.0), writes=[cb_])
    P.op("pool", lambda e: e.memset(mI[:], 1.0), writes=[cb_])
    P.op("pool", lambda e: e.memset(mL[:], 1.0), writes=[cb_])
    for g_ in range(4):
        P.op("pool", lambda e, o=idn[:, g_, :]: e.affine_select(out=o, in_=o, pattern=[[1, 128]], compare_op=ALU.is_equal, fill=0.0,
                                                               base=0, channel_multiplier=-1), reads=[cb_], writes=[cb_])
        for hf in range(2):
            sl = slice(64 * hf, 64 * hf + 64)
            P.op("pool", lambda e, o=mS[sl, g_, :]: e.affine_select(out=o, in_=o, pattern=[[1, 64]], compare_op=ALU.is_ge, fill=0.0,
                                                                    base=-1, channel_multiplier=-1), reads=[cb_], writes=[cb_])
            P.op("pool", lambda e, o=mI[sl, g_, :]: e.affine_select(out=o, in_=o, pattern=[[1, 64]], compare_op=ALU.is_ge, fill=0.0,
                                                                    base=0, channel_multiplier=-1), reads=[cb_], writes=[cb_])
            P.op("pool", lambda e, o=mL[sl, g_, :]: e.affine_select(out=o, in_=o, pattern=[[-1, 64]], compare_op=ALU.is_ge, fill=0.0,
                                                                    base=-1, channel_multiplier=1), reads=[cb_], writes=[cb_])

    RT = k.sb([128, 16, TT], BF16)
    KT = k.sb([128, 16, TT], BF16)
    BT = k.sb([128, 16, TT], BF16)
    AT = k.sb([128, 16, TT], BF16)
    VT = k.sb([128, 16, TT], BF16)
    GT = k.sb([128, 16, TT], BF16)
    BON = k.sb([128, 16, TT], BF16)
    GC = k.sb([128, 16, NCH], F32)
    RTb, KTb, BTb, ATb, VTb, GTb, BONb, GCb, YTb = (Buf() for _ in range(9))
    Hf = k.sb([128, 16, 64], F32)
    Hstk = k.sb([128, 16, 64], BF16)
    Hbd = k.sb([128, 16, 128], BF16)
    Hb_ = [Buf() for _ in range(4)]
    HL = k.sb([128, 16, 1], F32)
    HLb = Buf()
    P.op("pool", lambda e: e.memset(Hf[:], 0.0), writes=Hb_)
    P.op("pool", lambda e: e.memset(Hstk[:], 0.0), writes=Hb_)
    P.op("pool", lambda e: e.memset(Hbd[:], 0.0), writes=Hb_)
    P.op("pool", lambda e: e.memset(HL[:], 0.0), writes=[HLb])
    TW = k.sb([128, TT], BF16)
    TA = k.sb([128, TT], BF16)
    TG = k.sb([128, 2, TT], BF16)
    TWb, TAb, TGb = Buf(), Buf(), Buf()
    wl = WLoader(k, 16, 128, 3)
    wls = WLoader(k, 2, 128, 3)
    REG = k.sb([128, 18688], F32)
    REGbf = REG[:].bitcast(BF16)

    def f32v(o, n, a):
        return REG[:, o:o + n].rearrange("p (a b) -> p a b", a=a)

    def bfv(o, n, a):
        return REGbf[:, 2 * o:2 * o + 2 * n].rearrange("p (a b) -> p a b", a=a)

    X = f32v(0, 4096, 16)
    Hs = f32v(4096, 4352, 16)
    XX = bfv(8448, 2048, 16)
    XS = bfv(10496, 2048, 16)
    XR = bfv(12544, 2048, 16)
    XK = bfv(14592, 2048, 16)
    XV = bfv(16640, 2048, 16)
    o_ = [0]

    def nxt(n, a):
        v = bfv(o_[0], n, a)
        o_[0] += n
        return v
    ATbd, BTbd, KTbd, VTbd = nxt(1024, 16), nxt(1024, 16), nxt(1024, 16), nxt(1024, 16)
    def chbuf():
        t = k.sb([128, 4, 128], F32)
        return {"r": t[:], "w": t[:].bitcast(CHAIN_DT), "m": t[:].bitcast(CHAIN_DT)}
    CH = [dict(N=[chbuf(), chbuf()], L=[chbuf(), chbuf()], P=chbuf()) for _ in range(2)]
    PF = nxt(1024, 16)
    MakT = nxt(1024, 16)
    MrbT, MrkT = nxt(512, 16), nxt(512, 16)
    Vbd, Vstk = nxt(1024, 16), nxt(512, 16)
    Bbd, Kbd = nxt(1024, 16), nxt(1024, 16)
    Zs, Us, Ubd = nxt(512, 16), nxt(512, 16), nxt(1024, 16)
    ZERO_LIST = [ATbd, BTbd, KTbd, VTbd, CH[0]['N'][0]['w'], CH[0]['L'][0]['w'], CH[1]['N'][0]['w'], CH[1]['L'][0]['w'], MakT, Ubd]
    YT = f32v(12800, 4096, 16)
    OIN = bfv(4096, 2048, 16)
    Y2 = f32v(8448, 4096, 16)

    xT3 = xT.rearrange("(k p) t -> p k t", p=128)
    oT3 = oT.rearrange("(k p) t -> p k t", p=128)
    if dbg:
        dbf = k.dram("dbg_bf", [7, 128, 16 * TT], BF16, "ExternalOutput")
        dyt = k.dram("dbg_yt", [128, 16 * TT], F32, "ExternalOutput")
        dgc = k.dram("dbg_gc", [128, 16 * NCH], F32, "ExternalOutput")
        doin = k.dram("dbg_oin", [128, 16 * TT], BF16, "ExternalOutput")

    def tmp(name, n=1, dt=F32, w=TT):
        return k.rotbuf(name, [128, w], dt, n)

    for tt in range(1 if dbg else NTT):
        t0 = tt * TT
        Xb, Hsb, XXb, XSb, XRb, XKb, XVb = (Buf() for _ in range(7))
        P.dma("sp", X, xT3[:, :, t0:t0 + TT], writes=[Xb])
        rstd, rb = rms_rstd(k, c, X, Xb, TT)
        norm_mod(k, X, Xb, rstd, rb, A, Sh, mb, Hs, Hsb, TT, col0=1)
        P.op("pool", lambda e: e.tensor_copy(out=Hs[:, :, 0:1], in_=HL[:]), reads=[HLb], writes=[Hsb])
        P.op("dve", lambda e: e.tensor_tensor(out=XX, in0=Hs[:, :, 0:TT], in1=Hs[:, :, 1:TT + 1], op=ALU.subtract), reads=[Hsb], writes=[XXb])
        P.op("pool", lambda e: e.tensor_copy(out=HL[:], in_=Hs[:, :, TT:TT + 1]), reads=[Hsb], writes=[HLb])

        def make_xs(j, dst, dstb):
            for dt in range(16):
                P.op("dve", lambda e, o=dst[:, dt, :], i=XX[:, dt, :], s=vt[:, j, dt:dt + 1], h=Hs[:, dt, 1:TT + 1]:
                     e.scalar_tensor_tensor(out=o, in0=i, scalar=s, in1=h, op0=ALU.mult, op1=ALU.add),
                     reads=[XXb, Hsb, mb], writes=[dstb])
        for (j, W, ncol) in ((3, w1, 96), (4, a1, 96), (5, g1, 256)):
            make_xs(j, XS, XSb)
            for cbk in range((ncol + 127) // 128):
                nc_ = min(128, ncol - cbk * 128)
                wb, wbb = wl.load(W, 0, 16, cbk * 128, nc_)
                ps, psb = k.ps()
                for dt in range(16):
                    P.op("pe", lambda e, o=ps[:nc_, :TT], w=wb[:, dt, :nc_], r=XS[:, dt, :], s=(dt == 0), t=(dt == 15):
                         e.matmul(o, lhsT=w, rhs=r, start=s, stop=t), reads=[wbb, XSb], writes=[psb])
                if j == 3:
                    P.op("act", lambda e, i=ps[:96, :TT]: e.activation(out=TW[:96, :], in_=i, func=AF.Tanh), reads=[psb], writes=[TWb])
                elif j == 4:
                    P.op("act", lambda e, i=ps[:96, :TT]: e.activation(out=TA[:96, :], in_=i, func=AF.Copy), reads=[psb], writes=[TAb])
                else:
                    P.op("act", lambda e, i=ps[:, :TT], o=TG[:, cbk, :]: e.activation(out=o, in_=i, func=AF.Sigmoid), reads=[psb], writes=[TGb])
        make_xs(0, XR, XRb)
        make_xs(1, XK, XKb)
        make_xs(2, XV, XVb)
        for p in range(16):
            e0 = p * 128
            pA, pAb = k.ps()
            pB, pBb = k.ps()
            pC, pCb = k.ps()
            pD, pDb = k.ps()
            for (jj, src, srcb, ps, psb, co) in ((0, XR, XRb, pA, pAb, 0), (1, XK, XKb, pA, pAb, TT), (2, XV, XVb, pB, pBb, 0)):
                wb, wbb = wl.load(wrkv, jj * D, 16, e0, 128)
                for dt in range(16):
                    P.op("pe", lambda e, o=ps[:, co:co + TT], w=wb[:, dt, :], r=src[:, dt, :], s=(dt == 0), t=(dt == 15):
                         e.matmul(o, lhsT=w, rhs=r, start=s, stop=t), reads=[wbb, srcb], writes=[psb])
            wb, wbb = wls.load(w2, 0, 1, e0, 128, pp=96)
            P.op("pe", lambda e, o=pB[:, TT:2 * TT], w=wb[:96, 0, :]: e.matmul(o, lhsT=w, rhs=TW[:96, :], start=True, stop=True),
                 reads=[wbb, TWb], writes=[pBb])
            wb, wbb = wls.load(a2, 0, 1, e0, 128, pp=96)
            P.op("pe", lambda e, o=pC[:, 0:TT], w=wb[:96, 0, :]: e.matmul(o, lhsT=w, rhs=TA[:96, :], start=True, stop=True),
                 reads=[wbb, TAb], writes=[pCb])
            wb, wbb = wls.load(g2, 0, 2, e0, 128)
            for kt in range(2):
                P.op("pe", lambda e, o=pC[:, TT:2 * TT], w=wb[:, kt, :], r=TG[:, kt, :], s=(kt == 0), t=(kt == 1):
                     e.matmul(o, lhsT=w, rhs=r, start=s, stop=t), reads=[wbb, TGb], writes=[pCb])
            r_ps, k_ps, v_ps, w_ps, a_ps, g_ps = pA[:, 0:TT], pA[:, TT:2 * TT], pB[:, 0:TT], pB[:, TT:2 * TT], pC[:, 0:TT], pC[:, TT:2 * TT]
            LWS = -0.6065306597126334
            sg, sgb = tmp("sg", 2)
            cl, clb = tmp("cl", 2)
            av, avb = tmp("av", 2)
            vf, vfb = tmp("vf", 2)
            kk, kkb = tmp("kk", 2)
            rn, rnb = tmp("rn", 2)
            kf_, kfb = tmp("kf", 2)
            eg, egb = tmp("eg")
            eig, eigb = tmp("eig")
            eex, eexb = tmp("eex")
            k2, k2b = tmp("ksq", 1, BF16)
            bb, bbb = tmp("bb")
            rk_, rkb = tmp("rkp", 1, BF16)
            lw, lwb = sg, sgb
            P.op("act", lambda e, o=sg[:], i=w_ps, b=NEG[:, 0, p:p + 1]: e.activation(out=o, in_=i, func=AF.Exp, bias=b, scale=-1.0),
                 reads=[pBb, mb], writes=[sgb])
            P.op("dve", lambda e, o=kk[:], i=k_ps, s=KKv[:, p:p + 1]: e.tensor_scalar(out=o, in0=i, scalar1=s, scalar2=None, op0=ALU.mult),
                 reads=[pAb, mb], writes=[kkb])
            P.op("pool", lambda e, o=k2[:], i=kk[:]: e.tensor_tensor(out=o, in0=i, in1=i, op=ALU.mult), reads=[kkb], writes=[k2b])
            P.op("pe", lambda e, o=pD[:, 0:TT], r=k2[:]: e.matmul(o, lhsT=bo16[:], rhs=r, start=True, stop=True), reads=[k2b, cb_], writes=[pDb])
            P.op("act", lambda e, o=av[:], i=a_ps, b=NEG[:, 1, p:p + 1]: e.activation(out=o, in_=i, func=AF.Exp, bias=b, scale=-1.0),
                 reads=[pCb, mb], writes=[avb])
            P.op("act", lambda e, o=GT[:, p, :], i=g_ps: e.activation(out=o, in_=i, func=AF.Copy), reads=[pCb], writes=[GTb])
            P.op("act", lambda e, o=vf[:], i=v_ps: e.activation(out=o, in_=i, func=AF.Copy), reads=[pBb], writes=[vfb])
            P.op("pool", lambda e, o=VT[:, p, :], i=vf[:]: e.tensor_copy(out=o, in_=i), reads=[vfb], writes=[VTb])
            P.op("dve", lambda e, o=sg[:]: e.tensor_scalar(out=o, in0=o, scalar1=1.0, scalar2=None, op0=ALU.add), reads=[sgb], writes=[sgb])
            P.op("dve", lambda e, o=sg[:]: e.reciprocal(out=o, in_=o), reads=[sgb], writes=[sgb])
            P.op("dve", lambda e, o=cl[:], i=lw[:]: e.tensor_tensor_scan(out=o, data0=rmask[:], data1=i, initial=0.0, op0=ALU.mult, op1=ALU.add),
                 reads=[lwb, cb_], writes=[clb])
            P.op("dve", lambda e, o=rn[:], i=pD[:, 0:TT]: e.tensor_scalar(out=o, in0=i, scalar1=5.5e-20, scalar2=None, op0=ALU.max), reads=[pDb], writes=[rnb])
            P.op("pool", lambda e, o=eex[:], i=cl[:], j_=lw[:]: e.tensor_tensor(out=o, in0=i, in1=j_, op=ALU.subtract), reads=[clb, lwb], writes=[eexb])
            P.op("act", lambda e, o=rn[:]: e.activation(out=o, in_=o, func=AF.Ln), reads=[rnb], writes=[rnb])
            P.op("act", lambda e, o=rn[:]: e.activation(out=o, in_=o, func=AF.Exp, scale=-0.5), reads=[rnb], writes=[rnb])
            P.op("act", lambda e, o=eg[:], i=cl[:]: e.activation(out=o, in_=i, func=AF.Exp, scale=LWS), reads=[clb], writes=[egb])
            P.op("act", lambda e, o=eig[:], i=cl[:]: e.activation(out=o, in_=i, func=AF.Exp, scale=-LWS), reads=[clb], writes=[eigb])
            P.op("act", lambda e, o=eex[:]: e.activation(out=o, in_=o, func=AF.Exp, scale=LWS), reads=[eexb], writes=[eexb])
            P.op("pool", lambda e, o=GC[:, p, :], i=eg[:].rearrange("p (a b) -> p a b", a=NCH)[:, :, C - 1]: e.tensor_copy(out=o, in_=i),
                 reads=[egb], writes=[GCb])
            P.op("dve", lambda e, o=av[:]: e.tensor_scalar(out=o, in0=o, scalar1=1.0, scalar2=None, op0=ALU.add), reads=[avb], writes=[avb])
            P.op("dve", lambda e, o=av[:]: e.reciprocal(out=o, in_=o), reads=[avb], writes=[avb])
            P.op("dve", lambda e, o=kf_[:], i=av[:], s=KA[:, p:p + 1]: e.tensor_scalar(out=o, in0=i, scalar1=-1.0, scalar2=s, op0=ALU.add, op1=ALU.mult),
                 reads=[avb, mb], writes=[kfb])
            P.op("dve", lambda e, o=kf_[:], i=k_ps: e.scalar_tensor_tensor(out=o, in0=o, scalar=1.0, in1=i, op0=ALU.add, op1=ALU.mult),
                 reads=[kfb, pAb], writes=[kfb])
            P.op("pool", lambda e, o=kk[:], r=rn[:]: e.tensor_tensor(out=o, in0=o, in1=r, op=ALU.mult), reads=[kkb, rnb], writes=[kkb])
            P.op("dve", lambda e, o=RT[:, p, :], i=r_ps, g_=eg[:]: e.tensor_tensor(out=o, in0=i, in1=g_, op=ALU.mult), reads=[pAb, egb], writes=[RTb])
            P.op("pool", lambda e, o=KT[:, p, :], i=kf_[:], g_=eig[:]: e.tensor_tensor(out=o, in0=i, in1=g_, op=ALU.mult), reads=[kfb, eigb], writes=[KTb])
            P.op("pool", lambda e, o=bb[:], i=kk[:], a_=av[:]: e.tensor_tensor(out=o, in0=i, in1=a_, op=ALU.mult), reads=[kkb, avb], writes=[bbb])
            P.op("pool", lambda e, o=BT[:, p, :], i=bb[:], g_=eig[:]: e.tensor_tensor(out=o, in0=i, in1=g_, op=ALU.mult), reads=[bbb, eigb], writes=[BTb])
            P.op("dve", lambda e, o=AT[:, p, :], i=kk[:], g_=eex[:]: e.scalar_tensor_tensor(out=o, in0=i, scalar=-1.0, in1=g_, op0=ALU.mult, op1=ALU.mult),
                 reads=[kkb, eexb], writes=[ATb])
            P.op("dve", lambda e, o=rk_[:], i=r_ps, s=RK[:, p:p + 1], k_=kf_[:]: e.scalar_tensor_tensor(out=o, in0=i, scalar=s, in1=k_, op0=ALU.mult, op1=ALU.mult),
                 reads=[pAb, kfb, mb], writes=[rkb])
            P.op("pe", lambda e, o=pD[:, TT:2 * TT], r=rk_[:]: e.matmul(o, lhsT=bo16[:], rhs=r, start=True, stop=True), reads=[rkb, cb_], writes=[pDb])
            P.op("dve", lambda e, o=BON[:, p, :], i=pD[:, TT:2 * TT], v_=vf[:]: e.tensor_tensor(out=o, in0=i, in1=v_, op=ALU.mult),
                 reads=[pDb, vfb], writes=[BONb])

        P.barrier()
        if dbg:
            for i_, (t_, b_) in enumerate(((RT, RTb), (KT, KTb), (BT, BTb), (AT, ATb), (VT, VTb), (GT, GTb), (BON, BONb))):
                P.dma("sp", dbf[i_], t_[:].rearrange("p a b -> p (a b)"), reads=[b_])
            P.dma("sp", dgc, GC[:].rearrange("p a b -> p (a b)"), reads=[GCb])
            P.barrier()
        zb = Buf()
        SB = [dict(N=Buf(), L=Buf(), P=Buf()) for _ in range(2)]
        for z_ in ZERO_LIST:
            if z_.dtype == BF16:
                P.op("pool", lambda e, z_=z_: e.memset(z_, 0.0), writes=[zb])
            else:
                P.op("dve", lambda e, z_=z_: e.tensor_scalar(out=z_, in0=idn[:], scalar1=0.0, scalar2=None, op0=ALU.mult),
                     reads=[cb_], writes=[zb])
        for ch in range(NCH):
            cc = slice(ch * C, (ch + 1) * C)
            inb = [Buf() for _ in range(4)]
            for (src, srcb, dst) in ((AT, ATb, ATbd), (BT, BTb, BTbd), (KT, KTb, KTbd), (VT, VTb, VTbd)):
                for hf in range(2):
                    sl = slice(64 * hf, 64 * hf + 64)
                    P.op("dve" if hf == 0 else "act",
                         (lambda e, o=dst[sl, :, 64 * hf:64 * hf + 64], i=src[sl, :, cc]: e.tensor_copy(out=o, in_=i)) if hf == 0 else
                         (lambda e, o=dst[sl, :, 64 * hf:64 * hf + 64], i=src[sl, :, cc]: e.activation(out=o, in_=i, func=AF.Copy)),
                         reads=[srcb, zb], writes=inb)
            gb = [dict((n, Buf()) for n in ("N", "L", "Mak", "Mrb", "Mrk", "V", "B", "K", "P", "Z", "U")) for _ in range(4)]
            for g_ in range(4):
                pg = slice(g_ * 4, g_ * 4 + 4)
                b_ = gb[g_]
                for (src, dst, nm) in ((VTbd, Vbd, "V"), (BTbd, Bbd, "B"), (KTbd, Kbd, "K")):
                    ps, psb = k.ps()
                    psv = ps[:].bitcast(BF16)[:, 0:512].rearrange("p (a b) -> p a b", a=4)
                    for i in range(4):
                        P.op("pe", lambda e, o=psv[:, i, :], w=src[:, g_ * 4 + i, :]: e.transpose(out=o, in_=w, identity=idn[:, 0, :]),
                             reads=[inb[g_], cb_], writes=[psb])
                    P.op("act", lambda e, o=dst[:, pg, :], i=psv: e.activation(out=o, in_=i, func=AF.Copy), reads=[psb], writes=[b_[nm]])
                    if nm == "V":
                        for hf in range(2):
                            sl = slice(64 * hf, 64 * hf + 64)
                            P.op("dve", lambda e, o=Vstk[sl, pg, :], i=psv[sl, :, 64 * hf:64 * hf + 64]: e.tensor_copy(out=o, in_=i),
                                 reads=[psb], writes=[b_[nm]])
            def step1(g_):
                ps1, ps1b = k.ps()
                ps2, ps2b = k.ps()
                ps3, ps3b = k.ps()
                b_ = gb[g_]
                cs_ = CH[g_ % 2]
                sb_ = SB[g_ % 2]
                for i in range(4):
                    p = g_ * 4 + i
                    cs = slice(i * 64, i * 64 + 64)
                    cs2 = slice(256 + i * 64, 256 + i * 64 + 64)
                    for (ps, psb, csl, lh, rh, rhb) in ((ps1, ps1b, cs, BTbd, AT, ATb), (ps1, ps1b, cs2, ATbd, BT, BTb),
                                                        (ps2, ps2b, cs, KTbd, AT, ATb), (ps2, ps2b, cs2, BTbd, RT, RTb),
                                                        (ps3, ps3b, cs, KTbd, RT, RTb)):
                        P.op("pe", lambda e, o=ps[:, csl], w=lh[:, p, :], r=rh[:, p, cc]: e.matmul(o, lhsT=w, rhs=r, start=True, stop=True),
                             reads=[inb[g_], rhb], writes=[psb])
                pg = slice(g_ * 4, g_ * 4 + 4)
                v1 = ps1[:, 0:256].rearrange("p (a b) -> p a b", a=4)
                v1b = ps1[:, 256:512].rearrange("p (a b) -> p a b", a=4)
                v2 = ps2[:, 0:256].rearrange("p (a b) -> p a b", a=4)
                v2b = ps2[:, 256:512].rearrange("p (a b) -> p a b", a=4)
                v3 = ps3[:, 0:256].rearrange("p (a b) -> p a b", a=4)
                for hf in range(2):
                    sl = slice(64 * hf, 64 * hf + 64)
                    fs = slice(64 * hf, 64 * hf + 64)
                    P.op("dve", lambda e, o=cs_["N"][0]["w"][sl, :, fs], i=v1[sl], m=mS[sl]: e.tensor_tensor(out=o, in0=i, in1=m, op=ALU.mult),
                         reads=[ps1b, cb_, zb], writes=[sb_["N"]])
                    P.op("dve", lambda e, o=cs_["L"][0]["w"][sl, :, fs], i=v1b[sl], m=mL[sl]: e.tensor_tensor(out=o, in0=i, in1=m, op=ALU.mult),
                         reads=[ps1b, cb_, zb], writes=[sb_["L"]])
                    P.op("dve", lambda e, o=MakT[sl, pg, fs], i=v2[sl], m=mS[sl]: e.tensor_tensor(out=o, in0=i, in1=m, op=ALU.mult),
                         reads=[ps2b, cb_, zb], writes=[b_["Mak"]])
                P.op("dve", lambda e, o=MrbT[:, pg, :], i=v2b, m=mI[:]: e.tensor_tensor(out=o, in0=i, in1=m, op=ALU.mult),
                     reads=[ps2b, cb_], writes=[b_["Mrb"]])
                P.op("dve", lambda e, o=MrkT[:, pg, :], i=v3, m=mI[:]: e.tensor_tensor(out=o, in0=i, in1=m, op=ALU.mult),
                     reads=[ps3b, cb_], writes=[b_["Mrk"]])
                P.op("dve", lambda e, o=cs_["P"]["w"], i=cs_["N"][0]["r"]: e.tensor_tensor(out=o, in0=i, in1=idn[:], op=ALU.add),
                     reads=[sb_["N"], cb_], writes=[sb_["P"]])

            def chain_sq(g_, lev):
                a_, n_ = (lev - 1) % 2, lev % 2
                cs_ = CH[g_ % 2]
                sb_ = SB[g_ % 2]
                Ns, Ls = cs_["N"], cs_["L"]
                if lev < 5:
                    psn, psnb = k.ps()
                    for i in range(4):
                        P.op("pe", lambda e, o=psn[:, i * 128:(i + 1) * 128], w=Ls[a_]["m"][:, i, :], r=Ns[a_]["m"][:, i, :]:
                             e.matmul(o, lhsT=w, rhs=r, start=True, stop=True), reads=[sb_["L"], sb_["N"]], writes=[psnb])
                psl_, pslb = k.ps()
                for i in range(4):
                    P.op("pe", lambda e, o=psl_[:, i * 128:(i + 1) * 128], w=Ns[a_]["m"][:, i, :], r=Ls[a_]["m"][:, i, :]:
                         e.matmul(o, lhsT=w, rhs=r, start=True, stop=True), reads=[sb_["L"], sb_["N"]], writes=[pslb])
                P.op("act", lambda e, o=Ls[n_]["w"], i=psl_[:].rearrange("p (a b) -> p a b", a=4): e.activation(out=o, in_=i, func=AF.Copy),
                     reads=[pslb], writes=[sb_["L"]])
                if lev < 5:
                    P.op("act", lambda e, o=Ns[n_]["w"], i=psn[:].rearrange("p (a b) -> p a b", a=4): e.activation(out=o, in_=i, func=AF.Copy),
                         reads=[psnb], writes=[sb_["N"]])

            def chain_p(g_, lev):
                n_ = lev % 2
                pg = slice(g_ * 4, g_ * 4 + 4)
                cs_ = CH[g_ % 2]
                sb_ = SB[g_ % 2]
                Ls, Pc = cs_["L"], cs_["P"]
                psp, pspb = k.ps()
                for i in range(4):
                    P.op("pe", lambda e, o=psp[:, i * 128:(i + 1) * 128], w=Ls[n_]["m"][:, i, :], r=Pc["m"][:, i, :]:
                         e.matmul(o, lhsT=w, rhs=r, start=True, stop=True), reads=[sb_["L"], sb_["P"]], writes=[pspb])
                if lev < 5:
                    P.op("dve", lambda e, o=Pc["w"], q=Pc["r"], i=psp[:].rearrange("p (a b) -> p a b", a=4): e.tensor_tensor(out=o, in0=i, in1=q, op=ALU.add),
                         reads=[pspb, sb_["P"]], writes=[sb_["P"]])
                else:
                    P.op("dve", lambda e, o=PF[:, pg, :], i=psp[:].rearrange("p (a b) -> p a b", a=4), q=Pc["r"]: e.tensor_tensor(out=o, in0=i, in1=q, op=ALU.add),
                         reads=[pspb, sb_["P"]], writes=[gb[g_]["P"]])

            for gp in ((0, 1), (2, 3)):
                for g_ in gp:
                    step1(g_)
                for lev in range(1, 6):
                    for g_ in gp:
                        chain_sq(g_, lev)
                    for g_ in gp:
                        chain_p(g_, lev)
            zps, ups = {}, {}
            for g_ in range(4):
                pg = slice(g_ * 4, g_ * 4 + 4)
                b_ = gb[g_]
                hb_ = Hb_[g_]
                psz, pszb = k.ps()
                for i in range(4):
                    p = g_ * 4 + i
                    P.op("pe", lambda e, o=psz[:, i * 64:(i + 1) * 64], w=ATbd[:, p, :], r=Hstk[:, p, :]: e.matmul(o, lhsT=w, rhs=r, start=True, stop=False),
                         reads=[inb[g_], hb_], writes=[pszb])
                    P.op("pe", lambda e, o=psz[:, i * 64:(i + 1) * 64], w=MakT[:, p, :], r=Vstk[:, p, :]: e.matmul(o, lhsT=w, rhs=r, start=False, stop=True),
                         reads=[b_["Mak"], b_["V"]], writes=[pszb])
                P.op("act", lambda e, o=Zs[:, pg, :], i=psz[:, 0:256].rearrange("p (a b) -> p a b", a=4): e.activation(out=o, in_=i, func=AF.Copy),
                     reads=[pszb], writes=[b_["Z"]])
            for g_ in range(4):
                pg = slice(g_ * 4, g_ * 4 + 4)
                b_ = gb[g_]
                psu, psub = k.ps()
                for i in range(4):
                    p = g_ * 4 + i
                    P.op("pe", lambda e, o=psu[:, i * 64:(i + 1) * 64], w=PF[:, p, :], r=Zs[:, p, :]: e.matmul(o, lhsT=w, rhs=r, start=True, stop=True),
                         reads=[b_["P"], b_["Z"]], writes=[psub])
                psuv = psu[:, 0:256].rearrange("p (a b) -> p a b", a=4)
                P.op("act", lambda e, o=Us[:, pg, :], i=psuv: e.activation(out=o, in_=i, func=AF.Copy), reads=[psub], writes=[b_["U"]])
                for hf in range(2):
                    sl = slice(64 * hf, 64 * hf + 64)
                    P.op("dve", lambda e, o=Ubd[sl, pg, 64 * hf:64 * hf + 64], i=psuv[sl]: e.tensor_copy(out=o, in_=i), reads=[psub, zb], writes=[b_["U"]])
            for g_ in range(4):
                pg = slice(g_ * 4, g_ * 4 + 4)
                b_ = gb[g_]
                hb_ = Hb_[g_]
                psy, psyb = k.ps()
                psh, pshb = k.ps()
                for i in range(4):
                    p = g_ * 4 + i
                    oy = psy[:, i * 64:(i + 1) * 64]
                    P.op("pe", lambda e, o=oy, w=Hbd[:, p, :], r=RT[:, p, cc]: e.matmul(o, lhsT=w, rhs=r, start=True, stop=False),
                         reads=[hb_, RTb], writes=[psyb])
                    P.op("pe", lambda e, o=oy, w=Ubd[:, p, :], r=MrbT[:, p, :]: e.matmul(o, lhsT=w, rhs=r, start=False, stop=False),
                         reads=[b_["U"], b_["Mrb"]], writes=[psyb])
                    P.op("pe", lambda e, o=oy, w=Vbd[:, p, :], r=MrkT[:, p, :]: e.matmul(o, lhsT=w, rhs=r, start=False, stop=True),
                         reads=[b_["V"], b_["Mrk"]], writes=[psyb])
                    oh = psh[:, i * 64:(i + 1) * 64]
                    P.op("pe", lambda e, o=oh, w=Bbd[:, p, :], r=Us[:, p, :]: e.matmul(o, lhsT=w, rhs=r, start=True, stop=False),
                         reads=[b_["B"], b_["U"]], writes=[pshb])
                    P.op("pe", lambda e, o=oh, w=Kbd[:, p, :], r=Vstk[:, p, :]: e.matmul(o, lhsT=w, rhs=r, start=False, stop=True),
                         reads=[b_["K"], b_["V"]], writes=[pshb])
                P.op("act", lambda e, o=YT[:, pg, cc], i=psy[:, 0:256].rearrange("p (a b) -> p a b", a=4): e.activation(out=o, in_=i, func=AF.Copy),
                     reads=[psyb], writes=[YTb])
                P.op("dve", lambda e, o=Hf[:, pg, :], i=psh[:, 0:256].rearrange("p (a b) -> p a b", a=4): e.tensor_tensor(out=o, in0=o, in1=i, op=ALU.add),
                     reads=[pshb, hb_], writes=[hb_])
                for i in range(4):
                    p = g_ * 4 + i
                    P.op("dve", lambda e, o=Hf[:, p, :], s=GC[:, p, ch:ch + 1]: e.tensor_scalar(out=o, in0=o, scalar1=s, scalar2=None, op0=ALU.mult),
                         reads=[hb_, GCb], writes=[hb_])
                P.op("pool", lambda e, o=Hstk[:, pg, :], i=Hf[:, pg, :]: e.tensor_copy(out=o, in_=i), reads=[hb_], writes=[hb_])
                for hf in range(2):
                    sl = slice(64 * hf, 64 * hf + 64)
                    P.op("pool", lambda e, o=Hbd[sl, pg, 64 * hf:64 * hf + 64], i=Hf[sl, pg, :]: e.tensor_copy(out=o, in_=i), reads=[hb_], writes=[hb_])

        P.barrier()
        OINb, Y2b, Xb = Buf(), Buf(), Buf()
        P.dma("sp", X, xT3[:, :, t0:t0 + TT], writes=[Xb])
        for p in range(16):
            psm, psmb = k.ps()
            P.op("pe", lambda e, o=psm[:, 0:TT], r=YT[:, p, :]: e.matmul(o, lhsT=bo32[:], rhs=r, start=True, stop=True), reads=[YTb, cb_], writes=[psmb])
            yc, ycb = tmp("cl")
            P.op("dve", lambda e, o=yc[:], m=psm[:, 0:TT], y=YT[:, p, :]: e.scalar_tensor_tensor(out=o, in0=m, scalar=-1.0 / 64, in1=y, op0=ALU.mult, op1=ALU.add),
                 reads=[psmb, YTb], writes=[ycb])
            ysq, ysqb = tmp("rn")
            P.op("pool", lambda e, o=ysq[:], i=yc[:]: e.tensor_tensor(out=o, in0=i, in1=i, op=ALU.mult), reads=[ycb], writes=[ysqb])
            P.op("pe", lambda e, o=psm[:, TT:2 * TT], r=ysq[:]: e.matmul(o, lhsT=bo32[:], rhs=r, start=True, stop=True), reads=[ysqb, cb_], writes=[psmb])
            rs, rsb = tmp("bb")
            P.op("act", lambda e, o=rs[:], i=psm[:, TT:2 * TT]: e.activation(out=o, in_=i, func=AF.Sqrt, bias=c["eps"][:, 1:2], scale=1.0 / 64),
                 reads=[psmb, c["onesb"]], writes=[rsb])
            P.op("dve", lambda e, o=rs[:]: e.reciprocal(out=o, in_=o), reads=[rsb], writes=[rsb])
            P.op("dve", lambda e, o=yc[:], r=rs[:]: e.tensor_tensor(out=o, in0=o, in1=r, op=ALU.mult), reads=[ycb, rsb], writes=[ycb])
            P.op("dve", lambda e, o=yc[:], s=LNW[:, p:p + 1], b=BON[:, p, :]: e.scalar_tensor_tensor(out=o, in0=o, scalar=s, in1=b, op0=ALU.mult, op1=ALU.add),
                 reads=[ycb, BONb, mb], writes=[ycb])
            P.op("dve", lambda e, o=OIN[:, p, :], i=yc[:], s=LNB[:, p:p + 1], g_=GT[:, p, :]: e.scalar_tensor_tensor(out=o, in0=i, scalar=s, in1=g_, op0=ALU.add, op1=ALU.mult),
                 reads=[ycb, GTb, mb], writes=[OINb])
        if dbg:
            P.dma("sp", dyt, YT.rearrange("p a b -> p (a b)"), reads=[YTb])
            P.dma("sp", doin, OIN.rearrange("p a b -> p (a b)"), reads=[OINb])
        for eb in range(16):
            wb, wbb = wl.load(wo, 0, 16, eb * 128, 128)
            ps, psb = k.ps()
            for p in range(16):
                P.op("pe", lambda e, o=ps[:, :TT], w=wb[:, p, :], r=OIN[:, p, :], s=(p == 0), t=(p == 15):
                     e.matmul(o, lhsT=w, rhs=r, start=s, stop=t), reads=[wbb, OINb], writes=[psb])
            P.op("act", lambda e, o=Y2[:, eb, :], i=ps[:, :TT]: e.activation(out=o, in_=i, func=AF.Copy), reads=[psb], writes=[Y2b])
        post_residual(k, c, X, Xb, Y2, Y2b, G, mb, TT)
        P.dma("sp", oT3[:, :, t0:t0 + TT], X, reads=[Xb])
        P.barrier()
    k.close()
    return nc


def _pd(v, n=16):
    return np.ascontiguousarray(np.asarray(v, dtype=np.float32).reshape(n, 128).T)


def _run(nc, maps):
    res = run_bass_kernel_spmd(nc, maps, core_ids=list(range(len(maps))))
    return res.results


class _Root:
    pass


def build_fused():
    root = _Root()
    root.nc = bass.Bass("TRN2", target_bir_lowering=False)
    root.semstack = ExitStack()
    nc = root.nc
    xT = nc.dram_tensor("xT", [D, T], F32, kind="ExternalInput").ap()
    oT = nc.dram_tensor("oT", [D, T], F32, kind="ExternalOutput").ap()
    modsI = nc.dram_tensor("mods_i", [128, 192], F32, kind="Internal").ap()
    xa = nc.dram_tensor("xa_i", [D, T], F32, kind="Internal").ap()
    xb = nc.dram_tensor("xb_i", [D, T], F32, kind="Internal").ap()
    xc = nc.dram_tensor("xc_i", [D, T], F32, kind="Internal").ap()
    def decl(prefix, items):
        dm, pre = {}, []
        for name, shape in items:
            w = nc.dram_tensor(prefix + name, list(shape), F32, kind="ExternalInput").ap()
            wb = nc.dram_tensor(prefix + name + "_bf", list(shape), BF16, kind="Internal").ap()
            dm[name + "_bf"] = wb
            pre.append((wb, w))
        return dm, pre
    rw_dm, rw_pre = decl("r_", [("w_rkv", [3 * D, D]), ("w1", [D, 96]), ("w2", [96, D]), ("a1", [D, 96]), ("a2", [96, D]),
                                ("g1", [D, 256]), ("g2", [256, D]), ("w_o", [D, D])])
    f0_dm, f0_pre = decl("f0_", [("w_up", [D, FF]), ("w_dn", [FF, D])])
    f1_dm, f1_pre = decl("f1_", [("w_up", [D, FF]), ("w_dn", [FF, D])])
    a_dm, a_pre = decl("a_", [("kv_down", [D, 576]), ("kv_uk", [512, D]), ("kv_uv", [512, D]), ("w_dq", [D, 512]),
                              ("w_uq", [512, 16 * 192]), ("w_o", [D, D])])
    build_mods(env=(root, {"mods": modsI}, "m_", rw_pre))
    build_rwkv(0, env=(root, dict(rw_dm, xT=xT, mods=modsI, oT=xa), "r_", f0_pre + a_pre))
    build_mlp(0, env=(root, dict(f0_dm, xT=xa, mods=modsI, oT=xb), "f0_", f1_pre))
    build_mla(1, env=(root, dict(a_dm, xT=xb, mods=modsI, oT=xc), "a_"))
    build_mlp(1, env=(root, dict(f1_dm, xT=xc, mods=modsI, oT=oT), "f1_"))
    root.semstack.close()
    return nc


def kernel(x, c, positions, ada_w, ada_b, norm_g, mlp_up, mlp_down,
           rw_mu, rw_rkv, rw_w0, rw_w1, rw_w2, rw_a0, rw_a1, rw_a2, rw_g1, rw_g2,
           rw_kk, rw_ka, rw_rk, rw_lnx, rw_o,
           mla_dq, mla_qnorm, mla_uq, mla_o,
           kv_in_g, kv_down, kv_norm, kv_uk, kv_uv):
    f32 = np.float32
    A = lambda a: np.ascontiguousarray(np.asarray(a))
    x = A(x).astype(f32, copy=False)
    B = x.shape[0]
    cc = A(c)
    pos = A(positions).astype(np.int32, copy=False)
    vl = [A(rw_mu)[0][j] for j in range(6)] + [A(rw_w0)[0], A(rw_a0)[0], A(rw_kk)[0], A(rw_ka)[0], A(rw_rk)[0].reshape(-1),
                                              A(rw_lnx)[0][0], A(rw_lnx)[0][1]]
    inv = (1.0 / (10000.0 ** (np.arange(0, 64, 2, dtype=np.float32) / 64))).astype(f32)
    shared = {
        "m_ada_w": A(ada_w),
        "m_ada_b_pd": np.ascontiguousarray(A(ada_b).reshape(2, 96, 128).transpose(2, 0, 1)),
        "m_norm_g_pd": np.ascontiguousarray(A(norm_g).reshape(2, 4, 16, 128).transpose(3, 0, 1, 2)),
        "r_rw_vec": np.ascontiguousarray(np.stack([_pd(v) for v in vl], 1)).astype(f32),
        "r_w_rkv": A(rw_rkv)[0].reshape(3 * D, D), "r_w1": A(rw_w1)[0], "r_w2": A(rw_w2)[0], "r_a1": A(rw_a1)[0],
        "r_a2": A(rw_a2)[0], "r_g1": A(rw_g1)[0], "r_g2": A(rw_g2)[0], "r_w_o": A(rw_o)[0],
        "f0_w_up": A(mlp_up)[0], "f0_w_dn": A(mlp_down)[0], "f1_w_up": A(mlp_up)[1], "f1_w_dn": A(mlp_down)[1],
        "a_ropec": np.ascontiguousarray(np.stack([np.concatenate([inv, inv]), np.concatenate([-np.ones(32), np.ones(32)])], 1)).astype(f32),
        "a_mla_vec": np.ascontiguousarray(np.concatenate([_pd(kv_in_g), _pd(kv_norm, 4), _pd(A(mla_qnorm)[0], 4)], 1)).astype(f32),
        "a_kv_down": A(kv_down), "a_kv_uk": A(kv_uk).reshape(512, D), "a_kv_uv": A(kv_uv).reshape(512, D),
        "a_w_dq": A(mla_dq)[0], "a_w_uq": A(mla_uq)[0].reshape(512, 16 * 192), "a_w_o": A(mla_o)[0],
    }
    maps = [dict(shared, xT=np.ascontiguousarray(x[b].T), m_c_pd=_pd(cc[b]),
                 a_posr=np.ascontiguousarray(np.broadcast_to(pos[b][None, :], (64, T)))) for b in range(B)]
    r = _run(build_fused(), maps)
    return np.stack([np.ascontiguousarray(r[b]["oT"].T) for b in range(B)]).astype(f32, copy=False)
```

```python
import numpy as np
import concourse.bass as bass
import concourse.mybir as mybir
from concourse.bass_utils import run_bass_kernel_spmd
from contextlib import ExitStack

F32 = mybir.dt.float32
BF16 = mybir.dt.bfloat16
F32R = mybir.dt.float32r
MODS_DT = F32R
CHAIN_DT = F32
I32 = mybir.dt.int32
ALU = mybir.AluOpType
AF = mybir.ActivationFunctionType

D = 2048
T = 2048
ND = 16
FF = 8192
NCORES = 8
EPS = 1e-6

EPOCH = 30000
N_DMA_SEMS = 6
SAME_ENGINE_SYNC = False


class Buf:
    __slots__ = ("w", "r")

    def __init__(self):
        self.w = None
        self.r = {}


class Prog:
    ENGS = ("pe", "dve", "act", "pool", "sp")
    NSEM = 0

    def __init__(self, nc, semstack=None):
        self.nc = nc
        self.semstack = semstack
        self.q = {e: [] for e in self.ENGS}
        self.cnt = {e: 0 for e in self.ENGS}
        self.seen = {e: {} for e in self.ENGS}
        self.dma_cnt = {}
        self.dma_rr = {e: 0 for e in self.ENGS}
        self.keys = set()

    def _wait(self, eng, key, val):
        if self.seen[eng].get(key, 0) >= val:
            return
        self.seen[eng][key] = val
        self.q[eng].append(("wait", key, val))

    def _deps(self, eng, reads, writes):
        deps = {}
        for b in reads:
            if b.w is not None:
                k, v = b.w
                if deps.get(k, 0) < v:
                    deps[k] = v
        for b in writes:
            if b.w is not None:
                k, v = b.w
                if deps.get(k, 0) < v:
                    deps[k] = v
            for k, v in b.r.items():
                if deps.get(k, 0) < v:
                    deps[k] = v
        for k, v in deps.items():
            if k[0] == "E" and k[1] == eng and (eng == "pe" or not SAME_ENGINE_SYNC):
                continue
            self._wait(eng, k, v)

    def _mark(self, tok, reads, writes):
        k, v = tok
        for b in reads:
            if b.r.get(k, 0) < v:
                b.r[k] = v
        for b in writes:
            b.w = tok
            b.r = {}

    def op(self, eng, fn, reads=(), writes=()):
        self._deps(eng, reads, writes)
        n = self.cnt[eng]
        self.cnt[eng] = n + 1
        key = ("E", eng, n // EPOCH)
        self.keys.add(key)
        self.q[eng].append(("op", fn, key))
        self._mark((key, n % EPOCH + 1), reads, writes)

    def dma(self, qeng, out, in_, reads=(), writes=()):
        self._deps(qeng, reads, writes)
        s = self.dma_rr[qeng]
        self.dma_rr[qeng] = (s + 1) % N_DMA_SEMS
        gen = 0
        while self.dma_cnt.get(("D", qeng, s, gen), 0) + 16 > EPOCH:
            gen += 1
        key = ("D", qeng, s, gen)
        prev = self.dma_cnt.get(key, 0)
        if prev > 0:
            self._wait(qeng, key, prev)
        elif gen > 0:
            pk = ("D", qeng, s, gen - 1)
            self._wait(qeng, pk, self.dma_cnt[pk])
        self.dma_cnt[key] = prev + 16
        self.keys.add(key)
        self.q[qeng].append(("dma", (out, in_), key))
        self._mark((key, prev + 16), reads, writes)

    def barrier(self):
        toks = []
        for e in self.ENGS:
            n = self.cnt[e]
            if n > 0:
                toks.append((("E", e, (n - 1) // EPOCH), (n - 1) % EPOCH + 1))
        toks += list(self.dma_cnt.items())
        for e in self.ENGS:
            for key, v in toks:
                if key[0] == "E" and key[1] == e:
                    continue
                self._wait(e, key, v)

    def finish(self, eng="sp"):
        for key, v in list(self.dma_cnt.items()):
            self._wait(eng, key, v)

    def emit(self):
        nc = self.nc
        engmap = {"pe": "tensor", "dve": "vector", "act": "scalar", "pool": "gpsimd", "sp": "sync"}
        with ExitStack() as st:
            sems = {}
            semst = self.semstack if self.semstack is not None else st
            for i, key in enumerate(sorted(self.keys, key=str)):
                Prog.NSEM += 1
                sems[key] = semst.enter_context(nc.semaphore("s%d" % Prog.NSEM))
            block = st.enter_context(nc.Block())
            for e in self.ENGS:
                items = self.q[e]
                if not items:
                    continue

                def body(eng, items=items):
                    for it in items:
                        if it[0] == "wait":
                            eng.wait_ge(sems[it[1]], it[2])
                        elif it[0] == "op":
                            it[1](eng).then_inc(sems[it[2]], 1)
                        else:
                            eng.dma_start(out=it[1][0], in_=it[1][1]).then_inc(sems[it[2]], 16)

                getattr(block, engmap[e])(body)


class Ctx:
    NT = 0

    def __init__(self, name, env=None):
        if env is None:
            self.nc = bass.Bass("TRN2", target_bir_lowering=False)
            self.semstack = None
            self.dmap = {}
            self.prefix = ""
            self.pre = []
        else:
            root, self.dmap, self.prefix = env[:3]
            self.pre = env[3] if len(env) > 3 else []
            self.nc = root.nc
            self.semstack = root.semstack
        self.P = Prog(self.nc, self.semstack)
        self.st = ExitStack()
        self.n = 0
        self.psl = []
        self.psi = 0
        self.rot = {}
        self.rotw = 512

    def dram(self, name, shape, dt, kind):
        if name in self.dmap:
            return self.dmap[name]
        return self.nc.dram_tensor(self.prefix + name, list(shape), dt, kind=kind).ap()

    def sb(self, shape, dt):
        Ctx.NT += 1
        return self.st.enter_context(self.nc.sbuf_tensor("t%d" % Ctx.NT, list(shape), dt))

    def init_psum(self, nf32=8):
        for i in range(nf32):
            Ctx.NT += 1
            t = self.st.enter_context(self.nc.psum_tensor("ps%d" % Ctx.NT, [128, 512], F32))
            self.psl.append((t, Buf()))

    def ps(self):
        r = self.psl[self.psi]
        self.psi = (self.psi + 1) % len(self.psl)
        return r

    def rotbuf(self, key, shape, dt, n=2):
        if key not in self.rot:
            self.rot[key] = [[(self.sb(shape, dt), Buf()) for _ in range(n)], 0]
        lst, i = self.rot[key]
        self.rot[key][1] = (i + 1) % len(lst)
        return lst[i]

    def do_pre(self):
        for (dst, src) in self.pre:
            cast_dma(self, dst, src)

    def close(self):
        self.P.finish("sp")
        self.P.emit()
        self.st.close()


class WT:
    def __init__(self, ap, buf):
        self.ap = ap
        self.buf = buf


def cast_dma(k, dst, src, buf=None, max_bytes=8 << 20):
    rows, cols = src.shape[0], src.shape[1]
    step = max(1, min(rows, max_bytes // (cols * 4)))
    for r0 in range(0, rows, step):
        r1 = min(rows, r0 + step)
        k.P.dma("pool", dst[r0:r1, :], src[r0:r1, :], writes=[buf] if buf is not None else [])


def wsrc(k, name, shape):
    if name + "_bf" in k.dmap:
        return WT(k.dmap[name + "_bf"], Buf())
    w = k.dram(name, shape, F32, "ExternalInput")
    wb = k.nc.dram_tensor(k.prefix + name + "_bf", list(shape), BF16, kind="Internal").ap()
    b = Buf()
    cast_dma(k, wb, w, b)
    return WT(wb, b)


class WLoader:
    def __init__(self, k, nk=16, ncols=256, nbf=3):
        self.k = k
        self.wbf = [(k.sb([128, nk, ncols], BF16), Buf()) for _ in range(nbf)]
        self.j = 0

    def load(self, W, r0, nk, c0, ncols, pp=128):
        P = self.k.P
        wb, wbb = self.wbf[self.j]
        self.j = (self.j + 1) % len(self.wbf)
        src = W.ap[r0:r0 + nk * pp, c0:c0 + ncols].rearrange("(k p) c -> p k c", p=pp)
        P.dma("sp", wb[:pp, :nk, :ncols], src, reads=[W.buf], writes=[wbb])
        return wb, wbb


def make_consts(k):
    P = k.P
    c = {}
    ones = k.sb([128, 128], BF16)
    c["ones"] = ones
    c["onesb"] = Buf()
    P.op("pool", lambda e: e.memset(ones[:], 1.0), writes=[c["onesb"]])
    eps = k.sb([128, 3], F32)
    c["eps"] = eps
    P.op("pool", lambda e: e.memset(eps[:, 0:1], EPS), writes=[c["onesb"]])
    P.op("pool", lambda e: e.memset(eps[:, 1:2], 64e-5), writes=[c["onesb"]])
    P.op("pool", lambda e: e.memset(eps[:, 2:3], 1.0), writes=[c["onesb"]])
    return c


def rms_rstd(k, c, X, Xb, TT, scale_div=D, epsap=None, ntile=ND):
    P = k.P
    if epsap is None:
        epsap = c["eps"][:, 0:1]
    ps, psb = k.ps()
    for dt in range(ntile):
        sq, sqb = k.rotbuf("sq", [128, k.rotw], BF16, 3)
        P.op("act", lambda e, o=sq[:, :TT], i=X[:, dt, :]: e.activation(out=o, in_=i, func=AF.Square),
             reads=[Xb], writes=[sqb])
        P.op("pe", lambda e, o=ps[:, :TT], r=sq[:, :TT], s=(dt == 0), t=(dt == ntile - 1):
             e.matmul(o, lhsT=c["ones"][:], rhs=r, start=s, stop=t), reads=[sqb, c["onesb"]], writes=[psb])
    rstd, rb = k.rotbuf("rstd", [128, k.rotw], F32, 2)
    P.op("act", lambda e, o=rstd[:, :TT], i=ps[:, :TT]: e.activation(
        out=o, in_=i, func=AF.Sqrt, bias=epsap, scale=1.0 / scale_div), reads=[psb, c["onesb"]], writes=[rb])
    P.op("dve", lambda e, o=rstd[:, :TT]: e.reciprocal(out=o, in_=o), reads=[rb], writes=[rb])
    return rstd, rb


def norm_mod(k, X, Xb, rstd, rb, A, Sh, mb, H, Hb, TT, col0=0):
    P = k.P
    for dt in range(ND):
        tmp, tb = k.rotbuf("nm_tmp", [128, k.rotw], F32, 2)
        P.op("dve", lambda e, o=tmp[:, :TT], i=X[:, dt, :], s=A[:, dt:dt + 1], r=rstd[:, :TT]:
             e.scalar_tensor_tensor(out=o, in0=i, scalar=s, in1=r, op0=ALU.mult, op1=ALU.mult),
             reads=[Xb, rb, mb], writes=[tb])
        P.op("act", lambda e, o=H[:, dt, col0:col0 + TT], i=tmp[:, :TT], s=Sh[:, dt:dt + 1]:
             e.activation(out=o, in_=i, func=AF.Identity, bias=s, scale=1.0),
             reads=[tb, mb], writes=[Hb])


def post_residual(k, c, X, Xb, Y, Yb, G, mb, TT):
    P = k.P
    rstd, rb = rms_rstd(k, c, Y, Yb, TT)
    for dt in range(ND):
        tmp, tb = k.rotbuf("nm_tmp", [128, k.rotw], F32, 2)
        P.op("dve", lambda e, o=tmp[:, :TT], i=Y[:, dt, :], s=G[:, dt:dt + 1], r=rstd[:, :TT]:
             e.scalar_tensor_tensor(out=o, in0=i, scalar=s, in1=r, op0=ALU.mult, op1=ALU.mult),
             reads=[Yb, rb, mb], writes=[tb])
        P.op("pool" if dt % 2 else "dve", lambda e, o=X[:, dt, :], i=tmp[:, :TT]: e.tensor_tensor(out=o, in0=o, in1=i, op=ALU.add),
             reads=[tb], writes=[Xb])


def build_mods(env=None):
    k = Ctx("mods", env)
    nc, P = k.nc, k.P
    k.do_pre()
    c_pd = k.dram("c_pd", [128, 16], F32, "ExternalInput")
    ada_w = k.dram("ada_w", [2, D, 6 * D], F32, "ExternalInput")
    ada_b = k.dram("ada_b_pd", [128, 2, 96], F32, "ExternalInput")
    ng = k.dram("norm_g_pd", [128, 2, 4, 16], F32, "ExternalInput")
    mods = k.dram("mods", [128, 2 * 96], F32, "ExternalOutput")
    k.init_psum(2)
    cin = k.sb([128, 16], F32)
    cact = k.sb([128, 16], F32)
    abt = k.sb([128, 2, 96], F32)
    ngt = k.sb([128, 2, 4, 16], F32)
    raw = k.sb([128, 2, 96], F32)
    outt = k.sb([128, 2, 6, 16], F32)
    cb_, sm_ = Buf(), Buf()
    P.dma("sp", cin[:], c_pd, writes=[cb_])
    P.dma("sp", abt[:], ada_b, writes=[sm_])
    P.dma("sp", ngt[:], ng, writes=[sm_])
    P.op("act", lambda e: e.activation(out=cact[:].bitcast(MODS_DT), in_=cin[:], func=AF.Silu), reads=[cb_], writes=[cb_])
    stg = [(k.sb([128, 16, 512], F32), Buf()) for _ in range(3)]
    rawb, ob = Buf(), Buf()
    one1 = k.sb([1, 1], F32)
    row = k.sb([1, 6 * D], F32)
    rowb = Buf()
    P.op("dve", lambda e: e.memset(one1[:], 1.0), writes=[cb_])
    k.psl = k.psl + [(k.st.enter_context(nc.psum_tensor("psx%d" % i, [128, 512], F32)), Buf()) for i in range(4)]
    for l in range(2):
        for cb in range(24):
            st, stb = stg[(l * 24 + cb) % 3]
            src = ada_w[l, :, cb * 512:(cb + 1) * 512].rearrange("(k p) c -> p k c", p=128)
            if MODS_DT == F32:
                P.dma("sp", st[:], src, writes=[stb])
            else:
                P.dma("pool", st[:].bitcast(MODS_DT), src, writes=[stb])
            psr, psrb = k.ps()
            for dt in range(16):
                P.op("pe", lambda e, o=psr[0:1, :], w=cact[:, dt:dt + 1].bitcast(MODS_DT), r=st[:, dt, :].bitcast(MODS_DT), s=(dt == 0), t=(dt == 15):
                     e.matmul(o, lhsT=w, rhs=r, start=s, stop=t), reads=[stb, cb_], writes=[psrb])
            P.op("act" if cb % 2 else "dve",
                 (lambda e, o=row[0:1, cb * 512:(cb + 1) * 512], i=psr[0:1, :]: e.activation(out=o, in_=i, func=AF.Copy)) if cb % 2 else
                 (lambda e, o=row[0:1, cb * 512:(cb + 1) * 512], i=psr[0:1, :]: e.tensor_copy(out=o, in_=i)),
                 reads=[psrb], writes=[rowb])
        ps, psb = k.ps()
        for e_ in range(96):
            P.op("pe", lambda e, o=ps[:, e_:e_ + 1], w=row[0:1, e_ * 128:(e_ + 1) * 128]: e.matmul(o, lhsT=w, rhs=one1[0:1, 0:1], start=True, stop=True),
                 reads=[rowb, cb_], writes=[psb])
        P.op("dve", lambda e, o=raw[:, l, :], i=ps[:, 0:96], b=abt[:, l, :]: e.tensor_tensor(out=o, in0=i, in1=b, op=ALU.add),
             reads=[psb, sm_], writes=[rawb])
        for half, (gpre, gpost) in enumerate(((0, 1), (2, 3))):
            b0 = half * 3
            P.op("dve", lambda e, o=outt[:, l, b0 + 0, :], i=raw[:, l, (b0 + 1) * 16:(b0 + 2) * 16], g=ngt[:, l, gpre, :]:
                 e.scalar_tensor_tensor(out=o, in0=i, scalar=1.0, in1=g, op0=ALU.add, op1=ALU.mult),
                 reads=[rawb, sm_], writes=[ob])
            P.op("dve", lambda e, o=outt[:, l, b0 + 1, :], i=raw[:, l, (b0 + 0) * 16:(b0 + 1) * 16]:
                 e.tensor_copy(out=o, in_=i), reads=[rawb], writes=[ob])
            P.op("dve", lambda e, o=outt[:, l, b0 + 2, :], i=raw[:, l, (b0 + 2) * 16:(b0 + 3) * 16], g=ngt[:, l, gpost, :]:
                 e.tensor_tensor(out=o, in0=i, in1=g, op=ALU.mult), reads=[rawb, sm_], writes=[ob])
    P.dma("sp", mods, outt[:].rearrange("p l j d -> p (l j d)"), reads=[ob])
    k.close()
    return nc


def build_mlp(l, env=None):
    k = Ctx("mlp", env)
    nc, P = k.nc, k.P
    TT = 512
    xT = k.dram("xT", [D, T], F32, "ExternalInput")
    modsd = k.dram("mods", [128, 192], F32, "ExternalInput")
    k.do_pre()
    wup = wsrc(k, "w_up", [D, FF])
    wdn = wsrc(k, "w_dn", [FF, D])
    oT = k.dram("oT", [D, T], F32, "ExternalOutput")
    k.init_psum(8)
    c = make_consts(k)
    mt = k.sb([128, 2, 6, 16], F32)
    mb = Buf()
    P.dma("sp", mt[:].rearrange("p l j d -> p (l j d)"), modsd, writes=[mb])
    A, Sh, G = mt[:, l, 3, :], mt[:, l, 4, :], mt[:, l, 5, :]
    Xs = [(k.sb([128, ND, TT], F32), Buf()) for _ in range(2)]
    Hs_ = [(k.sb([128, ND, TT], BF16), Buf()) for _ in range(2)]
    U = k.sb([128, 32, TT], BF16)
    Y = k.sb([128, ND, TT], F32)
    Ub, Yb = Buf(), Buf()
    wl = WLoader(k, 16, 256, 3)
    xT3 = xT.rearrange("(k p) t -> p k t", p=128)
    oT3 = oT.rearrange("(k p) t -> p k t", p=128)
    NT_ = T // TT

    def load_norm(tt):
        X, Xb = Xs[tt % 2]
        H, Hb = Hs_[tt % 2]
        P.dma("sp", X[:], xT3[:, :, tt * TT:(tt + 1) * TT], writes=[Xb])
        rstd, rb = rms_rstd(k, c, X, Xb, TT)
        norm_mod(k, X, Xb, rstd, rb, A, Sh, mb, H, Hb, TT)

    load_norm(0)
    for tt in range(NT_):
        X, Xb = Xs[tt % 2]
        H, Hb = Hs_[tt % 2]
        for fh in range(2):
            for fb in range(16):
                f0 = fh * 4096 + fb * 256
                wb, wbb = wl.load(wup, 0, 16, f0, 256)
                for j in range(2):
                    ps, psb = k.ps()
                    for dt in range(16):
                        P.op("pe", lambda e, o=ps[:, :TT], w=wb[:, dt, j * 128:(j + 1) * 128], r=H[:, dt, :],
                             s=(dt == 0), t=(dt == 15): e.matmul(o, lhsT=w, rhs=r, start=s, stop=t),
                             reads=[wbb, Hb], writes=[psb])
                    rl, rlb = k.rotbuf("relu", [128, 512], F32, 3)
                    P.op("act", lambda e, o=rl[:, :TT], i=ps[:, :TT]: e.activation(out=o, in_=i, func=AF.Relu),
                         reads=[psb], writes=[rlb])
                    P.op("dve", lambda e, o=U[:, fb * 2 + j, :], i=rl[:, :TT]: e.tensor_tensor(out=o, in0=i, in1=i, op=ALU.mult),
                         reads=[rlb], writes=[Ub])
            if fh == 0 and tt + 1 < NT_:
                load_norm(tt + 1)
            for db in range(8):
                pss = [k.ps(), k.ps()]
                for kb in range(2):
                    wb, wbb = wl.load(wdn, fh * 4096 + kb * 2048, 16, db * 256, 256)
                    for j in range(2):
                        ps, psb = pss[j]
                        for ft in range(16):
                            P.op("pe", lambda e, o=ps[:, :TT], w=wb[:, ft, j * 128:(j + 1) * 128], r=U[:, kb * 16 + ft, :],
                                 s=(kb == 0 and ft == 0), t=(kb == 1 and ft == 15): e.matmul(o, lhsT=w, rhs=r, start=s, stop=t),
                                 reads=[wbb, Ub], writes=[psb])
                for j in range(2):
                    ps, psb = pss[j]
                    if fh == 0:
                        P.op("act", lambda e, o=Y[:, db * 2 + j, :], i=ps[:, :TT]: e.activation(out=o, in_=i, func=AF.Copy),
                             reads=[psb], writes=[Yb])
                    else:
                        P.op("dve", lambda e, o=Y[:, db * 2 + j, :], i=ps[:, :TT]: e.tensor_tensor(out=o, in0=o, in1=i, op=ALU.add),
                             reads=[psb], writes=[Yb])
        post_residual(k, c, X, Xb, Y, Yb, G, mb, TT)
        P.dma("sp", oT3[:, :, tt * TT:(tt + 1) * TT], X[:], reads=[Xb])
    k.close()
    return nc


def load_swap(wl, W, nk, c0):
    P = wl.k.P
    wb, wbb = wl.wbf[wl.j]
    wl.j = (wl.j + 1) % len(wl.wbf)
    for (a, b_) in ((0, 32), (32, 0)):
        src = W.ap[0:nk * 128, c0 + b_:c0 + b_ + 32].rearrange("(k p) c -> p k c", p=128)
        P.dma("sp", wb[:, :nk, a:a + 32], src, reads=[W.buf], writes=[wbb])
    return wb, wbb


def angle_reduce(k, ang, kf, ki, ab):
    import math
    P = k.P
    P.op("dve", lambda e: e.tensor_scalar(out=kf, in0=ang, scalar1=1.0 / (2 * math.pi), scalar2=None, op0=ALU.mult), reads=[ab], writes=[ab])
    P.op("dve", lambda e: e.tensor_copy(out=ki, in_=kf), reads=[ab], writes=[ab])
    P.op("dve", lambda e: e.tensor_copy(out=kf, in_=ki), reads=[ab], writes=[ab])
    P.op("dve", lambda e: e.scalar_tensor_tensor(out=ang, in0=kf, scalar=-2 * math.pi, in1=ang, op0=ALU.mult, op1=ALU.add), reads=[ab], writes=[ab])
    P.op("dve", lambda e: e.tensor_scalar(out=kf, in0=ang, scalar1=math.pi, scalar2=-2 * math.pi, op0=ALU.is_gt, op1=ALU.mult), reads=[ab], writes=[ab])
    P.op("dve", lambda e: e.tensor_tensor(out=ang, in0=ang, in1=kf, op=ALU.add), reads=[ab], writes=[ab])
    P.op("dve", lambda e: e.tensor_scalar(out=kf, in0=ang, scalar1=-math.pi, scalar2=2 * math.pi, op0=ALU.is_lt, op1=ALU.mult), reads=[ab], writes=[ab])
    P.op("dve", lambda e: e.tensor_tensor(out=ang, in0=ang, in1=kf, op=ALU.add), reads=[ab], writes=[ab])


def build_mla(l=1, env=None):
    import math
    k = Ctx("mla", env)
    nc, P = k.nc, k.P
    TT = 512
    NTT = T // TT
    xT = k.dram("xT", [D, T], F32, "ExternalInput")
    modsd = k.dram("mods", [128, 192], F32, "ExternalInput")
    posr = k.dram("posr", [64, T], I32, "ExternalInput")
    ropec = k.dram("ropec", [64, 2], F32, "ExternalInput")
    vec = k.dram("mla_vec", [128, 24], F32, "ExternalInput")
    k.do_pre()
    kvd = wsrc(k, "kv_down", [D, 576])
    wuk = wsrc(k, "kv_uk", [512, D])
    wuv = wsrc(k, "kv_uv", [512, D])
    wdq = wsrc(k, "w_dq", [D, 512])
    wuq = wsrc(k, "w_uq", [512, 16 * 192])
    wo = wsrc(k, "w_o", [D, D])
    oT = k.dram("oT", [D, T], F32, "ExternalOutput")
    otd = k.dram("ot_scratch", [16, 128, T], BF16, "Internal")
    k.init_psum(8)
    oacc = k.psl[4:]
    k.psl = k.psl[:4]
    c = make_consts(k)
    mt = k.sb([128, 2, 6, 16], F32)
    vt = k.sb([128, 24], F32)
    zer = k.sb([128, 16], F32)
    mb = Buf()
    P.dma("sp", mt[:].rearrange("p l j d -> p (l j d)"), modsd, writes=[mb])
    P.dma("sp", vt[:], vec, writes=[mb])
    P.op("pool", lambda e: e.memset(zer[:], 0.0), writes=[mb])
    A, Sh, G = mt[:, l, 0, :], mt[:, l, 1, :], mt[:, l, 2, :]

    X = k.sb([128, ND, TT], F32)
    Y = k.sb([128, ND, TT], F32)
    Xb, Yb = Buf(), Buf()
    Yf = Y[:].rearrange("p a b -> p (a b)")
    Ybf = Yf.bitcast(BF16)
    Xbf = X[:].rearrange("p a b -> p (a b)").bitcast(BF16)
    HS = Ybf[:, 0:8192].rearrange("p (a b) -> p a b", a=ND)
    HH = Ybf[:, 8192:16384].rearrange("p (a b) -> p a b", a=ND)
    rc = k.sb([64, 2], F32)
    cos2 = k.sb([64, T], F32)
    sinS = k.sb([64, T], F32)
    csb = Buf()
    P.dma("sp", rc[:], ropec, writes=[mb])
    pi_t = Yf[:64, 0:512].bitcast(I32)
    ang = Yf[:64, 512:1024]
    tmp = Yf[:64, 1024:1536]
    kf = Yf[:64, 1536:2048]
    ki = Yf[:64, 2048:2560].bitcast(I32)
    for ch in range(4):
        t0 = ch * 512
        P.dma("sp", pi_t, posr[:, t0:t0 + 512], writes=[Yb])
        P.op("dve", lambda e: e.tensor_copy(out=ang, in_=pi_t), reads=[Yb], writes=[Yb])
        P.op("dve", lambda e: e.tensor_scalar(out=ang, in0=ang, scalar1=rc[:, 0:1], scalar2=None, op0=ALU.mult), reads=[Yb, mb], writes=[Yb])
        P.op("dve", lambda e: e.tensor_scalar(out=tmp, in0=ang, scalar1=math.pi / 2, scalar2=None, op0=ALU.add), reads=[Yb], writes=[Yb])
        angle_reduce(k, tmp, kf, ki, Yb)
        P.op("act", lambda e, o=cos2[:, t0:t0 + 512]: e.activation(out=o, in_=tmp, func=AF.Sin), reads=[Yb], writes=[csb])
        angle_reduce(k, ang, kf, ki, Yb)
        P.op("act", lambda e, o=sinS[:, t0:t0 + 512]: e.activation(out=o, in_=ang, func=AF.Sin), reads=[Yb], writes=[csb])
        P.op("dve", lambda e, o=sinS[:, t0:t0 + 512]: e.tensor_scalar(out=o, in0=o, scalar1=rc[:, 1:2], scalar2=None, op0=ALU.mult), reads=[csb, mb], writes=[csb])

    CKQ = k.sb([128, 8, TT], F32)
    CK = CKQ[:, 0:4, :]
    CQ = CKQ[:, 4:8, :]
    CKb, CQb = Buf(), Buf()
    CKN = k.sb([128, 4, T], BF16)
    CQN = k.sb([128, 4, T], BF16)
    KR = k.sb([128, T], BF16)
    CKNb, CQNb, KRb = Buf(), Buf(), Buf()
    P.op("pool", lambda e: e.memset(KR[:], 0.0), writes=[KRb])
    wl = WLoader(k, 16, 128, 4)
    xT3 = xT.rearrange("(k p) t -> p k t", p=128)
    oT3 = oT.rearrange("(k p) t -> p k t", p=128)

    def rope_out(ps1, ps1b, ps2, ps2b, dst, dstb, t0):
        t1, t1b = k.rotbuf("rp1", [64, 512], F32, 1)
        t2, t2b = k.rotbuf("rp2", [64, 512], F32, 1)
        P.op("dve", lambda e: e.tensor_tensor(out=t1[:], in0=ps1[:64, :TT], in1=cos2[:, t0:t0 + TT], op=ALU.mult), reads=[ps1b, csb], writes=[t1b])
        P.op("dve", lambda e: e.tensor_tensor(out=t2[:], in0=ps2[:64, :TT], in1=sinS[:, t0:t0 + TT], op=ALU.mult), reads=[ps2b, csb], writes=[t2b])
        P.op("pool", lambda e: e.tensor_tensor(out=dst[:64, t0:t0 + TT], in0=t1[:], in1=t2[:], op=ALU.add), reads=[t1b, t2b], writes=[dstb])

    for tt in range(NTT):
        t0 = tt * TT
        P.dma("sp", X[:], xT3[:, :, t0:t0 + TT], writes=[Xb])
        rstd, rb = rms_rstd(k, c, X, Xb, TT)
        norm_mod(k, X, Xb, rstd, rb, vt[:, 0:16], zer, mb, HS, Yb, TT)
        norm_mod(k, X, Xb, rstd, rb, A, Sh, mb, HH, Yb, TT)
        for (W, src, dst, dstb) in ((kvd, HS, CK, CKb), (wdq, HH, CQ, CQb)):
            for cb in range(4):
                wb, wbb = wl.load(W, 0, 16, cb * 128, 128)
                ps, psb = k.ps()
                for dt in range(16):
                    P.op("pe", lambda e, o=ps[:, :TT], w=wb[:, dt, :], r=src[:, dt, :], s=(dt == 0), t=(dt == 15):
                         e.matmul(o, lhsT=w, rhs=r, start=s, stop=t), reads=[wbb, Yb], writes=[psb])
                P.op("act", lambda e, o=dst[:, cb, :], i=ps[:, :TT]: e.activation(out=o, in_=i, func=AF.Copy), reads=[psb], writes=[dstb])
        pss = []
        for sw in range(2):
            if sw == 0:
                wb, wbb = wl.load(kvd, 0, 16, 512, 64)
            else:
                wb, wbb = load_swap(wl, kvd, 16, 512)
            ps, psb = k.ps()
            for dt in range(16):
                P.op("pe", lambda e, o=ps[:64, :TT], w=wb[:, dt, 0:64], r=HS[:, dt, :], s=(dt == 0), t=(dt == 15):
                     e.matmul(o, lhsT=w, rhs=r, start=s, stop=t), reads=[wbb, Yb], writes=[psb])
            pss.append((ps, psb))
        rope_out(pss[0][0], pss[0][1], pss[1][0], pss[1][1], KR, KRb, t0)
        for (src, srcb, dst, dstb, v0) in ((CK, CKb, CKN, CKNb, 16), (CQ, CQb, CQN, CQNb, 20)):
            rs, rsb = rms_rstd(k, c, src, srcb, TT, scale_div=512, ntile=4)
            for ct in range(4):
                P.op("dve", lambda e, o=dst[:, ct, t0:t0 + TT], i=src[:, ct, :], s=vt[:, v0 + ct:v0 + ct + 1], r=rs[:, :TT]:
                     e.scalar_tensor_tensor(out=o, in0=i, scalar=s, in1=r, op0=ALU.mult, op1=ALU.mult),
                     reads=[srcb, rsb, mb], writes=[dstb])

    P.barrier()
    tri = k.sb([128, 128], BF16)
    trib = Buf()
    P.op("pool", lambda e: e.memset(tri[:], 1.0), writes=[trib])
    P.op("pool", lambda e: e.affine_select(out=tri[:], in_=tri[:], pattern=[[1, 128]], compare_op=ALU.is_ge, fill=0.0,
                                           base=0, channel_multiplier=-1), reads=[trib], writes=[trib])
    wl2 = WLoader(k, 4, 128, 8)
    scale = 192.0 ** -0.5
    hb = []
    for reg in (Ybf, Xbf):
        hb.append(dict(KN=reg[:, 0:2048], QN=reg[:, 2048:4096], QR=reg[:, 4096:6144], OH=reg[:, 6144:8192],
                       VH=reg[:, 8192:10240].rearrange("p (a b) -> p a b", a=16),
                       KNb=Buf(), QNb=Buf(), QRb=Buf(), OHb=Buf(), VHb=Buf()))
    for s_ in hb:
        P.op("pool", lambda e, o=s_["QR"]: e.memset(o, 0.0), writes=[s_["QRb"]])
    for h in range(16):
        s_ = hb[h % 2]
        KN, QN, QR, OH, VH = s_["KN"], s_["QN"], s_["QR"], s_["OH"], s_["VH"]
        KNb, QNb, QRb, OHb, VHb = s_["KNb"], s_["QNb"], s_["QRb"], s_["OHb"], s_["VHb"]
        wk, wkb = wl2.load(wuk, 0, 4, h * 128, 128)
        wq, wqb = wl2.load(wuq, 0, 4, h * 192, 128)
        wv, wvb = wl2.load(wuv, 0, 4, h * 128, 128)
        wr, wrb = wl2.load(wuq, 0, 4, h * 192 + 128, 64)
        ws, wsb = load_swap(wl2, wuq, 4, h * 192 + 128)
        for tq in range(NTT):
            t0 = tq * TT
            for (w_, wb_, src, srcb, dst, dstb) in ((wk, wkb, CKN, CKNb, KN, KNb), (wq, wqb, CQN, CQNb, QN, QNb)):
                ps, psb = k.ps()
                for ct in range(4):
                    P.op("pe", lambda e, o=ps[:, :TT], w=w_[:, ct, :], r=src[:, ct, t0:t0 + TT], s=(ct == 0), t=(ct == 3):
                         e.matmul(o, lhsT=w, rhs=r, start=s, stop=t), reads=[wb_, srcb], writes=[psb])
                P.op("act", lambda e, o=dst[:, t0:t0 + TT], i=ps[:, :TT]: e.activation(out=o, in_=i, func=AF.Copy), reads=[psb], writes=[dstb])
            pss = []
            for (w_, wb_) in ((wr, wrb), (ws, wsb)):
                ps, psb = k.ps()
                for ct in range(4):
                    P.op("pe", lambda e, o=ps[:64, :TT], w=w_[:, ct, 0:64], r=CQN[:, ct, t0:t0 + TT], s=(ct == 0), t=(ct == 3):
                         e.matmul(o, lhsT=w, rhs=r, start=s, stop=t), reads=[wb_, CQNb], writes=[psb])
                pss.append((ps, psb))
            rope_out(pss[0][0], pss[0][1], pss[1][0], pss[1][1], QR, QRb, t0)
        for tk4 in range(4):
            ps, psb = k.ps()
            for i in range(4):
                tk = tk4 * 4 + i
                for ct in range(4):
                    P.op("pe", lambda e, o=ps[:, i * 128:(i + 1) * 128], w=CKN[:, ct, tk * 128:(tk + 1) * 128], r=wv[:, ct, :], s=(ct == 0), t=(ct == 3):
                         e.matmul(o, lhsT=w, rhs=r, start=s, stop=t), reads=[wvb, CKNb], writes=[psb])
            P.op("act", lambda e, o=VH[:, tk4 * 4:tk4 * 4 + 4, :], i=ps[:, :].rearrange("p (a b) -> p a b", a=4):
                 e.activation(out=o, in_=i, func=AF.Copy), reads=[psb], writes=[VHb])
        for qt in range(NTT):
            oa, oab = oacc[(qt % 2) * 2]
            da, dab = oacc[(qt % 2) * 2 + 1]
            nk_ = 4 * (qt + 1)
            def score(kt):
                off = max(0, (kt - 4 * qt) * 128)
                q0 = qt * TT + off
                q1 = (qt + 1) * TT
                sp_, spb = k.ps()
                P.op("pe", lambda e, o=sp_[:, off:TT], w=KN[:, kt * 128:(kt + 1) * 128], r=QN[:, q0:q1]:
                     e.matmul(o, lhsT=w, rhs=r, start=True, stop=False), reads=[KNb, QNb], writes=[spb])
                P.op("pe", lambda e, o=sp_[:, off:TT], w=KR[:, kt * 128:(kt + 1) * 128], r=QR[:, q0:q1]:
                     e.matmul(o, lhsT=w, rhs=r, start=False, stop=True), reads=[KRb, QRb], writes=[spb])
                PT, PTb = k.rotbuf("PT", [128, TT], BF16, 4)
                P.op("act", lambda e, o=PT[:, off:TT], i=sp_[:, off:TT]: e.activation(out=o, in_=i, func=AF.Exp, scale=scale),
                     reads=[spb], writes=[PTb])
                if kt >= 4 * qt:
                    P.op("pool", lambda e, o=PT[:, off:off + 128]: e.tensor_tensor(out=o, in0=o, in1=tri[:], op=ALU.mult),
                         reads=[PTb, trib], writes=[PTb])
                return (kt, off, PT, PTb)

            def pv(st_):
                kt, off, PT, PTb = st_
                P.op("pe", lambda e, o=oa[:, off:TT], w=VH[:, kt, :], r=PT[:, off:TT], s=(kt == 0), t=(kt == nk_ - 1):
                     e.matmul(o, lhsT=w, rhs=r, start=s, stop=t), reads=[VHb, PTb], writes=[oab])
                P.op("pe", lambda e, o=da[:, off:TT], r=PT[:, off:TT], s=(kt == 0), t=(kt == nk_ - 1):
                     e.matmul(o, lhsT=c["ones"][:], rhs=r, start=s, stop=t), reads=[c["onesb"], PTb], writes=[dab])

            pend = []
            for kt in range(nk_):
                pend.append(score(kt))
                if len(pend) > 2:
                    pv(pend.pop(0))
            while pend:
                pv(pend.pop(0))
            rd, rdb = k.rotbuf("rden", [128, TT], F32, 2)
            P.op("dve", lambda e, o=rd[:], i=da[:, :TT]: e.reciprocal(out=o, in_=i), reads=[dab], writes=[rdb])
            P.op("dve", lambda e, o=OH[:, qt * TT:(qt + 1) * TT], i=oa[:, :TT], r=rd[:]: e.tensor_tensor(out=o, in0=i, in1=r, op=ALU.mult),
                 reads=[oab, rdb], writes=[OHb])
        P.dma("sp", otd[h], OH, reads=[OHb])

    P.barrier()
    OTt = CKQ[:].rearrange("p a b -> p (a b)").bitcast(BF16).rearrange("p (a b) -> p a b", a=16)
    OTb = Buf()
    otd3 = otd.rearrange("h p t -> p h t")
    Xb, Yb = Buf(), Buf()
    for tt in range(NTT):
        t0 = tt * TT
        P.dma("sp", OTt, otd3[:, :, t0:t0 + TT], writes=[OTb])
        P.dma("sp", X[:], xT3[:, :, t0:t0 + TT], writes=[Xb])
        for eb in range(16):
            wb, wbb = wl.load(wo, 0, 16, eb * 128, 128)
            ps, psb = k.ps()
            for hh in range(16):
                P.op("pe", lambda e, o=ps[:, :TT], w=wb[:, hh, :], r=OTt[:, hh, :], s=(hh == 0), t=(hh == 15):
                     e.matmul(o, lhsT=w, rhs=r, start=s, stop=t), reads=[wbb, OTb], writes=[psb])
            P.op("act", lambda e, o=Y[:, eb, :], i=ps[:, :TT]: e.activation(out=o, in_=i, func=AF.Copy), reads=[psb], writes=[Yb])
        post_residual(k, c, X, Xb, Y, Yb, G, mb, TT)
        P.dma("sp", oT3[:, :, t0:t0 + TT], X[:], reads=[Xb])
    k.close()
    return nc


def build_rwkv(l=0, dbg=False, env=None):
    k = Ctx("rwkv", env)
    k.rotw = 256
    nc, P = k.nc, k.P
    TT = 256
    NTT = T // TT
    C = 64
    NCH = TT // C
    xT = k.dram("xT", [D, T], F32, "ExternalInput")
    modsd = k.dram("mods", [128, 192], F32, "ExternalInput")
    vec = k.dram("rw_vec", [128, 13, 16], F32, "ExternalInput")
    k.do_pre()
    wrkv = wsrc(k, "w_rkv", [3 * D, D])
    w1 = wsrc(k, "w1", [D, 96])
    w2 = wsrc(k, "w2", [96, D])
    a1 = wsrc(k, "a1", [D, 96])
    a2 = wsrc(k, "a2", [96, D])
    g1 = wsrc(k, "g1", [D, 256])
    g2 = wsrc(k, "g2", [256, D])
    wo = wsrc(k, "w_o", [D, D])
    oT = k.dram("oT", [D, T], F32, "ExternalOutput")
    k.init_psum(8)
    c = make_consts(k)
    mt = k.sb([128, 2, 6, 16], F32)
    vt = k.sb([128, 13, 16], F32)
    mb = Buf()
    P.dma("sp", mt[:].rearrange("p l j d -> p (l j d)"), modsd, writes=[mb])
    P.dma("sp", vt[:], vec, writes=[mb])
    A, Sh, G = mt[:, l, 0, :], mt[:, l, 1, :], mt[:, l, 2, :]
    MU, W0, A0, KKv, KA, RK, LNW, LNB = (lambda j: vt[:, j, :]), vt[:, 6, :], vt[:, 7, :], vt[:, 8, :], vt[:, 9, :], vt[:, 10, :], vt[:, 11, :], vt[:, 12, :]

    NEG = k.sb([128, 2, 16], F32)
    P.op("dve", lambda e: e.tensor_scalar(out=NEG[:], in0=vt[:, 6:8, :], scalar1=-1.0, scalar2=None, op0=ALU.mult), reads=[mb], writes=[mb])
    cb_ = Buf()
    bo16 = k.sb([128, 128], BF16)
    bo32 = k.sb([128, 128], F32)
    idn = k.sb([128, 4, 128], BF16)
    mS = k.sb([128, 4, 64], BF16)
    mI = k.sb([128, 4, 64], BF16)
    mL = k.sb([128, 4, 64], BF16)
    ones64 = k.sb([128, 64], F32)
    for t_ in (bo16, bo32):
        P.op("pool", lambda e, t_=t_: e.memset(t_[:], 0.0), writes=[cb_])
        P.op("pool", lambda e, t_=t_: e.memset(t_[0:64, 0:64], 1.0), writes=[cb_])
        P.op("pool", lambda e, t_=t_: e.memset(t_[64:128, 64:128], 1.0), writes=[cb_])
    P.op("pool", lambda e: e.memset(ones64[:], 1.0), writes=[cb_])
    rmask = k.sb([128, TT], F32)
    P.op("pool", lambda e: e.memset(rmask[:], 1.0), writes=[cb_])
    for ch_ in range(NCH):
        P.op("pool", lambda e, o=rmask[:, ch_ * C:ch_ * C + 1]: e.memset(o, 0.0), writes=[cb_])
    P.op("pool", lambda e: e.memset(idn[:], 1.0), writes=[cb_])
    P.op("pool", lambda e: e.memset(mS[:], 1.0), writes=[cb_])
    P.op("pool", lambda e: e.memset(mI[:], 1.0), writes=[cb_])
    P.op("pool", lambda e: e.memset(mL[:], 1.0), writes=[cb_])
    for g_ in range(4):
        P.op("pool", lambda e, o=idn[:, g_, :]: e.affine_select(out=o, in_=o, pattern=[[1, 128]], compare_op=ALU.is_equal, fill=0.0,
                                                               base=0, channel_multiplier=-1), reads=[cb_], writes=[cb_])
        for hf in range(2):
            sl = slice(64 * hf, 64 * hf + 64)
            P.op("pool", lambda e, o=mS[sl, g_, :]: e.affine_select(out=o, in_=o, pattern=[[1, 64]], compare_op=ALU.is_ge, fill=0.0,
                                                                    base=-1, channel_multiplier=-1), reads=[cb_], writes=[cb_])
            P.op("pool", lambda e, o=mI[sl, g_, :]: e.affine_select(out=o, in_=o, pattern=[[1, 64]], compare_op=ALU.is_ge, fill=0.0,
                                                                    base=0, channel_multiplier=-1), reads=[cb_], writes=[cb_])
            P.op("pool", lambda e, o=mL[sl, g_, :]: e.affine_select(out=o, in_=o, pattern=[[-1, 64]], compare_op=ALU.is_ge, fill=0.0,
                                                                    base=-1, channel_multiplier=1), reads=[cb_], writes=[cb_])

    RT = k.sb([128, 16, TT], BF16)
    KT = k.sb([128, 16, TT], BF16)
    BT = k.sb([128, 16, TT], BF16)
    AT = k.sb([128, 16, TT], BF16)
    VT = k.sb([128, 16, TT], BF16)
    GT = k.sb([128, 16, TT], BF16)
    BON = k.sb([128, 16, TT], BF16)
    GC = k.sb([128, 16, NCH], F32)
    RTb, KTb, BTb, ATb, VTb, GTb, BONb, GCb, YTb = (Buf() for _ in range(9))
    Hf = k.sb([128, 16, 64], F32)
    Hstk = k.sb([128, 16, 64], BF16)
    Hbd = k.sb([128, 16, 128], BF16)
    Hb_ = [Buf() for _ in range(4)]
    HL = k.sb([128, 16, 1], F32)
    HLb = Buf()
    P.op("pool", lambda e: e.memset(Hf[:], 0.0), writes=Hb_)
    P.op("pool", lambda e: e.memset(Hstk[:], 0.0), writes=Hb_)
    P.op("pool", lambda e: e.memset(Hbd[:], 0.0), writes=Hb_)
    P.op("pool", lambda e: e.memset(HL[:], 0.0), writes=[HLb])
    TW = k.sb([128, TT], BF16)
    TA = k.sb([128, TT], BF16)
    TG = k.sb([128, 2, TT], BF16)
    TWb, TAb, TGb = Buf(), Buf(), Buf()
    wl = WLoader(k, 16, 128, 3)
    wls = WLoader(k, 2, 128, 3)
    REG = k.sb([128, 18688], F32)
    REGbf = REG[:].bitcast(BF16)

    def f32v(o, n, a):
        return REG[:, o:o + n].rearrange("p (a b) -> p a b", a=a)

    def bfv(o, n, a):
        return REGbf[:, 2 * o:2 * o + 2 * n].rearrange("p (a b) -> p a b", a=a)

    X = f32v(0, 4096, 16)
    Hs = f32v(4096, 4352, 16)
    XX = bfv(8448, 2048, 16)
    XS = bfv(10496, 2048, 16)
    XR = bfv(12544, 2048, 16)
    XK = bfv(14592, 2048, 16)
    XV = bfv(16640, 2048, 16)
    o_ = [0]

    def nxt(n, a):
        v = bfv(o_[0], n, a)
        o_[0] += n
        return v
    ATbd, BTbd, KTbd, VTbd = nxt(1024, 16), nxt(1024, 16), nxt(1024, 16), nxt(1024, 16)
    def chbuf():
        t = k.sb([128, 4, 128], F32)
        return {"r": t[:], "w": t[:].bitcast(CHAIN_DT), "m": t[:].bitcast(CHAIN_DT)}
    CH = [dict(N=[chbuf(), chbuf()], L=[chbuf(), chbuf()], P=chbuf()) for _ in range(2)]
    PF = nxt(1024, 16)
    MakT = nxt(1024, 16)
    MrbT, MrkT = nxt(512, 16), nxt(512, 16)
    Vbd, Vstk = nxt(1024, 16), nxt(512, 16)
    Bbd, Kbd = nxt(1024, 16), nxt(1024, 16)
    Zs, Us, Ubd = nxt(512, 16), nxt(512, 16), nxt(1024, 16)
    ZERO_LIST = [ATbd, BTbd, KTbd, VTbd, CH[0]['N'][0]['w'], CH[0]['L'][0]['w'], CH[1]['N'][0]['w'], CH[1]['L'][0]['w'], MakT, Ubd]
    YT = f32v(12800, 4096, 16)
    OIN = bfv(4096, 2048, 16)
    Y2 = f32v(8448, 4096, 16)

    xT3 = xT.rearrange("(k p) t -> p k t", p=128)
    oT3 = oT.rearrange("(k p) t -> p k t", p=128)
    if dbg:
        dbf = k.dram("dbg_bf", [7, 128, 16 * TT], BF16, "ExternalOutput")
        dyt = k.dram("dbg_yt", [128, 16 * TT], F32, "ExternalOutput")
        dgc = k.dram("dbg_gc", [128, 16 * NCH], F32, "ExternalOutput")
        doin = k.dram("dbg_oin", [128, 16 * TT], BF16, "ExternalOutput")

    def tmp(name, n=1, dt=F32, w=TT):
        return k.rotbuf(name, [128, w], dt, n)

    for tt in range(1 if dbg else NTT):
        t0 = tt * TT
        Xb, Hsb, XXb, XSb, XRb, XKb, XVb = (Buf() for _ in range(7))
        P.dma("sp", X, xT3[:, :, t0:t0 + TT], writes=[Xb])
        rstd, rb = rms_rstd(k, c, X, Xb, TT)
        norm_mod(k, X, Xb, rstd, rb, A, Sh, mb, Hs, Hsb, TT, col0=1)
        P.op("pool", lambda e: e.tensor_copy(out=Hs[:, :, 0:1], in_=HL[:]), reads=[HLb], writes=[Hsb])
        P.op("dve", lambda e: e.tensor_tensor(out=XX, in0=Hs[:, :, 0:TT], in1=Hs[:, :, 1:TT + 1], op=ALU.subtract), reads=[Hsb], writes=[XXb])
        P.op("pool", lambda e: e.tensor_copy(out=HL[:], in_=Hs[:, :, TT:TT + 1]), reads=[Hsb], writes=[HLb])

        def make_xs(j, dst, dstb):
            for dt in range(16):
                P.op("dve", lambda e, o=dst[:, dt, :], i=XX[:, dt, :], s=vt[:, j, dt:dt + 1], h=Hs[:, dt, 1:TT + 1]:
                     e.scalar_tensor_tensor(out=o, in0=i, scalar=s, in1=h, op0=ALU.mult, op1=ALU.add),
                     reads=[XXb, Hsb, mb], writes=[dstb])
        for (j, W, ncol) in ((3, w1, 96), (4, a1, 96), (5, g1, 256)):
            make_xs(j, XS, XSb)
            for cbk in range((ncol + 127) // 128):
                nc_ = min(128, ncol - cbk * 128)
                wb, wbb = wl.load(W, 0, 16, cbk * 128, nc_)
                ps, psb = k.ps()
                for dt in range(16):
                    P.op("pe", lambda e, o=ps[:nc_, :TT], w=wb[:, dt, :nc_], r=XS[:, dt, :], s=(dt == 0), t=(dt == 15):
                         e.matmul(o, lhsT=w, rhs=r, start=s, stop=t), reads=[wbb, XSb], writes=[psb])
                if j == 3:
                    P.op("act", lambda e, i=ps[:96, :TT]: e.activation(out=TW[:96, :], in_=i, func=AF.Tanh), reads=[psb], writes=[TWb])
                elif j == 4:
                    P.op("act", lambda e, i=ps[:96, :TT]: e.activation(out=TA[:96, :], in_=i, func=AF.Copy), reads=[psb], writes=[TAb])
                else:
                    P.op("act", lambda e, i=ps[:, :TT], o=TG[:, cbk, :]: e.activation(out=o, in_=i, func=AF.Sigmoid), reads=[psb], writes=[TGb])
        make_xs(0, XR, XRb)
        make_xs(1, XK, XKb)
        make_xs(2, XV, XVb)
        for p in range(16):
            e0 = p * 128
            pA, pAb = k.ps()
            pB, pBb = k.ps()
            pC, pCb = k.ps()
            pD, pDb = k.ps()
            for (jj, src, srcb, ps, psb, co) in ((0, XR, XRb, pA, pAb, 0), (1, XK, XKb, pA, pAb, TT), (2, XV, XVb, pB, pBb, 0)):
                wb, wbb = wl.load(wrkv, jj * D, 16, e0, 128)
                for dt in range(16):
                    P.op("pe", lambda e, o=ps[:, co:co + TT], w=wb[:, dt, :], r=src[:, dt, :], s=(dt == 0), t=(dt == 15):
                         e.matmul(o, lhsT=w, rhs=r, start=s, stop=t), reads=[wbb, srcb], writes=[psb])
            wb, wbb = wls.load(w2, 0, 1, e0, 128, pp=96)
            P.op("pe", lambda e, o=pB[:, TT:2 * TT], w=wb[:96, 0, :]: e.matmul(o, lhsT=w, rhs=TW[:96, :], start=True, stop=True),
                 reads=[wbb, TWb], writes=[pBb])
            wb, wbb = wls.load(a2, 0, 1, e0, 128, pp=96)
            P.op("pe", lambda e, o=pC[:, 0:TT], w=wb[:96, 0, :]: e.matmul(o, lhsT=w, rhs=TA[:96, :], start=True, stop=True),
                 reads=[wbb, TAb], writes=[pCb])
            wb, wbb = wls.load(g2, 0, 2, e0, 128)
            for kt in range(2):
                P.op("pe", lambda e, o=pC[:, TT:2 * TT], w=wb[:, kt, :], r=TG[:, kt, :], s=(kt == 0), t=(kt == 1):
                     e.matmul(o, lhsT=w, rhs=r, start=s, stop=t), reads=[wbb, TGb], writes=[pCb])
            r_ps, k_ps, v_ps, w_ps, a_ps, g_ps = pA[:, 0:TT], pA[:, TT:2 * TT], pB[:, 0:TT], pB[:, TT:2 * TT], pC[:, 0:TT], pC[:, TT:2 * TT]
            LWS = -0.6065306597126334
            sg, sgb = tmp("sg", 2)
            cl, clb = tmp("cl", 2)
            av, avb = tmp("av", 2)
            vf, vfb = tmp("vf", 2)
            kk, kkb = tmp("kk", 2)
            rn, rnb = tmp("rn", 2)
            kf_, kfb = tmp("kf", 2)
            eg, egb = tmp("eg")
            eig, eigb = tmp("eig")
            eex, eexb = tmp("eex")
            k2, k2b = tmp("ksq", 1, BF16)
            bb, bbb = tmp("bb")
            rk_, rkb = tmp("rkp", 1, BF16)
            lw, lwb = sg, sgb
            P.op("act", lambda e, o=sg[:], i=w_ps, b=NEG[:, 0, p:p + 1]: e.activation(out=o, in_=i, func=AF.Exp, bias=b, scale=-1.0),
                 reads=[pBb, mb], writes=[sgb])
            P.op("dve", lambda e, o=kk[:], i=k_ps, s=KKv[:, p:p + 1]: e.tensor_scalar(out=o, in0=i, scalar1=s, scalar2=None, op0=ALU.mult),
                 reads=[pAb, mb], writes=[kkb])
            P.op("pool", lambda e, o=k2[:], i=kk[:]: e.tensor_tensor(out=o, in0=i, in1=i, op=ALU.mult), reads=[kkb], writes=[k2b])
            P.op("pe", lambda e, o=pD[:, 0:TT], r=k2[:]: e.matmul(o, lhsT=bo16[:], rhs=r, start=True, stop=True), reads=[k2b, cb_], writes=[pDb])
            P.op("act", lambda e, o=av[:], i=a_ps, b=NEG[:, 1, p:p + 1]: e.activation(out=o, in_=i, func=AF.Exp, bias=b, scale=-1.0),
                 reads=[pCb, mb], writes=[avb])
            P.op("act", lambda e, o=GT[:, p, :], i=g_ps: e.activation(out=o, in_=i, func=AF.Copy), reads=[pCb], writes=[GTb])
            P.op("act", lambda e, o=vf[:], i=v_ps: e.activation(out=o, in_=i, func=AF.Copy), reads=[pBb], writes=[vfb])
            P.op("pool", lambda e, o=VT[:, p, :], i=vf[:]: e.tensor_copy(out=o, in_=i), reads=[vfb], writes=[VTb])
            P.op("act", lambda e, o=sg[:]: e.activation(out=o, in_=o, func=AF.Ln, bias=c["eps"][:, 2:3], scale=1.0), reads=[sgb, c["onesb"]], writes=[sgb])
            P.op("act", lambda e, o=sg[:]: e.activation(out=o, in_=o, func=AF.Exp, scale=-1.0), reads=[sgb], writes=[sgb])
            P.op("dve", lambda e, o=cl[:], i=lw[:]: e.tensor_tensor_scan(out=o, data0=rmask[:], data1=i, initial=0.0, op0=ALU.mult, op1=ALU.add),
                 reads=[lwb, cb_], writes=[clb])
            P.op("dve", lambda e, o=rn[:], i=pD[:, 0:TT]: e.tensor_scalar(out=o, in0=i, scalar1=5.5e-20, scalar2=None, op0=ALU.max), reads=[pDb], writes=[rnb])
            P.op("pool", lambda e, o=eex[:], i=cl[:], j_=lw[:]: e.tensor_tensor(out=o, in0=i, in1=j_, op=ALU.subtract), reads=[clb, lwb], writes=[eexb])
            P.op("act", lambda e, o=rn[:]: e.activation(out=o, in_=o, func=AF.Ln), reads=[rnb], writes=[rnb])
            P.op("act", lambda e, o=rn[:]: e.activation(out=o, in_=o, func=AF.Exp, scale=-0.5), reads=[rnb], writes=[rnb])
            P.op("act", lambda e, o=eg[:], i=cl[:]: e.activation(out=o, in_=i, func=AF.Exp, scale=LWS), reads=[clb], writes=[egb])
            P.op("act", lambda e, o=eig[:], i=cl[:]: e.activation(out=o, in_=i, func=AF.Exp, scale=-LWS), reads=[clb], writes=[eigb])
            P.op("act", lambda e, o=eex[:]: e.activation(out=o, in_=o, func=AF.Exp, scale=LWS), reads=[eexb], writes=[eexb])
            P.op("pool", lambda e, o=GC[:, p, :], i=eg[:].rearrange("p (a b) -> p a b", a=NCH)[:, :, C - 1]: e.tensor_copy(out=o, in_=i),
                 reads=[egb], writes=[GCb])
            P.op("act", lambda e, o=av[:]: e.activation(out=o, in_=o, func=AF.Ln, bias=c["eps"][:, 2:3], scale=1.0), reads=[avb, c["onesb"]], writes=[avb])
            P.op("act", lambda e, o=av[:]: e.activation(out=o, in_=o, func=AF.Exp, scale=-1.0), reads=[avb], writes=[avb])
            P.op("dve", lambda e, o=kf_[:], i=av[:], s=KA[:, p:p + 1]: e.tensor_scalar(out=o, in0=i, scalar1=-1.0, scalar2=s, op0=ALU.add, op1=ALU.mult),
                 reads=[avb, mb], writes=[kfb])
            P.op("dve", lambda e, o=kf_[:], i=k_ps: e.scalar_tensor_tensor(out=o, in0=o, scalar=1.0, in1=i, op0=ALU.add, op1=ALU.mult),
                 reads=[kfb, pAb], writes=[kfb])
            P.op("pool", lambda e, o=kk[:], r=rn[:]: e.tensor_tensor(out=o, in0=o, in1=r, op=ALU.mult), reads=[kkb, rnb], writes=[kkb])
            P.op("dve", lambda e, o=RT[:, p, :], i=r_ps, g_=eg[:]: e.tensor_tensor(out=o, in0=i, in1=g_, op=ALU.mult), reads=[pAb, egb], writes=[RTb])
            P.op("pool", lambda e, o=KT[:, p, :], i=kf_[:], g_=eig[:]: e.tensor_tensor(out=o, in0=i, in1=g_, op=ALU.mult), reads=[kfb, eigb], writes=[KTb])
            P.op("pool", lambda e, o=bb[:], i=kk[:], a_=av[:]: e.tensor_tensor(out=o, in0=i, in1=a_, op=ALU.mult), reads=[kkb, avb], writes=[bbb])
            P.op("pool", lambda e, o=BT[:, p, :], i=bb[:], g_=eig[:]: e.tensor_tensor(out=o, in0=i, in1=g_, op=ALU.mult), reads=[bbb, eigb], writes=[BTb])
            P.op("dve", lambda e, o=AT[:, p, :], i=kk[:], g_=eex[:]: e.scalar_tensor_tensor(out=o, in0=i, scalar=-1.0, in1=g_, op0=ALU.mult, op1=ALU.mult),
                 reads=[kkb, eexb], writes=[ATb])
            P.op("dve", lambda e, o=rk_[:], i=r_ps, s=RK[:, p:p + 1], k_=kf_[:]: e.scalar_tensor_tensor(out=o, in0=i, scalar=s, in1=k_, op0=ALU.mult, op1=ALU.mult),
                 reads=[pAb, kfb, mb], writes=[rkb])
            P.op("pe", lambda e, o=pD[:, TT:2 * TT], r=rk_[:]: e.matmul(o, lhsT=bo16[:], rhs=r, start=True, stop=True), reads=[rkb, cb_], writes=[pDb])
            P.op("dve", lambda e, o=BON[:, p, :], i=pD[:, TT:2 * TT], v_=vf[:]: e.tensor_tensor(out=o, in0=i, in1=v_, op=ALU.mult),
                 reads=[pDb, vfb], writes=[BONb])

        P.barrier()
        if dbg:
            for i_, (t_, b_) in enumerate(((RT, RTb), (KT, KTb), (BT, BTb), (AT, ATb), (VT, VTb), (GT, GTb), (BON, BONb))):
                P.dma("sp", dbf[i_], t_[:].rearrange("p a b -> p (a b)"), reads=[b_])
            P.dma("sp", dgc, GC[:].rearrange("p a b -> p (a b)"), reads=[GCb])
            P.barrier()
        zb = Buf()
        SB = [dict(N=Buf(), L=Buf(), P=Buf()) for _ in range(2)]
        for z_ in ZERO_LIST:
            if z_.dtype == BF16:
                P.op("pool", lambda e, z_=z_: e.memset(z_, 0.0), writes=[zb])
            else:
                P.op("dve", lambda e, z_=z_: e.tensor_scalar(out=z_, in0=idn[:], scalar1=0.0, scalar2=None, op0=ALU.mult),
                     reads=[cb_], writes=[zb])
        for ch in range(NCH):
            cc = slice(ch * C, (ch + 1) * C)
            inb = [Buf() for _ in range(4)]
            for (src, srcb, dst) in ((AT, ATb, ATbd), (BT, BTb, BTbd), (KT, KTb, KTbd), (VT, VTb, VTbd)):
                for hf in range(2):
                    sl = slice(64 * hf, 64 * hf + 64)
                    P.op("dve" if hf == 0 else "act",
                         (lambda e, o=dst[sl, :, 64 * hf:64 * hf + 64], i=src[sl, :, cc]: e.tensor_copy(out=o, in_=i)) if hf == 0 else
                         (lambda e, o=dst[sl, :, 64 * hf:64 * hf + 64], i=src[sl, :, cc]: e.activation(out=o, in_=i, func=AF.Copy)),
                         reads=[srcb, zb], writes=inb)
            gb = [dict((n, Buf()) for n in ("N", "L", "Mak", "Mrb", "Mrk", "V", "B", "K", "P", "Z", "U")) for _ in range(4)]
            for g_ in range(4):
                pg = slice(g_ * 4, g_ * 4 + 4)
                b_ = gb[g_]
                for (src, dst, nm) in ((VTbd, Vbd, "V"), (BTbd, Bbd, "B"), (KTbd, Kbd, "K")):
                    ps, psb = k.ps()
                    psv = ps[:].bitcast(BF16)[:, 0:512].rearrange("p (a b) -> p a b", a=4)
                    for i in range(4):
                        P.op("pe", lambda e, o=psv[:, i, :], w=src[:, g_ * 4 + i, :]: e.transpose(out=o, in_=w, identity=idn[:, 0, :]),
                             reads=[inb[g_], cb_], writes=[psb])
                    P.op("act", lambda e, o=dst[:, pg, :], i=psv: e.activation(out=o, in_=i, func=AF.Copy), reads=[psb], writes=[b_[nm]])
                    if nm == "V":
                        for hf in range(2):
                            sl = slice(64 * hf, 64 * hf + 64)
                            P.op("dve", lambda e, o=Vstk[sl, pg, :], i=psv[sl, :, 64 * hf:64 * hf + 64]: e.tensor_copy(out=o, in_=i),
                                 reads=[psb], writes=[b_[nm]])
            def step1(g_):
                ps1, ps1b = k.ps()
                ps2, ps2b = k.ps()
                ps3, ps3b = k.ps()
                b_ = gb[g_]
                cs_ = CH[g_ % 2]
                sb_ = SB[g_ % 2]
                for i in range(4):
                    p = g_ * 4 + i
                    cs = slice(i * 64, i * 64 + 64)
                    cs2 = slice(256 + i * 64, 256 + i * 64 + 64)
                    for (ps, psb, csl, lh, rh, rhb) in ((ps1, ps1b, cs, BTbd, AT, ATb), (ps1, ps1b, cs2, ATbd, BT, BTb),
                                                        (ps2, ps2b, cs, KTbd, AT, ATb), (ps2, ps2b, cs2, BTbd, RT, RTb),
                                                        (ps3, ps3b, cs, KTbd, RT, RTb)):
                        P.op("pe", lambda e, o=ps[:, csl], w=lh[:, p, :], r=rh[:, p, cc]: e.matmul(o, lhsT=w, rhs=r, start=True, stop=True),
                             reads=[inb[g_], rhb], writes=[psb])
                pg = slice(g_ * 4, g_ * 4 + 4)
                v1 = ps1[:, 0:256].rearrange("p (a b) -> p a b", a=4)
                v1b = ps1[:, 256:512].rearrange("p (a b) -> p a b", a=4)
                v2 = ps2[:, 0:256].rearrange("p (a b) -> p a b", a=4)
                v2b = ps2[:, 256:512].rearrange("p (a b) -> p a b", a=4)
                v3 = ps3[:, 0:256].rearrange("p (a b) -> p a b", a=4)
                for hf in range(2):
                    sl = slice(64 * hf, 64 * hf + 64)
                    fs = slice(64 * hf, 64 * hf + 64)
                    P.op("dve", lambda e, o=cs_["N"][0]["w"][sl, :, fs], i=v1[sl], m=mS[sl]: e.tensor_tensor(out=o, in0=i, in1=m, op=ALU.mult),
                         reads=[ps1b, cb_, zb], writes=[sb_["N"]])
                    P.op("dve", lambda e, o=cs_["L"][0]["w"][sl, :, fs], i=v1b[sl], m=mL[sl]: e.tensor_tensor(out=o, in0=i, in1=m, op=ALU.mult),
                         reads=[ps1b, cb_, zb], writes=[sb_["L"]])
                    P.op("dve", lambda e, o=MakT[sl, pg, fs], i=v2[sl], m=mS[sl]: e.tensor_tensor(out=o, in0=i, in1=m, op=ALU.mult),
                         reads=[ps2b, cb_, zb], writes=[b_["Mak"]])
                P.op("dve", lambda e, o=MrbT[:, pg, :], i=v2b, m=mI[:]: e.tensor_tensor(out=o, in0=i, in1=m, op=ALU.mult),
                     reads=[ps2b, cb_], writes=[b_["Mrb"]])
                P.op("dve", lambda e, o=MrkT[:, pg, :], i=v3, m=mI[:]: e.tensor_tensor(out=o, in0=i, in1=m, op=ALU.mult),
                     reads=[ps3b, cb_], writes=[b_["Mrk"]])
                P.op("dve", lambda e, o=cs_["P"]["w"], i=cs_["N"][0]["r"]: e.tensor_tensor(out=o, in0=i, in1=idn[:], op=ALU.add),
                     reads=[sb_["N"], cb_], writes=[sb_["P"]])

            def chain_sq(g_, lev):
                a_, n_ = (lev - 1) % 2, lev % 2
                cs_ = CH[g_ % 2]
                sb_ = SB[g_ % 2]
                Ns, Ls = cs_["N"], cs_["L"]
                if lev < 5:
                    psn, psnb = k.ps()
                    for i in range(4):
                        P.op("pe", lambda e, o=psn[:, i * 128:(i + 1) * 128], w=Ls[a_]["m"][:, i, :], r=Ns[a_]["m"][:, i, :]:
                             e.matmul(o, lhsT=w, rhs=r, start=True, stop=True), reads=[sb_["L"], sb_["N"]], writes=[psnb])
                psl_, pslb = k.ps()
                for i in range(4):
                    P.op("pe", lambda e, o=psl_[:, i * 128:(i + 1) * 128], w=Ns[a_]["m"][:, i, :], r=Ls[a_]["m"][:, i, :]:
                         e.matmul(o, lhsT=w, rhs=r, start=True, stop=True), reads=[sb_["L"], sb_["N"]], writes=[pslb])
                P.op("act", lambda e, o=Ls[n_]["w"], i=psl_[:].rearrange("p (a b) -> p a b", a=4): e.activation(out=o, in_=i, func=AF.Copy),
                     reads=[pslb], writes=[sb_["L"]])
                if lev < 5:
                    P.op("act", lambda e, o=Ns[n_]["w"], i=psn[:].rearrange("p (a b) -> p a b", a=4): e.activation(out=o, in_=i, func=AF.Copy),
                         reads=[psnb], writes=[sb_["N"]])

            def chain_p(g_, lev):
                n_ = lev % 2
                pg = slice(g_ * 4, g_ * 4 + 4)
                cs_ = CH[g_ % 2]
                sb_ = SB[g_ % 2]
                Ls, Pc = cs_["L"], cs_["P"]
                psp, pspb = k.ps()
                for i in range(4):
                    P.op("pe", lambda e, o=psp[:, i * 128:(i + 1) * 128], w=Ls[n_]["m"][:, i, :], r=Pc["m"][:, i, :]:
                         e.matmul(o, lhsT=w, rhs=r, start=True, stop=True), reads=[sb_["L"], sb_["P"]], writes=[pspb])
                if lev < 5:
                    P.op("dve", lambda e, o=Pc["w"], q=Pc["r"], i=psp[:].rearrange("p (a b) -> p a b", a=4): e.tensor_tensor(out=o, in0=i, in1=q, op=ALU.add),
                         reads=[pspb, sb_["P"]], writes=[sb_["P"]])
                else:
                    P.op("dve", lambda e, o=PF[:, pg, :], i=psp[:].rearrange("p (a b) -> p a b", a=4), q=Pc["r"]: e.tensor_tensor(out=o, in0=i, in1=q, op=ALU.add),
                         reads=[pspb, sb_["P"]], writes=[gb[g_]["P"]])

            for gp in ((0, 1), (2, 3)):
                for g_ in gp:
                    step1(g_)
                for lev in range(1, 6):
                    for g_ in gp:
                        chain_sq(g_, lev)
                    for g_ in gp:
                        chain_p(g_, lev)
            zps, ups = {}, {}
            for g_ in range(4):
                pg = slice(g_ * 4, g_ * 4 + 4)
                b_ = gb[g_]
                hb_ = Hb_[g_]
                psz, pszb = k.ps()
                for i in range(4):
                    p = g_ * 4 + i
                    P.op("pe", lambda e, o=psz[:, i * 64:(i + 1) * 64], w=ATbd[:, p, :], r=Hstk[:, p, :]: e.matmul(o, lhsT=w, rhs=r, start=True, stop=False),
                         reads=[inb[g_], hb_], writes=[pszb])
                    P.op("pe", lambda e, o=psz[:, i * 64:(i + 1) * 64], w=MakT[:, p, :], r=Vstk[:, p, :]: e.matmul(o, lhsT=w, rhs=r, start=False, stop=True),
                         reads=[b_["Mak"], b_["V"]], writes=[pszb])
                P.op("act", lambda e, o=Zs[:, pg, :], i=psz[:, 0:256].rearrange("p (a b) -> p a b", a=4): e.activation(out=o, in_=i, func=AF.Copy),
                     reads=[pszb], writes=[b_["Z"]])
            for g_ in range(4):
                pg = slice(g_ * 4, g_ * 4 + 4)
                b_ = gb[g_]
                psu, psub = k.ps()
                for i in range(4):
                    p = g_ * 4 + i
                    P.op("pe", lambda e, o=psu[:, i * 64:(i + 1) * 64], w=PF[:, p, :], r=Zs[:, p, :]: e.matmul(o, lhsT=w, rhs=r, start=True, stop=True),
                         reads=[b_["P"], b_["Z"]], writes=[psub])
                psuv = psu[:, 0:256].rearrange("p (a b) -> p a b", a=4)
                P.op("act", lambda e, o=Us[:, pg, :], i=psuv: e.activation(out=o, in_=i, func=AF.Copy), reads=[psub], writes=[b_["U"]])
                for hf in range(2):
                    sl = slice(64 * hf, 64 * hf + 64)
                    P.op("dve", lambda e, o=Ubd[sl, pg, 64 * hf:64 * hf + 64], i=psuv[sl]: e.tensor_copy(out=o, in_=i), reads=[psub, zb], writes=[b_["U"]])
            for g_ in range(4):
                pg = slice(g_ * 4, g_ * 4 + 4)
                b_ = gb[g_]
                hb_ = Hb_[g_]
                psy, psyb = k.ps()
                psh, pshb = k.ps()
                for i in range(4):
                    p = g_ * 4 + i
                    oy = psy[:, i * 64:(i + 1) * 64]
                    P.op("pe", lambda e, o=oy, w=Hbd[:, p, :], r=RT[:, p, cc]: e.matmul(o, lhsT=w, rhs=r, start=True, stop=False),
                         reads=[hb_, RTb], writes=[psyb])
                    P.op("pe", lambda e, o=oy, w=Ubd[:, p, :], r=MrbT[:, p, :]: e.matmul(o, lhsT=w, rhs=r, start=False, stop=False),
                         reads=[b_["U"], b_["Mrb"]], writes=[psyb])
                    P.op("pe", lambda e, o=oy, w=Vbd[:, p, :], r=MrkT[:, p, :]: e.matmul(o, lhsT=w, rhs=r, start=False, stop=True),
                         reads=[b_["V"], b_["Mrk"]], writes=[psyb])
                    oh = psh[:, i * 64:(i + 1) * 64]
                    P.op("pe", lambda e, o=oh, w=Bbd[:, p, :], r=Us[:, p, :]: e.matmul(o, lhsT=w, rhs=r, start=True, stop=False),
                         reads=[b_["B"], b_["U"]], writes=[pshb])
                    P.op("pe", lambda e, o=oh, w=Kbd[:, p, :], r=Vstk[:, p, :]: e.matmul(o, lhsT=w, rhs=r, start=False, stop=True),
                         reads=[b_["K"], b_["V"]], writes=[pshb])
                P.op("act", lambda e, o=YT[:, pg, cc], i=psy[:, 0:256].rearrange("p (a b) -> p a b", a=4): e.activation(out=o, in_=i, func=AF.Copy),
                     reads=[psyb], writes=[YTb])
                P.op("dve", lambda e, o=Hf[:, pg, :], i=psh[:, 0:256].rearrange("p (a b) -> p a b", a=4): e.tensor_tensor(out=o, in0=o, in1=i, op=ALU.add),
                     reads=[pshb, hb_], writes=[hb_])
                for i in range(4):
                    p = g_ * 4 + i
                    P.op("dve", lambda e, o=Hf[:, p, :], s=GC[:, p, ch:ch + 1]: e.tensor_scalar(out=o, in0=o, scalar1=s, scalar2=None, op0=ALU.mult),
                         reads=[hb_, GCb], writes=[hb_])
                P.op("pool", lambda e, o=Hstk[:, pg, :], i=Hf[:, pg, :]: e.tensor_copy(out=o, in_=i), reads=[hb_], writes=[hb_])
                for hf in range(2):
                    sl = slice(64 * hf, 64 * hf + 64)
                    P.op("pool", lambda e, o=Hbd[sl, pg, 64 * hf:64 * hf + 64], i=Hf[sl, pg, :]: e.tensor_copy(out=o, in_=i), reads=[hb_], writes=[hb_])

        P.barrier()
        OINb, Y2b, Xb = Buf(), Buf(), Buf()
        P.dma("sp", X, xT3[:, :, t0:t0 + TT], writes=[Xb])
        def gn_front(p):
            psm, psmb = k.ps()
            P.op("pe", lambda e, o=psm[:, 0:TT], r=YT[:, p, :]: e.matmul(o, lhsT=bo32[:], rhs=r, start=True, stop=True), reads=[YTb, cb_], writes=[psmb])
            yc, ycb = tmp("cl", 2)
            P.op("dve", lambda e, o=yc[:], m=psm[:, 0:TT], y=YT[:, p, :]: e.scalar_tensor_tensor(out=o, in0=m, scalar=-1.0 / 64, in1=y, op0=ALU.mult, op1=ALU.add),
                 reads=[psmb, YTb], writes=[ycb])
            ysq, ysqb = tmp("rn", 2)
            P.op("pool", lambda e, o=ysq[:], i=yc[:]: e.tensor_tensor(out=o, in0=i, in1=i, op=ALU.mult), reads=[ycb], writes=[ysqb])
            P.op("pe", lambda e, o=psm[:, TT:2 * TT], r=ysq[:]: e.matmul(o, lhsT=bo32[:], rhs=r, start=True, stop=True), reads=[ysqb, cb_], writes=[psmb])
            rs, rsb = tmp("kk", 2)
            P.op("act", lambda e, o=rs[:], i=psm[:, TT:2 * TT]: e.activation(out=o, in_=i, func=AF.Ln, bias=c["eps"][:, 1:2], scale=1.0 / 64),
                 reads=[psmb, c["onesb"]], writes=[rsb])
            P.op("act", lambda e, o=rs[:]: e.activation(out=o, in_=o, func=AF.Exp, scale=-0.5), reads=[rsb], writes=[rsb])
            return (p, yc, ycb, rs, rsb)

        def gn_back(st_):
            p, yc, ycb, rs, rsb = st_
            P.op("dve", lambda e, o=yc[:], r=rs[:]: e.tensor_tensor(out=o, in0=o, in1=r, op=ALU.mult), reads=[ycb, rsb], writes=[ycb])
            P.op("dve", lambda e, o=yc[:], s=LNW[:, p:p + 1], b=BON[:, p, :]: e.scalar_tensor_tensor(out=o, in0=o, scalar=s, in1=b, op0=ALU.mult, op1=ALU.add),
                 reads=[ycb, BONb, mb], writes=[ycb])
            P.op("dve", lambda e, o=OIN[:, p, :], i=yc[:], s=LNB[:, p:p + 1], g_=GT[:, p, :]: e.scalar_tensor_tensor(out=o, in0=i, scalar=s, in1=g_, op0=ALU.add, op1=ALU.mult),
                 reads=[ycb, GTb, mb], writes=[OINb])

        pend = []
        for p in range(16):
            pend.append(gn_front(p))
            if len(pend) > 1:
                gn_back(pend.pop(0))
        while pend:
            gn_back(pend.pop(0))
        if dbg:
            P.dma("sp", dyt, YT.rearrange("p a b -> p (a b)"), reads=[YTb])
            P.dma("sp", doin, OIN.rearrange("p a b -> p (a b)"), reads=[OINb])
        for eb in range(16):
            wb, wbb = wl.load(wo, 0, 16, eb * 128, 128)
            ps, psb = k.ps()
            for p in range(16):
                P.op("pe", lambda e, o=ps[:, :TT], w=wb[:, p, :], r=OIN[:, p, :], s=(p == 0), t=(p == 15):
                     e.matmul(o, lhsT=w, rhs=r, start=s, stop=t), reads=[wbb, OINb], writes=[psb])
            P.op("act", lambda e, o=Y2[:, eb, :], i=ps[:, :TT]: e.activation(out=o, in_=i, func=AF.Copy), reads=[psb], writes=[Y2b])
        post_residual(k, c, X, Xb, Y2, Y2b, G, mb, TT)
        P.dma("sp", oT3[:, :, t0:t0 + TT], X, reads=[Xb])
        P.barrier()
    k.close()
    return nc


def _pd(v, n=16):
    return np.ascontiguousarray(np.asarray(v, dtype=np.float32).reshape(n, 128).T)


def _run(nc, maps):
    res = run_bass_kernel_spmd(nc, maps, core_ids=list(range(len(maps))))
    return res.results


class _Root:
    pass


def build_fused():
    root = _Root()
    root.nc = bass.Bass("TRN2", target_bir_lowering=False)
    root.semstack = ExitStack()
    nc = root.nc
    xT = nc.dram_tensor("xT", [D, T], F32, kind="ExternalInput").ap()
    oT = nc.dram_tensor("oT", [D, T], F32, kind="ExternalOutput").ap()
    modsI = nc.dram_tensor("mods_i", [128, 192], F32, kind="Internal").ap()
    xa = nc.dram_tensor("xa_i", [D, T], F32, kind="Internal").ap()
    xb = nc.dram_tensor("xb_i", [D, T], F32, kind="Internal").ap()
    xc = nc.dram_tensor("xc_i", [D, T], F32, kind="Internal").ap()
    def decl(prefix, items):
        dm, pre = {}, []
        for name, shape in items:
            w = nc.dram_tensor(prefix + name, list(shape), F32, kind="ExternalInput").ap()
            wb = nc.dram_tensor(prefix + name + "_bf", list(shape), BF16, kind="Internal").ap()
            dm[name + "_bf"] = wb
            pre.append((wb, w))
        return dm, pre
    rw_dm, rw_pre = decl("r_", [("w_rkv", [3 * D, D]), ("w1", [D, 96]), ("w2", [96, D]), ("a1", [D, 96]), ("a2", [96, D]),
                                ("g1", [D, 256]), ("g2", [256, D]), ("w_o", [D, D])])
    f0_dm, f0_pre = decl("f0_", [("w_up", [D, FF]), ("w_dn", [FF, D])])
    f1_dm, f1_pre = decl("f1_", [("w_up", [D, FF]), ("w_dn", [FF, D])])
    a_dm, a_pre = decl("a_", [("kv_down", [D, 576]), ("kv_uk", [512, D]), ("kv_uv", [512, D]), ("w_dq", [D, 512]),
                              ("w_uq", [512, 16 * 192]), ("w_o", [D, D])])
    build_mods(env=(root, {"mods": modsI}, "m_", rw_pre))
    build_rwkv(0, env=(root, dict(rw_dm, xT=xT, mods=modsI, oT=xa), "r_", f0_pre + a_pre))
    build_mlp(0, env=(root, dict(f0_dm, xT=xa, mods=modsI, oT=xb), "f0_", f1_pre))
    build_mla(1, env=(root, dict(a_dm, xT=xb, mods=modsI, oT=xc), "a_"))
    build_mlp(1, env=(root, dict(f1_dm, xT=xc, mods=modsI, oT=oT), "f1_"))
    root.semstack.close()
    return nc


def kernel(x, c, positions, ada_w, ada_b, norm_g, mlp_up, mlp_down,
           rw_mu, rw_rkv, rw_w0, rw_w1, rw_w2, rw_a0, rw_a1, rw_a2, rw_g1, rw_g2,
           rw_kk, rw_ka, rw_rk, rw_lnx, rw_o,
           mla_dq, mla_qnorm, mla_uq, mla_o,
           kv_in_g, kv_down, kv_norm, kv_uk, kv_uv):
    f32 = np.float32
    A = lambda a: np.ascontiguousarray(np.asarray(a))
    x = A(x).astype(f32, copy=False)
    B = x.shape[0]
    cc = A(c)
    pos = A(positions).astype(np.int32, copy=False)
    vl = [A(rw_mu)[0][j] for j in range(6)] + [A(rw_w0)[0], A(rw_a0)[0], A(rw_kk)[0], A(rw_ka)[0], A(rw_rk)[0].reshape(-1),
                                              A(rw_lnx)[0][0], A(rw_lnx)[0][1]]
    inv = (1.0 / (10000.0 ** (np.arange(0, 64, 2, dtype=np.float32) / 64))).astype(f32)
    shared = {
        "m_ada_w": A(ada_w),
        "m_ada_b_pd": np.ascontiguousarray(A(ada_b).reshape(2, 96, 128).transpose(2, 0, 1)),
        "m_norm_g_pd": np.ascontiguousarray(A(norm_g).reshape(2, 4, 16, 128).transpose(3, 0, 1, 2)),
        "r_rw_vec": np.ascontiguousarray(np.stack([_pd(v) for v in vl], 1)).astype(f32),
        "r_w_rkv": A(rw_rkv)[0].reshape(3 * D, D), "r_w1": A(rw_w1)[0], "r_w2": A(rw_w2)[0], "r_a1": A(rw_a1)[0],
        "r_a2": A(rw_a2)[0], "r_g1": A(rw_g1)[0], "r_g2": A(rw_g2)[0], "r_w_o": A(rw_o)[0],
        "f0_w_up": A(mlp_up)[0], "f0_w_dn": A(mlp_down)[0], "f1_w_up": A(mlp_up)[1], "f1_w_dn": A(mlp_down)[1],
        "a_ropec": np.ascontiguousarray(np.stack([np.concatenate([inv, inv]), np.concatenate([-np.ones(32), np.ones(32)])], 1)).astype(f32),
        "a_mla_vec": np.ascontiguousarray(np.concatenate([_pd(kv_in_g), _pd(kv_norm, 4), _pd(A(mla_qnorm)[0], 4)], 1)).astype(f32),
        "a_kv_down": A(kv_down), "a_kv_uk": A(kv_uk).reshape(512, D), "a_kv_uv": A(kv_uv).reshape(512, D),
        "a_w_dq": A(mla_dq)[0], "a_w_uq": A(mla_uq)[0].reshape(512, 16 * 192), "a_w_o": A(mla_o)[0],
    }
    maps = [dict(shared, xT=np.ascontiguousarray(x[b].T), m_c_pd=_pd(cc[b]),
                 a_posr=np.ascontiguousarray(np.broadcast_to(pos[b][None, :], (64, T)))) for b in range(B)]
    r = _run(build_fused(), maps)
    return np.stack([np.ascontiguousarray(r[b]["oT"].T) for b in range(B)]).astype(f32, copy=False)
```

```python
import numpy as np
import concourse.bass as bass
import concourse.mybir as mybir
from concourse.bass_utils import run_bass_kernel_spmd
from contextlib import ExitStack

F32 = mybir.dt.float32
BF16 = mybir.dt.bfloat16
F32R = mybir.dt.float32r
MODS_DT = F32R
CHAIN_DT = F32
I32 = mybir.dt.int32
ALU = mybir.AluOpType
AF = mybir.ActivationFunctionType

D = 2048
T = 2048
ND = 16
FF = 8192
NCORES = 8
EPS = 1e-6

EPOCH = 30000
N_DMA_SEMS = 6
SAME_ENGINE_SYNC = False


class Buf:
    __slots__ = ("w", "r")

    def __init__(self):
        self.w = None
        self.r = {}


class Prog:
    ENGS = ("pe", "dve", "act", "pool", "sp")
    NSEM = 0

    def __init__(self, nc, semstack=None):
        self.nc = nc
        self.semstack = semstack
        self.q = {e: [] for e in self.ENGS}
        self.cnt = {e: 0 for e in self.ENGS}
        self.seen = {e: {} for e in self.ENGS}
        self.dma_cnt = {}
        self.dma_rr = {e: 0 for e in self.ENGS}
        self.keys = set()

    def _wait(self, eng, key, val):
        if self.seen[eng].get(key, 0) >= val:
            return
        self.seen[eng][key] = val
        self.q[eng].append(("wait", key, val))

    def _deps(self, eng, reads, writes):
        deps = {}
        for b in reads:
            if b.w is not None:
                k, v = b.w
                if deps.get(k, 0) < v:
                    deps[k] = v
        for b in writes:
            if b.w is not None:
                k, v = b.w
                if deps.get(k, 0) < v:
                    deps[k] = v
            for k, v in b.r.items():
                if deps.get(k, 0) < v:
                    deps[k] = v
        for k, v in deps.items():
            if k[0] == "E" and k[1] == eng and (eng == "pe" or not SAME_ENGINE_SYNC):
                continue
            self._wait(eng, k, v)

    def _mark(self, tok, reads, writes):
        k, v = tok
        for b in reads:
            if b.r.get(k, 0) < v:
                b.r[k] = v
        for b in writes:
            b.w = tok
            b.r = {}

    def op(self, eng, fn, reads=(), writes=()):
        self._deps(eng, reads, writes)
        n = self.cnt[eng]
        self.cnt[eng] = n + 1
        key = ("E", eng, n // EPOCH)
        self.keys.add(key)
        self.q[eng].append(("op", fn, key))
        self._mark((key, n % EPOCH + 1), reads, writes)

    def dma(self, qeng, out, in_, reads=(), writes=()):
        self._deps(qeng, reads, writes)
        s = self.dma_rr[qeng]
        self.dma_rr[qeng] = (s + 1) % N_DMA_SEMS
        gen = 0
        while self.dma_cnt.get(("D", qeng, s, gen), 0) + 16 > EPOCH:
            gen += 1
        key = ("D", qeng, s, gen)
        prev = self.dma_cnt.get(key, 0)
        if prev > 0:
            self._wait(qeng, key, prev)
        elif gen > 0:
            pk = ("D", qeng, s, gen - 1)
            self._wait(qeng, pk, self.dma_cnt[pk])
        self.dma_cnt[key] = prev + 16
        self.keys.add(key)
        self.q[qeng].append(("dma", (out, in_), key))
        self._mark((key, prev + 16), reads, writes)

    def barrier(self):
        toks = []
        for e in self.ENGS:
            n = self.cnt[e]
            if n > 0:
                toks.append((("E", e, (n - 1) // EPOCH), (n - 1) % EPOCH + 1))
        toks += list(self.dma_cnt.items())
        for e in self.ENGS:
            for key, v in toks:
                if key[0] == "E" and key[1] == e:
                    continue
                self._wait(e, key, v)

    def finish(self, eng="sp"):
        for key, v in list(self.dma_cnt.items()):
            self._wait(eng, key, v)

    def emit(self):
        nc = self.nc
        engmap = {"pe": "tensor", "dve": "vector", "act": "scalar", "pool": "gpsimd", "sp": "sync"}
        with ExitStack() as st:
            sems = {}
            semst = self.semstack if self.semstack is not None else st
            for i, key in enumerate(sorted(self.keys, key=str)):
                Prog.NSEM += 1
                sems[key] = semst.enter_context(nc.semaphore("s%d" % Prog.NSEM))
            block = st.enter_context(nc.Block())
            for e in self.ENGS:
                items = self.q[e]
                if not items:
                    continue

                def body(eng, items=items):
                    for it in items:
                        if it[0] == "wait":
                            eng.wait_ge(sems[it[1]], it[2])
                        elif it[0] == "op":
                            it[1](eng).then_inc(sems[it[2]], 1)
                        else:
                            eng.dma_start(out=it[1][0], in_=it[1][1]).then_inc(sems[it[2]], 16)

                getattr(block, engmap[e])(body)


class Ctx:
    NT = 0

    def __init__(self, name, env=None):
        if env is None:
            self.nc = bass.Bass("TRN2", target_bir_lowering=False)
            self.semstack = None
            self.dmap = {}
            self.prefix = ""
            self.pre = []
        else:
            root, self.dmap, self.prefix = env[:3]
            self.pre = env[3] if len(env) > 3 else []
            self.nc = root.nc
            self.semstack = root.semstack
        self.P = Prog(self.nc, self.semstack)
        self.st = ExitStack()
        self.n = 0
        self.psl = []
        self.psi = 0
        self.rot = {}
        self.rotw = 512

    def dram(self, name, shape, dt, kind):
        if name in self.dmap:
            return self.dmap[name]
        return self.nc.dram_tensor(self.prefix + name, list(shape), dt, kind=kind).ap()

    def sb(self, shape, dt):
        Ctx.NT += 1
        return self.st.enter_context(self.nc.sbuf_tensor("t%d" % Ctx.NT, list(shape), dt))

    def init_psum(self, nf32=8):
        for i in range(nf32):
            Ctx.NT += 1
            t = self.st.enter_context(self.nc.psum_tensor("ps%d" % Ctx.NT, [128, 512], F32))
            self.psl.append((t, Buf()))

    def ps(self):
        r = self.psl[self.psi]
        self.psi = (self.psi + 1) % len(self.psl)
        return r

    def rotbuf(self, key, shape, dt, n=2):
        if key not in self.rot:
            self.rot[key] = [[(self.sb(shape, dt), Buf()) for _ in range(n)], 0]
        lst, i = self.rot[key]
        self.rot[key][1] = (i + 1) % len(lst)
        return lst[i]

    def do_pre(self, n=None):
        if not hasattr(self, "pre_chunks"):
            self.pre_chunks = []
            for (dst, src) in self.pre:
                rows, cols = src.shape[0], src.shape[1]
                step = max(1, min(rows, (8 << 20) // (cols * 4)))
                for r0 in range(0, rows, step):
                    r1 = min(rows, r0 + step)
                    self.pre_chunks.append((dst[r0:r1, :], src[r0:r1, :]))
        m = len(self.pre_chunks) if n is None else min(n, len(self.pre_chunks))
        for _ in range(m):
            d_, s_ = self.pre_chunks.pop(0)
            self.P.dma("pool", d_, s_)

    def close(self):
        self.P.finish("sp")
        self.P.emit()
        self.st.close()


class WT:
    def __init__(self, ap, buf):
        self.ap = ap
        self.buf = buf


def cast_dma(k, dst, src, buf=None, max_bytes=8 << 20):
    rows, cols = src.shape[0], src.shape[1]
    step = max(1, min(rows, max_bytes // (cols * 4)))
    for r0 in range(0, rows, step):
        r1 = min(rows, r0 + step)
        k.P.dma("pool", dst[r0:r1, :], src[r0:r1, :], writes=[buf] if buf is not None else [])


def wsrc(k, name, shape):
    if name + "_bf" in k.dmap:
        return WT(k.dmap[name + "_bf"], Buf())
    w = k.dram(name, shape, F32, "ExternalInput")
    wb = k.nc.dram_tensor(k.prefix + name + "_bf", list(shape), BF16, kind="Internal").ap()
    b = Buf()
    cast_dma(k, wb, w, b)
    return WT(wb, b)


class WLoader:
    def __init__(self, k, nk=16, ncols=256, nbf=3):
        self.k = k
        self.wbf = [(k.sb([128, nk, ncols], BF16), Buf()) for _ in range(nbf)]
        self.j = 0

    def load(self, W, r0, nk, c0, ncols, pp=128):
        P = self.k.P
        wb, wbb = self.wbf[self.j]
        self.j = (self.j + 1) % len(self.wbf)
        src = W.ap[r0:r0 + nk * pp, c0:c0 + ncols].rearrange("(k p) c -> p k c", p=pp)
        P.dma("sp", wb[:pp, :nk, :ncols], src, reads=[W.buf], writes=[wbb])
        return wb, wbb


def make_consts(k):
    P = k.P
    c = {}
    ones = k.sb([128, 128], BF16)
    c["ones"] = ones
    c["onesb"] = Buf()
    P.op("pool", lambda e: e.memset(ones[:], 1.0), writes=[c["onesb"]])
    eps = k.sb([128, 3], F32)
    c["eps"] = eps
    P.op("pool", lambda e: e.memset(eps[:, 0:1], EPS), writes=[c["onesb"]])
    P.op("pool", lambda e: e.memset(eps[:, 1:2], 64e-5), writes=[c["onesb"]])
    P.op("pool", lambda e: e.memset(eps[:, 2:3], 1.0), writes=[c["onesb"]])
    return c


def rms_rstd(k, c, X, Xb, TT, scale_div=D, epsap=None, ntile=ND):
    P = k.P
    if epsap is None:
        epsap = c["eps"][:, 0:1]
    ps, psb = k.ps()
    for dt in range(ntile):
        sq, sqb = k.rotbuf("sq", [128, k.rotw], BF16, 3)
        P.op("act", lambda e, o=sq[:, :TT], i=X[:, dt, :]: e.activation(out=o, in_=i, func=AF.Square),
             reads=[Xb], writes=[sqb])
        P.op("pe", lambda e, o=ps[:, :TT], r=sq[:, :TT], s=(dt == 0), t=(dt == ntile - 1):
             e.matmul(o, lhsT=c["ones"][:], rhs=r, start=s, stop=t), reads=[sqb, c["onesb"]], writes=[psb])
    rstd, rb = k.rotbuf("rstd", [128, k.rotw], F32, 2)
    P.op("act", lambda e, o=rstd[:, :TT], i=ps[:, :TT]: e.activation(
        out=o, in_=i, func=AF.Sqrt, bias=epsap, scale=1.0 / scale_div), reads=[psb, c["onesb"]], writes=[rb])
    P.op("dve", lambda e, o=rstd[:, :TT]: e.reciprocal(out=o, in_=o), reads=[rb], writes=[rb])
    return rstd, rb


def norm_mod(k, X, Xb, rstd, rb, A, Sh, mb, H, Hb, TT, col0=0):
    P = k.P
    for dt in range(ND):
        tmp, tb = k.rotbuf("nm_tmp", [128, k.rotw], F32, 2)
        P.op("dve", lambda e, o=tmp[:, :TT], i=X[:, dt, :], s=A[:, dt:dt + 1], r=rstd[:, :TT]:
             e.scalar_tensor_tensor(out=o, in0=i, scalar=s, in1=r, op0=ALU.mult, op1=ALU.mult),
             reads=[Xb, rb, mb], writes=[tb])
        P.op("act", lambda e, o=H[:, dt, col0:col0 + TT], i=tmp[:, :TT], s=Sh[:, dt:dt + 1]:
             e.activation(out=o, in_=i, func=AF.Identity, bias=s, scale=1.0),
             reads=[tb, mb], writes=[Hb])


def post_residual(k, c, X, Xb, Y, Yb, G, mb, TT):
    P = k.P
    rstd, rb = rms_rstd(k, c, Y, Yb, TT)
    for dt in range(ND):
        tmp, tb = k.rotbuf("nm_tmp", [128, k.rotw], F32, 2)
        P.op("dve", lambda e, o=tmp[:, :TT], i=Y[:, dt, :], s=G[:, dt:dt + 1], r=rstd[:, :TT]:
             e.scalar_tensor_tensor(out=o, in0=i, scalar=s, in1=r, op0=ALU.mult, op1=ALU.mult),
             reads=[Yb, rb, mb], writes=[tb])
        P.op("pool" if dt % 2 else "dve", lambda e, o=X[:, dt, :], i=tmp[:, :TT]: e.tensor_tensor(out=o, in0=o, in1=i, op=ALU.add),
             reads=[tb], writes=[Xb])


def build_mods(env=None):
    k = Ctx("mods", env)
    nc, P = k.nc, k.P
    c_pd = k.dram("c_pd", [128, 16], F32, "ExternalInput")
    ada_w = k.dram("ada_w", [2, D, 6 * D], F32, "ExternalInput")
    ada_b = k.dram("ada_b_pd", [128, 2, 96], F32, "ExternalInput")
    ng = k.dram("norm_g_pd", [128, 2, 4, 16], F32, "ExternalInput")
    mods = k.dram("mods", [128, 2 * 96], F32, "ExternalOutput")
    k.init_psum(2)
    cin = k.sb([128, 16], F32)
    cact = k.sb([128, 16], F32)
    abt = k.sb([128, 2, 96], F32)
    ngt = k.sb([128, 2, 4, 16], F32)
    raw = k.sb([128, 2, 96], F32)
    outt = k.sb([128, 2, 6, 16], F32)
    cb_, sm_ = Buf(), Buf()
    P.dma("sp", cin[:], c_pd, writes=[cb_])
    P.dma("sp", abt[:], ada_b, writes=[sm_])
    P.dma("sp", ngt[:], ng, writes=[sm_])
    P.op("act", lambda e: e.activation(out=cact[:].bitcast(MODS_DT), in_=cin[:], func=AF.Silu), reads=[cb_], writes=[cb_])
    stg = [(k.sb([128, 16, 512], F32), Buf()) for _ in range(3)]
    rawb, ob = Buf(), Buf()
    one1 = k.sb([1, 1], F32)
    row = k.sb([1, 6 * D], F32)
    rowb = Buf()
    P.op("dve", lambda e: e.memset(one1[:], 1.0), writes=[cb_])
    k.psl = k.psl + [(k.st.enter_context(nc.psum_tensor("psx%d" % i, [128, 512], F32)), Buf()) for i in range(4)]
    for l in range(2):
        for cb in range(24):
            st, stb = stg[(l * 24 + cb) % 3]
            src = ada_w[l, :, cb * 512:(cb + 1) * 512].rearrange("(k p) c -> p k c", p=128)
            if MODS_DT == F32:
                P.dma("sp", st[:], src, writes=[stb])
            else:
                P.dma("pool", st[:].bitcast(MODS_DT), src, writes=[stb])
            if cb % 2 == 0:
                k.do_pre(1)
            psr, psrb = k.ps()
            for dt in range(16):
                P.op("pe", lambda e, o=psr[0:1, :], w=cact[:, dt:dt + 1].bitcast(MODS_DT), r=st[:, dt, :].bitcast(MODS_DT), s=(dt == 0), t=(dt == 15):
                     e.matmul(o, lhsT=w, rhs=r, start=s, stop=t), reads=[stb, cb_], writes=[psrb])
            P.op("act" if cb % 2 else "dve",
                 (lambda e, o=row[0:1, cb * 512:(cb + 1) * 512], i=psr[0:1, :]: e.activation(out=o, in_=i, func=AF.Copy)) if cb % 2 else
                 (lambda e, o=row[0:1, cb * 512:(cb + 1) * 512], i=psr[0:1, :]: e.tensor_copy(out=o, in_=i)),
                 reads=[psrb], writes=[rowb])
        ps, psb = k.ps()
        for e_ in range(96):
            P.op("pe", lambda e, o=ps[:, e_:e_ + 1], w=row[0:1, e_ * 128:(e_ + 1) * 128]: e.matmul(o, lhsT=w, rhs=one1[0:1, 0:1], start=True, stop=True),
                 reads=[rowb, cb_], writes=[psb])
        P.op("dve", lambda e, o=raw[:, l, :], i=ps[:, 0:96], b=abt[:, l, :]: e.tensor_tensor(out=o, in0=i, in1=b, op=ALU.add),
             reads=[psb, sm_], writes=[rawb])
        for half, (gpre, gpost) in enumerate(((0, 1), (2, 3))):
            b0 = half * 3
            P.op("dve", lambda e, o=outt[:, l, b0 + 0, :], i=raw[:, l, (b0 + 1) * 16:(b0 + 2) * 16], g=ngt[:, l, gpre, :]:
                 e.scalar_tensor_tensor(out=o, in0=i, scalar=1.0, in1=g, op0=ALU.add, op1=ALU.mult),
                 reads=[rawb, sm_], writes=[ob])
            P.op("dve", lambda e, o=outt[:, l, b0 + 1, :], i=raw[:, l, (b0 + 0) * 16:(b0 + 1) * 16]:
                 e.tensor_copy(out=o, in_=i), reads=[rawb], writes=[ob])
            P.op("dve", lambda e, o=outt[:, l, b0 + 2, :], i=raw[:, l, (b0 + 2) * 16:(b0 + 3) * 16], g=ngt[:, l, gpost, :]:
                 e.tensor_tensor(out=o, in0=i, in1=g, op=ALU.mult), reads=[rawb, sm_], writes=[ob])
    P.dma("sp", mods, outt[:].rearrange("p l j d -> p (l j d)"), reads=[ob])
    k.do_pre()
    k.close()
    return nc


def build_mlp(l, env=None):
    k = Ctx("mlp", env)
    nc, P = k.nc, k.P
    TT = 512
    xT = k.dram("xT", [D, T], F32, "ExternalInput")
    modsd = k.dram("mods", [128, 192], F32, "ExternalInput")
    k.do_pre()
    wup = wsrc(k, "w_up", [D, FF])
    wdn = wsrc(k, "w_dn", [FF, D])
    oT = k.dram("oT", [D, T], F32, "ExternalOutput")
    k.init_psum(8)
    c = make_consts(k)
    mt = k.sb([128, 2, 6, 16], F32)
    mb = Buf()
    P.dma("sp", mt[:].rearrange("p l j d -> p (l j d)"), modsd, writes=[mb])
    A, Sh, G = mt[:, l, 3, :], mt[:, l, 4, :], mt[:, l, 5, :]
    Xs = [(k.sb([128, ND, TT], F32), Buf()) for _ in range(2)]
    Hs_ = [(k.sb([128, ND, TT], BF16), Buf()) for _ in range(2)]
    U = k.sb([128, 32, TT], BF16)
    Y = k.sb([128, ND, TT], F32)
    Ub, Yb = Buf(), Buf()
    wl = WLoader(k, 16, 256, 3)
    xT3 = xT.rearrange("(k p) t -> p k t", p=128)
    oT3 = oT.rearrange("(k p) t -> p k t", p=128)
    NT_ = T // TT

    def load_norm(tt):
        X, Xb = Xs[tt % 2]
        H, Hb = Hs_[tt % 2]
        P.dma("sp", X[:], xT3[:, :, tt * TT:(tt + 1) * TT], writes=[Xb])
        rstd, rb = rms_rstd(k, c, X, Xb, TT)
        norm_mod(k, X, Xb, rstd, rb, A, Sh, mb, H, Hb, TT)

    load_norm(0)
    for tt in range(NT_):
        X, Xb = Xs[tt % 2]
        H, Hb = Hs_[tt % 2]
        for fh in range(2):
            for fb in range(16):
                f0 = fh * 4096 + fb * 256
                wb, wbb = wl.load(wup, 0, 16, f0, 256)
                for j in range(2):
                    ps, psb = k.ps()
                    for dt in range(16):
                        P.op("pe", lambda e, o=ps[:, :TT], w=wb[:, dt, j * 128:(j + 1) * 128], r=H[:, dt, :],
                             s=(dt == 0), t=(dt == 15): e.matmul(o, lhsT=w, rhs=r, start=s, stop=t),
                             reads=[wbb, Hb], writes=[psb])
                    rl, rlb = k.rotbuf("relu", [128, 512], F32, 3)
                    P.op("act", lambda e, o=rl[:, :TT], i=ps[:, :TT]: e.activation(out=o, in_=i, func=AF.Relu),
                         reads=[psb], writes=[rlb])
                    P.op("dve", lambda e, o=U[:, fb * 2 + j, :], i=rl[:, :TT]: e.tensor_tensor(out=o, in0=i, in1=i, op=ALU.mult),
                         reads=[rlb], writes=[Ub])
            if fh == 0:
                if tt > 0:
                    Xp, Xpb = Xs[(tt - 1) % 2]
                    post_residual(k, c, Xp, Xpb, Y, Yb, G, mb, TT)
                    P.dma("sp", oT3[:, :, (tt - 1) * TT:tt * TT], Xp[:], reads=[Xpb])
                if tt + 1 < NT_:
                    load_norm(tt + 1)
            for db in range(8):
                pss = [k.ps(), k.ps()]
                for kb in range(2):
                    wb, wbb = wl.load(wdn, fh * 4096 + kb * 2048, 16, db * 256, 256)
                    for j in range(2):
                        ps, psb = pss[j]
                        for ft in range(16):
                            P.op("pe", lambda e, o=ps[:, :TT], w=wb[:, ft, j * 128:(j + 1) * 128], r=U[:, kb * 16 + ft, :],
                                 s=(kb == 0 and ft == 0), t=(kb == 1 and ft == 15): e.matmul(o, lhsT=w, rhs=r, start=s, stop=t),
                                 reads=[wbb, Ub], writes=[psb])
                for j in range(2):
                    ps, psb = pss[j]
                    if fh == 0:
                        P.op("act", lambda e, o=Y[:, db * 2 + j, :], i=ps[:, :TT]: e.activation(out=o, in_=i, func=AF.Copy),
                             reads=[psb], writes=[Yb])
                    else:
                        P.op("dve", lambda e, o=Y[:, db * 2 + j, :], i=ps[:, :TT]: e.tensor_tensor(out=o, in0=o, in1=i, op=ALU.add),
                             reads=[psb], writes=[Yb])
    Xp, Xpb = Xs[(NT_ - 1) % 2]
    post_residual(k, c, Xp, Xpb, Y, Yb, G, mb, TT)
    P.dma("sp", oT3[:, :, (NT_ - 1) * TT:NT_ * TT], Xp[:], reads=[Xpb])
    k.close()
    return nc


def load_swap(wl, W, nk, c0):
    P = wl.k.P
    wb, wbb = wl.wbf[wl.j]
    wl.j = (wl.j + 1) % len(wl.wbf)
    for (a, b_) in ((0, 32), (32, 0)):
        src = W.ap[0:nk * 128, c0 + b_:c0 + b_ + 32].rearrange("(k p) c -> p k c", p=128)
        P.dma("sp", wb[:, :nk, a:a + 32], src, reads=[W.buf], writes=[wbb])
    return wb, wbb


def angle_reduce(k, ang, kf, ki, ab):
    import math
    P = k.P
    P.op("dve", lambda e: e.tensor_scalar(out=kf, in0=ang, scalar1=1.0 / (2 * math.pi), scalar2=None, op0=ALU.mult), reads=[ab], writes=[ab])
    P.op("dve", lambda e: e.tensor_copy(out=ki, in_=kf), reads=[ab], writes=[ab])
    P.op("dve", lambda e: e.tensor_copy(out=kf, in_=ki), reads=[ab], writes=[ab])
    P.op("dve", lambda e: e.scalar_tensor_tensor(out=ang, in0=kf, scalar=-2 * math.pi, in1=ang, op0=ALU.mult, op1=ALU.add), reads=[ab], writes=[ab])
    P.op("dve", lambda e: e.tensor_scalar(out=kf, in0=ang, scalar1=math.pi, scalar2=-2 * math.pi, op0=ALU.is_gt, op1=ALU.mult), reads=[ab], writes=[ab])
    P.op("dve", lambda e: e.tensor_tensor(out=ang, in0=ang, in1=kf, op=ALU.add), reads=[ab], writes=[ab])
    P.op("dve", lambda e: e.tensor_scalar(out=kf, in0=ang, scalar1=-math.pi, scalar2=2 * math.pi, op0=ALU.is_lt, op1=ALU.mult), reads=[ab], writes=[ab])
    P.op("dve", lambda e: e.tensor_tensor(out=ang, in0=ang, in1=kf, op=ALU.add), reads=[ab], writes=[ab])


def build_mla(l=1, env=None):
    import math
    k = Ctx("mla", env)
    nc, P = k.nc, k.P
    TT = 512
    NTT = T // TT
    xT = k.dram("xT", [D, T], F32, "ExternalInput")
    modsd = k.dram("mods", [128, 192], F32, "ExternalInput")
    posr = k.dram("posr", [64, T], I32, "ExternalInput")
    ropec = k.dram("ropec", [64, 2], F32, "ExternalInput")
    vec = k.dram("mla_vec", [128, 24], F32, "ExternalInput")
    k.do_pre()
    kvd = wsrc(k, "kv_down", [D, 576])
    wuk = wsrc(k, "kv_uk", [512, D])
    wuv = wsrc(k, "kv_uv", [512, D])
    wdq = wsrc(k, "w_dq", [D, 512])
    wuq = wsrc(k, "w_uq", [512, 16 * 192])
    wo = wsrc(k, "w_o", [D, D])
    oT = k.dram("oT", [D, T], F32, "ExternalOutput")
    otd = k.dram("ot_scratch", [16, 128, T], BF16, "Internal")
    k.init_psum(8)
    oacc = k.psl[4:]
    k.psl = k.psl[:4]
    c = make_consts(k)
    mt = k.sb([128, 2, 6, 16], F32)
    vt = k.sb([128, 24], F32)
    zer = k.sb([128, 16], F32)
    mb = Buf()
    P.dma("sp", mt[:].rearrange("p l j d -> p (l j d)"), modsd, writes=[mb])
    P.dma("sp", vt[:], vec, writes=[mb])
    P.op("pool", lambda e: e.memset(zer[:], 0.0), writes=[mb])
    A, Sh, G = mt[:, l, 0, :], mt[:, l, 1, :], mt[:, l, 2, :]

    X = k.sb([128, ND, TT], F32)
    Y = k.sb([128, ND, TT], F32)
    Xb, Yb = Buf(), Buf()
    Yf = Y[:].rearrange("p a b -> p (a b)")
    Ybf = Yf.bitcast(BF16)
    Xbf = X[:].rearrange("p a b -> p (a b)").bitcast(BF16)
    HS = Ybf[:, 0:8192].rearrange("p (a b) -> p a b", a=ND)
    HH = Ybf[:, 8192:16384].rearrange("p (a b) -> p a b", a=ND)
    rc = k.sb([64, 2], F32)
    cos2 = k.sb([64, T], F32)
    sinS = k.sb([64, T], F32)
    csb = Buf()
    P.dma("sp", rc[:], ropec, writes=[mb])
    pi_t = Yf[:64, 0:512].bitcast(I32)
    ang = Yf[:64, 512:1024]
    tmp = Yf[:64, 1024:1536]
    kf = Yf[:64, 1536:2048]
    ki = Yf[:64, 2048:2560].bitcast(I32)
    for ch in range(4):
        t0 = ch * 512
        P.dma("sp", pi_t, posr[:, t0:t0 + 512], writes=[Yb])
        P.op("dve", lambda e: e.tensor_copy(out=ang, in_=pi_t), reads=[Yb], writes=[Yb])
        P.op("dve", lambda e: e.tensor_scalar(out=ang, in0=ang, scalar1=rc[:, 0:1], scalar2=None, op0=ALU.mult), reads=[Yb, mb], writes=[Yb])
        P.op("dve", lambda e: e.tensor_scalar(out=tmp, in0=ang, scalar1=math.pi / 2, scalar2=None, op0=ALU.add), reads=[Yb], writes=[Yb])
        angle_reduce(k, tmp, kf, ki, Yb)
        P.op("act", lambda e, o=cos2[:, t0:t0 + 512]: e.activation(out=o, in_=tmp, func=AF.Sin), reads=[Yb], writes=[csb])
        angle_reduce(k, ang, kf, ki, Yb)
        P.op("act", lambda e, o=sinS[:, t0:t0 + 512]: e.activation(out=o, in_=ang, func=AF.Sin), reads=[Yb], writes=[csb])
        P.op("dve", lambda e, o=sinS[:, t0:t0 + 512]: e.tensor_scalar(out=o, in0=o, scalar1=rc[:, 1:2], scalar2=None, op0=ALU.mult), reads=[csb, mb], writes=[csb])

    CKQ = k.sb([128, 8, TT], F32)
    CK = CKQ[:, 0:4, :]
    CQ = CKQ[:, 4:8, :]
    CKb, CQb = Buf(), Buf()
    CKN = k.sb([128, 4, T], BF16)
    CQN = k.sb([128, 4, T], BF16)
    KR = k.sb([128, T], BF16)
    CKNb, CQNb, KRb = Buf(), Buf(), Buf()
    P.op("pool", lambda e: e.memset(KR[:], 0.0), writes=[KRb])
    wl = WLoader(k, 16, 128, 4)
    xT3 = xT.rearrange("(k p) t -> p k t", p=128)
    oT3 = oT.rearrange("(k p) t -> p k t", p=128)

    def rope_out(ps1, ps1b, ps2, ps2b, dst, dstb, t0):
        t1, t1b = k.rotbuf("rp1", [64, 512], F32, 1)
        t2, t2b = k.rotbuf("rp2", [64, 512], F32, 1)
        P.op("dve", lambda e: e.tensor_tensor(out=t1[:], in0=ps1[:64, :TT], in1=cos2[:, t0:t0 + TT], op=ALU.mult), reads=[ps1b, csb], writes=[t1b])
        P.op("dve", lambda e: e.tensor_tensor(out=t2[:], in0=ps2[:64, :TT], in1=sinS[:, t0:t0 + TT], op=ALU.mult), reads=[ps2b, csb], writes=[t2b])
        P.op("pool", lambda e: e.tensor_tensor(out=dst[:64, t0:t0 + TT], in0=t1[:], in1=t2[:], op=ALU.add), reads=[t1b, t2b], writes=[dstb])

    for tt in range(NTT):
        t0 = tt * TT
        P.dma("sp", X[:], xT3[:, :, t0:t0 + TT], writes=[Xb])
        rstd, rb = rms_rstd(k, c, X, Xb, TT)
        norm_mod(k, X, Xb, rstd, rb, vt[:, 0:16], zer, mb, HS, Yb, TT)
        norm_mod(k, X, Xb, rstd, rb, A, Sh, mb, HH, Yb, TT)
        for (W, src, dst, dstb) in ((kvd, HS, CK, CKb), (wdq, HH, CQ, CQb)):
            for cb in range(4):
                wb, wbb = wl.load(W, 0, 16, cb * 128, 128)
                ps, psb = k.ps()
                for dt in range(16):
                    P.op("pe", lambda e, o=ps[:, :TT], w=wb[:, dt, :], r=src[:, dt, :], s=(dt == 0), t=(dt == 15):
                         e.matmul(o, lhsT=w, rhs=r, start=s, stop=t), reads=[wbb, Yb], writes=[psb])
                P.op("act", lambda e, o=dst[:, cb, :], i=ps[:, :TT]: e.activation(out=o, in_=i, func=AF.Copy), reads=[psb], writes=[dstb])
        pss = []
        for sw in range(2):
            if sw == 0:
                wb, wbb = wl.load(kvd, 0, 16, 512, 64)
            else:
                wb, wbb = load_swap(wl, kvd, 16, 512)
            ps, psb = k.ps()
            for dt in range(16):
                P.op("pe", lambda e, o=ps[:64, :TT], w=wb[:, dt, 0:64], r=HS[:, dt, :], s=(dt == 0), t=(dt == 15):
                     e.matmul(o, lhsT=w, rhs=r, start=s, stop=t), reads=[wbb, Yb], writes=[psb])
            pss.append((ps, psb))
        rope_out(pss[0][0], pss[0][1], pss[1][0], pss[1][1], KR, KRb, t0)
        for (src, srcb, dst, dstb, v0) in ((CK, CKb, CKN, CKNb, 16), (CQ, CQb, CQN, CQNb, 20)):
            rs, rsb = rms_rstd(k, c, src, srcb, TT, scale_div=512, ntile=4)
            for ct in range(4):
                P.op("dve", lambda e, o=dst[:, ct, t0:t0 + TT], i=src[:, ct, :], s=vt[:, v0 + ct:v0 + ct + 1], r=rs[:, :TT]:
                     e.scalar_tensor_tensor(out=o, in0=i, scalar=s, in1=r, op0=ALU.mult, op1=ALU.mult),
                     reads=[srcb, rsb, mb], writes=[dstb])

    P.barrier()
    tri = k.sb([128, 128], BF16)
    trib = Buf()
    P.op("pool", lambda e: e.memset(tri[:], 1.0), writes=[trib])
    P.op("pool", lambda e: e.affine_select(out=tri[:], in_=tri[:], pattern=[[1, 128]], compare_op=ALU.is_ge, fill=0.0,
                                           base=0, channel_multiplier=-1), reads=[trib], writes=[trib])
    wl2 = WLoader(k, 4, 128, 8)
    scale = 192.0 ** -0.5
    hb = []
    for reg in (Ybf, Xbf):
        hb.append(dict(KN=reg[:, 0:2048], QN=reg[:, 2048:4096], QR=reg[:, 4096:6144], OH=reg[:, 6144:8192],
                       VH=reg[:, 8192:10240].rearrange("p (a b) -> p a b", a=16),
                       KNb=Buf(), QNb=Buf(), QRb=Buf(), OHb=Buf(), VHb=Buf()))
    for s_ in hb:
        P.op("pool", lambda e, o=s_["QR"]: e.memset(o, 0.0), writes=[s_["QRb"]])
    for h in range(16):
        s_ = hb[h % 2]
        KN, QN, QR, OH, VH = s_["KN"], s_["QN"], s_["QR"], s_["OH"], s_["VH"]
        KNb, QNb, QRb, OHb, VHb = s_["KNb"], s_["QNb"], s_["QRb"], s_["OHb"], s_["VHb"]
        wk, wkb = wl2.load(wuk, 0, 4, h * 128, 128)
        wq, wqb = wl2.load(wuq, 0, 4, h * 192, 128)
        wv, wvb = wl2.load(wuv, 0, 4, h * 128, 128)
        wr, wrb = wl2.load(wuq, 0, 4, h * 192 + 128, 64)
        ws, wsb = load_swap(wl2, wuq, 4, h * 192 + 128)
        for tq in range(NTT):
            t0 = tq * TT
            for (w_, wb_, src, srcb, dst, dstb) in ((wk, wkb, CKN, CKNb, KN, KNb), (wq, wqb, CQN, CQNb, QN, QNb)):
                ps, psb = k.ps()
                for ct in range(4):
                    P.op("pe", lambda e, o=ps[:, :TT], w=w_[:, ct, :], r=src[:, ct, t0:t0 + TT], s=(ct == 0), t=(ct == 3):
                         e.matmul(o, lhsT=w, rhs=r, start=s, stop=t), reads=[wb_, srcb], writes=[psb])
                P.op("act", lambda e, o=dst[:, t0:t0 + TT], i=ps[:, :TT]: e.activation(out=o, in_=i, func=AF.Copy), reads=[psb], writes=[dstb])
            pss = []
            for (w_, wb_) in ((wr, wrb), (ws, wsb)):
                ps, psb = k.ps()
                for ct in range(4):
                    P.op("pe", lambda e, o=ps[:64, :TT], w=w_[:, ct, 0:64], r=CQN[:, ct, t0:t0 + TT], s=(ct == 0), t=(ct == 3):
                         e.matmul(o, lhsT=w, rhs=r, start=s, stop=t), reads=[wb_, CQNb], writes=[psb])
                pss.append((ps, psb))
            rope_out(pss[0][0], pss[0][1], pss[1][0], pss[1][1], QR, QRb, t0)
        for tk4 in range(4):
            ps, psb = k.ps()
            for i in range(4):
                tk = tk4 * 4 + i
                for ct in range(4):
                    P.op("pe", lambda e, o=ps[:, i * 128:(i + 1) * 128], w=CKN[:, ct, tk * 128:(tk + 1) * 128], r=wv[:, ct, :], s=(ct == 0), t=(ct == 3):
                         e.matmul(o, lhsT=w, rhs=r, start=s, stop=t), reads=[wvb, CKNb], writes=[psb])
            P.op("act", lambda e, o=VH[:, tk4 * 4:tk4 * 4 + 4, :], i=ps[:, :].rearrange("p (a b) -> p a b", a=4):
                 e.activation(out=o, in_=i, func=AF.Copy), reads=[psb], writes=[VHb])
        for qt in range(NTT):
            oa, oab = oacc[(qt % 2) * 2]
            da, dab = oacc[(qt % 2) * 2 + 1]
            nk_ = 4 * (qt + 1)
            def score(kt):
                off = max(0, (kt - 4 * qt) * 128)
                q0 = qt * TT + off
                q1 = (qt + 1) * TT
                sp_, spb = k.ps()
                P.op("pe", lambda e, o=sp_[:, off:TT], w=KN[:, kt * 128:(kt + 1) * 128], r=QN[:, q0:q1]:
                     e.matmul(o, lhsT=w, rhs=r, start=True, stop=False), reads=[KNb, QNb], writes=[spb])
                P.op("pe", lambda e, o=sp_[:, off:TT], w=KR[:, kt * 128:(kt + 1) * 128], r=QR[:, q0:q1]:
                     e.matmul(o, lhsT=w, rhs=r, start=False, stop=True), reads=[KRb, QRb], writes=[spb])
                PT, PTb = k.rotbuf("PT", [128, TT], BF16, 4)
                P.op("act", lambda e, o=PT[:, off:TT], i=sp_[:, off:TT]: e.activation(out=o, in_=i, func=AF.Exp, scale=scale),
                     reads=[spb], writes=[PTb])
                if kt >= 4 * qt:
                    P.op("pool", lambda e, o=PT[:, off:off + 128]: e.tensor_tensor(out=o, in0=o, in1=tri[:], op=ALU.mult),
                         reads=[PTb, trib], writes=[PTb])
                return (kt, off, PT, PTb)

            def pv(st_):
                kt, off, PT, PTb = st_
                P.op("pe", lambda e, o=oa[:, off:TT], w=VH[:, kt, :], r=PT[:, off:TT], s=(kt == 0), t=(kt == nk_ - 1):
                     e.matmul(o, lhsT=w, rhs=r, start=s, stop=t), reads=[VHb, PTb], writes=[oab])
                P.op("pe", lambda e, o=da[:, off:TT], r=PT[:, off:TT], s=(kt == 0), t=(kt == nk_ - 1):
                     e.matmul(o, lhsT=c["ones"][:], rhs=r, start=s, stop=t), reads=[c["onesb"], PTb], writes=[dab])

            pend = []
            for kt in range(nk_):
                pend.append(score(kt))
                if len(pend) > 2:
                    pv(pend.pop(0))
            while pend:
                pv(pend.pop(0))
            rd, rdb = k.rotbuf("rden", [128, TT], F32, 2)
            P.op("dve", lambda e, o=rd[:], i=da[:, :TT]: e.reciprocal(out=o, in_=i), reads=[dab], writes=[rdb])
            P.op("dve", lambda e, o=OH[:, qt * TT:(qt + 1) * TT], i=oa[:, :TT], r=rd[:]: e.tensor_tensor(out=o, in0=i, in1=r, op=ALU.mult),
                 reads=[oab, rdb], writes=[OHb])
        P.dma("sp", otd[h], OH, reads=[OHb])

    P.barrier()
    OTt = CKQ[:].rearrange("p a b -> p (a b)").bitcast(BF16).rearrange("p (a b) -> p a b", a=16)
    OTb = Buf()
    otd3 = otd.rearrange("h p t -> p h t")
    Xb, Yb = Buf(), Buf()
    for tt in range(NTT):
        t0 = tt * TT
        P.dma("sp", OTt, otd3[:, :, t0:t0 + TT], writes=[OTb])
        P.dma("sp", X[:], xT3[:, :, t0:t0 + TT], writes=[Xb])
        for eb in range(16):
            wb, wbb = wl.load(wo, 0, 16, eb * 128, 128)
            ps, psb = k.ps()
            for hh in range(16):
                P.op("pe", lambda e, o=ps[:, :TT], w=wb[:, hh, :], r=OTt[:, hh, :], s=(hh == 0), t=(hh == 15):
                     e.matmul(o, lhsT=w, rhs=r, start=s, stop=t), reads=[wbb, OTb], writes=[psb])
            P.op("act", lambda e, o=Y[:, eb, :], i=ps[:, :TT]: e.activation(out=o, in_=i, func=AF.Copy), reads=[psb], writes=[Yb])
        post_residual(k, c, X, Xb, Y, Yb, G, mb, TT)
        P.dma("sp", oT3[:, :, t0:t0 + TT], X[:], reads=[Xb])
    k.close()
    return nc


def build_rwkv(l=0, dbg=False, env=None):
    k = Ctx("rwkv", env)
    k.rotw = 256
    nc, P = k.nc, k.P
    TT = 256
    NTT = T // TT
    C = 64
    NCH = TT // C
    xT = k.dram("xT", [D, T], F32, "ExternalInput")
    modsd = k.dram("mods", [128, 192], F32, "ExternalInput")
    vec = k.dram("rw_vec", [128, 13, 16], F32, "ExternalInput")
    wrkv = wsrc(k, "w_rkv", [3 * D, D])
    w1 = wsrc(k, "w1", [D, 96])
    w2 = wsrc(k, "w2", [96, D])
    a1 = wsrc(k, "a1", [D, 96])
    a2 = wsrc(k, "a2", [96, D])
    g1 = wsrc(k, "g1", [D, 256])
    g2 = wsrc(k, "g2", [256, D])
    wo = wsrc(k, "w_o", [D, D])
    oT = k.dram("oT", [D, T], F32, "ExternalOutput")
    k.init_psum(8)
    c = make_consts(k)
    mt = k.sb([128, 2, 6, 16], F32)
    vt = k.sb([128, 13, 16], F32)
    mb = Buf()
    P.dma("sp", mt[:].rearrange("p l j d -> p (l j d)"), modsd, writes=[mb])
    P.dma("sp", vt[:], vec, writes=[mb])
    A, Sh, G = mt[:, l, 0, :], mt[:, l, 1, :], mt[:, l, 2, :]
    MU, W0, A0, KKv, KA, RK, LNW, LNB = (lambda j: vt[:, j, :]), vt[:, 6, :], vt[:, 7, :], vt[:, 8, :], vt[:, 9, :], vt[:, 10, :], vt[:, 11, :], vt[:, 12, :]

    NEG = k.sb([128, 2, 16], F32)
    P.op("dve", lambda e: e.tensor_scalar(out=NEG[:], in0=vt[:, 6:8, :], scalar1=-1.0, scalar2=None, op0=ALU.mult), reads=[mb], writes=[mb])
    cb_ = Buf()
    bo16 = k.sb([128, 128], BF16)
    bo32 = k.sb([128, 128], F32)
    idn = k.sb([128, 4, 128], BF16)
    mS = k.sb([128, 4, 64], BF16)
    mI = k.sb([128, 4, 64], BF16)
    mL = k.sb([128, 4, 64], BF16)
    ones64 = k.sb([128, 64], F32)
    for t_ in (bo16, bo32):
        P.op("pool", lambda e, t_=t_: e.memset(t_[:], 0.0), writes=[cb_])
        P.op("pool", lambda e, t_=t_: e.memset(t_[0:64, 0:64], 1.0), writes=[cb_])
        P.op("pool", lambda e, t_=t_: e.memset(t_[64:128, 64:128], 1.0), writes=[cb_])
    P.op("pool", lambda e: e.memset(ones64[:], 1.0), writes=[cb_])
    rmask = k.sb([128, TT], F32)
    P.op("pool", lambda e: e.memset(rmask[:], 1.0), writes=[cb_])
    for ch_ in range(NCH):
        P.op("pool", lambda e, o=rmask[:, ch_ * C:ch_ * C + 1]: e.memset(o, 0.0), writes=[cb_])
    P.op("pool", lambda e: e.memset(idn[:], 1.0), writes=[cb_])
    P.op("pool", lambda e: e.memset(mS[:], 1.0), writes=[cb_])
    P.op("pool", lambda e: e.memset(mI[:], 1.0), writes=[cb_])
    P.op("pool", lambda e: e.memset(mL[:], 1.0), writes=[cb_])
    for g_ in range(4):
        P.op("pool", lambda e, o=idn[:, g_, :]: e.affine_select(out=o, in_=o, pattern=[[1, 128]], compare_op=ALU.is_equal, fill=0.0,
                                                               base=0, channel_multiplier=-1), reads=[cb_], writes=[cb_])
        for hf in range(2):
            sl = slice(64 * hf, 64 * hf + 64)
            P.op("pool", lambda e, o=mS[sl, g_, :]: e.affine_select(out=o, in_=o, pattern=[[1, 64]], compare_op=ALU.is_ge, fill=0.0,
                                                                    base=-1, channel_multiplier=-1), reads=[cb_], writes=[cb_])
            P.op("pool", lambda e, o=mI[sl, g_, :]: e.affine_select(out=o, in_=o, pattern=[[1, 64]], compare_op=ALU.is_ge, fill=0.0,
                                                                    base=0, channel_multiplier=-1), reads=[cb_], writes=[cb_])
            P.op("pool", lambda e, o=mL[sl, g_, :]: e.affine_select(out=o, in_=o, pattern=[[-1, 64]], compare_op=ALU.is_ge, fill=0.0,
                                                                    base=-1, channel_multiplier=1), reads=[cb_], writes=[cb_])

    RT = k.sb([128, 16, TT], BF16)
    KT = k.sb([128, 16, TT], BF16)
    BT = k.sb([128, 16, TT], BF16)
    AT = k.sb([128, 16, TT], BF16)
    VT = k.sb([128, 16, TT], BF16)
    GT = k.sb([128, 16, TT], BF16)
    BON = k.sb([128, 16, TT], BF16)
    GC = k.sb([128, 16, NCH], F32)
    RTb, KTb, BTb, ATb, VTb, GTb, BONb, GCb, YTb = (Buf() for _ in range(9))
    Hf = k.sb([128, 16, 64], F32)
    Hstk = k.sb([128, 16, 64], BF16)
    Hbd = k.sb([128, 16, 128], BF16)
    Hb_ = [Buf() for _ in range(4)]
    HL = k.sb([128, 16, 1], F32)
    HLb = Buf()
    P.op("pool", lambda e: e.memset(Hf[:], 0.0), writes=Hb_)
    P.op("pool", lambda e: e.memset(Hstk[:], 0.0), writes=Hb_)
    P.op("pool", lambda e: e.memset(Hbd[:], 0.0), writes=Hb_)
    P.op("pool", lambda e: e.memset(HL[:], 0.0), writes=[HLb])
    TW = k.sb([128, TT], BF16)
    TA = k.sb([128, TT], BF16)
    TG = k.sb([128, 2, TT], BF16)
    TWb, TAb, TGb = Buf(), Buf(), Buf()
    wl = WLoader(k, 16, 128, 3)
    wls = WLoader(k, 2, 128, 3)
    REG = k.sb([128, 18688], F32)
    REGbf = REG[:].bitcast(BF16)

    def f32v(o, n, a):
        return REG[:, o:o + n].rearrange("p (a b) -> p a b", a=a)

    def bfv(o, n, a):
        return REGbf[:, 2 * o:2 * o + 2 * n].rearrange("p (a b) -> p a b", a=a)

    X = f32v(0, 4096, 16)
    Hs = f32v(4096, 4352, 16)
    XX = bfv(8448, 2048, 16)
    XS = bfv(10496, 2048, 16)
    XR = bfv(12544, 2048, 16)
    XK = bfv(14592, 2048, 16)
    XV = bfv(16640, 2048, 16)
    o_ = [0]

    def nxt(n, a):
        v = bfv(o_[0], n, a)
        o_[0] += n
        return v
    ATbd, BTbd, KTbd, VTbd = nxt(1024, 16), nxt(1024, 16), nxt(1024, 16), nxt(1024, 16)
    def chbuf():
        t = k.sb([128, 4, 128], F32)
        return {"r": t[:], "w": t[:].bitcast(CHAIN_DT), "m": t[:].bitcast(CHAIN_DT)}
    CH = [dict(N=[chbuf(), chbuf()], L=[chbuf(), chbuf()], P=chbuf()) for _ in range(2)]
    PF = nxt(1024, 16)
    MakT = nxt(1024, 16)
    MrbT, MrkT = nxt(512, 16), nxt(512, 16)
    Vbd, Vstk = nxt(1024, 16), nxt(512, 16)
    Bbd, Kbd = nxt(1024, 16), nxt(1024, 16)
    Zs, Us, Ubd = nxt(512, 16), nxt(512, 16), nxt(1024, 16)
    ZERO_LIST = [ATbd, BTbd, KTbd, VTbd, CH[0]['N'][0]['w'], CH[0]['L'][0]['w'], CH[1]['N'][0]['w'], CH[1]['L'][0]['w'], MakT, Ubd]
    YT = f32v(12800, 4096, 16)
    OIN = bfv(4096, 2048, 16)
    Y2 = f32v(8448, 4096, 16)

    xT3 = xT.rearrange("(k p) t -> p k t", p=128)
    oT3 = oT.rearrange("(k p) t -> p k t", p=128)
    if dbg:
        dbf = k.dram("dbg_bf", [7, 128, 16 * TT], BF16, "ExternalOutput")
        dyt = k.dram("dbg_yt", [128, 16 * TT], F32, "ExternalOutput")
        dgc = k.dram("dbg_gc", [128, 16 * NCH], F32, "ExternalOutput")
        doin = k.dram("dbg_oin", [128, 16 * TT], BF16, "ExternalOutput")

    def tmp(name, n=1, dt=F32, w=TT):
        return k.rotbuf(name, [128, w], dt, n)

    for tt in range(1 if dbg else NTT):
        t0 = tt * TT
        Xb, Hsb, XXb, XSb, XRb, XKb, XVb = (Buf() for _ in range(7))
        k.do_pre(None if tt == NTT - 1 else 6)
        P.dma("sp", X, xT3[:, :, t0:t0 + TT], writes=[Xb])
        rstd, rb = rms_rstd(k, c, X, Xb, TT)
        norm_mod(k, X, Xb, rstd, rb, A, Sh, mb, Hs, Hsb, TT, col0=1)
        P.op("pool", lambda e: e.tensor_copy(out=Hs[:, :, 0:1], in_=HL[:]), reads=[HLb], writes=[Hsb])
        P.op("dve", lambda e: e.tensor_tensor(out=XX, in0=Hs[:, :, 0:TT], in1=Hs[:, :, 1:TT + 1], op=ALU.subtract), reads=[Hsb], writes=[XXb])
        P.op("pool", lambda e: e.tensor_copy(out=HL[:], in_=Hs[:, :, TT:TT + 1]), reads=[Hsb], writes=[HLb])

        def make_xs(j, dst, dstb):
            for dt in range(16):
                P.op("dve", lambda e, o=dst[:, dt, :], i=XX[:, dt, :], s=vt[:, j, dt:dt + 1], h=Hs[:, dt, 1:TT + 1]:
                     e.scalar_tensor_tensor(out=o, in0=i, scalar=s, in1=h, op0=ALU.mult, op1=ALU.add),
                     reads=[XXb, Hsb, mb], writes=[dstb])
        for (j, W, ncol) in ((3, w1, 96), (4, a1, 96), (5, g1, 256)):
            make_xs(j, XS, XSb)
            for cbk in range((ncol + 127) // 128):
                nc_ = min(128, ncol - cbk * 128)
                wb, wbb = wl.load(W, 0, 16, cbk * 128, nc_)
                ps, psb = k.ps()
                for dt in range(16):
                    P.op("pe", lambda e, o=ps[:nc_, :TT], w=wb[:, dt, :nc_], r=XS[:, dt, :], s=(dt == 0), t=(dt == 15):
                         e.matmul(o, lhsT=w, rhs=r, start=s, stop=t), reads=[wbb, XSb], writes=[psb])
                if j == 3:
                    P.op("act", lambda e, i=ps[:96, :TT]: e.activation(out=TW[:96, :], in_=i, func=AF.Tanh), reads=[psb], writes=[TWb])
                elif j == 4:
                    P.op("act", lambda e, i=ps[:96, :TT]: e.activation(out=TA[:96, :], in_=i, func=AF.Copy), reads=[psb], writes=[TAb])
                else:
                    P.op("act", lambda e, i=ps[:, :TT], o=TG[:, cbk, :]: e.activation(out=o, in_=i, func=AF.Sigmoid), reads=[psb], writes=[TGb])
        make_xs(0, XR, XRb)
        make_xs(1, XK, XKb)
        make_xs(2, XV, XVb)
        for p in range(16):
            e0 = p * 128
            pA, pAb = k.ps()
            pB, pBb = k.ps()
            pC, pCb = k.ps()
            pD, pDb = k.ps()
            for (jj, src, srcb, ps, psb, co) in ((0, XR, XRb, pA, pAb, 0), (1, XK, XKb, pA, pAb, TT), (2, XV, XVb, pB, pBb, 0)):
                wb, wbb = wl.load(wrkv, jj * D, 16, e0, 128)
                for dt in range(16):
                    P.op("pe", lambda e, o=ps[:, co:co + TT], w=wb[:, dt, :], r=src[:, dt, :], s=(dt == 0), t=(dt == 15):
                         e.matmul(o, lhsT=w, rhs=r, start=s, stop=t), reads=[wbb, srcb], writes=[psb])
            wb, wbb = wls.load(w2, 0, 1, e0, 128, pp=96)
            P.op("pe", lambda e, o=pB[:, TT:2 * TT], w=wb[:96, 0, :]: e.matmul(o, lhsT=w, rhs=TW[:96, :], start=True, stop=True),
                 reads=[wbb, TWb], writes=[pBb])
            wb, wbb = wls.load(a2, 0, 1, e0, 128, pp=96)
            P.op("pe", lambda e, o=pC[:, 0:TT], w=wb[:96, 0, :]: e.matmul(o, lhsT=w, rhs=TA[:96, :], start=True, stop=True),
                 reads=[wbb, TAb], writes=[pCb])
            wb, wbb = wls.load(g2, 0, 2, e0, 128)
            for kt in range(2):
                P.op("pe", lambda e, o=pC[:, TT:2 * TT], w=wb[:, kt, :], r=TG[:, kt, :], s=(kt == 0), t=(kt == 1):
                     e.matmul(o, lhsT=w, rhs=r, start=s, stop=t), reads=[wbb, TGb], writes=[pCb])
            r_ps, k_ps, v_ps, w_ps, a_ps, g_ps = pA[:, 0:TT], pA[:, TT:2 * TT], pB[:, 0:TT], pB[:, TT:2 * TT], pC[:, 0:TT], pC[:, TT:2 * TT]
            LWS = -0.6065306597126334
            sg, sgb = tmp("sg", 2)
            cl, clb = tmp("cl", 2)
            av, avb = tmp("av", 2)
            vf, vfb = tmp("vf", 2)
            kk, kkb = tmp("kk", 2)
            rn, rnb = tmp("rn", 2)
            kf_, kfb = tmp("kf", 2)
            eg, egb = tmp("eg")
            eig, eigb = tmp("eig")
            eex, eexb = tmp("eex")
            k2, k2b = tmp("ksq", 1, BF16)
            bb, bbb = tmp("bb")
            rk_, rkb = tmp("rkp", 1, BF16)
            lw, lwb = sg, sgb
            P.op("act", lambda e, o=sg[:], i=w_ps, b=NEG[:, 0, p:p + 1]: e.activation(out=o, in_=i, func=AF.Exp, bias=b, scale=-1.0),
                 reads=[pBb, mb], writes=[sgb])
            P.op("dve", lambda e, o=kk[:], i=k_ps, s=KKv[:, p:p + 1]: e.tensor_scalar(out=o, in0=i, scalar1=s, scalar2=None, op0=ALU.mult),
                 reads=[pAb, mb], writes=[kkb])
            P.op("pool", lambda e, o=k2[:], i=kk[:]: e.tensor_tensor(out=o, in0=i, in1=i, op=ALU.mult), reads=[kkb], writes=[k2b])
            P.op("pe", lambda e, o=pD[:, 0:TT], r=k2[:]: e.matmul(o, lhsT=bo16[:], rhs=r, start=True, stop=True), reads=[k2b, cb_], writes=[pDb])
            P.op("act", lambda e, o=av[:], i=a_ps, b=NEG[:, 1, p:p + 1]: e.activation(out=o, in_=i, func=AF.Exp, bias=b, scale=-1.0),
                 reads=[pCb, mb], writes=[avb])
            P.op("act", lambda e, o=GT[:, p, :], i=g_ps: e.activation(out=o, in_=i, func=AF.Copy), reads=[pCb], writes=[GTb])
            P.op("act", lambda e, o=vf[:], i=v_ps: e.activation(out=o, in_=i, func=AF.Copy), reads=[pBb], writes=[vfb])
            P.op("pool", lambda e, o=VT[:, p, :], i=vf[:]: e.tensor_copy(out=o, in_=i), reads=[vfb], writes=[VTb])
            P.op("act", lambda e, o=sg[:]: e.activation(out=o, in_=o, func=AF.Ln, bias=c["eps"][:, 2:3], scale=1.0), reads=[sgb, c["onesb"]], writes=[sgb])
            P.op("act", lambda e, o=sg[:]: e.activation(out=o, in_=o, func=AF.Exp, scale=-1.0), reads=[sgb], writes=[sgb])
            P.op("dve", lambda e, o=cl[:], i=lw[:]: e.tensor_tensor_scan(out=o, data0=rmask[:], data1=i, initial=0.0, op0=ALU.mult, op1=ALU.add),
                 reads=[lwb, cb_], writes=[clb])
            P.op("dve", lambda e, o=rn[:], i=pD[:, 0:TT]: e.tensor_scalar(out=o, in0=i, scalar1=5.5e-20, scalar2=None, op0=ALU.max), reads=[pDb], writes=[rnb])
            P.op("pool", lambda e, o=eex[:], i=cl[:], j_=lw[:]: e.tensor_tensor(out=o, in0=i, in1=j_, op=ALU.subtract), reads=[clb, lwb], writes=[eexb])
            P.op("act", lambda e, o=rn[:]: e.activation(out=o, in_=o, func=AF.Ln), reads=[rnb], writes=[rnb])
            P.op("act", lambda e, o=rn[:]: e.activation(out=o, in_=o, func=AF.Exp, scale=-0.5), reads=[rnb], writes=[rnb])
            P.op("act", lambda e, o=eg[:], i=cl[:]: e.activation(out=o, in_=i, func=AF.Exp, scale=LWS), reads=[clb], writes=[egb])
            P.op("act", lambda e, o=eig[:], i=cl[:]: e.activation(out=o, in_=i, func=AF.Exp, scale=-LWS), reads=[clb], writes=[eigb])
            P.op("act", lambda e, o=eex[:]: e.activation(out=o, in_=o, func=AF.Exp, scale=LWS), reads=[eexb], writes=[eexb])
            P.op("pool", lambda e, o=GC[:, p, :], i=eg[:].rearrange("p (a b) -> p a b", a=NCH)[:, :, C - 1]: e.tensor_copy(out=o, in_=i),
                 reads=[egb], writes=[GCb])
            P.op("act", lambda e, o=av[:]: e.activation(out=o, in_=o, func=AF.Ln, bias=c["eps"][:, 2:3], scale=1.0), reads=[avb, c["onesb"]], writes=[avb])
            P.op("act", lambda e, o=av[:]: e.activation(out=o, in_=o, func=AF.Exp, scale=-1.0), reads=[avb], writes=[avb])
            P.op("dve", lambda e, o=kf_[:], i=av[:], s=KA[:, p:p + 1]: e.tensor_scalar(out=o, in0=i, scalar1=-1.0, scalar2=s, op0=ALU.add, op1=ALU.mult),
                 reads=[avb, mb], writes=[kfb])
            P.op("dve", lambda e, o=kf_[:], i=k_ps: e.scalar_tensor_tensor(out=o, in0=o, scalar=1.0, in1=i, op0=ALU.add, op1=ALU.mult),
                 reads=[kfb, pAb], writes=[kfb])
            P.op("pool", lambda e, o=kk[:], r=rn[:]: e.tensor_tensor(out=o, in0=o, in1=r, op=ALU.mult), reads=[kkb, rnb], writes=[kkb])
            P.op("dve", lambda e, o=RT[:, p, :], i=r_ps, g_=eg[:]: e.tensor_tensor(out=o, in0=i, in1=g_, op=ALU.mult), reads=[pAb, egb], writes=[RTb])
            P.op("pool", lambda e, o=KT[:, p, :], i=kf_[:], g_=eig[:]: e.tensor_tensor(out=o, in0=i, in1=g_, op=ALU.mult), reads=[kfb, eigb], writes=[KTb])
            P.op("pool", lambda e, o=bb[:], i=kk[:], a_=av[:]: e.tensor_tensor(out=o, in0=i, in1=a_, op=ALU.mult), reads=[kkb, avb], writes=[bbb])
            P.op("pool", lambda e, o=BT[:, p, :], i=bb[:], g_=eig[:]: e.tensor_tensor(out=o, in0=i, in1=g_, op=ALU.mult), reads=[bbb, eigb], writes=[BTb])
            P.op("dve", lambda e, o=AT[:, p, :], i=kk[:], g_=eex[:]: e.scalar_tensor_tensor(out=o, in0=i, scalar=-1.0, in1=g_, op0=ALU.mult, op1=ALU.mult),
                 reads=[kkb, eexb], writes=[ATb])
            P.op("dve", lambda e, o=rk_[:], i=r_ps, s=RK[:, p:p + 1], k_=kf_[:]: e.scalar_tensor_tensor(out=o, in0=i, scalar=s, in1=k_, op0=ALU.mult, op1=ALU.mult),
                 reads=[pAb, kfb, mb], writes=[rkb])
            P.op("pe", lambda e, o=pD[:, TT:2 * TT], r=rk_[:]: e.matmul(o, lhsT=bo16[:], rhs=r, start=True, stop=True), reads=[rkb, cb_], writes=[pDb])
            P.op("dve", lambda e, o=BON[:, p, :], i=pD[:, TT:2 * TT], v_=vf[:]: e.tensor_tensor(out=o, in0=i, in1=v_, op=ALU.mult),
                 reads=[pDb, vfb], writes=[BONb])

        P.barrier()
        if dbg:
            for i_, (t_, b_) in enumerate(((RT, RTb), (KT, KTb), (BT, BTb), (AT, ATb), (VT, VTb), (GT, GTb), (BON, BONb))):
                P.dma("sp", dbf[i_], t_[:].rearrange("p a b -> p (a b)"), reads=[b_])
            P.dma("sp", dgc, GC[:].rearrange("p a b -> p (a b)"), reads=[GCb])
            P.barrier()
        zb = Buf()
        SB = [dict(N=Buf(), L=Buf(), P=Buf()) for _ in range(2)]
        for z_ in ZERO_LIST:
            if z_.dtype == BF16:
                P.op("pool", lambda e, z_=z_: e.memset(z_, 0.0), writes=[zb])
            else:
                P.op("dve", lambda e, z_=z_: e.tensor_scalar(out=z_, in0=idn[:], scalar1=0.0, scalar2=None, op0=ALU.mult),
                     reads=[cb_], writes=[zb])
        for ch in range(NCH):
            cc = slice(ch * C, (ch + 1) * C)
            inb = [Buf() for _ in range(4)]
            for (src, srcb, dst) in ((AT, ATb, ATbd), (BT, BTb, BTbd), (KT, KTb, KTbd), (VT, VTb, VTbd)):
                for hf in range(2):
                    sl = slice(64 * hf, 64 * hf + 64)
                    P.op("dve" if hf == 0 else "act",
                         (lambda e, o=dst[sl, :, 64 * hf:64 * hf + 64], i=src[sl, :, cc]: e.tensor_copy(out=o, in_=i)) if hf == 0 else
                         (lambda e, o=dst[sl, :, 64 * hf:64 * hf + 64], i=src[sl, :, cc]: e.activation(out=o, in_=i, func=AF.Copy)),
                         reads=[srcb, zb], writes=inb)
            gb = [dict((n, Buf()) for n in ("N", "L", "Mak", "Mrb", "Mrk", "V", "B", "K", "P", "Z", "U")) for _ in range(4)]
            for g_ in range(4):
                pg = slice(g_ * 4, g_ * 4 + 4)
                b_ = gb[g_]
                for (src, dst, nm) in ((VTbd, Vbd, "V"), (BTbd, Bbd, "B"), (KTbd, Kbd, "K")):
                    ps, psb = k.ps()
                    psv = ps[:].bitcast(BF16)[:, 0:512].rearrange("p (a b) -> p a b", a=4)
                    for i in range(4):
                        P.op("pe", lambda e, o=psv[:, i, :], w=src[:, g_ * 4 + i, :]: e.transpose(out=o, in_=w, identity=idn[:, 0, :]),
                             reads=[inb[g_], cb_], writes=[psb])
                    P.op("act", lambda e, o=dst[:, pg, :], i=psv: e.activation(out=o, in_=i, func=AF.Copy), reads=[psb], writes=[b_[nm]])
                    if nm == "V":
                        for hf in range(2):
                            sl = slice(64 * hf, 64 * hf + 64)
                            P.op("dve", lambda e, o=Vstk[sl, pg, :], i=psv[sl, :, 64 * hf:64 * hf + 64]: e.tensor_copy(out=o, in_=i),
                                 reads=[psb], writes=[b_[nm]])
            def step1(g_):
                ps1, ps1b = k.ps()
                ps2, ps2b = k.ps()
                ps3, ps3b = k.ps()
                b_ = gb[g_]
                cs_ = CH[g_ % 2]
                sb_ = SB[g_ % 2]
                for i in range(4):
                    p = g_ * 4 + i
                    cs = slice(i * 64, i * 64 + 64)
                    cs2 = slice(256 + i * 64, 256 + i * 64 + 64)
                    for (ps, psb, csl, lh, rh, rhb) in ((ps1, ps1b, cs, BTbd, AT, ATb), (ps1, ps1b, cs2, ATbd, BT, BTb),
                                                        (ps2, ps2b, cs, KTbd, AT, ATb), (ps2, ps2b, cs2, BTbd, RT, RTb),
                                                        (ps3, ps3b, cs, KTbd, RT, RTb)):
                        P.op("pe", lambda e, o=ps[:, csl], w=lh[:, p, :], r=rh[:, p, cc]: e.matmul(o, lhsT=w, rhs=r, start=True, stop=True),
                             reads=[inb[g_], rhb], writes=[psb])
                pg = slice(g_ * 4, g_ * 4 + 4)
                v1 = ps1[:, 0:256].rearrange("p (a b) -> p a b", a=4)
                v1b = ps1[:, 256:512].rearrange("p (a b) -> p a b", a=4)
                v2 = ps2[:, 0:256].rearrange("p (a b) -> p a b", a=4)
                v2b = ps2[:, 256:512].rearrange("p (a b) -> p a b", a=4)
                v3 = ps3[:, 0:256].rearrange("p (a b) -> p a b", a=4)
                for hf in range(2):
                    sl = slice(64 * hf, 64 * hf + 64)
                    fs = slice(64 * hf, 64 * hf + 64)
                    P.op("dve", lambda e, o=cs_["N"][0]["w"][sl, :, fs], i=v1[sl], m=mS[sl]: e.tensor_tensor(out=o, in0=i, in1=m, op=ALU.mult),
                         reads=[ps1b, cb_, zb], writes=[sb_["N"]])
                    P.op("dve", lambda e, o=cs_["L"][0]["w"][sl, :, fs], i=v1b[sl], m=mL[sl]: e.tensor_tensor(out=o, in0=i, in1=m, op=ALU.mult),
                         reads=[ps1b, cb_, zb], writes=[sb_["L"]])
                    P.op("dve", lambda e, o=MakT[sl, pg, fs], i=v2[sl], m=mS[sl]: e.tensor_tensor(out=o, in0=i, in1=m, op=ALU.mult),
                         reads=[ps2b, cb_, zb], writes=[b_["Mak"]])
                P.op("dve", lambda e, o=MrbT[:, pg, :], i=v2b, m=mI[:]: e.tensor_tensor(out=o, in0=i, in1=m, op=ALU.mult),
                     reads=[ps2b, cb_], writes=[b_["Mrb"]])
                P.op("dve", lambda e, o=MrkT[:, pg, :], i=v3, m=mI[:]: e.tensor_tensor(out=o, in0=i, in1=m, op=ALU.mult),
                     reads=[ps3b, cb_], writes=[b_["Mrk"]])
                P.op("dve", lambda e, o=cs_["P"]["w"], i=cs_["N"][0]["r"]: e.tensor_tensor(out=o, in0=i, in1=idn[:], op=ALU.add),
                     reads=[sb_["N"], cb_], writes=[sb_["P"]])

            def chain_sq(g_, lev):
                a_, n_ = (lev - 1) % 2, lev % 2
                cs_ = CH[g_ % 2]
                sb_ = SB[g_ % 2]
                Ns, Ls = cs_["N"], cs_["L"]
                if lev < 5:
                    psn, psnb = k.ps()
                    for i in range(4):
                        P.op("pe", lambda e, o=psn[:, i * 128:(i + 1) * 128], w=Ls[a_]["m"][:, i, :], r=Ns[a_]["m"][:, i, :]:
                             e.matmul(o, lhsT=w, rhs=r, start=True, stop=True), reads=[sb_["L"], sb_["N"]], writes=[psnb])
                psl_, pslb = k.ps()
                for i in range(4):
                    P.op("pe", lambda e, o=psl_[:, i * 128:(i + 1) * 128], w=Ns[a_]["m"][:, i, :], r=Ls[a_]["m"][:, i, :]:
                         e.matmul(o, lhsT=w, rhs=r, start=True, stop=True), reads=[sb_["L"], sb_["N"]], writes=[pslb])
                P.op("act", lambda e, o=Ls[n_]["w"], i=psl_[:].rearrange("p (a b) -> p a b", a=4): e.activation(out=o, in_=i, func=AF.Copy),
                     reads=[pslb], writes=[sb_["L"]])
                if lev < 5:
                    P.op("act", lambda e, o=Ns[n_]["w"], i=psn[:].rearrange("p (a b) -> p a b", a=4): e.activation(out=o, in_=i, func=AF.Copy),
                         reads=[psnb], writes=[sb_["N"]])

            def chain_p(g_, lev):
                n_ = lev % 2
                pg = slice(g_ * 4, g_ * 4 + 4)
                cs_ = CH[g_ % 2]
                sb_ = SB[g_ % 2]
                Ls, Pc = cs_["L"], cs_["P"]
                psp, pspb = k.ps()
                for i in range(4):
                    P.op("pe", lambda e, o=psp[:, i * 128:(i + 1) * 128], w=Ls[n_]["m"][:, i, :], r=Pc["m"][:, i, :]:
                         e.matmul(o, lhsT=w, rhs=r, start=True, stop=True), reads=[sb_["L"], sb_["P"]], writes=[pspb])
                if lev < 5:
                    P.op("dve", lambda e, o=Pc["w"], q=Pc["r"], i=psp[:].rearrange("p (a b) -> p a b", a=4): e.tensor_tensor(out=o, in0=i, in1=q, op=ALU.add),
                         reads=[pspb, sb_["P"]], writes=[sb_["P"]])
                else:
                    P.op("dve", lambda e, o=PF[:, pg, :], i=psp[:].rearrange("p (a b) -> p a b", a=4), q=Pc["r"]: e.tensor_tensor(out=o, in0=i, in1=q, op=ALU.add),
                         reads=[pspb, sb_["P"]], writes=[gb[g_]["P"]])

            for gp in ((0, 1), (2, 3)):
                for g_ in gp:
                    step1(g_)
                for lev in range(1, 6):
                    for g_ in gp:
                        chain_sq(g_, lev)
                    for g_ in gp:
                        chain_p(g_, lev)
            zps, ups = {}, {}
            for g_ in range(4):
                pg = slice(g_ * 4, g_ * 4 + 4)
                b_ = gb[g_]
                hb_ = Hb_[g_]
                psz, pszb = k.ps()
                for i in range(4):
                    p = g_ * 4 + i
                    P.op("pe", lambda e, o=psz[:, i * 64:(i + 1) * 64], w=ATbd[:, p, :], r=Hstk[:, p, :]: e.matmul(o, lhsT=w, rhs=r, start=True, stop=False),
                         reads=[inb[g_], hb_], writes=[pszb])
                    P.op("pe", lambda e, o=psz[:, i * 64:(i + 1) * 64], w=MakT[:, p, :], r=Vstk[:, p, :]: e.matmul(o, lhsT=w, rhs=r, start=False, stop=True),
                         reads=[b_["Mak"], b_["V"]], writes=[pszb])
                P.op("act", lambda e, o=Zs[:, pg, :], i=psz[:, 0:256].rearrange("p (a b) -> p a b", a=4): e.activation(out=o, in_=i, func=AF.Copy),
                     reads=[pszb], writes=[b_["Z"]])
            for g_ in range(4):
                pg = slice(g_ * 4, g_ * 4 + 4)
                b_ = gb[g_]
                psu, psub = k.ps()
                for i in range(4):
                    p = g_ * 4 + i
                    P.op("pe", lambda e, o=psu[:, i * 64:(i + 1) * 64], w=PF[:, p, :], r=Zs[:, p, :]: e.matmul(o, lhsT=w, rhs=r, start=True, stop=True),
                         reads=[b_["P"], b_["Z"]], writes=[psub])
                psuv = psu[:, 0:256].rearrange("p (a b) -> p a b", a=4)
                P.op("act", lambda e, o=Us[:, pg, :], i=psuv: e.activation(out=o, in_=i, func=AF.Copy), reads=[psub], writes=[b_["U"]])
                for hf in range(2):
                    sl = slice(64 * hf, 64 * hf + 64)
                    P.op("dve", lambda e, o=Ubd[sl, pg, 64 * hf:64 * hf + 64], i=psuv[sl]: e.tensor_copy(out=o, in_=i), reads=[psub, zb], writes=[b_["U"]])
            for g_ in range(4):
                pg = slice(g_ * 4, g_ * 4 + 4)
                b_ = gb[g_]
                hb_ = Hb_[g_]
                psy, psyb = k.ps()
                psh, pshb = k.ps()
                for i in range(4):
                    p = g_ * 4 + i
                    oy = psy[:, i * 64:(i + 1) * 64]
                    P.op("pe", lambda e, o=oy, w=Hbd[:, p, :], r=RT[:, p, cc]: e.matmul(o, lhsT=w, rhs=r, start=True, stop=False),
                         reads=[hb_, RTb], writes=[psyb])
                    P.op("pe", lambda e, o=oy, w=Ubd[:, p, :], r=MrbT[:, p, :]: e.matmul(o, lhsT=w, rhs=r, start=False, stop=False),
                         reads=[b_["U"], b_["Mrb"]], writes=[psyb])
                    P.op("pe", lambda e, o=oy, w=Vbd[:, p, :], r=MrkT[:, p, :]: e.matmul(o, lhsT=w, rhs=r, start=False, stop=True),
                         reads=[b_["V"], b_["Mrk"]], writes=[psyb])
                    oh = psh[:, i * 64:(i + 1) * 64]
                    P.op("pe", lambda e, o=oh, w=Bbd[:, p, :], r=Us[:, p, :]: e.matmul(o, lhsT=w, rhs=r, start=True, stop=False),
                         reads=[b_["B"], b_["U"]], writes=[pshb])
                    P.op("pe", lambda e, o=oh, w=Kbd[:, p, :], r=Vstk[:, p, :]: e.matmul(o, lhsT=w, rhs=r, start=False, stop=True),
                         reads=[b_["K"], b_["V"]], writes=[pshb])
                P.op("act", lambda e, o=YT[:, pg, cc], i=psy[:, 0:256].rearrange("p (a b) -> p a b", a=4): e.activation(out=o, in_=i, func=AF.Copy),
                     reads=[psyb], writes=[YTb])
                P.op("dve", lambda e, o=Hf[:, pg, :], i=psh[:, 0:256].rearrange("p (a b) -> p a b", a=4): e.tensor_tensor(out=o, in0=o, in1=i, op=ALU.add),
                     reads=[pshb, hb_], writes=[hb_])
                for i in range(4):
                    p = g_ * 4 + i
                    P.op("dve", lambda e, o=Hf[:, p, :], s=GC[:, p, ch:ch + 1]: e.tensor_scalar(out=o, in0=o, scalar1=s, scalar2=None, op0=ALU.mult),
                         reads=[hb_, GCb], writes=[hb_])
                P.op("pool", lambda e, o=Hstk[:, pg, :], i=Hf[:, pg, :]: e.tensor_copy(out=o, in_=i), reads=[hb_], writes=[hb_])
                for hf in range(2):
                    sl = slice(64 * hf, 64 * hf + 64)
                    P.op("pool", lambda e, o=Hbd[sl, pg, 64 * hf:64 * hf + 64], i=Hf[sl, pg, :]: e.tensor_copy(out=o, in_=i), reads=[hb_], writes=[hb_])

        P.barrier()
        OINb, Y2b, Xb = Buf(), Buf(), Buf()
        P.dma("sp", X, xT3[:, :, t0:t0 + TT], writes=[Xb])
        def gn_front(p):
            psm, psmb = k.ps()
            P.op("pe", lambda e, o=psm[:, 0:TT], r=YT[:, p, :]: e.matmul(o, lhsT=bo32[:], rhs=r, start=True, stop=True), reads=[YTb, cb_], writes=[psmb])
            yc, ycb = tmp("cl", 2)
            P.op("dve", lambda e, o=yc[:], m=psm[:, 0:TT], y=YT[:, p, :]: e.scalar_tensor_tensor(out=o, in0=m, scalar=-1.0 / 64, in1=y, op0=ALU.mult, op1=ALU.add),
                 reads=[psmb, YTb], writes=[ycb])
            ysq, ysqb = tmp("rn", 2)
            P.op("pool", lambda e, o=ysq[:], i=yc[:]: e.tensor_tensor(out=o, in0=i, in1=i, op=ALU.mult), reads=[ycb], writes=[ysqb])
            P.op("pe", lambda e, o=psm[:, TT:2 * TT], r=ysq[:]: e.matmul(o, lhsT=bo32[:], rhs=r, start=True, stop=True), reads=[ysqb, cb_], writes=[psmb])
            rs, rsb = tmp("kk", 2)
            P.op("act", lambda e, o=rs[:], i=psm[:, TT:2 * TT]: e.activation(out=o, in_=i, func=AF.Ln, bias=c["eps"][:, 1:2], scale=1.0 / 64),
                 reads=[psmb, c["onesb"]], writes=[rsb])
            P.op("act", lambda e, o=rs[:]: e.activation(out=o, in_=o, func=AF.Exp, scale=-0.5), reads=[rsb], writes=[rsb])
            return (p, yc, ycb, rs, rsb)

        def gn_back(st_):
            p, yc, ycb, rs, rsb = st_
            P.op("dve", lambda e, o=yc[:], r=rs[:]: e.tensor_tensor(out=o, in0=o, in1=r, op=ALU.mult), reads=[ycb, rsb], writes=[ycb])
            P.op("dve", lambda e, o=yc[:], s=LNW[:, p:p + 1], b=BON[:, p, :]: e.scalar_tensor_tensor(out=o, in0=o, scalar=s, in1=b, op0=ALU.mult, op1=ALU.add),
                 reads=[ycb, BONb, mb], writes=[ycb])
            P.op("dve", lambda e, o=OIN[:, p, :], i=yc[:], s=LNB[:, p:p + 1], g_=GT[:, p, :]: e.scalar_tensor_tensor(out=o, in0=i, scalar=s, in1=g_, op0=ALU.add, op1=ALU.mult),
                 reads=[ycb, GTb, mb], writes=[OINb])

        pend = []
        for p in range(16):
            pend.append(gn_front(p))
            if len(pend) > 1:
                gn_back(pend.pop(0))
        while pend:
            gn_back(pend.pop(0))
        if dbg:
            P.dma("sp", dyt, YT.rearrange("p a b -> p (a b)"), reads=[YTb])
            P.dma("sp", doin, OIN.rearrange("p a b -> p (a b)"), reads=[OINb])
        for eb in range(16):
            wb, wbb = wl.load(wo, 0, 16, eb * 128, 128)
            ps, psb = k.ps()
            for p in range(16):
                P.op("pe", lambda e, o=ps[:, :TT], w=wb[:, p, :], r=OIN[:, p, :], s=(p == 0), t=(p == 15):
                     e.matmul(o, lhsT=w, rhs=r, start=s, stop=t), reads=[wbb, OINb], writes=[psb])
            P.op("act", lambda e, o=Y2[:, eb, :], i=ps[:, :TT]: e.activation(out=o, in_=i, func=AF.Copy), reads=[psb], writes=[Y2b])
        post_residual(k, c, X, Xb, Y2, Y2b, G, mb, TT)
        P.dma("sp", oT3[:, :, t0:t0 + TT], X, reads=[Xb])
        P.barrier()
    k.close()
    return nc


def _pd(v, n=16):
    return np.ascontiguousarray(np.asarray(v, dtype=np.float32).reshape(n, 128).T)


def _run(nc, maps):
    res = run_bass_kernel_spmd(nc, maps, core_ids=list(range(len(maps))))
    return res.results


class _Root:
    pass


def build_fused():
    root = _Root()
    root.nc = bass.Bass("TRN2", target_bir_lowering=False)
    root.semstack = ExitStack()
    nc = root.nc
    xT = nc.dram_tensor("xT", [D, T], F32, kind="ExternalInput").ap()
    oT = nc.dram_tensor("oT", [D, T], F32, kind="ExternalOutput").ap()
    modsI = nc.dram_tensor("mods_i", [128, 192], F32, kind="Internal").ap()
    xa = nc.dram_tensor("xa_i", [D, T], F32, kind="Internal").ap()
    xb = nc.dram_tensor("xb_i", [D, T], F32, kind="Internal").ap()
    xc = nc.dram_tensor("xc_i", [D, T], F32, kind="Internal").ap()
    def decl(prefix, items):
        dm, pre = {}, []
        for name, shape in items:
            w = nc.dram_tensor(prefix + name, list(shape), F32, kind="ExternalInput").ap()
            wb = nc.dram_tensor(prefix + name + "_bf", list(shape), BF16, kind="Internal").ap()
            dm[name + "_bf"] = wb
            pre.append((wb, w))
        return dm, pre
    rw_dm, rw_pre = decl("r_", [("w_rkv", [3 * D, D]), ("w1", [D, 96]), ("w2", [96, D]), ("a1", [D, 96]), ("a2", [96, D]),
                                ("g1", [D, 256]), ("g2", [256, D]), ("w_o", [D, D])])
    f0_dm, f0_pre = decl("f0_", [("w_up", [D, FF]), ("w_dn", [FF, D])])
    f1_dm, f1_pre = decl("f1_", [("w_up", [D, FF]), ("w_dn", [FF, D])])
    a_dm, a_pre = decl("a_", [("kv_down", [D, 576]), ("kv_uk", [512, D]), ("kv_uv", [512, D]), ("w_dq", [D, 512]),
                              ("w_uq", [512, 16 * 192]), ("w_o", [D, D])])
    build_mods(env=(root, {"mods": modsI}, "m_", rw_pre))
    build_rwkv(0, env=(root, dict(rw_dm, xT=xT, mods=modsI, oT=xa), "r_", f0_pre + a_pre + f1_pre))
    build_mlp(0, env=(root, dict(f0_dm, xT=xa, mods=modsI, oT=xb), "f0_"))
    build_mla(1, env=(root, dict(a_dm, xT=xb, mods=modsI, oT=xc), "a_"))
    build_mlp(1, env=(root, dict(f1_dm, xT=xc, mods=modsI, oT=oT), "f1_"))
    root.semstack.close()
    return nc


def kernel(x, c, positions, ada_w, ada_b, norm_g, mlp_up, mlp_down,
           rw_mu, rw_rkv, rw_w0, rw_w1, rw_w2, rw_a0, rw_a1, rw_a2, rw_g1, rw_g2,
           rw_kk, rw_ka, rw_rk, rw_lnx, rw_o,
           mla_dq, mla_qnorm, mla_uq, mla_o,
           kv_in_g, kv_down, kv_norm, kv_uk, kv_uv):
    f32 = np.float32
    A = lambda a: np.ascontiguousarray(np.asarray(a))
    x = A(x).astype(f32, copy=False)
    B = x.shape[0]
    cc = A(c)
    pos = A(positions).astype(np.int32, copy=False)
    vl = [A(rw_mu)[0][j] for j in range(6)] + [A(rw_w0)[0], A(rw_a0)[0], A(rw_kk)[0], A(rw_ka)[0], A(rw_rk)[0].reshape(-1),
                                              A(rw_lnx)[0][0], A(rw_lnx)[0][1]]
    inv = (1.0 / (10000.0 ** (np.arange(0, 64, 2, dtype=np.float32) / 64))).astype(f32)
    shared = {
        "m_ada_w": A(ada_w),
        "m_ada_b_pd": np.ascontiguousarray(A(ada_b).reshape(2, 96, 128).transpose(2, 0, 1)),
        "m_norm_g_pd": np.ascontiguousarray(A(norm_g).reshape(2, 4, 16, 128).transpose(3, 0, 1, 2)),
        "r_rw_vec": np.ascontiguousarray(np.stack([_pd(v) for v in vl], 1)).astype(f32),
        "r_w_rkv": A(rw_rkv)[0].reshape(3 * D, D), "r_w1": A(rw_w1)[0], "r_w2": A(rw_w2)[0], "r_a1": A(rw_a1)[0],
        "r_a2": A(rw_a2)[0], "r_g1": A(rw_g1)[0], "r_g2": A(rw_g2)[0], "r_w_o": A(rw_o)[0],
        "f0_w_up": A(mlp_up)[0], "f0_w_dn": A(mlp_down)[0], "f1_w_up": A(mlp_up)[1], "f1_w_dn": A(mlp_down)[1],
        "a_ropec": np.ascontiguousarray(np.stack([np.concatenate([inv, inv]), np.concatenate([-np.ones(32), np.ones(32)])], 1)).astype(f32),
        "a_mla_vec": np.ascontiguousarray(np.concatenate([_pd(kv_in_g), _pd(kv_norm, 4), _pd(A(mla_qnorm)[0], 4)], 1)).astype(f32),
        "a_kv_down": A(kv_down), "a_kv_uk": A(kv_uk).reshape(512, D), "a_kv_uv": A(kv_uv).reshape(512, D),
        "a_w_dq": A(mla_dq)[0], "a_w_uq": A(mla_uq)[0].reshape(512, 16 * 192), "a_w_o": A(mla_o)[0],
    }
    maps = [dict(shared, xT=np.ascontiguousarray(x[b].T), m_c_pd=_pd(cc[b]),
                 a_posr=np.ascontiguousarray(np.broadcast_to(pos[b][None, :], (64, T)))) for b in range(B)]
    r = _run(build_fused(), maps)
    return np.stack([np.ascontiguousarray(r[b]["oT"].T) for b in range(B)]).astype(f32, copy=False)
```

```python
import numpy as np
import concourse.bass as bass
import concourse.mybir as mybir
from concourse.bass_utils import run_bass_kernel_spmd
from contextlib import ExitStack

F32 = mybir.dt.float32
BF16 = mybir.dt.bfloat16
F32R = mybir.dt.float32r
MODS_DT = F32R
CHAIN_DT = F32
I32 = mybir.dt.int32
ALU = mybir.AluOpType
AF = mybir.ActivationFunctionType

D = 2048
T = 2048
ND = 16
FF = 8192
NCORES = 8
EPS = 1e-6

EPOCH = 30000
N_DMA_SEMS = 6
SAME_ENGINE_SYNC = False


class Buf:
    __slots__ = ("w", "r")

    def __init__(self):
        self.w = None
        self.r = {}


class Prog:
    ENGS = ("pe", "dve", "act", "pool", "sp")
    NSEM = 0

    def __init__(self, nc, semstack=None):
        self.nc = nc
        self.semstack = semstack
        self.q = {e: [] for e in self.ENGS}
        self.cnt = {e: 0 for e in self.ENGS}
        self.seen = {e: {} for e in self.ENGS}
        self.dma_cnt = {}
        self.dma_rr = {e: 0 for e in self.ENGS}
        self.keys = set()

    def _wait(self, eng, key, val):
        if self.seen[eng].get(key, 0) >= val:
            return
        self.seen[eng][key] = val
        self.q[eng].append(("wait", key, val))

    def _deps(self, eng, reads, writes):
        deps = {}
        for b in reads:
            if b.w is not None:
                k, v = b.w
                if deps.get(k, 0) < v:
                    deps[k] = v
        for b in writes:
            if b.w is not None:
                k, v = b.w
                if deps.get(k, 0) < v:
                    deps[k] = v
            for k, v in b.r.items():
                if deps.get(k, 0) < v:
                    deps[k] = v
        for k, v in deps.items():
            if k[0] == "E" and k[1] == eng and (eng == "pe" or not SAME_ENGINE_SYNC):
                continue
            self._wait(eng, k, v)

    def _mark(self, tok, reads, writes):
        k, v = tok
        for b in reads:
            if b.r.get(k, 0) < v:
                b.r[k] = v
        for b in writes:
            b.w = tok
            b.r = {}

    def op(self, eng, fn, reads=(), writes=()):
        self._deps(eng, reads, writes)
        n = self.cnt[eng]
        self.cnt[eng] = n + 1
        key = ("E", eng, n // EPOCH)
        self.keys.add(key)
        self.q[eng].append(("op", fn, key))
        self._mark((key, n % EPOCH + 1), reads, writes)

    def dma(self, qeng, out, in_, reads=(), writes=()):
        self._deps(qeng, reads, writes)
        s = self.dma_rr[qeng]
        self.dma_rr[qeng] = (s + 1) % N_DMA_SEMS
        gen = 0
        while self.dma_cnt.get(("D", qeng, s, gen), 0) + 16 > EPOCH:
            gen += 1
        key = ("D", qeng, s, gen)
        prev = self.dma_cnt.get(key, 0)
        if prev > 0:
            self._wait(qeng, key, prev)
        elif gen > 0:
            pk = ("D", qeng, s, gen - 1)
            self._wait(qeng, pk, self.dma_cnt[pk])
        self.dma_cnt[key] = prev + 16
        self.keys.add(key)
        self.q[qeng].append(("dma", (out, in_), key))
        self._mark((key, prev + 16), reads, writes)

    def barrier(self):
        toks = []
        for e in self.ENGS:
            n = self.cnt[e]
            if n > 0:
                toks.append((("E", e, (n - 1) // EPOCH), (n - 1) % EPOCH + 1))
        toks += list(self.dma_cnt.items())
        for e in self.ENGS:
            for key, v in toks:
                if key[0] == "E" and key[1] == e:
                    continue
                self._wait(e, key, v)

    def finish(self, eng="sp"):
        for key, v in list(self.dma_cnt.items()):
            self._wait(eng, key, v)

    def emit(self):
        nc = self.nc
        engmap = {"pe": "tensor", "dve": "vector", "act": "scalar", "pool": "gpsimd", "sp": "sync"}
        with ExitStack() as st:
            sems = {}
            semst = self.semstack if self.semstack is not None else st
            for i, key in enumerate(sorted(self.keys, key=str)):
                Prog.NSEM += 1
                sems[key] = semst.enter_context(nc.semaphore("s%d" % Prog.NSEM))
            block = st.enter_context(nc.Block())
            for e in self.ENGS:
                items = self.q[e]
                if not items:
                    continue

                def body(eng, items=items):
                    for it in items:
                        if it[0] == "wait":
                            eng.wait_ge(sems[it[1]], it[2])
                        elif it[0] == "op":
                            it[1](eng).then_inc(sems[it[2]], 1)
                        else:
                            eng.dma_start(out=it[1][0], in_=it[1][1]).then_inc(sems[it[2]], 16)

                getattr(block, engmap[e])(body)


class Ctx:
    NT = 0

    def __init__(self, name, env=None):
        if env is None:
            self.nc = bass.Bass("TRN2", target_bir_lowering=False)
            self.semstack = None
            self.dmap = {}
            self.prefix = ""
            self.pre = []
        else:
            root, self.dmap, self.prefix = env[:3]
            self.pre = env[3] if len(env) > 3 else []
            self.nc = root.nc
            self.semstack = root.semstack
        self.P = Prog(self.nc, self.semstack)
        self.st = ExitStack()
        self.n = 0
        self.psl = []
        self.psi = 0
        self.rot = {}
        self.rotw = 512

    def dram(self, name, shape, dt, kind):
        if name in self.dmap:
            return self.dmap[name]
        return self.nc.dram_tensor(self.prefix + name, list(shape), dt, kind=kind).ap()

    def sb(self, shape, dt):
        Ctx.NT += 1
        return self.st.enter_context(self.nc.sbuf_tensor("t%d" % Ctx.NT, list(shape), dt))

    def init_psum(self, nf32=8):
        for i in range(nf32):
            Ctx.NT += 1
            t = self.st.enter_context(self.nc.psum_tensor("ps%d" % Ctx.NT, [128, 512], F32))
            self.psl.append((t, Buf()))

    def ps(self):
        r = self.psl[self.psi]
        self.psi = (self.psi + 1) % len(self.psl)
        return r

    def rotbuf(self, key, shape, dt, n=2):
        if key not in self.rot:
            self.rot[key] = [[(self.sb(shape, dt), Buf()) for _ in range(n)], 0]
        lst, i = self.rot[key]
        self.rot[key][1] = (i + 1) % len(lst)
        return lst[i]

    def do_pre(self, n=None):
        if not hasattr(self, "pre_chunks"):
            self.pre_chunks = []
            for (dst, src) in self.pre:
                rows, cols = src.shape[0], src.shape[1]
                step = max(1, min(rows, (8 << 20) // (cols * 4)))
                for r0 in range(0, rows, step):
                    r1 = min(rows, r0 + step)
                    self.pre_chunks.append((dst[r0:r1, :], src[r0:r1, :]))
        m = len(self.pre_chunks) if n is None else min(n, len(self.pre_chunks))
        for _ in range(m):
            d_, s_ = self.pre_chunks.pop(0)
            self.P.dma("pool", d_, s_)

    def close(self):
        self.P.finish("sp")
        self.P.emit()
        self.st.close()


class WT:
    def __init__(self, ap, buf):
        self.ap = ap
        self.buf = buf


def cast_dma(k, dst, src, buf=None, max_bytes=8 << 20):
    rows, cols = src.shape[0], src.shape[1]
    step = max(1, min(rows, max_bytes // (cols * 4)))
    for r0 in range(0, rows, step):
        r1 = min(rows, r0 + step)
        k.P.dma("pool", dst[r0:r1, :], src[r0:r1, :], writes=[buf] if buf is not None else [])


def wsrc(k, name, shape):
    if name + "_bf" in k.dmap:
        return WT(k.dmap[name + "_bf"], Buf())
    w = k.dram(name, shape, F32, "ExternalInput")
    wb = k.nc.dram_tensor(k.prefix + name + "_bf", list(shape), BF16, kind="Internal").ap()
    b = Buf()
    cast_dma(k, wb, w, b)
    return WT(wb, b)


class WLoader:
    def __init__(self, k, nk=16, ncols=256, nbf=3):
        self.k = k
        self.wbf = [(k.sb([128, nk, ncols], BF16), Buf()) for _ in range(nbf)]
        self.j = 0

    def load(self, W, r0, nk, c0, ncols, pp=128):
        P = self.k.P
        wb, wbb = self.wbf[self.j]
        self.j = (self.j + 1) % len(self.wbf)
        src = W.ap[r0:r0 + nk * pp, c0:c0 + ncols].rearrange("(k p) c -> p k c", p=pp)
        P.dma("sp", wb[:pp, :nk, :ncols], src, reads=[W.buf], writes=[wbb])
        return wb, wbb


def make_consts(k):
    P = k.P
    c = {}
    ones = k.sb([128, 128], BF16)
    c["ones"] = ones
    c["onesb"] = Buf()
    P.op("pool", lambda e: e.memset(ones[:], 1.0), writes=[c["onesb"]])
    eps = k.sb([128, 3], F32)
    c["eps"] = eps
    P.op("pool", lambda e: e.memset(eps[:, 0:1], EPS), writes=[c["onesb"]])
    P.op("pool", lambda e: e.memset(eps[:, 1:2], 64e-5), writes=[c["onesb"]])
    P.op("pool", lambda e: e.memset(eps[:, 2:3], 1.0), writes=[c["onesb"]])
    return c


def rms_rstd(k, c, X, Xb, TT, scale_div=D, epsap=None, ntile=ND):
    P = k.P
    if epsap is None:
        epsap = c["eps"][:, 0:1]
    ps, psb = k.ps()
    for dt in range(ntile):
        sq, sqb = k.rotbuf("sq", [128, k.rotw], BF16, 3)
        P.op("act", lambda e, o=sq[:, :TT], i=X[:, dt, :]: e.activation(out=o, in_=i, func=AF.Square),
             reads=[Xb], writes=[sqb])
        P.op("pe", lambda e, o=ps[:, :TT], r=sq[:, :TT], s=(dt == 0), t=(dt == ntile - 1):
             e.matmul(o, lhsT=c["ones"][:], rhs=r, start=s, stop=t), reads=[sqb, c["onesb"]], writes=[psb])
    rstd, rb = k.rotbuf("rstd", [128, k.rotw], F32, 2)
    P.op("act", lambda e, o=rstd[:, :TT], i=ps[:, :TT]: e.activation(
        out=o, in_=i, func=AF.Sqrt, bias=epsap, scale=1.0 / scale_div), reads=[psb, c["onesb"]], writes=[rb])
    P.op("dve", lambda e, o=rstd[:, :TT]: e.reciprocal(out=o, in_=o), reads=[rb], writes=[rb])
    return rstd, rb


def norm_mod(k, X, Xb, rstd, rb, A, Sh, mb, H, Hb, TT, col0=0):
    P = k.P
    for dt in range(ND):
        tmp, tb = k.rotbuf("nm_tmp", [128, k.rotw], F32, 2)
        P.op("dve", lambda e, o=tmp[:, :TT], i=X[:, dt, :], s=A[:, dt:dt + 1], r=rstd[:, :TT]:
             e.scalar_tensor_tensor(out=o, in0=i, scalar=s, in1=r, op0=ALU.mult, op1=ALU.mult),
             reads=[Xb, rb, mb], writes=[tb])
        P.op("act", lambda e, o=H[:, dt, col0:col0 + TT], i=tmp[:, :TT], s=Sh[:, dt:dt + 1]:
             e.activation(out=o, in_=i, func=AF.Identity, bias=s, scale=1.0),
             reads=[tb, mb], writes=[Hb])


def post_residual(k, c, X, Xb, Y, Yb, G, mb, TT):
    P = k.P
    rstd, rb = rms_rstd(k, c, Y, Yb, TT)
    for dt in range(ND):
        tmp, tb = k.rotbuf("nm_tmp", [128, k.rotw], F32, 2)
        P.op("dve", lambda e, o=tmp[:, :TT], i=Y[:, dt, :], s=G[:, dt:dt + 1], r=rstd[:, :TT]:
             e.scalar_tensor_tensor(out=o, in0=i, scalar=s, in1=r, op0=ALU.mult, op1=ALU.mult),
             reads=[Yb, rb, mb], writes=[tb])
        P.op("pool" if dt % 2 else "dve", lambda e, o=X[:, dt, :], i=tmp[:, :TT]: e.tensor_tensor(out=o, in0=o, in1=i, op=ALU.add),
             reads=[tb], writes=[Xb])


def build_mods(env=None):
    k = Ctx("mods", env)
    nc, P = k.nc, k.P
    c_pd = k.dram("c_pd", [128, 16], F32, "ExternalInput")
    ada_w = k.dram("ada_w", [2, D, 6 * D], F32, "ExternalInput")
    ada_b = k.dram("ada_b_pd", [128, 2, 96], F32, "ExternalInput")
    ng = k.dram("norm_g_pd", [128, 2, 4, 16], F32, "ExternalInput")
    mods = k.dram("mods", [128, 2 * 96], F32, "ExternalOutput")
    k.init_psum(2)
    cin = k.sb([128, 16], F32)
    cact = k.sb([128, 16], F32)
    abt = k.sb([128, 2, 96], F32)
    ngt = k.sb([128, 2, 4, 16], F32)
    raw = k.sb([128, 2, 96], F32)
    outt = k.sb([128, 2, 6, 16], F32)
    cb_, sm_ = Buf(), Buf()
    P.dma("sp", cin[:], c_pd, writes=[cb_])
    P.dma("sp", abt[:], ada_b, writes=[sm_])
    P.dma("sp", ngt[:], ng, writes=[sm_])
    P.op("act", lambda e: e.activation(out=cact[:].bitcast(MODS_DT), in_=cin[:], func=AF.Silu), reads=[cb_], writes=[cb_])
    stg = [(k.sb([128, 16, 512], F32), Buf()) for _ in range(3)]
    rawb, ob = Buf(), Buf()
    one1 = k.sb([1, 1], F32)
    row = k.sb([1, 6 * D], F32)
    rowb = Buf()
    P.op("dve", lambda e: e.memset(one1[:], 1.0), writes=[cb_])
    k.psl = k.psl + [(k.st.enter_context(nc.psum_tensor("psx%d" % i, [128, 512], F32)), Buf()) for i in range(4)]
    for l in range(2):
        for cb in range(24):
            st, stb = stg[(l * 24 + cb) % 3]
            src = ada_w[l, :, cb * 512:(cb + 1) * 512].rearrange("(k p) c -> p k c", p=128)
            if MODS_DT == F32:
                P.dma("sp", st[:], src, writes=[stb])
            else:
                P.dma("pool", st[:].bitcast(MODS_DT), src, writes=[stb])
            if cb % 2 == 0:
                k.do_pre(1)
            psr, psrb = k.ps()
            for dt in range(16):
                P.op("pe", lambda e, o=psr[0:1, :], w=cact[:, dt:dt + 1].bitcast(MODS_DT), r=st[:, dt, :].bitcast(MODS_DT), s=(dt == 0), t=(dt == 15):
                     e.matmul(o, lhsT=w, rhs=r, start=s, stop=t), reads=[stb, cb_], writes=[psrb])
            P.op("act" if cb % 2 else "dve",
                 (lambda e, o=row[0:1, cb * 512:(cb + 1) * 512], i=psr[0:1, :]: e.activation(out=o, in_=i, func=AF.Copy)) if cb % 2 else
                 (lambda e, o=row[0:1, cb * 512:(cb + 1) * 512], i=psr[0:1, :]: e.tensor_copy(out=o, in_=i)),
                 reads=[psrb], writes=[rowb])
        ps, psb = k.ps()
        for e_ in range(96):
            P.op("pe", lambda e, o=ps[:, e_:e_ + 1], w=row[0:1, e_ * 128:(e_ + 1) * 128]: e.matmul(o, lhsT=w, rhs=one1[0:1, 0:1], start=True, stop=True),
                 reads=[rowb, cb_], writes=[psb])
        P.op("dve", lambda e, o=raw[:, l, :], i=ps[:, 0:96], b=abt[:, l, :]: e.tensor_tensor(out=o, in0=i, in1=b, op=ALU.add),
             reads=[psb, sm_], writes=[rawb])
        for half, (gpre, gpost) in enumerate(((0, 1), (2, 3))):
            b0 = half * 3
            P.op("dve", lambda e, o=outt[:, l, b0 + 0, :], i=raw[:, l, (b0 + 1) * 16:(b0 + 2) * 16], g=ngt[:, l, gpre, :]:
                 e.scalar_tensor_tensor(out=o, in0=i, scalar=1.0, in1=g, op0=ALU.add, op1=ALU.mult),
                 reads=[rawb, sm_], writes=[ob])
            P.op("dve", lambda e, o=outt[:, l, b0 + 1, :], i=raw[:, l, (b0 + 0) * 16:(b0 + 1) * 16]:
                 e.tensor_copy(out=o, in_=i), reads=[rawb], writes=[ob])
            P.op("dve", lambda e, o=outt[:, l, b0 + 2, :], i=raw[:, l, (b0 + 2) * 16:(b0 + 3) * 16], g=ngt[:, l, gpost, :]:
                 e.tensor_tensor(out=o, in0=i, in1=g, op=ALU.mult), reads=[rawb, sm_], writes=[ob])
    P.dma("sp", mods, outt[:].rearrange("p l j d -> p (l j d)"), reads=[ob])
    k.do_pre()
    k.close()
    return nc


def build_mlp(l, env=None):
    k = Ctx("mlp", env)
    nc, P = k.nc, k.P
    TT = 512
    xT = k.dram("xT", [D, T], F32, "ExternalInput")
    modsd = k.dram("mods", [128, 192], F32, "ExternalInput")
    k.do_pre()
    wup = wsrc(k, "w_up", [D, FF])
    wdn = wsrc(k, "w_dn", [FF, D])
    oT = k.dram("oT", [D, T], F32, "ExternalOutput")
    k.init_psum(8)
    c = make_consts(k)
    mt = k.sb([128, 2, 6, 16], F32)
    mb = Buf()
    P.dma("sp", mt[:].rearrange("p l j d -> p (l j d)"), modsd, writes=[mb])
    A, Sh, G = mt[:, l, 3, :], mt[:, l, 4, :], mt[:, l, 5, :]
    Xs = [(k.sb([128, ND, TT], F32), Buf()) for _ in range(2)]
    Hs_ = [(k.sb([128, ND, TT], BF16), Buf()) for _ in range(2)]
    U = k.sb([128, 32, TT], BF16)
    Y = k.sb([128, ND, TT], F32)
    Ub, Yb = Buf(), Buf()
    wl = WLoader(k, 16, 256, 3)
    xT3 = xT.rearrange("(k p) t -> p k t", p=128)
    oT3 = oT.rearrange("(k p) t -> p k t", p=128)
    NT_ = T // TT

    def load_norm(tt):
        X, Xb = Xs[tt % 2]
        H, Hb = Hs_[tt % 2]
        P.dma("sp", X[:], xT3[:, :, tt * TT:(tt + 1) * TT], writes=[Xb])
        rstd, rb = rms_rstd(k, c, X, Xb, TT)
        norm_mod(k, X, Xb, rstd, rb, A, Sh, mb, H, Hb, TT)

    load_norm(0)
    for tt in range(NT_):
        X, Xb = Xs[tt % 2]
        H, Hb = Hs_[tt % 2]
        for fh in range(2):
            for fb in range(16):
                f0 = fh * 4096 + fb * 256
                wb, wbb = wl.load(wup, 0, 16, f0, 256)
                for j in range(2):
                    ps, psb = k.ps()
                    for dt in range(16):
                        P.op("pe", lambda e, o=ps[:, :TT], w=wb[:, dt, j * 128:(j + 1) * 128], r=H[:, dt, :],
                             s=(dt == 0), t=(dt == 15): e.matmul(o, lhsT=w, rhs=r, start=s, stop=t),
                             reads=[wbb, Hb], writes=[psb])
                    rl, rlb = k.rotbuf("relu", [128, 512], F32, 3)
                    P.op("act", lambda e, o=rl[:, :TT], i=ps[:, :TT]: e.activation(out=o, in_=i, func=AF.Relu),
                         reads=[psb], writes=[rlb])
                    P.op("dve", lambda e, o=U[:, fb * 2 + j, :], i=rl[:, :TT]: e.tensor_tensor(out=o, in0=i, in1=i, op=ALU.mult),
                         reads=[rlb], writes=[Ub])
            if fh == 0:
                if tt > 0:
                    Xp, Xpb = Xs[(tt - 1) % 2]
                    post_residual(k, c, Xp, Xpb, Y, Yb, G, mb, TT)
                    P.dma("sp", oT3[:, :, (tt - 1) * TT:tt * TT], Xp[:], reads=[Xpb])
                if tt + 1 < NT_:
                    load_norm(tt + 1)
            for db in range(8):
                pss = [k.ps(), k.ps()]
                for kb in range(2):
                    wb, wbb = wl.load(wdn, fh * 4096 + kb * 2048, 16, db * 256, 256)
                    for j in range(2):
                        ps, psb = pss[j]
                        for ft in range(16):
                            P.op("pe", lambda e, o=ps[:, :TT], w=wb[:, ft, j * 128:(j + 1) * 128], r=U[:, kb * 16 + ft, :],
                                 s=(kb == 0 and ft == 0), t=(kb == 1 and ft == 15): e.matmul(o, lhsT=w, rhs=r, start=s, stop=t),
                                 reads=[wbb, Ub], writes=[psb])
                for j in range(2):
                    ps, psb = pss[j]
                    if fh == 0:
                        P.op("act", lambda e, o=Y[:, db * 2 + j, :], i=ps[:, :TT]: e.activation(out=o, in_=i, func=AF.Copy),
                             reads=[psb], writes=[Yb])
                    else:
                        P.op("dve", lambda e, o=Y[:, db * 2 + j, :], i=ps[:, :TT]: e.tensor_tensor(out=o, in0=o, in1=i, op=ALU.add),
                             reads=[psb], writes=[Yb])
    Xp, Xpb = Xs[(NT_ - 1) % 2]
    post_residual(k, c, Xp, Xpb, Y, Yb, G, mb, TT)
    P.dma("sp", oT3[:, :, (NT_ - 1) * TT:NT_ * TT], Xp[:], reads=[Xpb])
    k.close()
    return nc


def load_swap(wl, W, nk, c0):
    P = wl.k.P
    wb, wbb = wl.wbf[wl.j]
    wl.j = (wl.j + 1) % len(wl.wbf)
    for (a, b_) in ((0, 32), (32, 0)):
        src = W.ap[0:nk * 128, c0 + b_:c0 + b_ + 32].rearrange("(k p) c -> p k c", p=128)
        P.dma("sp", wb[:, :nk, a:a + 32], src, reads=[W.buf], writes=[wbb])
    return wb, wbb


def angle_reduce(k, ang, kf, ki, ab):
    import math
    P = k.P
    P.op("dve", lambda e: e.tensor_scalar(out=kf, in0=ang, scalar1=1.0 / (2 * math.pi), scalar2=None, op0=ALU.mult), reads=[ab], writes=[ab])
    P.op("dve", lambda e: e.tensor_copy(out=ki, in_=kf), reads=[ab], writes=[ab])
    P.op("dve", lambda e: e.tensor_copy(out=kf, in_=ki), reads=[ab], writes=[ab])
    P.op("dve", lambda e: e.scalar_tensor_tensor(out=ang, in0=kf, scalar=-2 * math.pi, in1=ang, op0=ALU.mult, op1=ALU.add), reads=[ab], writes=[ab])
    P.op("dve", lambda e: e.tensor_scalar(out=kf, in0=ang, scalar1=math.pi, scalar2=-2 * math.pi, op0=ALU.is_gt, op1=ALU.mult), reads=[ab], writes=[ab])
    P.op("dve", lambda e: e.tensor_tensor(out=ang, in0=ang, in1=kf, op=ALU.add), reads=[ab], writes=[ab])
    P.op("dve", lambda e: e.tensor_scalar(out=kf, in0=ang, scalar1=-math.pi, scalar2=2 * math.pi, op0=ALU.is_lt, op1=ALU.mult), reads=[ab], writes=[ab])
    P.op("dve", lambda e: e.tensor_tensor(out=ang, in0=ang, in1=kf, op=ALU.add), reads=[ab], writes=[ab])


def build_mla(l=1, env=None):
    import math
    k = Ctx("mla", env)
    nc, P = k.nc, k.P
    TT = 512
    NTT = T // TT
    xT = k.dram("xT", [D, T], F32, "ExternalInput")
    modsd = k.dram("mods", [128, 192], F32, "ExternalInput")
    posr = k.dram("posr", [64, T], I32, "ExternalInput")
    ropec = k.dram("ropec", [64, 2], F32, "ExternalInput")
    vec = k.dram("mla_vec", [128, 24], F32, "ExternalInput")
    k.do_pre()
    kvd = wsrc(k, "kv_down", [D, 576])
    wuk = wsrc(k, "kv_uk", [512, D])
    wuv = wsrc(k, "kv_uv", [512, D])
    wdq = wsrc(k, "w_dq", [D, 512])
    wuq = wsrc(k, "w_uq", [512, 16 * 192])
    wo = wsrc(k, "w_o", [D, D])
    oT = k.dram("oT", [D, T], F32, "ExternalOutput")
    otd = k.dram("ot_scratch", [16, 128, T], BF16, "Internal")
    k.init_psum(8)
    oacc = k.psl[4:]
    k.psl = k.psl[:4]
    c = make_consts(k)
    mt = k.sb([128, 2, 6, 16], F32)
    vt = k.sb([128, 24], F32)
    zer = k.sb([128, 16], F32)
    mb = Buf()
    P.dma("sp", mt[:].rearrange("p l j d -> p (l j d)"), modsd, writes=[mb])
    P.dma("sp", vt[:], vec, writes=[mb])
    P.op("pool", lambda e: e.memset(zer[:], 0.0), writes=[mb])
    A, Sh, G = mt[:, l, 0, :], mt[:, l, 1, :], mt[:, l, 2, :]

    X = k.sb([128, ND, TT], F32)
    Y = k.sb([128, ND, TT], F32)
    Xb, Yb = Buf(), Buf()
    Yf = Y[:].rearrange("p a b -> p (a b)")
    Ybf = Yf.bitcast(BF16)
    Xbf = X[:].rearrange("p a b -> p (a b)").bitcast(BF16)
    HS = Ybf[:, 0:8192].rearrange("p (a b) -> p a b", a=ND)
    HH = Ybf[:, 8192:16384].rearrange("p (a b) -> p a b", a=ND)
    rc = k.sb([64, 2], F32)
    cos2 = k.sb([64, T], F32)
    sinS = k.sb([64, T], F32)
    csb = Buf()
    P.dma("sp", rc[:], ropec, writes=[mb])
    pi_t = Yf[:64, 0:512].bitcast(I32)
    ang = Yf[:64, 512:1024]
    tmp = Yf[:64, 1024:1536]
    kf = Yf[:64, 1536:2048]
    ki = Yf[:64, 2048:2560].bitcast(I32)
    for ch in range(4):
        t0 = ch * 512
        P.dma("sp", pi_t, posr[:, t0:t0 + 512], writes=[Yb])
        P.op("dve", lambda e: e.tensor_copy(out=ang, in_=pi_t), reads=[Yb], writes=[Yb])
        P.op("dve", lambda e: e.tensor_scalar(out=ang, in0=ang, scalar1=rc[:, 0:1], scalar2=None, op0=ALU.mult), reads=[Yb, mb], writes=[Yb])
        P.op("dve", lambda e: e.tensor_scalar(out=tmp, in0=ang, scalar1=math.pi / 2, scalar2=None, op0=ALU.add), reads=[Yb], writes=[Yb])
        angle_reduce(k, tmp, kf, ki, Yb)
        P.op("act", lambda e, o=cos2[:, t0:t0 + 512]: e.activation(out=o, in_=tmp, func=AF.Sin), reads=[Yb], writes=[csb])
        angle_reduce(k, ang, kf, ki, Yb)
        P.op("act", lambda e, o=sinS[:, t0:t0 + 512]: e.activation(out=o, in_=ang, func=AF.Sin), reads=[Yb], writes=[csb])
        P.op("dve", lambda e, o=sinS[:, t0:t0 + 512]: e.tensor_scalar(out=o, in0=o, scalar1=rc[:, 1:2], scalar2=None, op0=ALU.mult), reads=[csb, mb], writes=[csb])

    CKQ = k.sb([128, 8, TT], F32)
    CK = CKQ[:, 0:4, :]
    CQ = CKQ[:, 4:8, :]
    CKb, CQb = Buf(), Buf()
    CKN = k.sb([128, 4, T], BF16)
    CQN = k.sb([128, 4, T], BF16)
    KR = k.sb([128, T], BF16)
    CKNb, CQNb, KRb = Buf(), Buf(), Buf()
    P.op("pool", lambda e: e.memset(KR[:], 0.0), writes=[KRb])
    wl = WLoader(k, 16, 128, 4)
    xT3 = xT.rearrange("(k p) t -> p k t", p=128)
    oT3 = oT.rearrange("(k p) t -> p k t", p=128)

    def rope_out(ps1, ps1b, ps2, ps2b, dst, dstb, t0):
        t1, t1b = k.rotbuf("rp1", [64, 512], F32, 1)
        t2, t2b = k.rotbuf("rp2", [64, 512], F32, 1)
        P.op("dve", lambda e: e.tensor_tensor(out=t1[:], in0=ps1[:64, :TT], in1=cos2[:, t0:t0 + TT], op=ALU.mult), reads=[ps1b, csb], writes=[t1b])
        P.op("dve", lambda e: e.tensor_tensor(out=t2[:], in0=ps2[:64, :TT], in1=sinS[:, t0:t0 + TT], op=ALU.mult), reads=[ps2b, csb], writes=[t2b])
        P.op("pool", lambda e: e.tensor_tensor(out=dst[:64, t0:t0 + TT], in0=t1[:], in1=t2[:], op=ALU.add), reads=[t1b, t2b], writes=[dstb])

    for tt in range(NTT):
        t0 = tt * TT
        P.dma("sp", X[:], xT3[:, :, t0:t0 + TT], writes=[Xb])
        rstd, rb = rms_rstd(k, c, X, Xb, TT)
        norm_mod(k, X, Xb, rstd, rb, vt[:, 0:16], zer, mb, HS, Yb, TT)
        norm_mod(k, X, Xb, rstd, rb, A, Sh, mb, HH, Yb, TT)
        for (W, src, dst, dstb) in ((kvd, HS, CK, CKb), (wdq, HH, CQ, CQb)):
            for cb in range(4):
                wb, wbb = wl.load(W, 0, 16, cb * 128, 128)
                ps, psb = k.ps()
                for dt in range(16):
                    P.op("pe", lambda e, o=ps[:, :TT], w=wb[:, dt, :], r=src[:, dt, :], s=(dt == 0), t=(dt == 15):
                         e.matmul(o, lhsT=w, rhs=r, start=s, stop=t), reads=[wbb, Yb], writes=[psb])
                P.op("act", lambda e, o=dst[:, cb, :], i=ps[:, :TT]: e.activation(out=o, in_=i, func=AF.Copy), reads=[psb], writes=[dstb])
        pss = []
        for sw in range(2):
            if sw == 0:
                wb, wbb = wl.load(kvd, 0, 16, 512, 64)
            else:
                wb, wbb = load_swap(wl, kvd, 16, 512)
            ps, psb = k.ps()
            for dt in range(16):
                P.op("pe", lambda e, o=ps[:64, :TT], w=wb[:, dt, 0:64], r=HS[:, dt, :], s=(dt == 0), t=(dt == 15):
                     e.matmul(o, lhsT=w, rhs=r, start=s, stop=t), reads=[wbb, Yb], writes=[psb])
            pss.append((ps, psb))
        rope_out(pss[0][0], pss[0][1], pss[1][0], pss[1][1], KR, KRb, t0)
        for (src, srcb, dst, dstb, v0) in ((CK, CKb, CKN, CKNb, 16), (CQ, CQb, CQN, CQNb, 20)):
            rs, rsb = rms_rstd(k, c, src, srcb, TT, scale_div=512, ntile=4)
            for ct in range(4):
                P.op("dve", lambda e, o=dst[:, ct, t0:t0 + TT], i=src[:, ct, :], s=vt[:, v0 + ct:v0 + ct + 1], r=rs[:, :TT]:
                     e.scalar_tensor_tensor(out=o, in0=i, scalar=s, in1=r, op0=ALU.mult, op1=ALU.mult),
                     reads=[srcb, rsb, mb], writes=[dstb])

    P.barrier()
    tri = k.sb([128, 128], BF16)
    trib = Buf()
    P.op("pool", lambda e: e.memset(tri[:], 1.0), writes=[trib])
    P.op("pool", lambda e: e.affine_select(out=tri[:], in_=tri[:], pattern=[[1, 128]], compare_op=ALU.is_ge, fill=0.0,
                                           base=0, channel_multiplier=-1), reads=[trib], writes=[trib])
    wl2 = WLoader(k, 4, 128, 8)
    scale = 192.0 ** -0.5
    hb = []
    for reg in (Ybf, Xbf):
        hb.append(dict(KN=reg[:, 0:2048], QN=reg[:, 2048:4096], QR=reg[:, 4096:6144], OH=reg[:, 6144:8192],
                       VH=reg[:, 8192:10240].rearrange("p (a b) -> p a b", a=16),
                       KNb=Buf(), QNb=Buf(), QRb=Buf(), OHb=Buf(), VHb=Buf()))
    for s_ in hb:
        P.op("pool", lambda e, o=s_["QR"]: e.memset(o, 0.0), writes=[s_["QRb"]])
    for h in range(16):
        s_ = hb[h % 2]
        KN, QN, QR, OH, VH = s_["KN"], s_["QN"], s_["QR"], s_["OH"], s_["VH"]
        KNb, QNb, QRb, OHb, VHb = s_["KNb"], s_["QNb"], s_["QRb"], s_["OHb"], s_["VHb"]
        wk, wkb = wl2.load(wuk, 0, 4, h * 128, 128)
        wq, wqb = wl2.load(wuq, 0, 4, h * 192, 128)
        wv, wvb = wl2.load(wuv, 0, 4, h * 128, 128)
        wr, wrb = wl2.load(wuq, 0, 4, h * 192 + 128, 64)
        ws, wsb = load_swap(wl2, wuq, 4, h * 192 + 128)
        for tq in range(NTT):
            t0 = tq * TT
            for (w_, wb_, src, srcb, dst, dstb) in ((wk, wkb, CKN, CKNb, KN, KNb), (wq, wqb, CQN, CQNb, QN, QNb)):
                ps, psb = k.ps()
                for ct in range(4):
                    P.op("pe", lambda e, o=ps[:, :TT], w=w_[:, ct, :], r=src[:, ct, t0:t0 + TT], s=(ct == 0), t=(ct == 3):
                         e.matmul(o, lhsT=w, rhs=r, start=s, stop=t), reads=[wb_, srcb], writes=[psb])
                P.op("act", lambda e, o=dst[:, t0:t0 + TT], i=ps[:, :TT]: e.activation(out=o, in_=i, func=AF.Copy), reads=[psb], writes=[dstb])
            pss = []
            for (w_, wb_) in ((wr, wrb), (ws, wsb)):
                ps, psb = k.ps()
                for ct in range(4):
                    P.op("pe", lambda e, o=ps[:64, :TT], w=w_[:, ct, 0:64], r=CQN[:, ct, t0:t0 + TT], s=(ct == 0), t=(ct == 3):
                         e.matmul(o, lhsT=w, rhs=r, start=s, stop=t), reads=[wb_, CQNb], writes=[psb])
                pss.append((ps, psb))
            rope_out(pss[0][0], pss[0][1], pss[1][0], pss[1][1], QR, QRb, t0)
        for tk4 in range(4):
            ps, psb = k.ps()
            for i in range(4):
                tk = tk4 * 4 + i
                for ct in range(4):
                    P.op("pe", lambda e, o=ps[:, i * 128:(i + 1) * 128], w=CKN[:, ct, tk * 128:(tk + 1) * 128], r=wv[:, ct, :], s=(ct == 0), t=(ct == 3):
                         e.matmul(o, lhsT=w, rhs=r, start=s, stop=t), reads=[wvb, CKNb], writes=[psb])
            P.op("act", lambda e, o=VH[:, tk4 * 4:tk4 * 4 + 4, :], i=ps[:, :].rearrange("p (a b) -> p a b", a=4):
                 e.activation(out=o, in_=i, func=AF.Copy), reads=[psb], writes=[VHb])
        for qt in range(NTT):
            oa, oab = oacc[(qt % 2) * 2]
            da, dab = oacc[(qt % 2) * 2 + 1]
            nk_ = 4 * (qt + 1)
            def score(kt):
                off = max(0, (kt - 4 * qt) * 128)
                q0 = qt * TT + off
                q1 = (qt + 1) * TT
                sp_, spb = k.ps()
                P.op("pe", lambda e, o=sp_[:, off:TT], w=KN[:, kt * 128:(kt + 1) * 128], r=QN[:, q0:q1]:
                     e.matmul(o, lhsT=w, rhs=r, start=True, stop=False), reads=[KNb, QNb], writes=[spb])
                P.op("pe", lambda e, o=sp_[:, off:TT], w=KR[:, kt * 128:(kt + 1) * 128], r=QR[:, q0:q1]:
                     e.matmul(o, lhsT=w, rhs=r, start=False, stop=True), reads=[KRb, QRb], writes=[spb])
                PT, PTb = k.rotbuf("PT", [128, TT], BF16, 4)
                P.op("act", lambda e, o=PT[:, off:TT], i=sp_[:, off:TT]: e.activation(out=o, in_=i, func=AF.Exp, scale=scale),
                     reads=[spb], writes=[PTb])
                if kt >= 4 * qt:
                    P.op("pool", lambda e, o=PT[:, off:off + 128]: e.tensor_tensor(out=o, in0=o, in1=tri[:], op=ALU.mult),
                         reads=[PTb, trib], writes=[PTb])
                return (kt, off, PT, PTb)

            def pv(st_):
                kt, off, PT, PTb = st_
                P.op("pe", lambda e, o=oa[:, off:TT], w=VH[:, kt, :], r=PT[:, off:TT], s=(kt == 0), t=(kt == nk_ - 1):
                     e.matmul(o, lhsT=w, rhs=r, start=s, stop=t), reads=[VHb, PTb], writes=[oab])
                P.op("pe", lambda e, o=da[:, off:TT], r=PT[:, off:TT], s=(kt == 0), t=(kt == nk_ - 1):
                     e.matmul(o, lhsT=c["ones"][:], rhs=r, start=s, stop=t), reads=[c["onesb"], PTb], writes=[dab])

            pend = []
            for kt in range(nk_):
                pend.append(score(kt))
                if len(pend) > 2:
                    pv(pend.pop(0))
            while pend:
                pv(pend.pop(0))
            rd, rdb = k.rotbuf("rden", [128, TT], F32, 2)
            P.op("dve", lambda e, o=rd[:], i=da[:, :TT]: e.reciprocal(out=o, in_=i), reads=[dab], writes=[rdb])
            P.op("dve", lambda e, o=OH[:, qt * TT:(qt + 1) * TT], i=oa[:, :TT], r=rd[:]: e.tensor_tensor(out=o, in0=i, in1=r, op=ALU.mult),
                 reads=[oab, rdb], writes=[OHb])
        P.dma("sp", otd[h], OH, reads=[OHb])

    P.barrier()
    OTt = CKQ[:].rearrange("p a b -> p (a b)").bitcast(BF16).rearrange("p (a b) -> p a b", a=16)
    OTb = Buf()
    otd3 = otd.rearrange("h p t -> p h t")
    Xb, Yb = Buf(), Buf()
    for tt in range(NTT):
        t0 = tt * TT
        P.dma("sp", OTt, otd3[:, :, t0:t0 + TT], writes=[OTb])
        P.dma("sp", X[:], xT3[:, :, t0:t0 + TT], writes=[Xb])
        for eb in range(16):
            wb, wbb = wl.load(wo, 0, 16, eb * 128, 128)
            ps, psb = k.ps()
            for hh in range(16):
                P.op("pe", lambda e, o=ps[:, :TT], w=wb[:, hh, :], r=OTt[:, hh, :], s=(hh == 0), t=(hh == 15):
                     e.matmul(o, lhsT=w, rhs=r, start=s, stop=t), reads=[wbb, OTb], writes=[psb])
            P.op("act", lambda e, o=Y[:, eb, :], i=ps[:, :TT]: e.activation(out=o, in_=i, func=AF.Copy), reads=[psb], writes=[Yb])
        post_residual(k, c, X, Xb, Y, Yb, G, mb, TT)
        P.dma("sp", oT3[:, :, t0:t0 + TT], X[:], reads=[Xb])
    k.close()
    return nc


def build_rwkv(l=0, dbg=False, env=None):
    k = Ctx("rwkv", env)
    k.rotw = 256
    nc, P = k.nc, k.P
    TT = 256
    NTT = T // TT
    C = 64
    NCH = TT // C
    xT = k.dram("xT", [D, T], F32, "ExternalInput")
    modsd = k.dram("mods", [128, 192], F32, "ExternalInput")
    vec = k.dram("rw_vec", [128, 13, 16], F32, "ExternalInput")
    wrkv = wsrc(k, "w_rkv", [3 * D, D])
    w1 = wsrc(k, "w1", [D, 96])
    w2 = wsrc(k, "w2", [96, D])
    a1 = wsrc(k, "a1", [D, 96])
    a2 = wsrc(k, "a2", [96, D])
    g1 = wsrc(k, "g1", [D, 256])
    g2 = wsrc(k, "g2", [256, D])
    wo = wsrc(k, "w_o", [D, D])
    oT = k.dram("oT", [D, T], F32, "ExternalOutput")
    k.init_psum(8)
    c = make_consts(k)
    mt = k.sb([128, 2, 6, 16], F32)
    vt = k.sb([128, 13, 16], F32)
    mb = Buf()
    P.dma("sp", mt[:].rearrange("p l j d -> p (l j d)"), modsd, writes=[mb])
    P.dma("sp", vt[:], vec, writes=[mb])
    A, Sh, G = mt[:, l, 0, :], mt[:, l, 1, :], mt[:, l, 2, :]
    MU, W0, A0, KKv, KA, RK, LNW, LNB = (lambda j: vt[:, j, :]), vt[:, 6, :], vt[:, 7, :], vt[:, 8, :], vt[:, 9, :], vt[:, 10, :], vt[:, 11, :], vt[:, 12, :]

    NEG = k.sb([128, 2, 16], F32)
    P.op("dve", lambda e: e.tensor_scalar(out=NEG[:], in0=vt[:, 6:8, :], scalar1=-1.0, scalar2=None, op0=ALU.mult), reads=[mb], writes=[mb])
    cb_ = Buf()
    bo16 = k.sb([128, 128], BF16)
    bo32 = k.sb([128, 128], F32)
    idn = k.sb([128, 4, 128], BF16)
    mS = k.sb([128, 4, 64], BF16)
    mI = k.sb([128, 4, 64], BF16)
    mL = k.sb([128, 4, 64], BF16)
    ones64 = k.sb([128, 64], F32)
    for t_ in (bo16, bo32):
        P.op("pool", lambda e, t_=t_: e.memset(t_[:], 0.0), writes=[cb_])
        P.op("pool", lambda e, t_=t_: e.memset(t_[0:64, 0:64], 1.0), writes=[cb_])
        P.op("pool", lambda e, t_=t_: e.memset(t_[64:128, 64:128], 1.0), writes=[cb_])
    P.op("pool", lambda e: e.memset(ones64[:], 1.0), writes=[cb_])
    rmask = k.sb([128, TT], F32)
    P.op("pool", lambda e: e.memset(rmask[:], 1.0), writes=[cb_])
    for ch_ in range(NCH):
        P.op("pool", lambda e, o=rmask[:, ch_ * C:ch_ * C + 1]: e.memset(o, 0.0), writes=[cb_])
    P.op("pool", lambda e: e.memset(idn[:], 1.0), writes=[cb_])
    P.op("pool", lambda e: e.memset(mS[:], 1.0), writes=[cb_])
    P.op("pool", lambda e: e.memset(mI[:], 1.0), writes=[cb_])
    P.op("pool", lambda e: e.memset(mL[:], 1.0), writes=[cb_])
    for g_ in range(4):
        P.op("pool", lambda e, o=idn[:, g_, :]: e.affine_select(out=o, in_=o, pattern=[[1, 128]], compare_op=ALU.is_equal, fill=0.0,
                                                               base=0, channel_multiplier=-1), reads=[cb_], writes=[cb_])
        for hf in range(2):
            sl = slice(64 * hf, 64 * hf + 64)
            P.op("pool", lambda e, o=mS[sl, g_, :]: e.affine_select(out=o, in_=o, pattern=[[1, 64]], compare_op=ALU.is_ge, fill=0.0,
                                                                    base=-1, channel_multiplier=-1), reads=[cb_], writes=[cb_])
            P.op("pool", lambda e, o=mI[sl, g_, :]: e.affine_select(out=o, in_=o, pattern=[[1, 64]], compare_op=ALU.is_ge, fill=0.0,
                                                                    base=0, channel_multiplier=-1), reads=[cb_], writes=[cb_])
            P.op("pool", lambda e, o=mL[sl, g_, :]: e.affine_select(out=o, in_=o, pattern=[[-1, 64]], compare_op=ALU.is_ge, fill=0.0,
                                                                    base=-1, channel_multiplier=1), reads=[cb_], writes=[cb_])

    RT = k.sb([128, 16, TT], BF16)
    KT = k.sb([128, 16, TT], BF16)
    BT = k.sb([128, 16, TT], BF16)
    AT = k.sb([128, 16, TT], BF16)
    VT = k.sb([128, 16, TT], BF16)
    GT = k.sb([128, 16, TT], BF16)
    BON = k.sb([128, 16, TT], BF16)
    GC = k.sb([128, 16, NCH], F32)
    RTb, KTb, BTb, ATb, VTb, GTb, BONb, GCb, YTb = (Buf() for _ in range(9))
    Hf = k.sb([128, 16, 64], F32)
    Hstk = k.sb([128, 16, 64], BF16)
    Hbd = k.sb([128, 16, 128], BF16)
    Hb_ = [Buf() for _ in range(4)]
    HL = k.sb([128, 16, 1], F32)
    HLb = Buf()
    P.op("pool", lambda e: e.memset(Hf[:], 0.0), writes=Hb_)
    P.op("pool", lambda e: e.memset(Hstk[:], 0.0), writes=Hb_)
    P.op("pool", lambda e: e.memset(Hbd[:], 0.0), writes=Hb_)
    P.op("pool", lambda e: e.memset(HL[:], 0.0), writes=[HLb])
    TW = k.sb([128, TT], BF16)
    TA = k.sb([128, TT], BF16)
    TG = k.sb([128, 2, TT], BF16)
    TWb, TAb, TGb = Buf(), Buf(), Buf()
    wl = WLoader(k, 16, 128, 3)
    wls = WLoader(k, 2, 128, 3)
    REG = k.sb([128, 18688], F32)
    REGbf = REG[:].bitcast(BF16)

    def f32v(o, n, a):
        return REG[:, o:o + n].rearrange("p (a b) -> p a b", a=a)

    def bfv(o, n, a):
        return REGbf[:, 2 * o:2 * o + 2 * n].rearrange("p (a b) -> p a b", a=a)

    X = f32v(0, 4096, 16)
    Hs = f32v(4096, 4352, 16)
    XX = bfv(8448, 2048, 16)
    XS = bfv(10496, 2048, 16)
    XR = bfv(12544, 2048, 16)
    XK = bfv(14592, 2048, 16)
    XV = bfv(16640, 2048, 16)
    o_ = [0]

    def nxt(n, a):
        v = bfv(o_[0], n, a)
        o_[0] += n
        return v
    ATbd, BTbd, KTbd, VTbd = nxt(1024, 16), nxt(1024, 16), nxt(1024, 16), nxt(1024, 16)
    def chbuf():
        t = k.sb([128, 4, 128], F32)
        return {"r": t[:], "w": t[:].bitcast(CHAIN_DT), "m": t[:].bitcast(CHAIN_DT)}
    CH = [dict(N=[chbuf(), chbuf()], L=[chbuf(), chbuf()], P=chbuf()) for _ in range(2)]
    PF = nxt(1024, 16)
    MakT = nxt(1024, 16)
    MrbT, MrkT = nxt(512, 16), nxt(512, 16)
    Vbd, Vstk = nxt(1024, 16), nxt(512, 16)
    Bbd, Kbd = nxt(1024, 16), nxt(1024, 16)
    Zs, Us, Ubd = nxt(512, 16), nxt(512, 16), nxt(1024, 16)
    ZERO_LIST = [ATbd, BTbd, KTbd, VTbd, CH[0]['N'][0]['w'], CH[0]['L'][0]['w'], CH[1]['N'][0]['w'], CH[1]['L'][0]['w'], MakT, Ubd]
    YT = f32v(12800, 4096, 16)
    OIN = bfv(4096, 2048, 16)
    Y2 = f32v(8448, 4096, 16)

    xT3 = xT.rearrange("(k p) t -> p k t", p=128)
    oT3 = oT.rearrange("(k p) t -> p k t", p=128)
    if dbg:
        dbf = k.dram("dbg_bf", [7, 128, 16 * TT], BF16, "ExternalOutput")
        dyt = k.dram("dbg_yt", [128, 16 * TT], F32, "ExternalOutput")
        dgc = k.dram("dbg_gc", [128, 16 * NCH], F32, "ExternalOutput")
        doin = k.dram("dbg_oin", [128, 16 * TT], BF16, "ExternalOutput")

    def tmp(name, n=1, dt=F32, w=TT):
        return k.rotbuf(name, [128, w], dt, n)

    for tt in range(1 if dbg else NTT):
        t0 = tt * TT
        Xb, Hsb, XXb, XSb, XRb, XKb, XVb = (Buf() for _ in range(7))
        P.dma("sp", X, xT3[:, :, t0:t0 + TT], writes=[Xb])
        rstd, rb = rms_rstd(k, c, X, Xb, TT)
        norm_mod(k, X, Xb, rstd, rb, A, Sh, mb, Hs, Hsb, TT, col0=1)
        P.op("pool", lambda e: e.tensor_copy(out=Hs[:, :, 0:1], in_=HL[:]), reads=[HLb], writes=[Hsb])
        P.op("dve", lambda e: e.tensor_tensor(out=XX, in0=Hs[:, :, 0:TT], in1=Hs[:, :, 1:TT + 1], op=ALU.subtract), reads=[Hsb], writes=[XXb])
        P.op("pool", lambda e: e.tensor_copy(out=HL[:], in_=Hs[:, :, TT:TT + 1]), reads=[Hsb], writes=[HLb])

        def make_xs(j, dst, dstb):
            for dt in range(16):
                P.op("dve", lambda e, o=dst[:, dt, :], i=XX[:, dt, :], s=vt[:, j, dt:dt + 1], h=Hs[:, dt, 1:TT + 1]:
                     e.scalar_tensor_tensor(out=o, in0=i, scalar=s, in1=h, op0=ALU.mult, op1=ALU.add),
                     reads=[XXb, Hsb, mb], writes=[dstb])
        for (j, W, ncol) in ((3, w1, 96), (4, a1, 96), (5, g1, 256)):
            make_xs(j, XS, XSb)
            for cbk in range((ncol + 127) // 128):
                nc_ = min(128, ncol - cbk * 128)
                wb, wbb = wl.load(W, 0, 16, cbk * 128, nc_)
                ps, psb = k.ps()
                for dt in range(16):
                    P.op("pe", lambda e, o=ps[:nc_, :TT], w=wb[:, dt, :nc_], r=XS[:, dt, :], s=(dt == 0), t=(dt == 15):
                         e.matmul(o, lhsT=w, rhs=r, start=s, stop=t), reads=[wbb, XSb], writes=[psb])
                if j == 3:
                    P.op("act", lambda e, i=ps[:96, :TT]: e.activation(out=TW[:96, :], in_=i, func=AF.Tanh), reads=[psb], writes=[TWb])
                elif j == 4:
                    P.op("act", lambda e, i=ps[:96, :TT]: e.activation(out=TA[:96, :], in_=i, func=AF.Copy), reads=[psb], writes=[TAb])
                else:
                    P.op("act", lambda e, i=ps[:, :TT], o=TG[:, cbk, :]: e.activation(out=o, in_=i, func=AF.Sigmoid), reads=[psb], writes=[TGb])
        make_xs(0, XR, XRb)
        make_xs(1, XK, XKb)
        make_xs(2, XV, XVb)
        for p in range(16):
            e0 = p * 128
            pA, pAb = k.ps()
            pB, pBb = k.ps()
            pC, pCb = k.ps()
            pD, pDb = k.ps()
            for (jj, src, srcb, ps, psb, co) in ((0, XR, XRb, pA, pAb, 0), (1, XK, XKb, pA, pAb, TT), (2, XV, XVb, pB, pBb, 0)):
                wb, wbb = wl.load(wrkv, jj * D, 16, e0, 128)
                for dt in range(16):
                    P.op("pe", lambda e, o=ps[:, co:co + TT], w=wb[:, dt, :], r=src[:, dt, :], s=(dt == 0), t=(dt == 15):
                         e.matmul(o, lhsT=w, rhs=r, start=s, stop=t), reads=[wbb, srcb], writes=[psb])
            wb, wbb = wls.load(w2, 0, 1, e0, 128, pp=96)
            P.op("pe", lambda e, o=pB[:, TT:2 * TT], w=wb[:96, 0, :]: e.matmul(o, lhsT=w, rhs=TW[:96, :], start=True, stop=True),
                 reads=[wbb, TWb], writes=[pBb])
            wb, wbb = wls.load(a2, 0, 1, e0, 128, pp=96)
            P.op("pe", lambda e, o=pC[:, 0:TT], w=wb[:96, 0, :]: e.matmul(o, lhsT=w, rhs=TA[:96, :], start=True, stop=True),
                 reads=[wbb, TAb], writes=[pCb])
            wb, wbb = wls.load(g2, 0, 2, e0, 128)
            for kt in range(2):
                P.op("pe", lambda e, o=pC[:, TT:2 * TT], w=wb[:, kt, :], r=TG[:, kt, :], s=(kt == 0), t=(kt == 1):
                     e.matmul(o, lhsT=w, rhs=r, start=s, stop=t), reads=[wbb, TGb], writes=[pCb])
            r_ps, k_ps, v_ps, w_ps, a_ps, g_ps = pA[:, 0:TT], pA[:, TT:2 * TT], pB[:, 0:TT], pB[:, TT:2 * TT], pC[:, 0:TT], pC[:, TT:2 * TT]
            LWS = -0.6065306597126334
            sg, sgb = tmp("sg", 2)
            cl, clb = tmp("cl", 2)
            av, avb = tmp("av", 2)
            vf, vfb = tmp("vf", 2)
            kk, kkb = tmp("kk", 2)
            rn, rnb = tmp("rn", 2)
            kf_, kfb = tmp("kf", 2)
            eg, egb = tmp("eg")
            eig, eigb = tmp("eig")
            eex, eexb = tmp("eex")
            k2, k2b = tmp("ksq", 1, BF16)
            bb, bbb = tmp("bb")
            rk_, rkb = tmp("rkp", 1, BF16)
            lw, lwb = sg, sgb
            P.op("act", lambda e, o=sg[:], i=w_ps, b=NEG[:, 0, p:p + 1]: e.activation(out=o, in_=i, func=AF.Exp, bias=b, scale=-1.0),
                 reads=[pBb, mb], writes=[sgb])
            P.op("dve", lambda e, o=kk[:], i=k_ps, s=KKv[:, p:p + 1]: e.tensor_scalar(out=o, in0=i, scalar1=s, scalar2=None, op0=ALU.mult),
                 reads=[pAb, mb], writes=[kkb])
            P.op("pool", lambda e, o=k2[:], i=kk[:]: e.tensor_tensor(out=o, in0=i, in1=i, op=ALU.mult), reads=[kkb], writes=[k2b])
            P.op("pe", lambda e, o=pD[:, 0:TT], r=k2[:]: e.matmul(o, lhsT=bo16[:], rhs=r, start=True, stop=True), reads=[k2b, cb_], writes=[pDb])
            P.op("act", lambda e, o=av[:], i=a_ps, b=NEG[:, 1, p:p + 1]: e.activation(out=o, in_=i, func=AF.Exp, bias=b, scale=-1.0),
                 reads=[pCb, mb], writes=[avb])
            P.op("act", lambda e, o=GT[:, p, :], i=g_ps: e.activation(out=o, in_=i, func=AF.Copy), reads=[pCb], writes=[GTb])
            P.op("act", lambda e, o=vf[:], i=v_ps: e.activation(out=o, in_=i, func=AF.Copy), reads=[pBb], writes=[vfb])
            P.op("pool", lambda e, o=VT[:, p, :], i=vf[:]: e.tensor_copy(out=o, in_=i), reads=[vfb], writes=[VTb])
            P.op("act", lambda e, o=sg[:]: e.activation(out=o, in_=o, func=AF.Ln, bias=c["eps"][:, 2:3], scale=1.0), reads=[sgb, c["onesb"]], writes=[sgb])
            P.op("act", lambda e, o=sg[:]: e.activation(out=o, in_=o, func=AF.Exp, scale=-1.0), reads=[sgb], writes=[sgb])
            P.op("dve", lambda e, o=cl[:], i=lw[:]: e.tensor_tensor_scan(out=o, data0=rmask[:], data1=i, initial=0.0, op0=ALU.mult, op1=ALU.add),
                 reads=[lwb, cb_], writes=[clb])
            P.op("dve", lambda e, o=rn[:], i=pD[:, 0:TT]: e.tensor_scalar(out=o, in0=i, scalar1=5.5e-20, scalar2=None, op0=ALU.max), reads=[pDb], writes=[rnb])
            P.op("pool", lambda e, o=eex[:], i=cl[:], j_=lw[:]: e.tensor_tensor(out=o, in0=i, in1=j_, op=ALU.subtract), reads=[clb, lwb], writes=[eexb])
            P.op("act", lambda e, o=rn[:]: e.activation(out=o, in_=o, func=AF.Ln), reads=[rnb], writes=[rnb])
            P.op("act", lambda e, o=rn[:]: e.activation(out=o, in_=o, func=AF.Exp, scale=-0.5), reads=[rnb], writes=[rnb])
            P.op("act", lambda e, o=eg[:], i=cl[:]: e.activation(out=o, in_=i, func=AF.Exp, scale=LWS), reads=[clb], writes=[egb])
            P.op("act", lambda e, o=eig[:], i=cl[:]: e.activation(out=o, in_=i, func=AF.Exp, scale=-LWS), reads=[clb], writes=[eigb])
            P.op("act", lambda e, o=eex[:]: e.activation(out=o, in_=o, func=AF.Exp, scale=LWS), reads=[eexb], writes=[eexb])
            P.op("pool", lambda e, o=GC[:, p, :], i=eg[:].rearrange("p (a b) -> p a b", a=NCH)[:, :, C - 1]: e.tensor_copy(out=o, in_=i),
                 reads=[egb], writes=[GCb])
            P.op("act", lambda e, o=av[:]: e.activation(out=o, in_=o, func=AF.Ln, bias=c["eps"][:, 2:3], scale=1.0), reads=[avb, c["onesb"]], writes=[avb])
            P.op("act", lambda e, o=av[:]: e.activation(out=o, in_=o, func=AF.Exp, scale=-1.0), reads=[avb], writes=[avb])
            P.op("dve", lambda e, o=kf_[:], i=av[:], s=KA[:, p:p + 1]: e.tensor_scalar(out=o, in0=i, scalar1=-1.0, scalar2=s, op0=ALU.add, op1=ALU.mult),
                 reads=[avb, mb], writes=[kfb])
            P.op("dve", lambda e, o=kf_[:], i=k_ps: e.scalar_tensor_tensor(out=o, in0=o, scalar=1.0, in1=i, op0=ALU.add, op1=ALU.mult),
                 reads=[kfb, pAb], writes=[kfb])
            P.op("pool", lambda e, o=kk[:], r=rn[:]: e.tensor_tensor(out=o, in0=o, in1=r, op=ALU.mult), reads=[kkb, rnb], writes=[kkb])
            P.op("dve", lambda e, o=RT[:, p, :], i=r_ps, g_=eg[:]: e.tensor_tensor(out=o, in0=i, in1=g_, op=ALU.mult), reads=[pAb, egb], writes=[RTb])
            P.op("pool", lambda e, o=KT[:, p, :], i=kf_[:], g_=eig[:]: e.tensor_tensor(out=o, in0=i, in1=g_, op=ALU.mult), reads=[kfb, eigb], writes=[KTb])
            P.op("pool", lambda e, o=bb[:], i=kk[:], a_=av[:]: e.tensor_tensor(out=o, in0=i, in1=a_, op=ALU.mult), reads=[kkb, avb], writes=[bbb])
            P.op("pool", lambda e, o=BT[:, p, :], i=bb[:], g_=eig[:]: e.tensor_tensor(out=o, in0=i, in1=g_, op=ALU.mult), reads=[bbb, eigb], writes=[BTb])
            P.op("dve", lambda e, o=AT[:, p, :], i=kk[:], g_=eex[:]: e.scalar_tensor_tensor(out=o, in0=i, scalar=-1.0, in1=g_, op0=ALU.mult, op1=ALU.mult),
                 reads=[kkb, eexb], writes=[ATb])
            P.op("dve", lambda e, o=rk_[:], i=r_ps, s=RK[:, p:p + 1], k_=kf_[:]: e.scalar_tensor_tensor(out=o, in0=i, scalar=s, in1=k_, op0=ALU.mult, op1=ALU.mult),
                 reads=[pAb, kfb, mb], writes=[rkb])
            P.op("pe", lambda e, o=pD[:, TT:2 * TT], r=rk_[:]: e.matmul(o, lhsT=bo16[:], rhs=r, start=True, stop=True), reads=[rkb, cb_], writes=[pDb])
            P.op("dve", lambda e, o=BON[:, p, :], i=pD[:, TT:2 * TT], v_=vf[:]: e.tensor_tensor(out=o, in0=i, in1=v_, op=ALU.mult),
                 reads=[pDb, vfb], writes=[BONb])

        P.barrier()
        if dbg:
            for i_, (t_, b_) in enumerate(((RT, RTb), (KT, KTb), (BT, BTb), (AT, ATb), (VT, VTb), (GT, GTb), (BON, BONb))):
                P.dma("sp", dbf[i_], t_[:].rearrange("p a b -> p (a b)"), reads=[b_])
            P.dma("sp", dgc, GC[:].rearrange("p a b -> p (a b)"), reads=[GCb])
            P.barrier()
        zb = Buf()
        SB = [dict(N=Buf(), L=Buf(), P=Buf()) for _ in range(2)]
        for z_ in ZERO_LIST:
            if z_.dtype == BF16:
                P.op("pool", lambda e, z_=z_: e.memset(z_, 0.0), writes=[zb])
            else:
                P.op("dve", lambda e, z_=z_: e.tensor_scalar(out=z_, in0=idn[:], scalar1=0.0, scalar2=None, op0=ALU.mult),
                     reads=[cb_], writes=[zb])
        for ch in range(NCH):
            cc = slice(ch * C, (ch + 1) * C)
            inb = [Buf() for _ in range(4)]
            for (src, srcb, dst) in ((AT, ATb, ATbd), (BT, BTb, BTbd), (KT, KTb, KTbd), (VT, VTb, VTbd)):
                for hf in range(2):
                    sl = slice(64 * hf, 64 * hf + 64)
                    P.op("dve" if hf == 0 else "act",
                         (lambda e, o=dst[sl, :, 64 * hf:64 * hf + 64], i=src[sl, :, cc]: e.tensor_copy(out=o, in_=i)) if hf == 0 else
                         (lambda e, o=dst[sl, :, 64 * hf:64 * hf + 64], i=src[sl, :, cc]: e.activation(out=o, in_=i, func=AF.Copy)),
                         reads=[srcb, zb], writes=inb)
            gb = [dict((n, Buf()) for n in ("N", "L", "Mak", "Mrb", "Mrk", "V", "B", "K", "P", "Z", "U")) for _ in range(4)]
            for g_ in range(4):
                pg = slice(g_ * 4, g_ * 4 + 4)
                b_ = gb[g_]
                for (src, dst, nm) in ((VTbd, Vbd, "V"), (BTbd, Bbd, "B"), (KTbd, Kbd, "K")):
                    ps, psb = k.ps()
                    psv = ps[:].bitcast(BF16)[:, 0:512].rearrange("p (a b) -> p a b", a=4)
                    for i in range(4):
                        P.op("pe", lambda e, o=psv[:, i, :], w=src[:, g_ * 4 + i, :]: e.transpose(out=o, in_=w, identity=idn[:, 0, :]),
                             reads=[inb[g_], cb_], writes=[psb])
                    P.op("act", lambda e, o=dst[:, pg, :], i=psv: e.activation(out=o, in_=i, func=AF.Copy), reads=[psb], writes=[b_[nm]])
                    if nm == "V":
                        for hf in range(2):
                            sl = slice(64 * hf, 64 * hf + 64)
                            P.op("dve", lambda e, o=Vstk[sl, pg, :], i=psv[sl, :, 64 * hf:64 * hf + 64]: e.tensor_copy(out=o, in_=i),
                                 reads=[psb], writes=[b_[nm]])
            def step1(g_):
                ps1, ps1b = k.ps()
                ps2, ps2b = k.ps()
                ps3, ps3b = k.ps()
                b_ = gb[g_]
                cs_ = CH[g_ % 2]
                sb_ = SB[g_ % 2]
                for i in range(4):
                    p = g_ * 4 + i
                    cs = slice(i * 64, i * 64 + 64)
                    cs2 = slice(256 + i * 64, 256 + i * 64 + 64)
                    for (ps, psb, csl, lh, rh, rhb) in ((ps1, ps1b, cs, BTbd, AT, ATb), (ps1, ps1b, cs2, ATbd, BT, BTb),
                                                        (ps2, ps2b, cs, KTbd, AT, ATb), (ps2, ps2b, cs2, BTbd, RT, RTb),
                                                        (ps3, ps3b, cs, KTbd, RT, RTb)):
                        P.op("pe", lambda e, o=ps[:, csl], w=lh[:, p, :], r=rh[:, p, cc]: e.matmul(o, lhsT=w, rhs=r, start=True, stop=True),
                             reads=[inb[g_], rhb], writes=[psb])
                pg = slice(g_ * 4, g_ * 4 + 4)
                v1 = ps1[:, 0:256].rearrange("p (a b) -> p a b", a=4)
                v1b = ps1[:, 256:512].rearrange("p (a b) -> p a b", a=4)
                v2 = ps2[:, 0:256].rearrange("p (a b) -> p a b", a=4)
                v2b = ps2[:, 256:512].rearrange("p (a b) -> p a b", a=4)
                v3 = ps3[:, 0:256].rearrange("p (a b) -> p a b", a=4)
                for hf in range(2):
                    sl = slice(64 * hf, 64 * hf + 64)
                    fs = slice(64 * hf, 64 * hf + 64)
                    P.op("dve", lambda e, o=cs_["N"][0]["w"][sl, :, fs], i=v1[sl], m=mS[sl]: e.tensor_tensor(out=o, in0=i, in1=m, op=ALU.mult),
                         reads=[ps1b, cb_, zb], writes=[sb_["N"]])
                    P.op("dve", lambda e, o=cs_["L"][0]["w"][sl, :, fs], i=v1b[sl], m=mL[sl]: e.tensor_tensor(out=o, in0=i, in1=m, op=ALU.mult),
                         reads=[ps1b, cb_, zb], writes=[sb_["L"]])
                    P.op("dve", lambda e, o=MakT[sl, pg, fs], i=v2[sl], m=mS[sl]: e.tensor_tensor(out=o, in0=i, in1=m, op=ALU.mult),
                         reads=[ps2b, cb_, zb], writes=[b_["Mak"]])
                P.op("dve", lambda e, o=MrbT[:, pg, :], i=v2b, m=mI[:]: e.tensor_tensor(out=o, in0=i, in1=m, op=ALU.mult),
                     reads=[ps2b, cb_], writes=[b_["Mrb"]])
                P.op("dve", lambda e, o=MrkT[:, pg, :], i=v3, m=mI[:]: e.tensor_tensor(out=o, in0=i, in1=m, op=ALU.mult),
                     reads=[ps3b, cb_], writes=[b_["Mrk"]])
                P.op("dve", lambda e, o=cs_["P"]["w"], i=cs_["N"][0]["r"]: e.tensor_tensor(out=o, in0=i, in1=idn[:], op=ALU.add),
                     reads=[sb_["N"], cb_], writes=[sb_["P"]])

            def chain_sq(g_, lev):
                a_, n_ = (lev - 1) % 2, lev % 2
                cs_ = CH[g_ % 2]
                sb_ = SB[g_ % 2]
                Ns, Ls = cs_["N"], cs_["L"]
                if lev < 5:
                    psn, psnb = k.ps()
                    for i in range(4):
                        P.op("pe", lambda e, o=psn[:, i * 128:(i + 1) * 128], w=Ls[a_]["m"][:, i, :], r=Ns[a_]["m"][:, i, :]:
                             e.matmul(o, lhsT=w, rhs=r, start=True, stop=True), reads=[sb_["L"], sb_["N"]], writes=[psnb])
                psl_, pslb = k.ps()
                for i in range(4):
                    P.op("pe", lambda e, o=psl_[:, i * 128:(i + 1) * 128], w=Ns[a_]["m"][:, i, :], r=Ls[a_]["m"][:, i, :]:
                         e.matmul(o, lhsT=w, rhs=r, start=True, stop=True), reads=[sb_["L"], sb_["N"]], writes=[pslb])
                P.op("act", lambda e, o=Ls[n_]["w"], i=psl_[:].rearrange("p (a b) -> p a b", a=4): e.activation(out=o, in_=i, func=AF.Copy),
                     reads=[pslb], writes=[sb_["L"]])
                if lev < 5:
                    P.op("act", lambda e, o=Ns[n_]["w"], i=psn[:].rearrange("p (a b) -> p a b", a=4): e.activation(out=o, in_=i, func=AF.Copy),
                         reads=[psnb], writes=[sb_["N"]])

            def chain_p(g_, lev):
                n_ = lev % 2
                pg = slice(g_ * 4, g_ * 4 + 4)
                cs_ = CH[g_ % 2]
                sb_ = SB[g_ % 2]
                Ls, Pc = cs_["L"], cs_["P"]
                psp, pspb = k.ps()
                for i in range(4):
                    P.op("pe", lambda e, o=psp[:, i * 128:(i + 1) * 128], w=Ls[n_]["m"][:, i, :], r=Pc["m"][:, i, :]:
                         e.matmul(o, lhsT=w, rhs=r, start=True, stop=True), reads=[sb_["L"], sb_["P"]], writes=[pspb])
                if lev < 5:
                    P.op("dve", lambda e, o=Pc["w"], q=Pc["r"], i=psp[:].rearrange("p (a b) -> p a b", a=4): e.tensor_tensor(out=o, in0=i, in1=q, op=ALU.add),
                         reads=[pspb, sb_["P"]], writes=[sb_["P"]])
                else:
                    P.op("dve", lambda e, o=PF[:, pg, :], i=psp[:].rearrange("p (a b) -> p a b", a=4), q=Pc["r"]: e.tensor_tensor(out=o, in0=i, in1=q, op=ALU.add),
                         reads=[pspb, sb_["P"]], writes=[gb[g_]["P"]])

            for gp in ((0, 1), (2, 3)):
                for g_ in gp:
                    step1(g_)
                k.do_pre(1)
                for lev in range(1, 6):
                    for g_ in gp:
                        chain_sq(g_, lev)
                    for g_ in gp:
                        chain_p(g_, lev)
            zps, ups = {}, {}
            for g_ in range(4):
                pg = slice(g_ * 4, g_ * 4 + 4)
                b_ = gb[g_]
                hb_ = Hb_[g_]
                psz, pszb = k.ps()
                for i in range(4):
                    p = g_ * 4 + i
                    P.op("pe", lambda e, o=psz[:, i * 64:(i + 1) * 64], w=ATbd[:, p, :], r=Hstk[:, p, :]: e.matmul(o, lhsT=w, rhs=r, start=True, stop=False),
                         reads=[inb[g_], hb_], writes=[pszb])
                    P.op("pe", lambda e, o=psz[:, i * 64:(i + 1) * 64], w=MakT[:, p, :], r=Vstk[:, p, :]: e.matmul(o, lhsT=w, rhs=r, start=False, stop=True),
                         reads=[b_["Mak"], b_["V"]], writes=[pszb])
                P.op("act", lambda e, o=Zs[:, pg, :], i=psz[:, 0:256].rearrange("p (a b) -> p a b", a=4): e.activation(out=o, in_=i, func=AF.Copy),
                     reads=[pszb], writes=[b_["Z"]])
            for g_ in range(4):
                pg = slice(g_ * 4, g_ * 4 + 4)
                b_ = gb[g_]
                psu, psub = k.ps()
                for i in range(4):
                    p = g_ * 4 + i
                    P.op("pe", lambda e, o=psu[:, i * 64:(i + 1) * 64], w=PF[:, p, :], r=Zs[:, p, :]: e.matmul(o, lhsT=w, rhs=r, start=True, stop=True),
                         reads=[b_["P"], b_["Z"]], writes=[psub])
                psuv = psu[:, 0:256].rearrange("p (a b) -> p a b", a=4)
                P.op("act", lambda e, o=Us[:, pg, :], i=psuv: e.activation(out=o, in_=i, func=AF.Copy), reads=[psub], writes=[b_["U"]])
                for hf in range(2):
                    sl = slice(64 * hf, 64 * hf + 64)
                    P.op("dve", lambda e, o=Ubd[sl, pg, 64 * hf:64 * hf + 64], i=psuv[sl]: e.tensor_copy(out=o, in_=i), reads=[psub, zb], writes=[b_["U"]])
            for g_ in range(4):
                pg = slice(g_ * 4, g_ * 4 + 4)
                b_ = gb[g_]
                hb_ = Hb_[g_]
                psy, psyb = k.ps()
                psh, pshb = k.ps()
                for i in range(4):
                    p = g_ * 4 + i
                    oy = psy[:, i * 64:(i + 1) * 64]
                    P.op("pe", lambda e, o=oy, w=Hbd[:, p, :], r=RT[:, p, cc]: e.matmul(o, lhsT=w, rhs=r, start=True, stop=False),
                         reads=[hb_, RTb], writes=[psyb])
                    P.op("pe", lambda e, o=oy, w=Ubd[:, p, :], r=MrbT[:, p, :]: e.matmul(o, lhsT=w, rhs=r, start=False, stop=False),
                         reads=[b_["U"], b_["Mrb"]], writes=[psyb])
                    P.op("pe", lambda e, o=oy, w=Vbd[:, p, :], r=MrkT[:, p, :]: e.matmul(o, lhsT=w, rhs=r, start=False, stop=True),
                         reads=[b_["V"], b_["Mrk"]], writes=[psyb])
                    oh = psh[:, i * 64:(i + 1) * 64]
                    P.op("pe", lambda e, o=oh, w=Bbd[:, p, :], r=Us[:, p, :]: e.matmul(o, lhsT=w, rhs=r, start=True, stop=False),
                         reads=[b_["B"], b_["U"]], writes=[pshb])
                    P.op("pe", lambda e, o=oh, w=Kbd[:, p, :], r=Vstk[:, p, :]: e.matmul(o, lhsT=w, rhs=r, start=False, stop=True),
                         reads=[b_["K"], b_["V"]], writes=[pshb])
                P.op("act", lambda e, o=YT[:, pg, cc], i=psy[:, 0:256].rearrange("p (a b) -> p a b", a=4): e.activation(out=o, in_=i, func=AF.Copy),
                     reads=[psyb], writes=[YTb])
                P.op("dve", lambda e, o=Hf[:, pg, :], i=psh[:, 0:256].rearrange("p (a b) -> p a b", a=4): e.tensor_tensor(out=o, in0=o, in1=i, op=ALU.add),
                     reads=[pshb, hb_], writes=[hb_])
                for i in range(4):
                    p = g_ * 4 + i
                    P.op("dve", lambda e, o=Hf[:, p, :], s=GC[:, p, ch:ch + 1]: e.tensor_scalar(out=o, in0=o, scalar1=s, scalar2=None, op0=ALU.mult),
                         reads=[hb_, GCb], writes=[hb_])
                P.op("pool", lambda e, o=Hstk[:, pg, :], i=Hf[:, pg, :]: e.tensor_copy(out=o, in_=i), reads=[hb_], writes=[hb_])
                for hf in range(2):
                    sl = slice(64 * hf, 64 * hf + 64)
                    P.op("pool", lambda e, o=Hbd[sl, pg, 64 * hf:64 * hf + 64], i=Hf[sl, pg, :]: e.tensor_copy(out=o, in_=i), reads=[hb_], writes=[hb_])

        P.barrier()
        OINb, Y2b, Xb = Buf(), Buf(), Buf()
        P.dma("sp", X, xT3[:, :, t0:t0 + TT], writes=[Xb])
        def gn_front(p):
            psm, psmb = k.ps()
            P.op("pe", lambda e, o=psm[:, 0:TT], r=YT[:, p, :]: e.matmul(o, lhsT=bo32[:], rhs=r, start=True, stop=True), reads=[YTb, cb_], writes=[psmb])
            yc, ycb = tmp("cl", 2)
            P.op("dve", lambda e, o=yc[:], m=psm[:, 0:TT], y=YT[:, p, :]: e.scalar_tensor_tensor(out=o, in0=m, scalar=-1.0 / 64, in1=y, op0=ALU.mult, op1=ALU.add),
                 reads=[psmb, YTb], writes=[ycb])
            ysq, ysqb = tmp("rn", 2)
            P.op("pool", lambda e, o=ysq[:], i=yc[:]: e.tensor_tensor(out=o, in0=i, in1=i, op=ALU.mult), reads=[ycb], writes=[ysqb])
            P.op("pe", lambda e, o=psm[:, TT:2 * TT], r=ysq[:]: e.matmul(o, lhsT=bo32[:], rhs=r, start=True, stop=True), reads=[ysqb, cb_], writes=[psmb])
            rs, rsb = tmp("kk", 2)
            P.op("act", lambda e, o=rs[:], i=psm[:, TT:2 * TT]: e.activation(out=o, in_=i, func=AF.Ln, bias=c["eps"][:, 1:2], scale=1.0 / 64),
                 reads=[psmb, c["onesb"]], writes=[rsb])
            P.op("act", lambda e, o=rs[:]: e.activation(out=o, in_=o, func=AF.Exp, scale=-0.5), reads=[rsb], writes=[rsb])
            return (p, yc, ycb, rs, rsb)

        def gn_back(st_):
            p, yc, ycb, rs, rsb = st_
            P.op("dve", lambda e, o=yc[:], r=rs[:]: e.tensor_tensor(out=o, in0=o, in1=r, op=ALU.mult), reads=[ycb, rsb], writes=[ycb])
            P.op("dve", lambda e, o=yc[:], s=LNW[:, p:p + 1], b=BON[:, p, :]: e.scalar_tensor_tensor(out=o, in0=o, scalar=s, in1=b, op0=ALU.mult, op1=ALU.add),
                 reads=[ycb, BONb, mb], writes=[ycb])
            P.op("dve", lambda e, o=OIN[:, p, :], i=yc[:], s=LNB[:, p:p + 1], g_=GT[:, p, :]: e.scalar_tensor_tensor(out=o, in0=i, scalar=s, in1=g_, op0=ALU.add, op1=ALU.mult),
                 reads=[ycb, GTb, mb], writes=[OINb])

        pend = []
        for p in range(16):
            pend.append(gn_front(p))
            if len(pend) > 1:
                gn_back(pend.pop(0))
        while pend:
            gn_back(pend.pop(0))
        if dbg:
            P.dma("sp", dyt, YT.rearrange("p a b -> p (a b)"), reads=[YTb])
            P.dma("sp", doin, OIN.rearrange("p a b -> p (a b)"), reads=[OINb])
        for eb in range(16):
            wb, wbb = wl.load(wo, 0, 16, eb * 128, 128)
            ps, psb = k.ps()
            for p in range(16):
                P.op("pe", lambda e, o=ps[:, :TT], w=wb[:, p, :], r=OIN[:, p, :], s=(p == 0), t=(p == 15):
                     e.matmul(o, lhsT=w, rhs=r, start=s, stop=t), reads=[wbb, OINb], writes=[psb])
            P.op("act", lambda e, o=Y2[:, eb, :], i=ps[:, :TT]: e.activation(out=o, in_=i, func=AF.Copy), reads=[psb], writes=[Y2b])
        post_residual(k, c, X, Xb, Y2, Y2b, G, mb, TT)
        P.dma("sp", oT3[:, :, t0:t0 + TT], X, reads=[Xb])
        if tt == NTT - 1:
            k.do_pre()
        P.barrier()
    k.close()
    return nc


def _pd(v, n=16):
    return np.ascontiguousarray(np.asarray(v, dtype=np.float32).reshape(n, 128).T)


def _run(nc, maps):
    res = run_bass_kernel_spmd(nc, maps, core_ids=list(range(len(maps))))
    return res.results


class _Root:
    pass


def build_fused():
    root = _Root()
    root.nc = bass.Bass("TRN2", target_bir_lowering=False)
    root.semstack = ExitStack()
    nc = root.nc
    xT = nc.dram_tensor("xT", [D, T], F32, kind="ExternalInput").ap()
    oT = nc.dram_tensor("oT", [D, T], F32, kind="ExternalOutput").ap()
    modsI = nc.dram_tensor("mods_i", [128, 192], F32, kind="Internal").ap()
    xa = nc.dram_tensor("xa_i", [D, T], F32, kind="Internal").ap()
    xb = nc.dram_tensor("xb_i", [D, T], F32, kind="Internal").ap()
    xc = nc.dram_tensor("xc_i", [D, T], F32, kind="Internal").ap()
    def decl(prefix, items):
        dm, pre = {}, []
        for name, shape in items:
            w = nc.dram_tensor(prefix + name, list(shape), F32, kind="ExternalInput").ap()
            wb = nc.dram_tensor(prefix + name + "_bf", list(shape), BF16, kind="Internal").ap()
            dm[name + "_bf"] = wb
            pre.append((wb, w))
        return dm, pre
    rw_dm, rw_pre = decl("r_", [("w_rkv", [3 * D, D]), ("w1", [D, 96]), ("w2", [96, D]), ("a1", [D, 96]), ("a2", [96, D]),
                                ("g1", [D, 256]), ("g2", [256, D]), ("w_o", [D, D])])
    f0_dm, f0_pre = decl("f0_", [("w_up", [D, FF]), ("w_dn", [FF, D])])
    f1_dm, f1_pre = decl("f1_", [("w_up", [D, FF]), ("w_dn", [FF, D])])
    a_dm, a_pre = decl("a_", [("kv_down", [D, 576]), ("kv_uk", [512, D]), ("kv_uv", [512, D]), ("w_dq", [D, 512]),
                              ("w_uq", [512, 16 * 192]), ("w_o", [D, D])])
    build_mods(env=(root, {"mods": modsI}, "m_", rw_pre))
    build_rwkv(0, env=(root, dict(rw_dm, xT=xT, mods=modsI, oT=xa), "r_", f0_pre + a_pre + f1_pre))
    build_mlp(0, env=(root, dict(f0_dm, xT=xa, mods=modsI, oT=xb), "f0_"))
    build_mla(1, env=(root, dict(a_dm, xT=xb, mods=modsI, oT=xc), "a_"))
    build_mlp(1, env=(root, dict(f1_dm, xT=xc, mods=modsI, oT=oT), "f1_"))
    root.semstack.close()
    return nc


def kernel(x, c, positions, ada_w, ada_b, norm_g, mlp_up, mlp_down,
           rw_mu, rw_rkv, rw_w0, rw_w1, rw_w2, rw_a0, rw_a1, rw_a2, rw_g1, rw_g2,
           rw_kk, rw_ka, rw_rk, rw_lnx, rw_o,
           mla_dq, mla_qnorm, mla_uq, mla_o,
           kv_in_g, kv_down, kv_norm, kv_uk, kv_uv):
    f32 = np.float32
    A = lambda a: np.ascontiguousarray(np.asarray(a))
    x = A(x).astype(f32, copy=False)
    B = x.shape[0]
    cc = A(c)
    pos = A(positions).astype(np.int32, copy=False)
    vl = [A(rw_mu)[0][j] for j in range(6)] + [A(rw_w0)[0], A(rw_a0)[0], A(rw_kk)[0], A(rw_ka)[0], A(rw_rk)[0].reshape(-1),
                                              A(rw_lnx)[0][0], A(rw_lnx)[0][1]]
    inv = (1.0 / (10000.0 ** (np.arange(0, 64, 2, dtype=np.float32) / 64))).astype(f32)
    shared = {
        "m_ada_w": A(ada_w),
        "m_ada_b_pd": np.ascontiguousarray(A(ada_b).reshape(2, 96, 128).transpose(2, 0, 1)),
        "m_norm_g_pd": np.ascontiguousarray(A(norm_g).reshape(2, 4, 16, 128).transpose(3, 0, 1, 2)),
        "r_rw_vec": np.ascontiguousarray(np.stack([_pd(v) for v in vl], 1)).astype(f32),
        "r_w_rkv": A(rw_rkv)[0].reshape(3 * D, D), "r_w1": A(rw_w1)[0], "r_w2": A(rw_w2)[0], "r_a1": A(rw_a1)[0],
        "r_a2": A(rw_a2)[0], "r_g1": A(rw_g1)[0], "r_g2": A(rw_g2)[0], "r_w_o": A(rw_o)[0],
        "f0_w_up": A(mlp_up)[0], "f0_w_dn": A(mlp_down)[0], "f1_w_up": A(mlp_up)[1], "f1_w_dn": A(mlp_down)[1],
        "a_ropec": np.ascontiguousarray(np.stack([np.concatenate([inv, inv]), np.concatenate([-np.ones(32), np.ones(32)])], 1)).astype(f32),
        "a_mla_vec": np.ascontiguousarray(np.concatenate([_pd(kv_in_g), _pd(kv_norm, 4), _pd(A(mla_qnorm)[0], 4)], 1)).astype(f32),
        "a_kv_down": A(kv_down), "a_kv_uk": A(kv_uk).reshape(512, D), "a_kv_uv": A(kv_uv).reshape(512, D),
        "a_w_dq": A(mla_dq)[0], "a_w_uq": A(mla_uq)[0].reshape(512, 16 * 192), "a_w_o": A(mla_o)[0],
    }
    maps = [dict(shared, xT=np.ascontiguousarray(x[b].T), m_c_pd=_pd(cc[b]),
                 a_posr=np.ascontiguousarray(np.broadcast_to(pos[b][None, :], (64, T)))) for b in range(B)]
    r = _run(build_fused(), maps)
    return np.stack([np.ascontiguousarray(r[b]["oT"].T) for b in range(B)]).astype(f32, copy=False)
```
